# Optimizing a Trainium2 kernel written in Bass

```python
import jax
import jax.numpy as jnp
from jax import lax
import numpy as np

D_MODEL = 1024
BATCH = 1
SEQ = 16384
DEPTH = 2

D_MIX = 1024
HG_HEADS = 4
HG_KDIM = 128
HG_VDIM = 128
HG_WIDTH = HG_HEADS * HG_VDIM
HG_CHUNK = 64
SW_HEADS = 4
SW_HDIM = 64
SW_WIDTH = SW_HEADS * SW_HDIM
SW_PATTERNS = ((128, 1), (512, 4), (2048, 16))
SW_BLOCK = 128
RK_HEADS = 4
RK_HDIM = 64
RK_WIDTH = RK_HEADS * RK_HDIM
RK_DECAY_LORA = 64
RK_AAA_LORA = 64
RK_MV_LORA = 32
RK_GATE_LORA = 128
RK_GN_EPS = 64e-5
HG_IN = 4 * HG_WIDTH
SW_IN = 3 * SW_WIDTH
RK_IN = 3 * RK_WIDTH + RK_DECAY_LORA + RK_AAA_LORA + RK_GATE_LORA
N_IN = HG_IN + SW_IN + RK_IN
D_FF = 2816
CONV_W = 3
PLE_DIM = 256
NORM_EPS = 1e-6

kernel_name = 'hymba_style_hgrn2_dilated_rwkv7_hybrid'


def rmsnorm(x, g):
    xf = x.astype(jnp.float32)
    y = xf * lax.rsqrt(jnp.mean(xf * xf, axis=-1, keepdims=True) + NORM_EPS)
    return (y * g.astype(jnp.float32)).astype(x.dtype)


def hgrn2_mix(q_raw, f_raw, i_raw, g_raw, lower_bound, gnorm_g):
    f32 = jnp.float32
    B_, S_, _ = q_raw.shape
    nc = S_ // HG_CHUNK
    q = jax.nn.silu(q_raw.astype(f32))
    f = lower_bound + (1.0 - lower_bound) * jax.nn.sigmoid(f_raw.astype(f32))
    k = 1.0 - f
    log_f = jnp.log(f)

    def chunks(t):
        return t.reshape(B_, nc, HG_CHUNK, HG_HEADS, -1).transpose(1, 0, 3, 2, 4)

    causal = jnp.tril(jnp.ones((HG_CHUNK, HG_CHUNK), dtype=bool))

    def step(state, inp):
        qc, kc, vc, lfc = inp
        b = jnp.cumsum(lfc, axis=2)
        b_last = b[:, :, -1:, :]
        o_inter = jnp.einsum('bhck,bhkv->bhcv', qc * jnp.exp(b), state)
        rel = jnp.where(causal[:, :, None], b[:, :, :, None, :] - b[:, :, None, :, :], -jnp.inf)
        scores = jnp.einsum('bhik,bhjk,bhijk->bhij', qc, kc, jnp.exp(rel))
        o_intra = jnp.einsum('bhij,bhjv->bhiv', scores, vc)
        state = (jnp.exp(b_last[:, :, 0, :])[..., None] * state
                 + jnp.einsum('bhck,bhcv->bhkv', kc * jnp.exp(b_last - b), vc))
        return state, o_inter + o_intra

    s0 = jnp.zeros((B_, HG_HEADS, HG_KDIM, HG_VDIM), f32)
    _, o = lax.scan(step, s0, (chunks(q), chunks(k), chunks(i_raw.astype(f32)), chunks(log_f)))
    o = o.transpose(1, 0, 3, 2, 4).reshape(B_, S_, HG_HEADS, HG_VDIM)
    o = o * lax.rsqrt(jnp.mean(o * o, axis=-1, keepdims=True) + NORM_EPS)
    o = o.reshape(B_, S_, HG_WIDTH) * gnorm_g.astype(f32) * jax.nn.silu(g_raw.astype(f32))
    return o.astype(q_raw.dtype)


def banded_causal_attention(q, k, v, n_back):
    f32 = jnp.float32
    N, L, H, Dh = q.shape
    blk = SW_BLOCK
    nb = -(-L // blk)
    padw = ((0, 0), (0, nb * blk - L), (0, 0), (0, 0))
    qb = jnp.pad(q.astype(f32), padw).reshape(N, nb, blk, H, Dh)
    kb = jnp.pad(k.astype(f32), padw).reshape(N, nb, blk, H, Dh)
    vb = jnp.pad(v.astype(f32), padw).reshape(N, nb, blk, H, Dh)

    def with_prev(t):
        prev = jnp.pad(t, ((0, 0), (1, 0), (0, 0), (0, 0), (0, 0)))[:, :-1]
        return jnp.concatenate([prev, t], axis=2)

    kw, vw = with_prev(kb), with_prev(vb)
    s = jnp.einsum('nbqhd,nbkhd->nbhqk', qb, kw) * (Dh ** -0.5)
    qi = jnp.arange(blk)[:, None]
    kj = jnp.arange(2 * blk)[None, :]
    dist = qi - kj + blk
    kpos = jnp.arange(nb)[:, None, None] * blk + kj[None] - blk
    valid = (dist >= 0)[None] & (dist <= n_back)[None] & (kpos >= 0)
    s = jnp.where(valid[None, :, None], s, -jnp.inf)
    lse = jax.nn.logsumexp(s, axis=-1)
    prob = jnp.exp(s - lse[..., None])
    o = jnp.einsum('nbhqk,nbkhd->nbqhd', prob, vw).reshape(N, nb * blk, H, Dh)[:, :L]
    lse = lse.transpose(0, 1, 3, 2).reshape(N, nb * blk, H)[:, :L]
    return o, lse


def stride_gather(t, dil):
    B_, S_ = t.shape[:2]
    rest = t.shape[2:]
    return t.reshape((B_, S_ // dil, dil) + rest).swapaxes(1, 2).reshape((B_ * dil, S_ // dil) + rest)


def stride_scatter(t, B_, dil):
    L = t.shape[1]
    rest = t.shape[2:]
    return t.reshape((B_, dil, L) + rest).swapaxes(1, 2).reshape((B_, L * dil) + rest)


def dilated_window_mix(q_raw, k_raw, v_raw):
    B_, S_, _ = q_raw.shape
    q, k, v = (t.reshape(B_, S_, SW_HEADS, SW_HDIM) for t in (q_raw, k_raw, v_raw))
    outs, lses = [], []
    for window, dil in SW_PATTERNS:
        o, lse = banded_causal_attention(stride_gather(q, dil), stride_gather(k, dil),
                                         stride_gather(v, dil), window // dil)
        outs.append(stride_scatter(o, B_, dil))
        lses.append(stride_scatter(lse, B_, dil))
    wts = jax.nn.softmax(jnp.stack(lses, 0), axis=0)
    o = jnp.sum(wts[..., None] * jnp.stack(outs, 0), axis=0)
    return o.reshape(B_, S_, SW_WIDTH).astype(q_raw.dtype)


def rwkv7_mix(c, mu, w0, w2, a0, a2, g2, k_k, k_a, r_k, ln_w, ln_b, v_first, v_res):
    f32 = jnp.float32
    B_, S_, _ = c.shape
    cf = c.astype(f32)
    c_prev = jnp.pad(cf, ((0, 0), (1, 0), (0, 0)))[:, :-1]
    cm = cf + (c_prev - cf) * mu
    splits = [RK_WIDTH, 2 * RK_WIDTH, 3 * RK_WIDTH, 3 * RK_WIDTH + RK_DECAY_LORA,
              3 * RK_WIDTH + RK_DECAY_LORA + RK_AAA_LORA]
    r, k, v, wd, ad, gd = jnp.split(cm, splits, axis=-1)
    w = -jax.nn.softplus(-(w0 + jnp.tanh(wd) @ w2)) - 0.5
    decay = jnp.exp(-jnp.exp(w))
    a = jax.nn.sigmoid(a0 + ad @ a2)
    g = jax.nn.sigmoid(gd) @ g2
    if v_res is None:
        v_first = v
    else:
        v0, v1, v2 = v_res
        v = v + (v_first - v) * jax.nn.sigmoid(v0 + (v @ v1) @ v2)

    def heads(t):
        return t.reshape(B_, S_, RK_HEADS, RK_HDIM)

    kk = heads(k * k_k)
    kk = kk / jnp.maximum(jnp.linalg.norm(kk, axis=-1, keepdims=True), 1e-12)
    k = k * (1.0 + (a - 1.0) * k_a)
    rh, kh, vh, wh, ah = heads(r), heads(k), heads(v), heads(decay), heads(a)

    def step(state, inp):
        r_t, w_t, k_t, v_t, kk_t, a_t = inp
        sa = jnp.einsum('bhvk,bhk->bhv', state, -kk_t)
        state = (state * w_t[:, :, None, :] + sa[..., None] * (kk_t * a_t)[:, :, None, :]
                 + v_t[..., None] * k_t[:, :, None, :])
        return state, jnp.einsum('bhvk,bhk->bhv', state, r_t)

    s0 = jnp.zeros((B_, RK_HEADS, RK_HDIM, RK_HDIM), f32)
    xs = (rh.swapaxes(0, 1), wh.swapaxes(0, 1), kh.swapaxes(0, 1), vh.swapaxes(0, 1),
          kk.swapaxes(0, 1), ah.swapaxes(0, 1))
    _, y = lax.scan(step, s0, xs)
    y = y.swapaxes(0, 1)
    mean = jnp.mean(y, axis=-1, keepdims=True)
    var = jnp.mean((y - mean) ** 2, axis=-1, keepdims=True)
    y = ((y - mean) * lax.rsqrt(var + RK_GN_EPS)).reshape(B_, S_, RK_WIDTH) * ln_w + ln_b
    bonus = jnp.sum(rh * kh * r_k, axis=-1, keepdims=True) * vh
    y = (y + bonus.reshape(B_, S_, RK_WIDTH)) * g
    return y.astype(c.dtype), v_first


def conv_glu_ffn(hn, w_up, conv_w, conv_b, w_down):
    u = hn @ w_up
    S_ = u.shape[1]
    up = jnp.pad(u, ((0, 0), (CONV_W - 1, 0), (0, 0)))
    uc = conv_b + conv_w[0] * up[:, 0:S_]
    for j in range(1, CONV_W):
        uc = uc + conv_w[j] * up[:, j:j + S_]
    gate, val = jnp.split(uc, 2, axis=-1)
    return (jax.nn.silu(gate) * val) @ w_down


def setup_inputs(seed: int = 0) -> dict:
    key = jax.random.key(seed)
    ks = iter(jax.random.split(key, 40))

    def nrm(shape, scale):
        return jax.random.normal(next(ks), shape, jnp.float32) * scale

    def unif(shape, lo, hi):
        return jax.random.uniform(next(ks), shape, jnp.float32, lo, hi)

    L = DEPTH
    return {
        'x': nrm((BATCH, SEQ, D_MODEL), 1.0),
        'p': nrm((DEPTH, BATCH, SEQ, PLE_DIM), 1.0),
        'w_in': nrm((L, D_MODEL, N_IN), D_MODEL ** -0.5),
        'w_out': nrm((L, D_MIX, D_MODEL), D_MIX ** -0.5),
        'norm_mix_g': 1.0 + nrm((L, D_MODEL), 0.02),
        'norm_ffn_g': 1.0 + nrm((L, D_MODEL), 0.02),
        'norm_ple_g': 1.0 + nrm((L, D_MODEL), 0.02),
        'final_norm_g': 1.0 + nrm((D_MODEL,), 0.02),
        'hgrn_lower_bounds': nrm((L, HG_HEADS * HG_KDIM), 0.5),
        'hgrn_gnorm_g': 1.0 + nrm((L, HG_WIDTH), 0.02),
        'rwkv_mu': unif((L, RK_IN), 0.0, 1.0),
        'rwkv_w0': unif((L, RK_WIDTH), -5.0, -0.5),
        'rwkv_w2': nrm((L, RK_DECAY_LORA, RK_WIDTH), 0.1 * RK_DECAY_LORA ** -0.5),
        'rwkv_a0': nrm((L, RK_WIDTH), 0.1),
        'rwkv_a2': nrm((L, RK_AAA_LORA, RK_WIDTH), 0.1 * RK_AAA_LORA ** -0.5),
        'rwkv_g2': nrm((L, RK_GATE_LORA, RK_WIDTH), RK_GATE_LORA ** -0.5),
        'rwkv_k_k': 0.85 + nrm((L, RK_WIDTH), 0.05),
        'rwkv_k_a': 1.0 + nrm((L, RK_WIDTH), 0.05),
        'rwkv_r_k': nrm((L, RK_HEADS, RK_HDIM), 0.1),
        'rwkv_ln_w': 1.0 + nrm((L, RK_WIDTH), 0.02),
        'rwkv_ln_b': nrm((L, RK_WIDTH), 0.02),
        'rwkv_v0': 1.0 + nrm((L - 1, RK_WIDTH), 0.1),
        'rwkv_v1': nrm((L - 1, RK_WIDTH, RK_MV_LORA), 0.1 * RK_WIDTH ** -0.5),
        'rwkv_v2': nrm((L - 1, RK_MV_LORA, RK_WIDTH), 0.1 * RK_MV_LORA ** -0.5),
        'ffn_up': nrm((L, D_MODEL, 2 * D_FF), D_MODEL ** -0.5),
        'ffn_conv_w': nrm((L, CONV_W, 2 * D_FF), CONV_W ** -0.5),
        'ffn_conv_b': nrm((L, 2 * D_FF), 0.02),
        'ffn_down': nrm((L, D_FF, D_MODEL), D_FF ** -0.5),
        'ple_proj': nrm((L, PLE_DIM, D_MODEL), PLE_DIM ** -0.5),
        'ple_gate': nrm((L, D_MODEL, D_MODEL), D_MODEL ** -0.5),
    }


def reference(x, p, w_in, w_out, norm_mix_g, norm_ffn_g, norm_ple_g, final_norm_g,
              hgrn_lower_bounds, hgrn_gnorm_g, rwkv_mu, rwkv_w0, rwkv_w2, rwkv_a0, rwkv_a2,
              rwkv_g2, rwkv_k_k, rwkv_k_a, rwkv_r_k, rwkv_ln_w, rwkv_ln_b, rwkv_v0, rwkv_v1,
              rwkv_v2, ffn_up, ffn_conv_w, ffn_conv_b, ffn_down, ple_proj, ple_gate):
    lbs = jax.nn.softmax(hgrn_lower_bounds.astype(jnp.float32), axis=0)
    lbs = jnp.cumsum(lbs, axis=0) - lbs[0]
    splits = [HG_WIDTH, 2 * HG_WIDTH, 3 * HG_WIDTH, 4 * HG_WIDTH,
              HG_IN + SW_WIDTH, HG_IN + 2 * SW_WIDTH, HG_IN + 3 * SW_WIDTH]
    h = x
    v_first = None
    for l in range(DEPTH):
        z = rmsnorm(h, norm_mix_g[l]) @ w_in[l]
        a_q, a_f, a_i, a_g, b_q, b_k, b_v, c_in = jnp.split(z, splits, axis=-1)
        o_a = hgrn2_mix(a_q, a_f, a_i, a_g, lbs[l], hgrn_gnorm_g[l])
        o_b = dilated_window_mix(b_q, b_k, b_v)
        v_res = None if l == 0 else (rwkv_v0[l - 1], rwkv_v1[l - 1], rwkv_v2[l - 1])
        o_c, v_first = rwkv7_mix(c_in, rwkv_mu[l], rwkv_w0[l], rwkv_w2[l], rwkv_a0[l], rwkv_a2[l],
                                 rwkv_g2[l], rwkv_k_k[l], rwkv_k_a[l], rwkv_r_k[l], rwkv_ln_w[l],
                                 rwkv_ln_b[l], v_first, v_res)
        h = h + jnp.concatenate([o_a, o_b.astype(o_a.dtype), o_c.astype(o_a.dtype)], axis=-1) @ w_out[l]
        h = h + conv_glu_ffn(rmsnorm(h, norm_ffn_g[l]), ffn_up[l], ffn_conv_w[l], ffn_conv_b[l], ffn_down[l])
        gate = jax.nn.sigmoid(rmsnorm(h, norm_ple_g[l]) @ ple_gate[l])
        h = h + gate * (p[l] @ ple_proj[l])
    return rmsnorm(h, final_norm_g)
```

```python
import numpy as np
import ml_dtypes
from concourse.bass_utils import run_bass_kernel_spmd


import concourse.bass as bass
import concourse.mybir as mybir

F32 = mybir.dt.float32
BF16 = mybir.dt.bfloat16
AF = mybir.ActivationFunctionType
ALU = mybir.AluOpType
AX = mybir.AxisListType

ENGS = ("pe", "act", "dve", "pool", "sp")


class T:
    __slots__ = ("name", "h", "last_w", "readers", "sem", "cnt")

    def __init__(self, name, h):
        self.name = name
        self.h = h
        self.last_w = {}
        self.readers = []
        self.sem = None
        self.cnt = 0

    def __getitem__(self, idx):
        return self.h[idx]


class _Rec:
    def __getattr__(self, name):
        def f(*a, **k):
            self.call = (name, a, k)
        return f


def _eager(fn):
    r = _Rec()
    fn(r)
    name, a, k = r.call
    return lambda e: getattr(e, name)(*a, **k)


class Prog:
    def __init__(self, nc):
        self.nc = nc
        self.ops = {e: [] for e in ENGS}
        self.count = {e: 0 for e in ENGS}
        self.waited = {e: {} for e in ENGS}
        self.dma_sems = []
        self.ctx = []
        self.ntiles = 0

    def sbuf(self, name, shape, dt):
        g = self.nc.sbuf_tensor(name, list(shape), dt)
        h = g.__enter__()
        self.ctx.append(g)
        return T(name, h)

    def psum(self, name, shape, dt=F32):
        g = self.nc.psum_tensor(name, list(shape), dt)
        h = g.__enter__()
        self.ctx.append(g)
        return T(name, h)

    def dram(self, name, shape, dt, kind="Internal"):
        h = self.nc.dram_tensor(name, list(shape), dt, kind=kind)
        return T(name, h.ap() if hasattr(h, "ap") else h)

    def view(self, name, h):
        return T(name, h)

    def _deps(self, eng, reads, writes):
        deps = []
        for t in reads:
            for ev in t.last_w.items():
                deps.append((ev, "raw"))
        for t in writes:
            if not getattr(self, "_disjoint", False):
                for ev in t.last_w.items():
                    deps.append((ev, "waw"))
            for r in t.readers:
                deps.append((r, "war"))
        out = {}
        for (key, val), kind in deps:
            if key == eng:
                if eng == "pe":
                    continue
                if kind == "war":
                    continue
            if out.get(key, 0) < val:
                out[key] = val
        res = []
        w = self.waited[eng]
        for key, val in out.items():
            if w.get(key, 0) >= val:
                continue
            w[key] = val
            res.append((key, val))
        return res

    def op(self, eng, fn, reads=(), writes=()):
        waits = self._deps(eng, reads, writes)
        self.count[eng] += 1
        ev = (eng, self.count[eng])
        for t in reads:
            t.readers.append(ev)
        for t in writes:
            t.last_w = {ev[0]: ev[1]}
            t.readers = []
        self.ops[eng].append((waits, _eager(fn), None))

    def dma(self, eng, out_t, out_ap, in_t, in_ap, owner=None, disjoint=False, **kw):
        self._disjoint = disjoint
        waits = self._deps(eng, [in_t], [out_t])
        self._disjoint = False
        ow = owner if owner is not None else out_t
        if ow.sem is None:
            g = self.nc.semaphore("ds%d" % len(self.dma_sems))
            ow.sem = g.__enter__()
            self.ctx.append(g)
            self.dma_sems.append(ow.sem)
        ow.cnt += 16
        ev = (ow.sem, ow.cnt)
        in_t.readers.append(ev)
        if disjoint:
            out_t.last_w[ev[0]] = ev[1]
        else:
            out_t.last_w = {ev[0]: ev[1]}
            out_t.readers = []

        def fn(e, out_ap=out_ap, in_ap=in_ap, kw=kw):
            return e.dma_start(out=out_ap, in_=in_ap, **kw)
        self.ops[eng].append((waits, fn, ow.sem))

    def final_wait(self, eng, tiles):
        waits = self._deps(eng, tiles, [])
        self.ops[eng].append((waits, None, None))

    def emit(self):
        nc = self.nc
        esem = {}
        for e in ENGS:
            g = nc.semaphore("es_" + e)
            esem[e] = g.__enter__()
            self.ctx.append(g)
        engobj = {"pe": "tensor", "act": "scalar", "dve": "vector", "pool": "gpsimd", "sp": "sync"}

        def run(e, eng):
            for waits, fn, dsem in self.ops[e]:
                for key, val in waits:
                    s = esem[key] if isinstance(key, str) else key
                    eng.wait_ge(s, val)
                if fn is None:
                    continue
                ins = fn(eng)
                if dsem is not None:
                    ins.then_inc(dsem, 16)
                else:
                    ins.then_inc(esem[e], 1)

        with nc.Block() as block:
            for e in ENGS:
                if not self.ops[e]:
                    continue
                getattr(block, engobj[e])(lambda eng, e=e: run(e, eng))

    def close(self):
        for g in reversed(self.ctx):
            g.__exit__(None, None, None)
        self.ctx = []


A_NT = 2048
A_TG = 512
A_NG = A_NT // A_TG
FM_QS, FM_LF, FM_GS, FM_BQ, FM_BK = 0, 512, 1024, 1536, 1792
FM_R, FM_KP, FM_KK, FM_A, FM_LD, FM_V, FM_G = [2048 + 256 * i for i in range(7)]
FM_ROWS = 3840
PP_GMIX, PP_MU, PP_W0, PP_A0, PP_KK, PP_KA, PP_V0, PP_HB0, PP_HB1, PP_N = 0, 8, 16, 18, 20, 22, 24, 26, 30, 34
PM_W2, PM_A2, PM_G2, PM_V1, PM_V2, PM_N = 0, 256, 512, 768, 832, 1088


def build_A(layer):
    nc = bass.Bass("TRN2", target_bir_lowering=False)
    P = Prog(nc)
    h = P.dram("h", [A_NT, 1024], F32, "ExternalInput")
    hh = P.dram("hh", [128, 1024], F32, "ExternalInput")
    w_in = P.dram("w_in", [1024, 3840], F32, "ExternalInput")
    pp_d = P.dram("pp", [128, PP_N], F32, "ExternalInput")
    pm_d = P.dram("pm", [128, PM_N], F32, "ExternalInput")
    id_d = P.dram("ident", [128, 128], BF16, "ExternalInput")
    blk_d = P.dram("blk64", [128, 128], BF16, "ExternalInput")
    if layer == 1:
        vf_d = P.dram("vfirst", [256, A_NT], F32, "ExternalInput")
    fm = P.dram("fm", [FM_ROWS, A_NT], F32, "ExternalOutput")
    tm = P.dram("tm", [A_NT, 768], F32, "ExternalOutput")

    wbf = [P.sbuf("wbf%d" % k, [128, 3840], BF16) for k in range(8)]
    wst = [P.sbuf("wst%d" % i, [128, 1920], F32) for i in range(2)]
    pp = P.sbuf("pp_s", [128, PP_N], F32)
    pm32 = P.sbuf("pm32", [128, PM_N], F32)
    pm = P.sbuf("pm_s", [128, PM_N], BF16)
    ident = P.sbuf("ident_s", [128, 128], BF16)
    blk = P.sbuf("blk_s", [128, 128], BF16)
    hin = [P.sbuf("hin%d" % i, [128, 1024], F32) for i in range(2)]
    hsq = P.sbuf("hsq", [128, 1024], F32)
    hnb = [P.sbuf("hnb%d" % i, [128, 1024], BF16) for i in range(2)]
    st = [P.sbuf("st%d" % i, [128, 4], F32) for i in range(2)]
    hnT = [P.sbuf("hnT%d" % i, [128, 8, A_TG], BF16) for i in range(2)]
    gb = P.sbuf("gb", [128, 8, 128], F32)
    CB = [P.sbuf("CB%d" % i, [128, 8, A_TG + 1], F32) for i in range(2)]
    stg = [P.sbuf("stg%d" % i, [128, A_TG], F32) for i in range(6)]
    stt = [P.sbuf("stt%d" % i, [128, 768], F32) for i in range(2)]
    cm = P.sbuf("cm", [128, 8, A_TG], F32)
    tmpA = [P.sbuf("tmpA%d" % i, [128, A_TG], F32) for i in range(4)]
    tb = [P.sbuf("tb%d" % i, [128, A_TG], BF16) for i in range(4)]
    lbc = P.sbuf("lbc", [128, 8], F32)
    kac = P.sbuf("kac", [128, 2], F32)
    epsc = P.sbuf("epsc", [128, 1], F32)
    vfs = P.sbuf("vfs", [128, 2, A_TG], F32) if layer == 1 else None
    ps = [P.psum("ps%d" % i, [128, 512], F32) for i in range(6)]
    pst = P.psum("pst", [128, 1024], BF16)
    psm = P.psum("psm", [128, 512], F32)

    P.dma("sp", pp, pp[:], pp_d, pp_d[:])
    P.dma("sp", pm32, pm32[:], pm_d, pm_d[:])
    P.dma("sp", ident, ident[:], id_d, id_d[:])
    P.dma("sp", blk, blk[:], blk_d, blk_d[:])
    P.op("dve", lambda e: e.tensor_copy(out=pm[:], in_=pm32[:]), [pm32], [pm])
    P.op("dve", lambda e: e.memset(epsc[:], 1e-6), [], [epsc])
    for c in range(8):
        P.op("dve", lambda e, c=c: e.memset(gb[:, c, :], 1.0), [], [gb])
    for c in range(8):
        P.op("dve", lambda e, c=c: e.tensor_scalar(out=gb[:, c, :], in0=gb[:, c, :], scalar1=pp[:, PP_GMIX + c:PP_GMIX + c + 1],
                                                    scalar2=None, op0=ALU.mult), [gb, pp], [gb])
    P.op("dve", lambda e: e.tensor_scalar(out=kac[:], in0=pp[:, PP_KA:PP_KA + 2], scalar1=-1.0, scalar2=1.0,
                                          op0=ALU.mult, op1=ALU.add), [pp], [kac])
    if layer == 1:
        P.op("dve", lambda e: e.tensor_tensor(out=lbc[:, 0:4], in0=pp[:, PP_HB1:PP_HB1 + 4], in1=pp[:, PP_HB0:PP_HB0 + 4],
                                              op=ALU.subtract), [pp], [lbc])
        P.op("act", lambda e: e.activation(out=lbc[:, 0:4], in_=lbc[:, 0:4], func=AF.Sigmoid), [lbc], [lbc])
        P.op("dve", lambda e: e.tensor_scalar(out=lbc[:, 4:8], in0=lbc[:, 0:4], scalar1=-1.0, scalar2=1.0,
                                              op0=ALU.mult, op1=ALU.add), [lbc], [lbc])
    for k in range(8):
        for hf in range(2):
            s_ = wst[hf]
            P.dma("sp" if hf == 0 else "act", s_, s_[:], w_in, w_in[k * 128:(k + 1) * 128, hf * 1920:(hf + 1) * 1920])
            P.op("pool", lambda e, k=k, s_=s_, hf=hf: e.tensor_copy(out=wbf[k][:, hf * 1920:(hf + 1) * 1920], in_=s_[:]), [s_], [wbf[k]])

    outq = ["sp", "act", "pool"]
    oq = [0]

    def out_dma(dst_t, dst_ap, src_t, src_ap):
        q = outq[oq[0] % 3]
        oq[0] += 1
        P.dma(q, dst_t, dst_ap, src_t, src_ap, owner=src_t, disjoint=True)

    tcount = [0]

    def norm_tile(src_ap, dstT, col0):
        i = tcount[0] % 2
        tcount[0] += 1
        hi, hb, s_ = hin[i], hnb[i], st[i]
        P.dma("sp", hi, hi[:], h, src_ap)
        P.op("act", lambda e: e.activation(out=hsq[:], in_=hi[:], func=AF.Square, accum_out=s_[:, 0:1]), [hi], [hsq, s_])
        P.op("act", lambda e: e.activation(out=s_[:, 1:2], in_=s_[:, 0:1], func=AF.Sqrt, scale=1.0 / 1024, bias=epsc[:, 0:1]),
             [s_, epsc], [s_])
        P.op("dve", lambda e: e.reciprocal(out=s_[:, 2:3], in_=s_[:, 1:2]), [s_], [s_])
        P.op("dve", lambda e: e.tensor_scalar(out=hb[:], in0=hi[:], scalar1=s_[:, 2:3], scalar2=None, op0=ALU.mult),
             [hi, s_], [hb])
        for c in range(8):
            P.op("pe", lambda e, c=c: e.transpose(out=pst[:, c * 128:(c + 1) * 128], in_=hb[:, c * 128:(c + 1) * 128],
                                                   identity=ident[:]), [hb, ident], [pst])
        P.op("dve", lambda e: e.tensor_tensor(out=dstT[:, :, col0:col0 + 128],
                                              in0=pst[:].rearrange("p (c t) -> p c t", c=8), in1=gb[:], op=ALU.mult),
             [pst, gb], [dstT])

    def mm_fm(dst_ps, cc, src):
        for k in range(8):
            P.op("pe", lambda e, k=k: e.matmul(dst_ps[:], lhsT=wbf[k][:, cc * 128:(cc + 1) * 128], rhs=src[:, k, :],
                                                start=(k == 0), stop=(k == 7)), [wbf[k], src], [dst_ps])

    hT_h = hnT[1]
    P.hsrc = hh
    i0 = tcount[0]
    hi, hb, s_ = hin[0], hnb[0], st[0]
    tcount[0] += 1
    P.dma("sp", hi, hi[:], hh, hh[:])
    P.op("act", lambda e: e.activation(out=hsq[:], in_=hi[:], func=AF.Square, accum_out=s_[:, 0:1]), [hi], [hsq, s_])
    P.op("act", lambda e: e.activation(out=s_[:, 1:2], in_=s_[:, 0:1], func=AF.Sqrt, scale=1.0 / 1024, bias=epsc[:, 0:1]),
         [s_, epsc], [s_])
    P.op("dve", lambda e: e.reciprocal(out=s_[:, 2:3], in_=s_[:, 1:2]), [s_], [s_])
    P.op("dve", lambda e: e.tensor_scalar(out=hb[:], in0=hi[:], scalar1=s_[:, 2:3], scalar2=None, op0=ALU.mult), [hi, s_], [hb])
    for c in range(8):
        P.op("pe", lambda e, c=c: e.transpose(out=pst[:, c * 128:(c + 1) * 128], in_=hb[:, c * 128:(c + 1) * 128],
                                               identity=ident[:]), [hb, ident], [pst])
    P.op("dve", lambda e: e.tensor_tensor(out=hT_h[:, :, 0:128], in0=pst[:].rearrange("p (c t) -> p c t", c=8),
                                          in1=gb[:], op=ALU.mult), [pst, gb], [hT_h])
    for c8 in range(8):
        cc = 22 + c8
        pz = ps[c8 % 6]
        for k in range(8):
            P.op("pe", lambda e, k=k, cc=cc, pz=pz: e.matmul(pz[:, 0:128], lhsT=wbf[k][:, cc * 128:(cc + 1) * 128],
                                                              rhs=hT_h[:, k, 0:128], start=(k == 0), stop=(k == 7)),
                 [wbf[k], hT_h], [pz])
        P.op("act", lambda e, c8=c8, pz=pz: e.activation(out=CB[0][:, c8, 0:1], in_=pz[:, 127:128], func=AF.Copy), [pz], [CB[0]])

    sti = [0]

    def stage():
        s_ = stg[sti[0] % 6]
        sti[0] += 1
        return s_

    psi = [0]

    def nps():
        p_ = ps[psi[0] % 6]
        psi[0] += 1
        return p_

    for g in range(A_NG):
        hT = hnT[g % 2]
        cb = CB[g % 2]
        cbn = CB[(g + 1) % 2]
        t0 = g * A_TG
        for t in range(4):
            norm_tile(h[t0 + t * 128:t0 + (t + 1) * 128, :], hT, t * 128)
        for c in range(4):
            pz = nps(); mm_fm(pz, c, hT); s_ = stage()
            P.op("act", lambda e, pz=pz, s_=s_: e.activation(out=s_[:], in_=pz[:], func=AF.Silu), [pz], [s_])
            out_dma(fm, fm[FM_QS + c * 128:FM_QS + (c + 1) * 128, t0:t0 + A_TG], s_, s_[:])
        for c in range(4):
            pz = nps(); mm_fm(pz, 4 + c, hT); s_ = stage()
            P.op("act", lambda e, pz=pz, s_=s_: e.activation(out=s_[:], in_=pz[:], func=AF.Sigmoid), [pz], [s_])
            if layer == 1:
                P.op("dve", lambda e, s_=s_, c=c: e.tensor_scalar(out=s_[:], in0=s_[:], scalar1=lbc[:, 4 + c:5 + c],
                                                                    scalar2=lbc[:, c:c + 1], op0=ALU.mult, op1=ALU.add),
                     [s_, lbc], [s_])
            P.op("act", lambda e, s_=s_: e.activation(out=s_[:], in_=s_[:], func=AF.Ln), [s_], [s_])
            out_dma(fm, fm[FM_LF + c * 128:FM_LF + (c + 1) * 128, t0:t0 + A_TG], s_, s_[:])
        for c in range(4):
            pz = nps(); mm_fm(pz, 12 + c, hT); s_ = stage()
            P.op("act", lambda e, pz=pz, s_=s_: e.activation(out=s_[:], in_=pz[:], func=AF.Silu), [pz], [s_])
            out_dma(fm, fm[FM_GS + c * 128:FM_GS + (c + 1) * 128, t0:t0 + A_TG], s_, s_[:])
        for c in range(4):
            pz = nps(); mm_fm(pz, 16 + c, hT); s_ = stage()
            P.op("dve", lambda e, pz=pz, s_=s_: e.tensor_copy(out=s_[:], in_=pz[:]), [pz], [s_])
            out_dma(fm, fm[FM_BQ + c * 128:FM_BQ + (c + 1) * 128, t0:t0 + A_TG], s_, s_[:])
        for t in range(4):
            pz = nps(); pz2 = nps(); s_ = stt[t % 2]
            for k in range(8):
                P.op("pe", lambda e, k=k, t=t, pz=pz: e.matmul(pz[:], lhsT=hT[:, k, t * 128:(t + 1) * 128],
                                                                rhs=wbf[k][:, 1024:1536], start=(k == 0), stop=(k == 7)),
                     [wbf[k], hT], [pz])
            for k in range(8):
                P.op("pe", lambda e, k=k, t=t, pz2=pz2: e.matmul(pz2[:, 0:256], lhsT=hT[:, k, t * 128:(t + 1) * 128],
                                                                  rhs=wbf[k][:, 2560:2816], start=(k == 0), stop=(k == 7)),
                     [wbf[k], hT], [pz2])
            P.op("dve", lambda e, pz=pz, s_=s_: e.tensor_copy(out=s_[:, 0:512], in_=pz[:]), [pz], [s_])
            P.op("act", lambda e, pz2=pz2, s_=s_: e.activation(out=s_[:, 512:768], in_=pz2[:, 0:256], func=AF.Copy), [pz2], [s_])
            out_dma(tm, tm[t0 + t * 128:t0 + (t + 1) * 128, :], s_, s_[:])
        for c8 in range(8):
            pz = nps(); mm_fm(pz, 22 + c8, hT)
            if c8 % 2 == 0:
                P.op("dve", lambda e, pz=pz, c8=c8: e.tensor_copy(out=cb[:, c8, 1:A_TG + 1], in_=pz[:]), [pz], [cb])
            else:
                P.op("act", lambda e, pz=pz, c8=c8: e.activation(out=cb[:, c8, 1:A_TG + 1], in_=pz[:], func=AF.Copy), [pz], [cb])
        P.op("pool", lambda e: e.tensor_copy(out=cbn[:, :, 0:1], in_=cb[:, :, A_TG:A_TG + 1]), [cb], [cbn])
        for c8 in range(8):
            ta = tmpA[c8 % 2]
            eng = "dve"
            P.op(eng, lambda e, c8=c8, ta=ta: e.tensor_tensor(out=ta[:], in0=cb[:, c8, 0:A_TG], in1=cb[:, c8, 1:A_TG + 1],
                                                              op=ALU.subtract), [cb], [ta])
            P.op(eng, lambda e, c8=c8, ta=ta: e.scalar_tensor_tensor(out=cm[:, c8, :], in0=ta[:], scalar=pp[:, PP_MU + c8:PP_MU + c8 + 1],
                                                                     in1=cb[:, c8, 1:A_TG + 1], op0=ALU.mult, op1=ALU.add),
                 [ta, pp, cb], [cm])
        for c in range(2):
            out_dma(fm, fm[FM_R + c * 128:FM_R + (c + 1) * 128, t0:t0 + A_TG], cm, cm[:, c, :])
        P.op("act", lambda e: e.activation(out=tb[0][0:64, :], in_=cm[0:64, 6, :], func=AF.Tanh), [cm], [tb[0]])
        P.op("dve", lambda e: e.tensor_copy(out=tb[0][64:128, :], in_=cm[64:128, 6, :]), [cm], [tb[0]])
        P.op("act", lambda e: e.activation(out=tb[1][:], in_=cm[:, 7, :], func=AF.Sigmoid), [cm], [tb[1]])
        E05 = float(np.exp(-0.5))
        for c in range(2):
            pz = nps(); s_ = stage()
            P.op("pe", lambda e, pz=pz, c=c: e.matmul(pz[:], lhsT=pm[0:64, PM_W2 + c * 128:PM_W2 + (c + 1) * 128],
                                                       rhs=tb[0][0:64, :], start=True, stop=True), [pm, tb[0]], [pz])
            P.op("act", lambda e, pz=pz, s_=s_, c=c: e.activation(out=s_[:], in_=pz[:], func=AF.Sigmoid,
                                                                   bias=pp[:, PP_W0 + c:PP_W0 + c + 1]), [pz, pp], [s_])
            P.op("dve", lambda e, s_=s_: e.tensor_scalar(out=s_[:], in0=s_[:], scalar1=-E05, scalar2=None, op0=ALU.mult), [s_], [s_])
            out_dma(fm, fm[FM_LD + c * 128:FM_LD + (c + 1) * 128, t0:t0 + A_TG], s_, s_[:])
        a_t = [tmpA[2], tmpA[3]]
        for c in range(2):
            pz = nps()
            P.op("pe", lambda e, pz=pz, c=c: e.matmul(pz[:], lhsT=pm[64:128, PM_A2 + c * 128:PM_A2 + (c + 1) * 128],
                                                       rhs=tb[0][64:128, :], start=True, stop=True), [pm, tb[0]], [pz])
            P.op("act", lambda e, pz=pz, c=c: e.activation(out=a_t[c][:], in_=pz[:], func=AF.Sigmoid,
                                                           bias=pp[:, PP_A0 + c:PP_A0 + c + 1]), [pz, pp], [a_t[c]])
            out_dma(fm, fm[FM_A + c * 128:FM_A + (c + 1) * 128, t0:t0 + A_TG], a_t[c], a_t[c][:])
        for c in range(2):
            pz = nps(); s_ = stage()
            P.op("pe", lambda e, pz=pz, c=c: e.matmul(pz[:], lhsT=pm[:, PM_G2 + c * 128:PM_G2 + (c + 1) * 128],
                                                       rhs=tb[1][:], start=True, stop=True), [pm, tb[1]], [pz])
            P.op("dve", lambda e, pz=pz, s_=s_: e.tensor_copy(out=s_[:], in_=pz[:]), [pz], [s_])
            out_dma(fm, fm[FM_G + c * 128:FM_G + (c + 1) * 128, t0:t0 + A_TG], s_, s_[:])
        if layer == 1:
            P.dma("sp", vfs, vfs[:], vf_d, vf_d[:, t0:t0 + A_TG].rearrange("(c p) t -> p c t", p=128))
            for c in range(2):
                P.op("dve", lambda e, c=c: e.tensor_copy(out=tb[2 + c][:], in_=cm[:, 4 + c, :]), [cm], [tb[2 + c]])
            for c in range(2):
                P.op("pe", lambda e, c=c: e.matmul(psm[0:32, :], lhsT=pm[:, PM_V1 + c * 32:PM_V1 + (c + 1) * 32],
                                                   rhs=tb[2 + c][:], start=(c == 0), stop=(c == 1)), [pm, tb[2 + c]], [psm])
            P.op("dve", lambda e: e.tensor_copy(out=tb[1][0:32, :], in_=psm[0:32, :]), [psm], [tb[1]])
            for c in range(2):
                pz = nps(); ta = tmpA[c]
                P.op("pe", lambda e, pz=pz, c=c: e.matmul(pz[:], lhsT=pm[0:32, PM_V2 + c * 128:PM_V2 + (c + 1) * 128],
                                                           rhs=tb[1][0:32, :], start=True, stop=True), [pm, tb[1]], [pz])
                P.op("act", lambda e, pz=pz, c=c, ta=ta: e.activation(out=ta[:], in_=pz[:], func=AF.Sigmoid,
                                                                        bias=pp[:, PP_V0 + c:PP_V0 + c + 1]), [pz, pp], [ta])
                s_ = stage()
                P.op("dve", lambda e, c=c, s_=s_: e.tensor_tensor(out=s_[:], in0=vfs[:, c, :], in1=cm[:, 4 + c, :],
                                                                   op=ALU.subtract), [vfs, cm], [s_])
                P.op("dve", lambda e, s_=s_, ta=ta: e.tensor_tensor(out=s_[:], in0=s_[:], in1=ta[:], op=ALU.mult), [s_, ta], [s_])
                P.op("dve", lambda e, s_=s_, c=c: e.tensor_tensor(out=s_[:], in0=s_[:], in1=cm[:, 4 + c, :], op=ALU.add),
                     [s_, cm], [s_])
                out_dma(fm, fm[FM_V + c * 128:FM_V + (c + 1) * 128, t0:t0 + A_TG], s_, s_[:])
        else:
            for c in range(2):
                out_dma(fm, fm[FM_V + c * 128:FM_V + (c + 1) * 128, t0:t0 + A_TG], cm, cm[:, 4 + c, :])
        for c in range(2):
            kx = tmpA[c]; s_ = stage(); s2 = stage(); pz = nps()
            P.op("dve", lambda e, c=c, kx=kx: e.tensor_scalar(out=kx[:], in0=cm[:, 2 + c, :], scalar1=pp[:, PP_KK + c:PP_KK + c + 1],
                                                               scalar2=None, op0=ALU.mult), [cm, pp], [kx])
            P.op("pool", lambda e, c=c, kx=kx: e.tensor_tensor(out=tb[2 + c][:], in0=kx[:], in1=kx[:], op=ALU.mult), [kx], [tb[2 + c]])
            P.op("pe", lambda e, pz=pz, c=c: e.matmul(pz[:], lhsT=blk[:], rhs=tb[2 + c][:], start=True, stop=True),
                 [blk, tb[2 + c]], [pz])
            P.op("act", lambda e, pz=pz, s_=s_: e.activation(out=s_[:], in_=pz[:], func=AF.Sqrt), [pz], [s_])
            P.op("dve", lambda e, s_=s_: e.tensor_scalar(out=s_[:], in0=s_[:], scalar1=1e-12, scalar2=None, op0=ALU.max), [s_], [s_])
            P.op("dve", lambda e, s_=s_: e.reciprocal(out=s_[:], in_=s_[:]), [s_], [s_])
            P.op("dve", lambda e, s_=s_, kx=kx: e.tensor_tensor(out=s_[:], in0=s_[:], in1=kx[:], op=ALU.mult), [s_, kx], [s_])
            out_dma(fm, fm[FM_KK + c * 128:FM_KK + (c + 1) * 128, t0:t0 + A_TG], s_, s_[:])
            P.op("dve", lambda e, s2=s2, c=c: e.tensor_scalar(out=s2[:], in0=a_t[c][:], scalar1=pp[:, PP_KA + c:PP_KA + c + 1],
                                                               scalar2=kac[:, c:c + 1], op0=ALU.mult, op1=ALU.add),
                 [a_t[c], pp, kac], [s2])
            P.op("dve", lambda e, s2=s2, c=c: e.tensor_tensor(out=s2[:], in0=s2[:], in1=cm[:, 2 + c, :], op=ALU.mult), [s2, cm], [s2])
            out_dma(fm, fm[FM_KP + c * 128:FM_KP + (c + 1) * 128, t0:t0 + A_TG], s2, s2[:])

    P.final_wait("sp", [fm, tm])
    P.emit()
    return nc, P


def host_inputs_A(layer, inp, h_full, vfirst_full=None):
    l = layer
    pp = np.zeros((128, PP_N), np.float32)
    fmj = lambda v: np.ascontiguousarray(v.reshape(-1, 128).T)
    pp[:, PP_GMIX:PP_GMIX + 8] = fmj(inp['norm_mix_g'][l])
    pp[:, PP_MU:PP_MU + 8] = fmj(inp['rwkv_mu'][l])
    pp[:, PP_W0:PP_W0 + 2] = fmj(inp['rwkv_w0'][l])
    pp[:, PP_A0:PP_A0 + 2] = fmj(inp['rwkv_a0'][l])
    pp[:, PP_KK:PP_KK + 2] = fmj(inp['rwkv_k_k'][l])
    pp[:, PP_KA:PP_KA + 2] = fmj(inp['rwkv_k_a'][l])
    if l == 1:
        pp[:, PP_V0:PP_V0 + 2] = fmj(inp['rwkv_v0'][0])
    pp[:, PP_HB0:PP_HB0 + 4] = fmj(inp['hgrn_lower_bounds'][0])
    pp[:, PP_HB1:PP_HB1 + 4] = fmj(inp['hgrn_lower_bounds'][1])
    pm = np.zeros((128, PM_N), np.float32)
    pm[0:64, PM_W2:PM_W2 + 256] = inp['rwkv_w2'][l]
    pm[64:128, PM_A2:PM_A2 + 256] = inp['rwkv_a2'][l]
    pm[:, PM_G2:PM_G2 + 256] = inp['rwkv_g2'][l]
    if l == 1:
        pm[:, PM_V1:PM_V1 + 64] = inp['rwkv_v1'][0].reshape(2, 128, 32).transpose(1, 0, 2).reshape(128, 64)
        pm[0:32, PM_V2:PM_V2 + 256] = inp['rwkv_v2'][0]
    ident = np.eye(128).astype(ml_dtypes.bfloat16)
    blk = np.kron(np.eye(2), np.ones((64, 64))).astype(ml_dtypes.bfloat16)
    maps = []
    w = np.ascontiguousarray(inp['w_in'][l])
    for c in range(8):
        m = {"h": np.ascontiguousarray(h_full[c * A_NT:(c + 1) * A_NT]),
             "hh": np.ascontiguousarray(h_full[c * A_NT - 128:c * A_NT]) if c > 0 else np.zeros((128, 1024), np.float32),
             "w_in": w, "pp": pp, "pm": pm, "ident": ident, "blk64": blk}
        if l == 1:
            m["vfirst"] = np.ascontiguousarray(vfirst_full[:, c * A_NT:(c + 1) * A_NT])
        maps.append(m)
    return maps


NOWN = 8192
NHALO = 2048
NTOT = NOWN + NHALO
PATTERNS = (1, 4, 16)


def emit_sw(P):
    qT_d = P.dram("sw_qT", [64, NOWN], F32, "ExternalInput")
    kT_d = P.dram("sw_kT", [64, NTOT], F32, "ExternalInput")
    v_d = P.dram("sw_v", [NTOT, 64], F32, "ExternalInput")
    msk_d = P.dram("sw_mask", [128, 512], BF16, "ExternalInput")
    id_d = P.dram("sw_ident", [128, 128], BF16, "ExternalInput")
    flag_d = P.dram("sw_flag", [128, 1], F32, "ExternalInput")
    o_d = P.dram("sw_o", [64, NOWN], F32, "ExternalOutput")

    q32 = P.sbuf("q32", [64, NOWN], F32)
    k32 = P.sbuf("k32", [64, NTOT], F32)
    qd = P.sbuf("qd", [64, NOWN], BF16)
    kd = P.sbuf("kd", [64, NTOT], BF16)
    vst = P.sbuf("vst", [128, 85 * 64], F32)
    vaug = P.sbuf("vaug", [128, 85, 65], BF16)
    acc = P.sbuf("acc", [65, NOWN], F32)
    msk = P.sbuf("msk", [128, 512], BF16)
    ident = P.sbuf("identsw", [128, 128], BF16)
    flag = P.sbuf("flag", [128, 1], F32)
    ones = P.sbuf("ones", [65, 64], F32)
    PT = [P.sbuf("PT%d" % i, [128, 512], BF16) for i in range(3)]
    ost = [P.sbuf("ost%d" % i, [64, 512], F32) for i in range(2)]
    rz = [P.sbuf("rz%d" % i, [64, 512], F32) for i in range(2)]
    psS = [P.psum("psS%d" % i, [128, 512], F32) for i in range(3)]
    psN = [P.psum("psN%d" % i, [128, 512], F32) for i in range(3)]

    P.dma("sp", msk, msk[:], msk_d, msk_d[:])
    P.dma("sp", ident, ident[:], id_d, id_d[:])
    P.dma("sp", flag, flag[:], flag_d, flag_d[:])
    for i in range(4):
        P.dma("sp" if i % 2 == 0 else "act", q32, q32[:, i * 2048:(i + 1) * 2048], qT_d, qT_d[:, i * 2048:(i + 1) * 2048],
              disjoint=True)
    for i in range(5):
        P.dma("act" if i % 2 == 0 else "sp", k32, k32[:, i * 2048:(i + 1) * 2048], kT_d, kT_d[:, i * 2048:(i + 1) * 2048],
              disjoint=True)
    P.op("dve", lambda e: e.memset(ones[:], 1.0), [], [ones])

    si = [0]
    for pi, D in enumerate(PATTERNS):
        nb = NOWN // (128 * D)
        nbt = nb + 1
        LQ = NOWN // D
        LK = LQ + 128
        koff = NHALO - 128 * D
        if D == 1:
            P.op("dve", lambda e: e.tensor_copy(out=qd[:, 0:NOWN], in_=q32[:, :]), [q32], [qd])
            P.op("pool", lambda e: e.tensor_copy(out=kd[:, 0:LK], in_=k32[:, koff:koff + LK]), [k32], [kd])
        else:
            P.op("dve", lambda e: e.tensor_copy(out=qd[:, 0:NOWN].rearrange("p (r l) -> p r l", r=D),
                                                in_=q32[:, :].rearrange("p (l r) -> p r l", r=D)), [q32], [qd])
            P.op("pool", lambda e: e.tensor_copy(out=kd[:, 0:D * LK].rearrange("p (r l) -> p r l", r=D),
                                                 in_=k32[:, koff:koff + D * LK].rearrange("p (l r) -> p r l", r=D)), [k32], [kd])
        vsrc = v_d[koff:koff + nbt * 128 * D, :].rearrange("(bb i r) c -> i r bb c", i=128, r=D)
        vv = vst[:, 0:D * nbt * 64].rearrange("p (r bb c) -> p r bb c", r=D, bb=nbt)
        for r in range(D):
            P.dma("sp" if r % 2 == 0 else "act", vst, vv[:, r], v_d, vsrc[:, r], disjoint=(r > 0))
        va = vaug[:, 0:D * nbt, :]
        P.op("dve", lambda e: e.memset(va[:, :, 64:65], 1.0), [], [vaug])
        P.op("dve", lambda e: e.tensor_copy(out=va[:, :, 0:64], in_=vst[:, 0:D * nbt * 64].rearrange("p (n c) -> p n c", c=64)),
             [vst], [vaug])
        va4 = va.rearrange("p (r bb) c -> p r bb c", r=D)
        P.op("dve", lambda e: e.tensor_scalar(out=va4[:, :, 0, :], in0=va4[:, :, 0, :], scalar1=flag[:, 0:1], scalar2=None,
                                              op0=ALU.mult), [vaug, flag], [vaug])
        for r in range(D):
            for b0 in range(0, nb, 4):
                pn = psN[si[0] % 3]
                pts = []
                for pr in range(2):
                    p_s = psS[(2 * si[0] + pr) % 3]
                    pt = PT[(2 * si[0] + pr) % 3]
                    P.op("pe", lambda e: e.matmul(p_s[:], lhsT=ident[:], rhs=msk[:], start=True, stop=False), [ident, msk], [p_s])
                    for j in range(2):
                        b = b0 + 2 * pr + j
                        qb = qd[:, r * LQ + b * 128: r * LQ + (b + 1) * 128]
                        for kb in range(2):
                            kblk = kd[:, r * LK + (b + kb) * 128: r * LK + (b + kb + 1) * 128]
                            P.op("pe", lambda e: e.matmul(p_s[:, (2 * j + kb) * 128:(2 * j + kb + 1) * 128], lhsT=kblk, rhs=qb,
                                                          start=False, stop=True), [kd, qd], [p_s])
                    P.op("act", lambda e: e.activation(out=pt[:], in_=p_s[:], func=AF.Exp, scale=0.125), [p_s], [pt])
                    pts.append(pt)
                for pr in range(2):
                    for j in range(2):
                        b = b0 + 2 * pr + j
                        for kb in range(2):
                            P.op("pe", lambda e: e.matmul(pn[0:65, (2 * pr + j) * 128:(2 * pr + j + 1) * 128],
                                                          lhsT=vaug[:, r * nbt + b + kb, :],
                                                          rhs=pts[pr][:, (2 * j + kb) * 128:(2 * j + kb + 1) * 128],
                                                          start=(kb == 0), stop=(kb == 1)), [vaug, pts[pr]], [pn])
                tstart = r + D * 128 * b0
                av = acc[:, tstart: tstart + 512 * D] if D == 1 else \
                    acc[:, D * 128 * b0: D * 128 * b0 + 512 * D].rearrange("p (l r) -> p r l", r=D)[:, r, :]
                if pi == 0:
                    P.op("dve", lambda e: e.tensor_copy(out=av, in_=pn[0:65, :]), [pn], [acc])
                else:
                    P.op("dve", lambda e: e.tensor_tensor(out=av, in0=av, in1=pn[0:65, :], op=ALU.add), [pn, acc], [acc])
                si[0] += 1
    for i in range(NOWN // 512):
        pz = psS[i % 3]
        P.op("pe", lambda e: e.matmul(pz[0:64, :], lhsT=ones[64:65, :], rhs=acc[64:65, i * 512:(i + 1) * 512], start=True, stop=True),
             [ones, acc], [pz])
        rzi = rz[i % 2]; o_ = ost[i % 2]
        P.op("dve", lambda e: e.reciprocal(out=rzi[:], in_=pz[0:64, :]), [pz], [rzi])
        P.op("pool", lambda e: e.tensor_tensor(out=o_[:], in0=acc[0:64, i * 512:(i + 1) * 512], in1=rzi[:], op=ALU.mult), [acc, rzi], [o_])
        P.dma("sp" if i % 2 == 0 else "act", o_d, o_d[:, i * 512:(i + 1) * 512], o_, o_[:], owner=o_, disjoint=True)
    return [o_d]


def sw_consts():
    j = np.arange(128)[:, None]
    i = np.arange(128)[None, :]
    mp = np.where(j >= i, 0.0, -30000.0)
    mo = np.where(j <= i, 0.0, -30000.0)
    m = np.concatenate([mp, mo, mp, mo], axis=1).astype(ml_dtypes.bfloat16)
    return {"sw_mask": m, "sw_ident": np.eye(128).astype(ml_dtypes.bfloat16)}


def sw_inputs(core, bq_fm, bk_fm, bv_tm):
    hd, s = core // 2, core % 2
    t0 = s * NOWN
    qT = np.ascontiguousarray(bq_fm[hd * 64:(hd + 1) * 64, t0:t0 + NOWN])
    kT = np.zeros((64, NTOT), np.float32)
    v = np.zeros((NTOT, 64), np.float32)
    kT[:, NHALO:] = bk_fm[hd * 64:(hd + 1) * 64, t0:t0 + NOWN]
    v[NHALO:] = bv_tm[t0:t0 + NOWN, hd * 64:(hd + 1) * 64]
    if s > 0:
        kT[:, :NHALO] = bk_fm[hd * 64:(hd + 1) * 64, t0 - NHALO:t0]
        v[:NHALO] = bv_tm[t0 - NHALO:t0, hd * 64:(hd + 1) * 64]
    m = {"sw_qT": qT, "sw_kT": kT, "sw_v": v, "sw_flag": np.full((128, 1), 1.0 if s > 0 else 0.0, np.float32)}
    m.update(sw_consts())
    return m


HG_SEQ = 16384
HG_ST = 2048
HG_NST = HG_SEQ // HG_ST
HG_CL = 40.0


def emit_hgrn(P):
    q_d = P.dram("hg_q", [128, HG_SEQ], F32, "ExternalInput")
    lf_d = P.dram("hg_lf", [128, HG_SEQ], F32, "ExternalInput")
    i_d = P.dram("hg_i", [HG_SEQ, 64], F32, "ExternalInput")
    rm_d = P.dram("hg_rmask", [128, HG_ST], F32, "ExternalInput")
    cm_d = P.dram("hg_cmask", [128, 128], F32, "ExternalInput")
    id_d = P.dram("hg_ident", [128, 128], BF16, "ExternalInput")
    o_d = P.dram("hg_o", [64, HG_SEQ], F32, "ExternalOutput")

    qs = P.sbuf("hqs", [128, HG_ST], F32)
    lf = P.sbuf("hlf", [128, HG_ST], F32)
    bb = P.sbuf("hb", [128, HG_ST], F32)
    kf = P.sbuf("hkf", [128, HG_ST], F32)
    t1 = P.sbuf("ht1", [128, HG_ST], F32)
    t2 = P.sbuf("ht2", [128, HG_ST], F32)
    dch = P.sbuf("hdch", [128, 32], F32)
    Qt = P.sbuf("hQt", [128, HG_ST], BF16)
    Kt = P.sbuf("hKt", [128, HG_ST], BF16)
    Qh = P.sbuf("hQh", [128, HG_ST], BF16)
    Kh = P.sbuf("hKh", [128, HG_ST], BF16)
    rmask = P.sbuf("hrmask", [128, HG_ST], F32)
    cmask = P.sbuf("hcmask", [128, 128], F32)
    ident = P.sbuf("hident", [128, 128], BF16)
    v32 = P.sbuf("hv32", [128, 16, 64], F32)
    vb = P.sbuf("hvb", [128, 16, 64], BF16)
    KhT = [P.sbuf("hKhT%d" % i, [128, 128], BF16) for i in range(2)]
    Am = [P.sbuf("hAm%d" % i, [128, 128], BF16) for i in range(2)]
    S32 = P.sbuf("hS32", [128, 64], F32)
    Sb = [P.sbuf("hSb%d" % i, [128, 64], BF16) for i in range(2)]
    ost = P.sbuf("host", [64, HG_ST], F32)
    psT = [P.psum("hpsT%d" % i, [128, 128], BF16) for i in range(2)]
    psA = [P.psum("hpsA%d" % i, [128, 128], F32) for i in range(2)]
    psO = [P.psum("hpsO%d" % i, [128, 128], F32) for i in range(2)]
    psU = [P.psum("hpsU%d" % i, [128, 64], F32) for i in range(2)]

    P.dma("sp", rmask, rmask[:], rm_d, rm_d[:])
    P.dma("sp", cmask, cmask[:], cm_d, cm_d[:])
    P.dma("sp", ident, ident[:], id_d, id_d[:])
    P.op("dve", lambda e: e.memset(S32[:], 0.0), [], [S32])
    P.op("dve", lambda e: e.memset(Sb[0][:], 0.0), [], [Sb[0]])
    sbi = 0
    pc = 0
    b3 = bb[:, :].rearrange("p (n c) -> p n c", c=64)
    for st in range(HG_NST):
        t0 = st * HG_ST
        P.dma("sp", qs, qs[:], q_d, q_d[:, t0:t0 + HG_ST])
        P.dma("act", lf, lf[:], lf_d, lf_d[:, t0:t0 + HG_ST])
        P.dma("pool", v32, v32[:], i_d, i_d[t0:t0 + HG_ST, :].rearrange("(n i) c -> i n c", i=128))
        P.op("pool", lambda e: e.tensor_copy(out=vb[:], in_=v32[:]), [v32], [vb])
        P.op("act", lambda e: e.activation(out=kf[:], in_=lf[:], func=AF.Exp), [lf], [kf])
        P.op("pool", lambda e: e.tensor_scalar(out=kf[:], in0=kf[:], scalar1=-1.0, scalar2=1.0, op0=ALU.mult, op1=ALU.add), [kf], [kf])
        P.op("dve", lambda e: e.tensor_tensor_scan(out=bb[:], data0=rmask[:], data1=lf[:], initial=0.0, op0=ALU.mult, op1=ALU.add),
             [rmask, lf], [bb])
        bm = b3[:, :, 31:32].to_broadcast([128, 32, 64])
        bl = b3[:, :, 63:64].to_broadcast([128, 32, 64])
        t1v = t1[:, :].rearrange("p (n c) -> p n c", c=64)
        t2v = t2[:, :].rearrange("p (n c) -> p n c", c=64)
        P.op("dve", lambda e: e.tensor_tensor(out=t1v, in0=b3, in1=bm, op=ALU.subtract), [bb], [t1])
        P.op("pool", lambda e: e.tensor_scalar(out=t2[:], in0=t1[:], scalar1=-1.0, scalar2=HG_CL, op0=ALU.mult, op1=ALU.min), [t1], [t2])
        P.op("dve", lambda e: e.tensor_scalar(out=t1[:], in0=t1[:], scalar1=HG_CL, scalar2=None, op0=ALU.min), [t1], [t1])
        P.op("act", lambda e: e.activation(out=t1[:], in_=t1[:], func=AF.Exp), [t1], [t1])
        P.op("act", lambda e: e.activation(out=t2[:], in_=t2[:], func=AF.Exp), [t2], [t2])
        P.op("dve", lambda e: e.tensor_tensor(out=Qt[:], in0=qs[:], in1=t1[:], op=ALU.mult), [qs, t1], [Qt])
        P.op("pool", lambda e: e.tensor_tensor(out=Kt[:], in0=kf[:], in1=t2[:], op=ALU.mult), [kf, t2], [Kt])
        P.op("act", lambda e: e.activation(out=t1[:], in_=bb[:], func=AF.Exp), [bb], [t1])
        P.op("dve", lambda e: e.tensor_tensor(out=Qh[:], in0=qs[:], in1=t1[:], op=ALU.mult), [qs, t1], [Qh])
        P.op("dve", lambda e: e.tensor_tensor(out=t2v, in0=bl, in1=b3, op=ALU.subtract), [bb], [t2])
        P.op("act", lambda e: e.activation(out=t2[:], in_=t2[:], func=AF.Exp), [t2], [t2])
        P.op("pool", lambda e: e.tensor_tensor(out=Kh[:], in0=kf[:], in1=t2[:], op=ALU.mult), [kf, t2], [Kh])
        P.op("act", lambda e: e.activation(out=dch[:], in_=b3[:, :, 63], func=AF.Exp), [bb], [dch])
        for pr in range(16):
            c0 = pr * 128
            kh_t = KhT[pc % 2]; am = Am[pc % 2]; p_t = psT[pc % 2]; p_a = psA[pc % 2]; p_o = psO[pc % 2]
            pc += 1
            P.op("pe", lambda e: e.transpose(out=p_t[:], in_=Kh[:, c0:c0 + 128], identity=ident[:]), [Kh, ident], [p_t])
            P.op("act", lambda e: e.activation(out=kh_t[:], in_=p_t[:], func=AF.Copy), [p_t], [kh_t])
            P.op("pe", lambda e: e.matmul(p_a[:], lhsT=Kt[:, c0:c0 + 128], rhs=Qt[:, c0:c0 + 128], start=True, stop=True), [Kt, Qt], [p_a])
            P.op("dve", lambda e: e.tensor_tensor(out=am[:], in0=p_a[:], in1=cmask[:], op=ALU.mult), [p_a, cmask], [am])
            for ch in range(2):
                r0 = ch * 64
                s_cur = Sb[sbi % 2]; s_nxt = Sb[(sbi + 1) % 2]; p_u = psU[sbi % 2]
                sbi += 1
                P.op("pe", lambda e: e.matmul(p_o[0:64, r0:r0 + 64], lhsT=vb[r0:r0 + 64, pr, :], rhs=am[r0:r0 + 64, r0:r0 + 64],
                                              start=True, stop=False), [vb, am], [p_o])
                P.op("pe", lambda e: e.matmul(p_o[0:64, r0:r0 + 64], lhsT=s_cur[:], rhs=Qh[:, c0 + r0:c0 + r0 + 64],
                                              start=False, stop=True), [s_cur, Qh], [p_o])
                P.op("pe", lambda e: e.matmul(p_u[:], lhsT=kh_t[r0:r0 + 64, :], rhs=vb[r0:r0 + 64, pr, :], start=True, stop=True),
                     [kh_t, vb], [p_u])
                cidx = pr * 2 + ch
                P.op("dve", lambda e: e.scalar_tensor_tensor(out=S32[:], in0=S32[:], scalar=dch[:, cidx:cidx + 1], in1=p_u[:],
                                                             op0=ALU.mult, op1=ALU.add), [S32, dch, p_u], [S32])
                P.op("pool", lambda e: e.tensor_copy(out=s_nxt[:], in_=S32[:]), [S32], [s_nxt])
            P.op("act", lambda e: e.activation(out=ost[:, c0:c0 + 128], in_=p_o[0:64, :], func=AF.Copy), [p_o], [ost])
        P.dma("sp", o_d, o_d[:, t0:t0 + HG_ST], ost, ost[:], owner=ost, disjoint=True)
    return [o_d]


def hg_consts():
    t = np.arange(HG_ST)
    rm = np.tile(((t % 64) != 0).astype(np.float32)[None, :], (128, 1))
    j = np.arange(128)[:, None]; i = np.arange(128)[None, :]
    cmk = ((j // 64 == i // 64) & (j <= i)).astype(np.float32)
    return {"hg_rmask": rm, "hg_cmask": cmk, "hg_ident": np.eye(128).astype(ml_dtypes.bfloat16)}


def hg_inputs(core, qs_fm, lf_fm, i_tm):
    hd, vh = core // 2, core % 2
    m = {"hg_q": np.ascontiguousarray(qs_fm[hd * 128:(hd + 1) * 128]),
         "hg_lf": np.ascontiguousarray(lf_fm[hd * 128:(hd + 1) * 128]),
         "hg_i": np.ascontiguousarray(i_tm[:, hd * 128 + vh * 64: hd * 128 + (vh + 1) * 64])}
    m.update(hg_consts())
    return m


RW_SEQ = 16384
RW_ST = 2048
RW_NST = RW_SEQ // RW_ST
RW_C = 128
RW_NCH = RW_ST // RW_C


def emit_rwkv(P):
    r_d = P.dram("rw_r", [64, RW_SEQ], F32, "ExternalInput")
    kp_d = P.dram("rw_kp", [64, RW_SEQ], F32, "ExternalInput")
    kk_d = P.dram("rw_kk", [64, RW_SEQ], F32, "ExternalInput")
    a_d = P.dram("rw_a", [64, RW_SEQ], F32, "ExternalInput")
    ld_d = P.dram("rw_ld", [64, RW_SEQ], F32, "ExternalInput")
    v_d = P.dram("rw_v", [32, RW_SEQ], F32, "ExternalInput")
    rm_d = P.dram("rw_rmask", [64, RW_ST], F32, "ExternalInput")
    mk_d = P.dram("rw_masks", [128, 4 * 128], F32, "ExternalInput")
    id_d = P.dram("rw_ident", [128, 128], BF16, "ExternalInput")
    y_d = P.dram("rw_y", [32, RW_SEQ], F32, "ExternalOutput")

    def sb(name, shape, dt=F32):
        return P.sbuf("rws_" + name, shape, dt)
    r_s, kp_s, kk_s, a_s, ld_s = [sb(n, [64, RW_ST]) for n in ("r", "kp", "kk", "a", "ld")]
    v_s = sb("v", [32, RW_ST])
    G = sb("G", [64, RW_ST]); x1 = sb("x1", [64, RW_ST]); x2 = sb("x2", [64, RW_ST]); x3 = sb("x3", [64, RW_ST])
    dC = sb("dC", [64, RW_NCH])
    AR = sb("AR", [64, RW_NCH, 2, RW_C], BF16)
    Bt = sb("Bt", [64, RW_ST], BF16); Kt = sb("Kt", [64, RW_ST], BF16)
    Bh = sb("Bh", [64, RW_ST], BF16); Kh = sb("Kh", [64, RW_ST], BF16)
    vb = sb("vb", [32, RW_ST], BF16)
    rmask = sb("rmask", [64, RW_ST])
    masks = sb("masks", [128, 4 * 128])
    ident = sb("ident", [128, 128], BF16)
    tok = [sb("tok%d" % i, [128, 160], BF16) for i in range(2)]
    Mf = [sb("Mf%d" % i, [128, 128]) for i in range(2)]
    Nf = [sb("Nf%d" % i, [128, 128]) for i in range(2)]
    XT = [sb("XT%d" % i, [128, 128]) for i in range(2)]
    XTb = [sb("XTb%d" % i, [128, 128], BF16) for i in range(2)]
    Arb = [sb("Arb%d" % i, [128, 128], BF16) for i in range(2)]
    Ak = [sb("Ak%d" % i, [128, 256], BF16) for i in range(2)]
    H32 = sb("H32", [64, 32])
    Hb = [sb("Hb%d" % i, [64, 32], BF16) for i in range(2)]
    Wb = [sb("Wb%d" % i, [128, 32], BF16) for i in range(2)]
    Ub = [sb("Ub%d" % i, [128, 32], BF16) for i in range(2)]
    yst = sb("yst", [32, RW_ST])
    psT = P.psum("rw_psT", [128, 160], BF16)
    psA = [P.psum("rw_psA%d" % i, [128, 256], F32) for i in range(2)]
    psC = [P.psum("rw_psC%d" % i, [128, 128], F32) for i in range(3)]
    psS = [P.psum("rw_psS%d" % i, [128, 128], F32) for i in range(2)]

    mSU = masks[:, 0:128]; mIU = masks[:, 128:256]; mSL = masks[:, 256:384]; mI = masks[:, 384:512]
    P.dma("sp", rmask, rmask[:], rm_d, rm_d[:])
    P.dma("sp", masks, masks[:], mk_d, mk_d[:])
    P.dma("sp", ident, ident[:], id_d, id_d[:])
    P.op("dve", lambda e: e.memset(H32[:], 0.0), [], [H32])
    P.op("dve", lambda e: e.memset(Hb[0][:], 0.0), [], [Hb[0]])
    hbi = 0
    cc = 0
    G3 = G[:, :].rearrange("p (n c) -> p n c", c=RW_C)
    x13 = x1[:, :].rearrange("p (n c) -> p n c", c=RW_C)
    x23 = x2[:, :].rearrange("p (n c) -> p n c", c=RW_C)
    x33 = x3[:, :].rearrange("p (n c) -> p n c", c=RW_C)

    def EW(eng, fn, reads, writes):
        P.op(eng, fn, reads, writes)

    for st in range(RW_NST):
        t0 = st * RW_ST
        for i, (s_, d_) in enumerate(((r_s, r_d), (kp_s, kp_d), (kk_s, kk_d), (a_s, a_d), (ld_s, ld_d))):
            P.dma(("sp", "act", "pool")[i % 3], s_, s_[:], d_, d_[:, t0:t0 + RW_ST])
        P.dma("sp", v_s, v_s[:], v_d, v_d[:, t0:t0 + RW_ST])
        EW("pool", lambda e: e.tensor_copy(out=vb[:], in_=v_s[:]), [v_s], [vb])
        EW("dve", lambda e: e.tensor_tensor_scan(out=G[:], data0=rmask[:], data1=ld_s[:], initial=0.0, op0=ALU.mult, op1=ALU.add),
           [rmask, ld_s], [G])
        Gl = G3[:, :, RW_C - 1:RW_C].to_broadcast([64, RW_NCH, RW_C])
        EW("dve", lambda e: e.tensor_tensor(out=x1[:], in0=G[:], in1=ld_s[:], op=ALU.subtract), [G, ld_s], [x1])
        EW("act", lambda e: e.activation(out=x1[:], in_=x1[:], func=AF.Exp), [x1], [x1])
        EW("dve", lambda e: e.scalar_tensor_tensor(out=AR[:, :, 0, :], in0=x13, scalar=-1.0, in1=kk_s[:, :].rearrange("p (n c) -> p n c", c=RW_C),
                                                   op0=ALU.mult, op1=ALU.mult), [x1, kk_s], [AR])
        EW("act", lambda e: e.activation(out=x2[:], in_=G[:], func=AF.Exp), [G], [x2])
        EW("pool", lambda e: e.tensor_tensor(out=AR[:, :, 1, :], in0=x23, in1=r_s[:, :].rearrange("p (n c) -> p n c", c=RW_C), op=ALU.mult),
           [x2, r_s], [AR])
        EW("pool", lambda e: e.tensor_tensor(out=x3[:], in0=kk_s[:], in1=a_s[:], op=ALU.mult), [kk_s, a_s], [x3])
        EW("act", lambda e: e.activation(out=x1[:], in_=G[:], func=AF.Exp, scale=-1.0), [G], [x1])
        EW("dve", lambda e: e.tensor_tensor(out=Bt[:], in0=x3[:], in1=x1[:], op=ALU.mult), [x3, x1], [Bt])
        EW("pool", lambda e: e.tensor_tensor(out=Kt[:], in0=kp_s[:], in1=x1[:], op=ALU.mult), [kp_s, x1], [Kt])
        EW("dve", lambda e: e.tensor_tensor(out=x23, in0=Gl, in1=G3, op=ALU.subtract), [G], [x2])
        EW("act", lambda e: e.activation(out=x2[:], in_=x2[:], func=AF.Exp), [x2], [x2])
        EW("dve", lambda e: e.tensor_tensor(out=Bh[:], in0=x3[:], in1=x2[:], op=ALU.mult), [x3, x2], [Bh])
        EW("pool", lambda e: e.tensor_tensor(out=Kh[:], in0=kp_s[:], in1=x2[:], op=ALU.mult), [kp_s, x2], [Kh])
        EW("act", lambda e: e.activation(out=dC[:], in_=G3[:, :, RW_C - 1], func=AF.Exp), [G], [dC])

        for n in range(RW_NCH):
            c0 = n * RW_C
            tk = tok[cc % 2]; pa = psA[0]; pa2 = psA[1]
            mf = Mf[0]; nf = Nf[0]; xt = XT[0]
            xtb = XTb[cc % 2]; arb = Arb[cc % 2]; ak = Ak[cc % 2]
            cc += 1
            P.op("pe", lambda e: e.transpose(out=psT[:, 0:64], in_=Bh[:, c0:c0 + RW_C], identity=ident[0:64, 0:64]), [Bh, ident], [psT])
            P.op("pe", lambda e: e.transpose(out=psT[:, 64:128], in_=Kh[:, c0:c0 + RW_C], identity=ident[0:64, 0:64]), [Kh, ident], [psT])
            P.op("pe", lambda e: e.transpose(out=psT[:, 128:160], in_=vb[:, c0:c0 + RW_C], identity=ident[0:32, 0:32]), [vb, ident], [psT])
            P.op("act", lambda e: e.activation(out=tk[:], in_=psT[:], func=AF.Copy), [psT], [tk])
            arv = AR[:, n, :, :].rearrange("p a c -> p (a c)")
            P.op("pe", lambda e: e.matmul(pa[:], lhsT=Bt[:, c0:c0 + RW_C], rhs=arv, start=True, stop=True), [Bt, AR], [pa])
            P.op("pe", lambda e: e.matmul(pa2[:], lhsT=Kt[:, c0:c0 + RW_C], rhs=arv, start=True, stop=True), [Kt, AR], [pa2])
            p3 = psC[0]
            P.op("pe", lambda e: e.matmul(p3[:], lhsT=AR[:, n, 0, :], rhs=Bt[:, c0:c0 + RW_C], start=True, stop=True), [AR, Bt], [p3])
            P.op("dve", lambda e: e.tensor_tensor(out=mf[:], in0=pa[:, 0:128], in1=mSU, op=ALU.mult), [pa, masks], [mf])
            P.op("dve", lambda e: e.tensor_tensor(out=arb[:], in0=pa[:, 128:256], in1=mIU, op=ALU.mult), [pa, masks], [arb])
            P.op("dve", lambda e: e.tensor_tensor(out=ak[:], in0=pa2[:], in1=masks[:, 0:256], op=ALU.mult), [pa2, masks], [ak])
            P.op("dve", lambda e: e.tensor_tensor(out=nf[:], in0=p3[:], in1=mSL, op=ALU.mult), [p3, masks], [nf])
            P.op("pool", lambda e: e.tensor_tensor(out=xt[:], in0=mf[:], in1=mI, op=ALU.add), [mf, masks], [xt])
            cur = 0
            for it in range(6):
                mo, no = Mf[cur], Nf[cur]
                mn, nn = Mf[1 - cur], Nf[1 - cur]
                xo, xn = XT[cur], XT[1 - cur]
                pn2 = psC[1]
                P.op("pe", lambda e: e.matmul(pn2[:], lhsT=mo[:], rhs=no[:], start=True, stop=True), [mo, no], [pn2])
                if it < 5:
                    pm2 = psC[2]
                    P.op("pe", lambda e: e.matmul(pm2[:], lhsT=no[:], rhs=mo[:], start=True, stop=True), [mo, no], [pm2])
                P.op("act", lambda e: e.activation(out=nn[:], in_=pn2[:], func=AF.Copy), [pn2], [nn])
                if it < 5:
                    P.op("dve", lambda e: e.tensor_copy(out=mn[:], in_=pm2[:]), [pm2], [mn])
                px = psC[0]
                P.op("pe", lambda e: e.matmul(px[:], lhsT=nn[:], rhs=xo[:], start=True, stop=True), [nn, xo], [px])
                P.op("dve", lambda e: e.tensor_tensor(out=xn[:], in0=px[:], in1=xo[:], op=ALU.add), [px, xo], [xn])
                cur = 1 - cur
            P.op("pool", lambda e: e.tensor_copy(out=xtb[:], in_=XT[cur][:]), [XT[cur]], [xtb])
            hb_cur = Hb[hbi % 2]; hb_nxt = Hb[(hbi + 1) % 2]; wb = Wb[hbi % 2]; ub = Ub[hbi % 2]
            hbi += 1
            pw = psS[0]
            P.op("pe", lambda e: e.matmul(pw[:, 0:32], lhsT=AR[:, n, 0, :], rhs=hb_cur[:], start=True, stop=False), [AR, hb_cur], [pw])
            P.op("pe", lambda e: e.matmul(pw[:, 0:32], lhsT=ak[:, 0:128], rhs=tk[:, 128:160], start=False, stop=True), [ak, tk], [pw])
            P.op("act", lambda e: e.activation(out=wb[:], in_=pw[:, 0:32], func=AF.Copy), [pw], [wb])
            P.op("pe", lambda e: e.matmul(pw[:, 32:64], lhsT=xtb[:], rhs=wb[:], start=True, stop=True), [xtb, wb], [pw])
            P.op("act", lambda e: e.activation(out=ub[:], in_=pw[:, 32:64], func=AF.Copy), [pw], [ub])
            P.op("pe", lambda e: e.matmul(pw[0:64, 64:96], lhsT=tk[:, 0:64], rhs=ub[:], start=True, stop=False), [tk, ub], [pw])
            P.op("pe", lambda e: e.matmul(pw[0:64, 64:96], lhsT=tk[:, 64:128], rhs=tk[:, 128:160], start=False, stop=True), [tk], [pw])
            py = psS[1]
            P.op("pe", lambda e: e.matmul(py[0:32, :], lhsT=hb_cur[:], rhs=AR[:, n, 1, :], start=True, stop=False), [hb_cur, AR], [py])
            P.op("pe", lambda e: e.matmul(py[0:32, :], lhsT=ub[:], rhs=arb[:], start=False, stop=False), [ub, arb], [py])
            P.op("pe", lambda e: e.matmul(py[0:32, :], lhsT=tk[:, 128:160], rhs=ak[:, 128:256], start=False, stop=True), [tk, ak], [py])
            P.op("dve", lambda e: e.scalar_tensor_tensor(out=H32[:], in0=H32[:], scalar=dC[:, n:n + 1], in1=pw[0:64, 64:96],
                                                         op0=ALU.mult, op1=ALU.add), [H32, dC, pw], [H32])
            P.op("pool", lambda e: e.tensor_copy(out=hb_nxt[:], in_=H32[:]), [H32], [hb_nxt])
            P.op("act", lambda e: e.activation(out=yst[:, c0:c0 + RW_C], in_=py[0:32, :], func=AF.Copy), [py], [yst])
        P.dma("sp", y_d, y_d[:, t0:t0 + RW_ST], yst, yst[:], owner=yst, disjoint=True)
    return [y_d]


def rw_consts():
    t = np.arange(RW_ST)
    rm = np.tile(((t % RW_C) != 0).astype(np.float32)[None, :], (64, 1))
    j = np.arange(128)[:, None]; i = np.arange(128)[None, :]
    su = (i > j).astype(np.float32); iu = (i >= j).astype(np.float32); sl = (j > i).astype(np.float32)
    masks = np.concatenate([su, iu, sl, np.eye(128, dtype=np.float32)], axis=1)
    return {"rw_rmask": rm, "rw_masks": masks, "rw_ident": np.eye(128).astype(ml_dtypes.bfloat16)}


def rw_inputs(core, fmr):
    hd, vh = core // 2, core % 2
    m = {"rw_" + n: np.ascontiguousarray(fmr[n][hd * 64:(hd + 1) * 64]) for n in ("r", "kp", "kk", "a", "ld")}
    m["rw_v"] = np.ascontiguousarray(fmr["v"][hd * 64 + vh * 32: hd * 64 + (vh + 1) * 32])
    m.update(rw_consts())
    return m


C1_NT = 2048
C1_TG = 512
C1_NG = C1_NT // C1_TG
CF_HGO, CF_GS, CF_SWO, CF_RWY, CF_R, CF_KP, CF_V, CF_G = 0, 512, 1024, 1280, 1536, 1792, 2048, 2304
CF_ROWS = 2560
CP_GN, CP_LNW, CP_LNB, CP_RK, CP_N = 0, 4, 6, 8, 10


def emit_c1(P, layer):
    h_d = P.dram("c1_h", [C1_NT, 1024], F32, "ExternalInput")
    cf_d = P.dram("c1_cf", [CF_ROWS, C1_NT], F32, "ExternalInput")
    wo_d = P.dram("c1_wout", [1024, 1024], F32, "ExternalInput")
    cp_d = P.dram("c1_cp", [128, CP_N], F32, "ExternalInput")
    k_d = P.dram("c1_consts", [128, 256], F32, "ExternalInput")
    hm_d = P.dram("c1_hmid", [C1_NT, 1024], F32, "ExternalOutput")

    def sb(name, shape, dt=F32):
        return P.sbuf("c1s_" + name, shape, dt)
    wob = sb("wob", [128, 8, 1024], BF16)
    wst = [sb("wst%d" % i, [128, 1024]) for i in range(2)]
    cp = sb("cp", [128, CP_N])
    kc = sb("kc", [128, 256])
    eps1 = sb("eps1", [128, 1]); eps2 = sb("eps2", [128, 1])
    oT = [sb("oT%d" % i, [128, 8, C1_TG], BF16) for i in range(2)]
    fin = [sb("fin%d" % i, [128, C1_TG]) for i in range(6)]
    tmp = [sb("tmp%d" % i, [128, C1_TG]) for i in range(6)]
    hin = [sb("hin%d" % i, [128, 1024]) for i in range(2)]
    ps = [P.psum("c1_ps%d" % i, [128, 512], F32) for i in range(4)]
    pd = [P.psum("c1_pd%d" % i, [128, 1024], F32) for i in range(2)]

    ones_m = kc[:, 0:128]; blk_m = kc[:, 128:256]
    P.dma("sp", cp, cp[:], cp_d, cp_d[:])
    P.dma("sp", kc, kc[:], k_d, k_d[:])
    P.op("dve", lambda e: e.memset(eps1[:], 1e-6), [], [eps1])
    P.op("dve", lambda e: e.memset(eps2[:], 64e-5), [], [eps2])
    for k in range(8):
        s_ = wst[k % 2]
        P.dma("sp" if k % 2 == 0 else "act", s_, s_[:], wo_d, wo_d[k * 128:(k + 1) * 128, :])
        P.op("pool", lambda e: e.tensor_copy(out=wob[:, k, :], in_=s_[:]), [s_], [wob])

    fi = [0]; ti = [0]; pi = [0]; qi = [0]

    def load(row0, t0):
        f = fin[fi[0] % 6]; fi[0] += 1
        q = ("sp", "act", "pool")[qi[0] % 3]; qi[0] += 1
        P.dma(q, f, f[:], cf_d, cf_d[row0:row0 + 128, t0:t0 + C1_TG])
        return f

    def T_():
        t = tmp[ti[0] % 6]; ti[0] += 1
        return t

    def PS():
        p = ps[pi[0] % 4]; pi[0] += 1
        return p

    for g in range(C1_NG):
        t0 = g * C1_TG
        ot = oT[g % 2]
        for c in range(4):
            o = load(CF_HGO + c * 128, t0); gs = load(CF_GS + c * 128, t0)
            sq = T_(); p_ = PS(); rs = T_()
            P.op("pool", lambda e: e.tensor_tensor(out=sq[:], in0=o[:], in1=o[:], op=ALU.mult), [o], [sq])
            P.op("pe", lambda e: e.matmul(p_[:], lhsT=ones_m, rhs=sq[:], start=True, stop=True), [kc, sq], [p_])
            P.op("act", lambda e: e.activation(out=rs[:], in_=p_[:], func=AF.Sqrt, bias=eps1[:, 0:1]), [p_, eps1], [rs])
            P.op("dve", lambda e: e.reciprocal(out=rs[:], in_=rs[:]), [rs], [rs])
            P.op("dve", lambda e: e.scalar_tensor_tensor(out=rs[:], in0=rs[:], scalar=cp[:, CP_GN + c:CP_GN + c + 1], in1=o[:],
                                                         op0=ALU.mult, op1=ALU.mult), [rs, cp, o], [rs])
            P.op("pool", lambda e: e.tensor_tensor(out=ot[:, c, :], in0=rs[:], in1=gs[:], op=ALU.mult), [rs, gs], [ot])
        for c in range(2):
            o = load(CF_SWO + c * 128, t0)
            P.op("pool", lambda e: e.tensor_copy(out=ot[:, 4 + c, :], in_=o[:]), [o], [ot])
        for c in range(2):
            y = load(CF_RWY + c * 128, t0); r_ = load(CF_R + c * 128, t0); kp = load(CF_KP + c * 128, t0)
            v_ = load(CF_V + c * 128, t0); g_ = load(CF_G + c * 128, t0)
            pm = PS(); pq = PS(); pb = PS()
            ysq = T_(); mean = T_(); var = T_(); rk = T_()
            P.op("pe", lambda e: e.matmul(pm[:], lhsT=blk_m, rhs=y[:], start=True, stop=True), [kc, y], [pm])
            P.op("pool", lambda e: e.tensor_tensor(out=ysq[:], in0=y[:], in1=y[:], op=ALU.mult), [y], [ysq])
            P.op("pe", lambda e: e.matmul(pq[:], lhsT=blk_m, rhs=ysq[:], start=True, stop=True), [kc, ysq], [pq])
            P.op("act", lambda e: e.activation(out=mean[:], in_=pm[:], func=AF.Copy), [pm], [mean])
            P.op("pool", lambda e: e.tensor_tensor(out=var[:], in0=mean[:], in1=mean[:], op=ALU.mult), [mean], [var])
            P.op("dve", lambda e: e.tensor_tensor(out=var[:], in0=pq[:], in1=var[:], op=ALU.subtract), [pq, var], [var])
            P.op("act", lambda e: e.activation(out=var[:], in_=var[:], func=AF.Sqrt, bias=eps2[:, 0:1]), [var, eps2], [var])
            P.op("dve", lambda e: e.reciprocal(out=var[:], in_=var[:]), [var], [var])
            P.op("dve", lambda e: e.tensor_tensor(out=mean[:], in0=y[:], in1=mean[:], op=ALU.subtract), [y, mean], [mean])
            P.op("dve", lambda e: e.tensor_tensor(out=mean[:], in0=mean[:], in1=var[:], op=ALU.mult), [mean, var], [mean])
            P.op("dve", lambda e: e.tensor_scalar(out=mean[:], in0=mean[:], scalar1=cp[:, CP_LNW + c:CP_LNW + c + 1],
                                                   scalar2=cp[:, CP_LNB + c:CP_LNB + c + 1], op0=ALU.mult, op1=ALU.add), [mean, cp], [mean])
            P.op("dve", lambda e: e.scalar_tensor_tensor(out=rk[:], in0=r_[:], scalar=cp[:, CP_RK + c:CP_RK + c + 1], in1=kp[:],
                                                         op0=ALU.mult, op1=ALU.mult), [r_, cp, kp], [rk])
            P.op("pe", lambda e: e.matmul(pb[:], lhsT=blk_m, rhs=rk[:], start=True, stop=True), [kc, rk], [pb])
            P.op("dve", lambda e: e.scalar_tensor_tensor(out=rk[:], in0=pb[:], scalar=64.0, in1=v_[:], op0=ALU.mult, op1=ALU.mult),
                 [pb, v_], [rk])
            P.op("pool", lambda e: e.tensor_tensor(out=mean[:], in0=mean[:], in1=rk[:], op=ALU.add), [mean, rk], [mean])
            P.op("pool", lambda e: e.tensor_tensor(out=ot[:, 6 + c, :], in0=mean[:], in1=g_[:], op=ALU.mult), [mean, g_], [ot])
        for t in range(4):
            hi = hin[t % 2]; p_d = pd[t % 2]
            r0 = t0 + t * 128
            P.dma("sp", hi, hi[:], h_d, h_d[r0:r0 + 128, :])
            for half in range(2):
                for c in range(8):
                    P.op("pe", lambda e: e.matmul(p_d[:, half * 512:(half + 1) * 512], lhsT=ot[:, c, t * 128:(t + 1) * 128],
                                                  rhs=wob[:, c, half * 512:(half + 1) * 512], start=(c == 0), stop=(c == 7)),
                         [ot, wob], [p_d])
            for half in range(2):
                P.op("dve", lambda e: e.tensor_tensor(out=hi[:, half * 512:(half + 1) * 512], in0=hi[:, half * 512:(half + 1) * 512],
                                                      in1=p_d[:, half * 512:(half + 1) * 512], op=ALU.add), [hi, p_d], [hi])
            P.dma("act", hm_d, hm_d[r0:r0 + 128, :], hi, hi[:], owner=hi, disjoint=True)
    return [hm_d]


def c1_inputs(layer, inp, core, h_full, cf_full):
    l = layer
    fmj = lambda v: np.ascontiguousarray(v.reshape(-1, 128).T)
    cp = np.zeros((128, CP_N), np.float32)
    cp[:, CP_GN:CP_GN + 4] = fmj(inp['hgrn_gnorm_g'][l])
    cp[:, CP_LNW:CP_LNW + 2] = fmj(inp['rwkv_ln_w'][l])
    cp[:, CP_LNB:CP_LNB + 2] = fmj(inp['rwkv_ln_b'][l])
    cp[:, CP_RK:CP_RK + 2] = fmj(inp['rwkv_r_k'][l].reshape(-1))
    kc = np.concatenate([np.full((128, 128), 1.0 / 128), np.kron(np.eye(2), np.full((64, 64), 1.0 / 64))], axis=1).astype(np.float32)
    return {"c1_h": np.ascontiguousarray(h_full[core * C1_NT:(core + 1) * C1_NT]),
            "c1_cf": np.ascontiguousarray(cf_full[:, core * C1_NT:(core + 1) * C1_NT]),
            "c1_wout": np.ascontiguousarray(inp['w_out'][l]), "c1_cp": cp, "c1_consts": kc}


C2_NT = 2048
C2_GT = 256
C2_NGR = C2_NT // C2_GT
C2_DFF = 2816
C2_NJ = C2_DFF // 128


def norm_transpose(P, src_dram_t, src_ap, hres, hnb, st, junk, epsc, ident, pst, dstT, col0, dma_q="sp"):
    P.dma(dma_q, hres, hres[:], src_dram_t, src_ap)
    P.op("act", lambda e: e.activation(out=junk[:], in_=hres[:], func=AF.Square, accum_out=st[:, 0:1]), [hres], [junk, st])
    P.op("act", lambda e: e.activation(out=st[:, 1:2], in_=st[:, 0:1], func=AF.Sqrt, scale=1.0 / 1024, bias=epsc[:, 0:1]), [st, epsc], [st])
    P.op("dve", lambda e: e.reciprocal(out=st[:, 2:3], in_=st[:, 1:2]), [st], [st])
    P.op("dve", lambda e: e.tensor_scalar(out=hnb[:], in0=hres[:], scalar1=st[:, 2:3], scalar2=None, op0=ALU.mult), [hres, st], [hnb])
    for c in range(8):
        P.op("pe", lambda e: e.transpose(out=pst[:, c * 128:(c + 1) * 128], in_=hnb[:, c * 128:(c + 1) * 128], identity=ident[:]),
             [hnb, ident], [pst])
    P.op("act", lambda e: e.activation(out=dstT[:, :, col0:col0 + 128], in_=pst[:].rearrange("p (c t) -> p c t", c=8), func=AF.Copy),
         [pst], [dstT])


def emit_c2(P):
    h_d = P.dram("c2_h", [C2_NT, 1024], F32, "ExternalInput")
    hh_d = P.dram("c2_hh", [128, 1024], F32, "ExternalInput")
    up_d = P.dram("c2_up", [1024, 2 * C2_DFF], F32, "ExternalInput")
    dn_d = P.dram("c2_dn", [C2_DFF, 1024], F32, "ExternalInput")
    pp_d = P.dram("c2_pp", [128, 8 + 44 * 4], F32, "ExternalInput")
    id_d = P.dram("c2_ident", [128, 128], BF16, "ExternalInput")
    o_d = P.dram("c2_out", [C2_NT, 1024], F32, "ExternalOutput")

    def sb(name, shape, dt=F32):
        return P.sbuf("c2s_" + name, shape, dt)
    upb = [sb("upb%d" % k, [128, 2 * C2_DFF], BF16) for k in range(8)]
    dnb = sb("dnb", [128, C2_NJ, 1024], BF16)
    pp = sb("pp", [128, 8 + 44 * 4])
    ident = sb("ident", [128, 128], BF16)
    epsc = sb("epsc", [128, 1])
    hres = [sb("hres%d" % i, [128, 1024]) for i in range(2)]
    hnb = [sb("hnb%d" % i, [128, 1024], BF16) for i in range(2)]
    st = [sb("st%d" % i, [128, 4]) for i in range(2)]
    junk = sb("junk", [128, 1024])
    hnT = sb("hnT", [128, 8, C2_GT], BF16)
    ug = [sb("ug%d" % i, [128, C2_GT + 2]) for i in range(2)]
    uv = [sb("uv%d" % i, [128, C2_GT + 2]) for i in range(2)]
    tg = [sb("tg%d" % i, [128, C2_GT]) for i in range(2)]
    tv = [sb("tv%d" % i, [128, C2_GT]) for i in range(2)]
    actT = sb("actT", [128, C2_NJ, C2_GT], BF16)
    uprev = sb("uprev", [128, 44, 2])
    pst = P.psum("c2_pst", [128, 1024], BF16)
    pu = [P.psum("c2_pu%d" % i, [128, 512], F32) for i in range(3)]
    pd = [P.psum("c2_pd%d" % i, [128, 512], F32) for i in range(4)]

    P.dma("sp", pp, pp[:], pp_d, pp_d[:])
    P.dma("sp", ident, ident[:], id_d, id_d[:])
    P.op("dve", lambda e: e.memset(epsc[:], 1e-6), [], [epsc])
    wi = 0
    for k in range(8):
        for c0 in range(0, 2 * C2_DFF, 1024):
            w_ = min(1024, 2 * C2_DFF - c0)
            s_ = hres[wi % 2]; wi += 1
            P.dma("sp" if wi % 2 == 0 else "act", s_, s_[:, 0:w_], up_d, up_d[k * 128:(k + 1) * 128, c0:c0 + w_])
            P.op("dve", lambda e: e.tensor_scalar(out=upb[k][:, c0:c0 + w_], in0=s_[:, 0:w_], scalar1=pp[:, k:k + 1], scalar2=None,
                                                  op0=ALU.mult), [s_, pp], [upb[k]])
    for j in range(C2_NJ):
        s_ = hres[wi % 2]; wi += 1
        P.dma("sp" if wi % 2 == 0 else "act", s_, s_[:], dn_d, dn_d[j * 128:(j + 1) * 128, :])
        P.op("pool", lambda e: e.tensor_copy(out=dnb[:, j, :], in_=s_[:]), [s_], [dnb])

    def cw(c, i):
        o = 8 + c * 4 + i
        return pp[:, o:o + 1]

    norm_transpose(P, hh_d, hh_d[:], hres[0], hnb[0], st[0], junk, epsc, ident, pst, hnT, 0)
    for c in range(44):
        p_ = pu[c % 3]
        for k in range(8):
            P.op("pe", lambda e: e.matmul(p_[:, 0:2], lhsT=upb[k][:, c * 128:(c + 1) * 128], rhs=hnT[:, k, 126:128],
                                          start=(k == 0), stop=(k == 7)), [upb[k], hnT], [p_])
        P.op("act", lambda e: e.activation(out=uprev[:, c, :], in_=p_[:, 0:2], func=AF.Copy), [p_], [uprev])

    ui = 0
    for g in range(C2_NGR):
        r0 = g * C2_GT
        for t in range(2):
            norm_transpose(P, h_d, h_d[r0 + t * 128:r0 + (t + 1) * 128, :], hres[t], hnb[t], st[t], junk, epsc, ident, pst, hnT, t * 128)
        for j in range(C2_NJ):
            cg, cv = j, C2_NJ + j
            p_ = pu[ui % 3]; u_g = ug[ui % 2]; u_v = uv[ui % 2]; t_g = tg[ui % 2]; t_v = tv[ui % 2]
            ui += 1
            for k in range(8):
                P.op("pe", lambda e: e.matmul(p_[:, 0:C2_GT], lhsT=upb[k][:, cg * 128:(cg + 1) * 128], rhs=hnT[:, k, :],
                                              start=(k == 0), stop=(k == 7)), [upb[k], hnT], [p_])
            for k in range(8):
                P.op("pe", lambda e: e.matmul(p_[:, C2_GT:2 * C2_GT], lhsT=upb[k][:, cv * 128:(cv + 1) * 128], rhs=hnT[:, k, :],
                                              start=(k == 0), stop=(k == 7)), [upb[k], hnT], [p_])
            P.op("pool", lambda e: e.tensor_copy(out=u_g[:, 0:2], in_=uprev[:, cg, :]), [uprev], [u_g])
            P.op("pool", lambda e: e.tensor_copy(out=u_v[:, 0:2], in_=uprev[:, cv, :]), [uprev], [u_v])
            P.op("act", lambda e: e.activation(out=u_g[:, 2:C2_GT + 2], in_=p_[:, 0:C2_GT], func=AF.Copy), [p_], [u_g])
            P.op("act", lambda e: e.activation(out=u_v[:, 2:C2_GT + 2], in_=p_[:, C2_GT:2 * C2_GT], func=AF.Copy), [p_], [u_v])
            P.op("pool", lambda e: e.tensor_copy(out=uprev[:, cg, :], in_=u_g[:, C2_GT:C2_GT + 2]), [u_g], [uprev])
            P.op("pool", lambda e: e.tensor_copy(out=uprev[:, cv, :], in_=u_v[:, C2_GT:C2_GT + 2]), [u_v], [uprev])
            for (u_, t_, c_) in ((u_g, t_g, cg), (u_v, t_v, cv)):
                P.op("dve", lambda e: e.tensor_scalar(out=t_[:], in0=u_[:, 2:C2_GT + 2], scalar1=cw(c_, 2), scalar2=cw(c_, 3),
                                                      op0=ALU.mult, op1=ALU.add), [u_, pp], [t_])
                P.op("dve", lambda e: e.scalar_tensor_tensor(out=t_[:], in0=u_[:, 1:C2_GT + 1], scalar=cw(c_, 1), in1=t_[:],
                                                             op0=ALU.mult, op1=ALU.add), [u_, pp, t_], [t_])
                P.op("dve", lambda e: e.scalar_tensor_tensor(out=t_[:], in0=u_[:, 0:C2_GT], scalar=cw(c_, 0), in1=t_[:],
                                                             op0=ALU.mult, op1=ALU.add), [u_, pp, t_], [t_])
            P.op("act", lambda e: e.activation(out=t_g[:], in_=t_g[:], func=AF.Silu), [t_g], [t_g])
            P.op("pool", lambda e: e.tensor_tensor(out=actT[:, j, :], in0=t_g[:], in1=t_v[:], op=ALU.mult), [t_g, t_v], [actT])
        for t in range(2):
            for half in range(2):
                p_d = pd[(2 * t + half) % 4]
                for j in range(C2_NJ):
                    P.op("pe", lambda e: e.matmul(p_d[:], lhsT=actT[:, j, t * 128:(t + 1) * 128], rhs=dnb[:, j, half * 512:(half + 1) * 512],
                                                  start=(j == 0), stop=(j == C2_NJ - 1)), [actT, dnb], [p_d])
                P.op("dve", lambda e: e.tensor_tensor(out=hres[t][:, half * 512:(half + 1) * 512], in0=hres[t][:, half * 512:(half + 1) * 512],
                                                      in1=p_d[:], op=ALU.add), [hres[t], p_d], [hres[t]])
            P.dma("act", o_d, o_d[r0 + t * 128:r0 + (t + 1) * 128, :], hres[t], hres[t][:], owner=hres[t], disjoint=True)
    return [o_d]


def c2_inputs(layer, inp, core, hmid_full):
    l = layer
    fmj = lambda v: np.ascontiguousarray(v.reshape(-1, 128).T)
    pp = np.zeros((128, 8 + 44 * 4), np.float32)
    pp[:, 0:8] = fmj(inp['norm_ffn_g'][l])
    cwb = np.concatenate([inp['ffn_conv_w'][l], inp['ffn_conv_b'][l][None]], axis=0)
    pp[:, 8:] = cwb.reshape(4, 44, 128).transpose(2, 1, 0).reshape(128, 176)
    return {"c2_h": np.ascontiguousarray(hmid_full[core * C2_NT:(core + 1) * C2_NT]),
            "c2_hh": np.ascontiguousarray(hmid_full[core * C2_NT - 128:core * C2_NT]) if core > 0 else np.zeros((128, 1024), np.float32),
            "c2_up": np.ascontiguousarray(inp['ffn_up'][l]), "c2_dn": np.ascontiguousarray(inp['ffn_down'][l]),
            "c2_pp": pp, "c2_ident": np.eye(128).astype(ml_dtypes.bfloat16)}


C3_NT = 2048


def emit_c3(P, final):
    h_d = P.dram("c3_h", [C3_NT, 1024], F32, "ExternalInput")
    pT_d = P.dram("c3_pT", [256, C3_NT], F32, "ExternalInput")
    gt_d = P.dram("c3_gate", [1024, 1024], F32, "ExternalInput")
    pj_d = P.dram("c3_proj", [256, 1024], F32, "ExternalInput")
    pp_d = P.dram("c3_pp", [128, 8], F32, "ExternalInput")
    id_d = P.dram("c3_ident", [128, 128], BF16, "ExternalInput")
    if final:
        gf_d = P.dram("c3_gfin", [128, 1024], F32, "ExternalInput")
    o_d = P.dram("c3_out", [C3_NT, 1024], F32, "ExternalOutput")

    def sb(name, shape, dt=F32):
        return P.sbuf("c3s_" + name, shape, dt)
    gtb = sb("gtb", [128, 8, 1024], BF16)
    pjb = sb("pjb", [128, 2, 1024], BF16)
    wst = [sb("wst%d" % i, [128, 1024]) for i in range(2)]
    pp = sb("pp", [128, 8])
    ident = sb("ident", [128, 128], BF16)
    epsc = sb("epsc", [128, 1])
    p32 = sb("p32", [128, 2, C3_NT])
    pTb = sb("pTb", [128, 2, C3_NT], BF16)
    hres = [sb("hres%d" % i, [128, 1024]) for i in range(2)]
    hnb = [sb("hnb%d" % i, [128, 1024], BF16) for i in range(2)]
    st = [sb("st%d" % i, [128, 4]) for i in range(2)]
    st2 = [sb("st2%d" % i, [128, 4]) for i in range(2)]
    junk = sb("junk", [128, 1024])
    hnT = [sb("hnT%d" % i, [128, 8, 128], BF16) for i in range(2)]
    sig = [sb("sig%d" % i, [128, 1024]) for i in range(2)]
    gfin = sb("gfin", [128, 1024]) if final else None
    pst = P.psum("c3_pst", [128, 1024], BF16)
    pg = [P.psum("c3_pg%d" % i, [128, 512], F32) for i in range(4)]
    pq = [P.psum("c3_pq%d" % i, [128, 512], F32) for i in range(2)]

    P.dma("sp", pp, pp[:], pp_d, pp_d[:])
    P.dma("sp", ident, ident[:], id_d, id_d[:])
    if final:
        P.dma("sp", gfin, gfin[:], gf_d, gf_d[:])
    P.op("dve", lambda e: e.memset(epsc[:], 1e-6), [], [epsc])
    for k in range(8):
        s_ = wst[k % 2]
        P.dma("sp" if k % 2 == 0 else "act", s_, s_[:], gt_d, gt_d[k * 128:(k + 1) * 128, :])
        P.op("dve", lambda e: e.tensor_scalar(out=gtb[:, k, :], in0=s_[:], scalar1=pp[:, k:k + 1], scalar2=None, op0=ALU.mult),
             [s_, pp], [gtb])
    for c in range(2):
        s_ = wst[c % 2]
        P.dma("sp" if c % 2 == 0 else "act", s_, s_[:], pj_d, pj_d[c * 128:(c + 1) * 128, :])
        P.op("pool", lambda e: e.tensor_copy(out=pjb[:, c, :], in_=s_[:]), [s_], [pjb])
    for c in range(2):
        P.dma("pool", p32, p32[:, c, :], pT_d, pT_d[c * 128:(c + 1) * 128, :], disjoint=(c > 0))
    P.op("pool", lambda e: e.tensor_copy(out=pTb[:], in_=p32[:]), [p32], [pTb])

    for t in range(C3_NT // 128):
        i = t % 2
        r0 = t * 128
        hr = hres[i]; hT = hnT[i]; sg = sig[i]
        norm_transpose(P, h_d, h_d[r0:r0 + 128, :], hr, hnb[i], st[i], junk, epsc, ident, pst, hT, 0)
        for half in range(2):
            p_g = pg[(2 * t + half) % 4]; p_q = pq[half]
            for k in range(8):
                P.op("pe", lambda e: e.matmul(p_g[:], lhsT=hT[:, k, :], rhs=gtb[:, k, half * 512:(half + 1) * 512],
                                              start=(k == 0), stop=(k == 7)), [hT, gtb], [p_g])
            for c in range(2):
                P.op("pe", lambda e: e.matmul(p_q[:], lhsT=pTb[:, c, r0:r0 + 128], rhs=pjb[:, c, half * 512:(half + 1) * 512],
                                              start=(c == 0), stop=(c == 1)), [pTb, pjb], [p_q])
            hs = slice(half * 512, (half + 1) * 512)
            P.op("act", lambda e: e.activation(out=sg[:, hs], in_=p_g[:], func=AF.Sigmoid), [p_g], [sg])
            P.op("dve", lambda e: e.tensor_tensor(out=sg[:, hs], in0=sg[:, hs], in1=p_q[:], op=ALU.mult), [sg, p_q], [sg])
            P.op("pool", lambda e: e.tensor_tensor(out=hr[:, hs], in0=hr[:, hs], in1=sg[:, hs], op=ALU.add), [hr, sg], [hr])
        if final:
            s2 = st2[i]
            P.op("act", lambda e: e.activation(out=junk[:], in_=hr[:], func=AF.Square, accum_out=s2[:, 0:1]), [hr], [junk, s2])
            P.op("act", lambda e: e.activation(out=s2[:, 1:2], in_=s2[:, 0:1], func=AF.Sqrt, scale=1.0 / 1024, bias=epsc[:, 0:1]),
                 [s2, epsc], [s2])
            P.op("dve", lambda e: e.reciprocal(out=s2[:, 2:3], in_=s2[:, 1:2]), [s2], [s2])
            P.op("dve", lambda e: e.scalar_tensor_tensor(out=hr[:], in0=hr[:], scalar=s2[:, 2:3], in1=gfin[:], op0=ALU.mult, op1=ALU.mult),
                 [hr, s2, gfin], [hr])
        P.dma("act", o_d, o_d[r0:r0 + 128, :], hr, hr[:], owner=hr, disjoint=True)
    return [o_d]


def c3_inputs(layer, inp, core, hffn_full, final):
    l = layer
    fmj = lambda v: np.ascontiguousarray(v.reshape(-1, 128).T)
    m = {"c3_h": np.ascontiguousarray(hffn_full[core * C3_NT:(core + 1) * C3_NT]),
         "c3_pT": np.ascontiguousarray(inp['p'][l, 0, core * C3_NT:(core + 1) * C3_NT, :].T),
         "c3_gate": np.ascontiguousarray(inp['ple_gate'][l]), "c3_proj": np.ascontiguousarray(inp['ple_proj'][l]),
         "c3_pp": fmj(inp['norm_ple_g'][l]), "c3_ident": np.eye(128).astype(ml_dtypes.bfloat16)}
    if final:
        m["c3_gfin"] = np.ascontiguousarray(np.broadcast_to(inp['final_norm_g'][None, :], (128, 1024))).astype(np.float32)
    return m


def _launch(build, maps):
    nc = bass.Bass("TRN2", target_bir_lowering=False)
    P = Prog(nc)
    outs = build(P)
    P.final_wait("sp", outs)
    P.emit()
    res = run_bass_kernel_spmd(nc, maps, core_ids=list(range(8)))
    return res.results


def kernel(**inputs):
    inp = {k: np.asarray(v) for k, v in inputs.items()}
    S = 16384
    h = np.ascontiguousarray(inp['x'][0], dtype=np.float32)
    vfirst = None
    for l in range(2):
        nc, P = build_A(l)
        res = run_bass_kernel_spmd(nc, host_inputs_A(l, inp, h, vfirst), core_ids=list(range(8))).results
        fm = np.concatenate([r["fm"] for r in res], axis=1)
        tm = np.concatenate([r["tm"] for r in res], axis=0)
        del res
        if l == 0:
            vfirst = np.ascontiguousarray(fm[FM_V:FM_V + 256])
        res = _launch(emit_sw, [sw_inputs(c, fm[FM_BQ:FM_BQ + 256], fm[FM_BK:FM_BK + 256], tm[:, 512:768]) for c in range(8)])
        cf = np.empty((CF_ROWS, S), np.float32)
        for c in range(8):
            hd, s = c // 2, c % 2
            cf[CF_SWO + hd * 64:CF_SWO + (hd + 1) * 64, s * NOWN:(s + 1) * NOWN] = res[c]["sw_o"]
        res = _launch(emit_hgrn, [hg_inputs(c, fm[FM_QS:FM_QS + 512], fm[FM_LF:FM_LF + 512], tm[:, 0:512]) for c in range(8)])
        for c in range(8):
            hd, vh = c // 2, c % 2
            cf[CF_HGO + hd * 128 + vh * 64:CF_HGO + hd * 128 + (vh + 1) * 64] = res[c]["hg_o"]
        fmr = {"r": fm[FM_R:FM_R + 256], "kp": fm[FM_KP:FM_KP + 256], "kk": fm[FM_KK:FM_KK + 256],
               "a": fm[FM_A:FM_A + 256], "ld": fm[FM_LD:FM_LD + 256], "v": fm[FM_V:FM_V + 256]}
        res = _launch(emit_rwkv, [rw_inputs(c, fmr) for c in range(8)])
        for c in range(8):
            hd, vh = c // 2, c % 2
            cf[CF_RWY + hd * 64 + vh * 32:CF_RWY + hd * 64 + (vh + 1) * 32] = res[c]["rw_y"]
        cf[CF_GS:CF_GS + 512] = fm[FM_GS:FM_GS + 512]
        cf[CF_R:CF_R + 256] = fm[FM_R:FM_R + 256]
        cf[CF_KP:CF_KP + 256] = fm[FM_KP:FM_KP + 256]
        cf[CF_V:CF_V + 256] = fm[FM_V:FM_V + 256]
        cf[CF_G:CF_G + 256] = fm[FM_G:FM_G + 256]
        del fm, tm, fmr
        res = _launch(lambda P: emit_c1(P, l), [c1_inputs(l, inp, c, h, cf) for c in range(8)])
        hmid = np.concatenate([r["c1_hmid"] for r in res], axis=0)
        del cf
        res = _launch(emit_c2, [c2_inputs(l, inp, c, hmid) for c in range(8)])
        hffn = np.concatenate([r["c2_out"] for r in res], axis=0)
        final = (l == 1)
        res = _launch(lambda P: emit_c3(P, final), [c3_inputs(l, inp, c, hffn, final) for c in range(8)])
        h = np.concatenate([r["c3_out"] for r in res], axis=0)
    return h[None].astype(np.float32)
```

```python
import numpy as np
import ml_dtypes
from concourse.bass_utils import run_bass_kernel_spmd


import concourse.bass as bass
import concourse.mybir as mybir

F32 = mybir.dt.float32
BF16 = mybir.dt.bfloat16
AF = mybir.ActivationFunctionType
ALU = mybir.AluOpType
AX = mybir.AxisListType

ENGS = ("pe", "act", "dve", "pool", "sp")


class T:
    __slots__ = ("name", "h", "last_w", "readers", "sem", "cnt", "excl")

    def __init__(self, name, h):
        self.name = name
        self.h = h
        self.last_w = {}
        self.readers = []
        self.sem = None
        self.cnt = 0
        self.excl = False

    def __getitem__(self, idx):
        return self.h[idx]


class _Rec:
    def __getattr__(self, name):
        def f(*a, **k):
            self.call = (name, a, k)
        return f


def _eager(fn):
    r = _Rec()
    fn(r)
    name, a, k = r.call
    return lambda e: getattr(e, name)(*a, **k)


class Prog:
    def __init__(self, nc):
        self.nc = nc
        self.ops = {e: [] for e in ENGS}
        self.count = {e: 0 for e in ENGS}
        self.waited = {e: {} for e in ENGS}
        self.dma_sems = []
        self.ctx = []
        self.ntiles = 0

    def sbuf(self, name, shape, dt):
        g = self.nc.sbuf_tensor(name, list(shape), dt)
        h = g.__enter__()
        self.ctx.append(g)
        return T(name, h)

    def psum(self, name, shape, dt=F32):
        g = self.nc.psum_tensor(name, list(shape), dt)
        h = g.__enter__()
        self.ctx.append(g)
        t = T(name, h)
        t.excl = True
        return t

    def dram(self, name, shape, dt, kind="Internal"):
        h = self.nc.dram_tensor(name, list(shape), dt, kind=kind)
        return T(name, h.ap() if hasattr(h, "ap") else h)

    def view(self, name, h):
        return T(name, h)

    def _deps(self, eng, reads, writes):
        deps = []
        for t in reads:
            for ev in t.last_w.items():
                deps.append((ev, "raw"))
            if t.excl:
                for r in t.readers:
                    if r[0] != eng:
                        deps.append((r, "rar"))
        for t in writes:
            if not getattr(self, "_disjoint", False):
                for ev in t.last_w.items():
                    deps.append((ev, "waw"))
            for r in t.readers:
                deps.append((r, "war"))
        out = {}
        for (key, val), kind in deps:
            if key == eng:
                if eng == "pe":
                    continue
            if out.get(key, 0) < val:
                out[key] = val
        res = []
        w = self.waited[eng]
        for key, val in out.items():
            if w.get(key, 0) >= val:
                continue
            w[key] = val
            res.append((key, val))
        return res

    def op(self, eng, fn, reads=(), writes=()):
        waits = self._deps(eng, reads, writes)
        self.count[eng] += 1
        ev = (eng, self.count[eng])
        for t in reads:
            t.readers.append(ev)
        for t in writes:
            t.last_w = {ev[0]: ev[1]}
            t.readers = []
        self.ops[eng].append((waits, _eager(fn), None))

    def dma(self, eng, out_t, out_ap, in_t, in_ap, owner=None, disjoint=False, **kw):
        self._disjoint = disjoint
        waits = self._deps(eng, [in_t], [out_t])
        self._disjoint = False
        ow = owner if owner is not None else out_t
        if ow.sem is None:
            g = self.nc.semaphore("ds%d" % len(self.dma_sems))
            ow.sem = g.__enter__()
            self.ctx.append(g)
            self.dma_sems.append(ow.sem)
        ow.cnt += 16
        ev = (ow.sem, ow.cnt)
        in_t.readers.append(ev)
        if disjoint:
            out_t.last_w[ev[0]] = ev[1]
        else:
            out_t.last_w = {ev[0]: ev[1]}
            out_t.readers = []

        def fn(e, out_ap=out_ap, in_ap=in_ap, kw=kw):
            return e.dma_start(out=out_ap, in_=in_ap, **kw)
        self.ops[eng].append((waits, fn, ow.sem))

    def final_wait(self, eng, tiles):
        waits = self._deps(eng, tiles, [])
        self.ops[eng].append((waits, None, None))

    def emit(self):
        nc = self.nc
        esem = {}
        for e in ENGS:
            g = nc.semaphore("es_" + e)
            esem[e] = g.__enter__()
            self.ctx.append(g)
        engobj = {"pe": "tensor", "act": "scalar", "dve": "vector", "pool": "gpsimd", "sp": "sync"}

        def run(e, eng):
            for waits, fn, dsem in self.ops[e]:
                for key, val in waits:
                    s = esem[key] if isinstance(key, str) else key
                    eng.wait_ge(s, val)
                if fn is None:
                    continue
                ins = fn(eng)
                if dsem is not None:
                    ins.then_inc(dsem, 16)
                else:
                    ins.then_inc(esem[e], 1)

        with nc.Block() as block:
            for e in ENGS:
                if not self.ops[e]:
                    continue
                getattr(block, engobj[e])(lambda eng, e=e: run(e, eng))

    def close(self):
        for g in reversed(self.ctx):
            g.__exit__(None, None, None)
        self.ctx = []


A_NT = 2048
A_TG = 512
A_NG = A_NT // A_TG
FM_QS, FM_LF, FM_GS, FM_BQ, FM_BK = 0, 512, 1024, 1536, 1792
FM_R, FM_KP, FM_KK, FM_A, FM_LD, FM_V, FM_G = [2048 + 256 * i for i in range(7)]
FM_ROWS = 3840
PP_GMIX, PP_MU, PP_W0, PP_A0, PP_KK, PP_KA, PP_V0, PP_HB0, PP_HB1, PP_N = 0, 8, 16, 18, 20, 22, 24, 26, 30, 34
PM_W2, PM_A2, PM_G2, PM_V1, PM_V2, PM_N = 0, 256, 512, 768, 832, 1088


def build_A(layer):
    nc = bass.Bass("TRN2", target_bir_lowering=False)
    P = Prog(nc)
    h = P.dram("h", [A_NT, 1024], F32, "ExternalInput")
    hh = P.dram("hh", [128, 1024], F32, "ExternalInput")
    w_in = P.dram("w_in", [1024, 3840], F32, "ExternalInput")
    pp_d = P.dram("pp", [128, PP_N], F32, "ExternalInput")
    pm_d = P.dram("pm", [128, PM_N], F32, "ExternalInput")
    id_d = P.dram("ident", [128, 128], BF16, "ExternalInput")
    blk_d = P.dram("blk64", [128, 128], BF16, "ExternalInput")
    if layer == 1:
        vf_d = P.dram("vfirst", [256, A_NT], F32, "ExternalInput")
    fm = P.dram("fm", [FM_ROWS, A_NT], F32, "ExternalOutput")
    tm = P.dram("tm", [A_NT, 768], F32, "ExternalOutput")

    wbf = [P.sbuf("wbf%d" % k, [128, 3840], BF16) for k in range(8)]
    wst = [P.sbuf("wst%d" % i, [128, 1920], F32) for i in range(2)]
    pp = P.sbuf("pp_s", [128, PP_N], F32)
    pm32 = P.sbuf("pm32", [128, PM_N], F32)
    pm = P.sbuf("pm_s", [128, PM_N], BF16)
    ident = P.sbuf("ident_s", [128, 128], BF16)
    blk = P.sbuf("blk_s", [128, 128], BF16)
    hin = [P.sbuf("hin%d" % i, [128, 1024], F32) for i in range(2)]
    hsq = P.sbuf("hsq", [128, 1024], F32)
    hnb = [P.sbuf("hnb%d" % i, [128, 1024], BF16) for i in range(2)]
    st = [P.sbuf("st%d" % i, [128, 4], F32) for i in range(2)]
    hnT = [P.sbuf("hnT%d" % i, [128, 8, A_TG], BF16) for i in range(2)]
    gb = P.sbuf("gb", [128, 8, 128], F32)
    CB = [P.sbuf("CB%d" % i, [128, 8, A_TG + 1], F32) for i in range(2)]
    stg = [P.sbuf("stg%d" % i, [128, A_TG], F32) for i in range(6)]
    stt = [P.sbuf("stt%d" % i, [128, 768], F32) for i in range(2)]
    cm = P.sbuf("cm", [128, 8, A_TG], F32)
    tmpA = [P.sbuf("tmpA%d" % i, [128, A_TG], F32) for i in range(4)]
    tb = [P.sbuf("tb%d" % i, [128, A_TG], BF16) for i in range(4)]
    lbc = P.sbuf("lbc", [128, 8], F32)
    kac = P.sbuf("kac", [128, 2], F32)
    epsc = P.sbuf("epsc", [128, 1], F32)
    vfs = P.sbuf("vfs", [128, 2, A_TG], F32) if layer == 1 else None
    ps = [P.psum("ps%d" % i, [128, 512], F32) for i in range(6)]
    pst = P.psum("pst", [128, 1024], BF16)
    psm = P.psum("psm", [128, 512], F32)

    P.dma("sp", pp, pp[:], pp_d, pp_d[:])
    P.dma("sp", pm32, pm32[:], pm_d, pm_d[:])
    P.dma("sp", ident, ident[:], id_d, id_d[:])
    P.dma("sp", blk, blk[:], blk_d, blk_d[:])
    P.op("dve", lambda e: e.tensor_copy(out=pm[:], in_=pm32[:]), [pm32], [pm])
    P.op("dve", lambda e: e.memset(epsc[:], 1e-6), [], [epsc])
    for c in range(8):
        P.op("dve", lambda e, c=c: e.memset(gb[:, c, :], 1.0), [], [gb])
    for c in range(8):
        P.op("dve", lambda e, c=c: e.tensor_scalar(out=gb[:, c, :], in0=gb[:, c, :], scalar1=pp[:, PP_GMIX + c:PP_GMIX + c + 1],
                                                    scalar2=None, op0=ALU.mult), [gb, pp], [gb])
    P.op("dve", lambda e: e.tensor_scalar(out=kac[:], in0=pp[:, PP_KA:PP_KA + 2], scalar1=-1.0, scalar2=1.0,
                                          op0=ALU.mult, op1=ALU.add), [pp], [kac])
    if layer == 1:
        P.op("dve", lambda e: e.tensor_tensor(out=lbc[:, 0:4], in0=pp[:, PP_HB1:PP_HB1 + 4], in1=pp[:, PP_HB0:PP_HB0 + 4],
                                              op=ALU.subtract), [pp], [lbc])
        P.op("act", lambda e: e.activation(out=lbc[:, 0:4], in_=lbc[:, 0:4], func=AF.Sigmoid), [lbc], [lbc])
        P.op("dve", lambda e: e.tensor_scalar(out=lbc[:, 4:8], in0=lbc[:, 0:4], scalar1=-1.0, scalar2=1.0,
                                              op0=ALU.mult, op1=ALU.add), [lbc], [lbc])
    for k in range(8):
        for hf in range(2):
            s_ = wst[hf]
            P.dma("sp" if hf == 0 else "act", s_, s_[:], w_in, w_in[k * 128:(k + 1) * 128, hf * 1920:(hf + 1) * 1920])
            P.op("pool", lambda e, k=k, s_=s_, hf=hf: e.tensor_copy(out=wbf[k][:, hf * 1920:(hf + 1) * 1920], in_=s_[:]), [s_], [wbf[k]])

    outq = ["sp", "act", "pool"]
    oq = [0]

    def out_dma(dst_t, dst_ap, src_t, src_ap):
        q = outq[oq[0] % 3]
        oq[0] += 1
        P.dma(q, dst_t, dst_ap, src_t, src_ap, owner=src_t, disjoint=True)

    tcount = [0]

    def norm_tile(src_ap, dstT, col0):
        i = tcount[0] % 2
        tcount[0] += 1
        hi, hb, s_ = hin[i], hnb[i], st[i]
        P.dma("sp", hi, hi[:], h, src_ap)
        P.op("act", lambda e: e.activation(out=hsq[:], in_=hi[:], func=AF.Square, accum_out=s_[:, 0:1]), [hi], [hsq, s_])
        P.op("act", lambda e: e.activation(out=s_[:, 1:2], in_=s_[:, 0:1], func=AF.Sqrt, scale=1.0 / 1024, bias=epsc[:, 0:1]),
             [s_, epsc], [s_])
        P.op("dve", lambda e: e.reciprocal(out=s_[:, 2:3], in_=s_[:, 1:2]), [s_], [s_])
        P.op("dve", lambda e: e.tensor_scalar(out=hb[:], in0=hi[:], scalar1=s_[:, 2:3], scalar2=None, op0=ALU.mult),
             [hi, s_], [hb])
        for c in range(8):
            P.op("pe", lambda e, c=c: e.transpose(out=pst[:, c * 128:(c + 1) * 128], in_=hb[:, c * 128:(c + 1) * 128],
                                                   identity=ident[:]), [hb, ident], [pst])
        P.op("dve", lambda e: e.tensor_tensor(out=dstT[:, :, col0:col0 + 128],
                                              in0=pst[:].rearrange("p (c t) -> p c t", c=8), in1=gb[:], op=ALU.mult),
             [pst, gb], [dstT])

    def mm_fm(dst_ps, cc, src):
        for k in range(8):
            P.op("pe", lambda e, k=k: e.matmul(dst_ps[:], lhsT=wbf[k][:, cc * 128:(cc + 1) * 128], rhs=src[:, k, :],
                                                start=(k == 0), stop=(k == 7)), [wbf[k], src], [dst_ps])

    hT_h = hnT[1]
    P.hsrc = hh
    i0 = tcount[0]
    hi, hb, s_ = hin[0], hnb[0], st[0]
    tcount[0] += 1
    P.dma("sp", hi, hi[:], hh, hh[:])
    P.op("act", lambda e: e.activation(out=hsq[:], in_=hi[:], func=AF.Square, accum_out=s_[:, 0:1]), [hi], [hsq, s_])
    P.op("act", lambda e: e.activation(out=s_[:, 1:2], in_=s_[:, 0:1], func=AF.Sqrt, scale=1.0 / 1024, bias=epsc[:, 0:1]),
         [s_, epsc], [s_])
    P.op("dve", lambda e: e.reciprocal(out=s_[:, 2:3], in_=s_[:, 1:2]), [s_], [s_])
    P.op("dve", lambda e: e.tensor_scalar(out=hb[:], in0=hi[:], scalar1=s_[:, 2:3], scalar2=None, op0=ALU.mult), [hi, s_], [hb])
    for c in range(8):
        P.op("pe", lambda e, c=c: e.transpose(out=pst[:, c * 128:(c + 1) * 128], in_=hb[:, c * 128:(c + 1) * 128],
                                               identity=ident[:]), [hb, ident], [pst])
    P.op("dve", lambda e: e.tensor_tensor(out=hT_h[:, :, 0:128], in0=pst[:].rearrange("p (c t) -> p c t", c=8),
                                          in1=gb[:], op=ALU.mult), [pst, gb], [hT_h])
    for c8 in range(8):
        cc = 22 + c8
        pz = ps[c8 % 6]
        for k in range(8):
            P.op("pe", lambda e, k=k, cc=cc, pz=pz: e.matmul(pz[:, 0:128], lhsT=wbf[k][:, cc * 128:(cc + 1) * 128],
                                                              rhs=hT_h[:, k, 0:128], start=(k == 0), stop=(k == 7)),
                 [wbf[k], hT_h], [pz])
        P.op("act", lambda e, c8=c8, pz=pz: e.activation(out=CB[0][:, c8, 0:1], in_=pz[:, 127:128], func=AF.Copy), [pz], [CB[0]])

    sti = [0]

    def stage():
        s_ = stg[sti[0] % 6]
        sti[0] += 1
        return s_

    psi = [0]

    def nps():
        p_ = ps[psi[0] % 6]
        psi[0] += 1
        return p_

    for g in range(A_NG):
        hT = hnT[g % 2]
        cb = CB[g % 2]
        cbn = CB[(g + 1) % 2]
        t0 = g * A_TG
        for t in range(4):
            norm_tile(h[t0 + t * 128:t0 + (t + 1) * 128, :], hT, t * 128)
        for c in range(4):
            pz = nps(); mm_fm(pz, c, hT); s_ = stage()
            P.op("act", lambda e, pz=pz, s_=s_: e.activation(out=s_[:], in_=pz[:], func=AF.Silu), [pz], [s_])
            out_dma(fm, fm[FM_QS + c * 128:FM_QS + (c + 1) * 128, t0:t0 + A_TG], s_, s_[:])
        for c in range(4):
            pz = nps(); mm_fm(pz, 4 + c, hT); s_ = stage()
            P.op("act", lambda e, pz=pz, s_=s_: e.activation(out=s_[:], in_=pz[:], func=AF.Sigmoid), [pz], [s_])
            if layer == 1:
                P.op("dve", lambda e, s_=s_, c=c: e.tensor_scalar(out=s_[:], in0=s_[:], scalar1=lbc[:, 4 + c:5 + c],
                                                                    scalar2=lbc[:, c:c + 1], op0=ALU.mult, op1=ALU.add),
                     [s_, lbc], [s_])
            P.op("act", lambda e, s_=s_: e.activation(out=s_[:], in_=s_[:], func=AF.Ln), [s_], [s_])
            out_dma(fm, fm[FM_LF + c * 128:FM_LF + (c + 1) * 128, t0:t0 + A_TG], s_, s_[:])
        for c in range(4):
            pz = nps(); mm_fm(pz, 12 + c, hT); s_ = stage()
            P.op("act", lambda e, pz=pz, s_=s_: e.activation(out=s_[:], in_=pz[:], func=AF.Silu), [pz], [s_])
            out_dma(fm, fm[FM_GS + c * 128:FM_GS + (c + 1) * 128, t0:t0 + A_TG], s_, s_[:])
        for c in range(4):
            pz = nps(); mm_fm(pz, 16 + c, hT); s_ = stage()
            P.op("dve", lambda e, pz=pz, s_=s_: e.tensor_copy(out=s_[:], in_=pz[:]), [pz], [s_])
            out_dma(fm, fm[FM_BQ + c * 128:FM_BQ + (c + 1) * 128, t0:t0 + A_TG], s_, s_[:])
        for t in range(4):
            pz = nps(); pz2 = nps(); s_ = stt[t % 2]
            for k in range(8):
                P.op("pe", lambda e, k=k, t=t, pz=pz: e.matmul(pz[:], lhsT=hT[:, k, t * 128:(t + 1) * 128],
                                                                rhs=wbf[k][:, 1024:1536], start=(k == 0), stop=(k == 7)),
                     [wbf[k], hT], [pz])
            for k in range(8):
                P.op("pe", lambda e, k=k, t=t, pz2=pz2: e.matmul(pz2[:, 0:256], lhsT=hT[:, k, t * 128:(t + 1) * 128],
                                                                  rhs=wbf[k][:, 2560:2816], start=(k == 0), stop=(k == 7)),
                     [wbf[k], hT], [pz2])
            P.op("dve", lambda e, pz=pz, s_=s_: e.tensor_copy(out=s_[:, 0:512], in_=pz[:]), [pz], [s_])
            P.op("act", lambda e, pz2=pz2, s_=s_: e.activation(out=s_[:, 512:768], in_=pz2[:, 0:256], func=AF.Copy), [pz2], [s_])
            out_dma(tm, tm[t0 + t * 128:t0 + (t + 1) * 128, :], s_, s_[:])
        for c8 in range(8):
            pz = nps(); mm_fm(pz, 22 + c8, hT)
            if c8 % 2 == 0:
                P.op("dve", lambda e, pz=pz, c8=c8: e.tensor_copy(out=cb[:, c8, 1:A_TG + 1], in_=pz[:]), [pz], [cb])
            else:
                P.op("act", lambda e, pz=pz, c8=c8: e.activation(out=cb[:, c8, 1:A_TG + 1], in_=pz[:], func=AF.Copy), [pz], [cb])
        P.op("pool", lambda e: e.tensor_copy(out=cbn[:, :, 0:1], in_=cb[:, :, A_TG:A_TG + 1]), [cb], [cbn])
        for c8 in range(8):
            ta = tmpA[c8 % 2]
            eng = "dve"
            P.op(eng, lambda e, c8=c8, ta=ta: e.tensor_tensor(out=ta[:], in0=cb[:, c8, 0:A_TG], in1=cb[:, c8, 1:A_TG + 1],
                                                              op=ALU.subtract), [cb], [ta])
            P.op(eng, lambda e, c8=c8, ta=ta: e.scalar_tensor_tensor(out=cm[:, c8, :], in0=ta[:], scalar=pp[:, PP_MU + c8:PP_MU + c8 + 1],
                                                                     in1=cb[:, c8, 1:A_TG + 1], op0=ALU.mult, op1=ALU.add),
                 [ta, pp, cb], [cm])
        for c in range(2):
            out_dma(fm, fm[FM_R + c * 128:FM_R + (c + 1) * 128, t0:t0 + A_TG], cm, cm[:, c, :])
        P.op("act", lambda e: e.activation(out=tb[0][0:64, :], in_=cm[0:64, 6, :], func=AF.Tanh), [cm], [tb[0]])
        P.op("dve", lambda e: e.tensor_copy(out=tb[0][64:128, :], in_=cm[64:128, 6, :]), [cm], [tb[0]])
        P.op("act", lambda e: e.activation(out=tb[1][:], in_=cm[:, 7, :], func=AF.Sigmoid), [cm], [tb[1]])
        E05 = float(np.exp(-0.5))
        for c in range(2):
            pz = nps(); s_ = stage()
            P.op("pe", lambda e, pz=pz, c=c: e.matmul(pz[:], lhsT=pm[0:64, PM_W2 + c * 128:PM_W2 + (c + 1) * 128],
                                                       rhs=tb[0][0:64, :], start=True, stop=True), [pm, tb[0]], [pz])
            P.op("act", lambda e, pz=pz, s_=s_, c=c: e.activation(out=s_[:], in_=pz[:], func=AF.Sigmoid,
                                                                   bias=pp[:, PP_W0 + c:PP_W0 + c + 1]), [pz, pp], [s_])
            P.op("dve", lambda e, s_=s_: e.tensor_scalar(out=s_[:], in0=s_[:], scalar1=-E05, scalar2=None, op0=ALU.mult), [s_], [s_])
            out_dma(fm, fm[FM_LD + c * 128:FM_LD + (c + 1) * 128, t0:t0 + A_TG], s_, s_[:])
        a_t = [tmpA[2], tmpA[3]]
        for c in range(2):
            pz = nps()
            P.op("pe", lambda e, pz=pz, c=c: e.matmul(pz[:], lhsT=pm[64:128, PM_A2 + c * 128:PM_A2 + (c + 1) * 128],
                                                       rhs=tb[0][64:128, :], start=True, stop=True), [pm, tb[0]], [pz])
            P.op("act", lambda e, pz=pz, c=c: e.activation(out=a_t[c][:], in_=pz[:], func=AF.Sigmoid,
                                                           bias=pp[:, PP_A0 + c:PP_A0 + c + 1]), [pz, pp], [a_t[c]])
            out_dma(fm, fm[FM_A + c * 128:FM_A + (c + 1) * 128, t0:t0 + A_TG], a_t[c], a_t[c][:])
        for c in range(2):
            pz = nps(); s_ = stage()
            P.op("pe", lambda e, pz=pz, c=c: e.matmul(pz[:], lhsT=pm[:, PM_G2 + c * 128:PM_G2 + (c + 1) * 128],
                                                       rhs=tb[1][:], start=True, stop=True), [pm, tb[1]], [pz])
            P.op("dve", lambda e, pz=pz, s_=s_: e.tensor_copy(out=s_[:], in_=pz[:]), [pz], [s_])
            out_dma(fm, fm[FM_G + c * 128:FM_G + (c + 1) * 128, t0:t0 + A_TG], s_, s_[:])
        if layer == 1:
            P.dma("sp", vfs, vfs[:], vf_d, vf_d[:, t0:t0 + A_TG].rearrange("(c p) t -> p c t", p=128))
            for c in range(2):
                P.op("dve", lambda e, c=c: e.tensor_copy(out=tb[2 + c][:], in_=cm[:, 4 + c, :]), [cm], [tb[2 + c]])
            for c in range(2):
                P.op("pe", lambda e, c=c: e.matmul(psm[0:32, :], lhsT=pm[:, PM_V1 + c * 32:PM_V1 + (c + 1) * 32],
                                                   rhs=tb[2 + c][:], start=(c == 0), stop=(c == 1)), [pm, tb[2 + c]], [psm])
            P.op("dve", lambda e: e.tensor_copy(out=tb[1][0:32, :], in_=psm[0:32, :]), [psm], [tb[1]])
            for c in range(2):
                pz = nps(); ta = tmpA[c]
                P.op("pe", lambda e, pz=pz, c=c: e.matmul(pz[:], lhsT=pm[0:32, PM_V2 + c * 128:PM_V2 + (c + 1) * 128],
                                                           rhs=tb[1][0:32, :], start=True, stop=True), [pm, tb[1]], [pz])
                P.op("act", lambda e, pz=pz, c=c, ta=ta: e.activation(out=ta[:], in_=pz[:], func=AF.Sigmoid,
                                                                        bias=pp[:, PP_V0 + c:PP_V0 + c + 1]), [pz, pp], [ta])
                s_ = stage()
                P.op("dve", lambda e, c=c, s_=s_: e.tensor_tensor(out=s_[:], in0=vfs[:, c, :], in1=cm[:, 4 + c, :],
                                                                   op=ALU.subtract), [vfs, cm], [s_])
                P.op("dve", lambda e, s_=s_, ta=ta: e.tensor_tensor(out=s_[:], in0=s_[:], in1=ta[:], op=ALU.mult), [s_, ta], [s_])
                P.op("dve", lambda e, s_=s_, c=c: e.tensor_tensor(out=s_[:], in0=s_[:], in1=cm[:, 4 + c, :], op=ALU.add),
                     [s_, cm], [s_])
                out_dma(fm, fm[FM_V + c * 128:FM_V + (c + 1) * 128, t0:t0 + A_TG], s_, s_[:])
        else:
            for c in range(2):
                out_dma(fm, fm[FM_V + c * 128:FM_V + (c + 1) * 128, t0:t0 + A_TG], cm, cm[:, 4 + c, :])
        for c in range(2):
            kx = tmpA[c]; s_ = stage(); s2 = stage(); pz = nps()
            P.op("dve", lambda e, c=c, kx=kx: e.tensor_scalar(out=kx[:], in0=cm[:, 2 + c, :], scalar1=pp[:, PP_KK + c:PP_KK + c + 1],
                                                               scalar2=None, op0=ALU.mult), [cm, pp], [kx])
            P.op("pool", lambda e, c=c, kx=kx: e.tensor_tensor(out=tb[2 + c][:], in0=kx[:], in1=kx[:], op=ALU.mult), [kx], [tb[2 + c]])
            P.op("pe", lambda e, pz=pz, c=c: e.matmul(pz[:], lhsT=blk[:], rhs=tb[2 + c][:], start=True, stop=True),
                 [blk, tb[2 + c]], [pz])
            P.op("act", lambda e, pz=pz, s_=s_: e.activation(out=s_[:], in_=pz[:], func=AF.Sqrt), [pz], [s_])
            P.op("dve", lambda e, s_=s_: e.tensor_scalar(out=s_[:], in0=s_[:], scalar1=1e-12, scalar2=None, op0=ALU.max), [s_], [s_])
            P.op("dve", lambda e, s_=s_: e.reciprocal(out=s_[:], in_=s_[:]), [s_], [s_])
            P.op("dve", lambda e, s_=s_, kx=kx: e.tensor_tensor(out=s_[:], in0=s_[:], in1=kx[:], op=ALU.mult), [s_, kx], [s_])
            out_dma(fm, fm[FM_KK + c * 128:FM_KK + (c + 1) * 128, t0:t0 + A_TG], s_, s_[:])
            P.op("dve", lambda e, s2=s2, c=c: e.tensor_scalar(out=s2[:], in0=a_t[c][:], scalar1=pp[:, PP_KA + c:PP_KA + c + 1],
                                                               scalar2=kac[:, c:c + 1], op0=ALU.mult, op1=ALU.add),
                 [a_t[c], pp, kac], [s2])
            P.op("dve", lambda e, s2=s2, c=c: e.tensor_tensor(out=s2[:], in0=s2[:], in1=cm[:, 2 + c, :], op=ALU.mult), [s2, cm], [s2])
            out_dma(fm, fm[FM_KP + c * 128:FM_KP + (c + 1) * 128, t0:t0 + A_TG], s2, s2[:])

    P.final_wait("sp", [fm, tm])
    P.emit()
    return nc, P


def host_inputs_A(layer, inp, h_full, vfirst_full=None):
    l = layer
    pp = np.zeros((128, PP_N), np.float32)
    fmj = lambda v: np.ascontiguousarray(v.reshape(-1, 128).T)
    pp[:, PP_GMIX:PP_GMIX + 8] = fmj(inp['norm_mix_g'][l])
    pp[:, PP_MU:PP_MU + 8] = fmj(inp['rwkv_mu'][l])
    pp[:, PP_W0:PP_W0 + 2] = fmj(inp['rwkv_w0'][l])
    pp[:, PP_A0:PP_A0 + 2] = fmj(inp['rwkv_a0'][l])
    pp[:, PP_KK:PP_KK + 2] = fmj(inp['rwkv_k_k'][l])
    pp[:, PP_KA:PP_KA + 2] = fmj(inp['rwkv_k_a'][l])
    if l == 1:
        pp[:, PP_V0:PP_V0 + 2] = fmj(inp['rwkv_v0'][0])
    pp[:, PP_HB0:PP_HB0 + 4] = fmj(inp['hgrn_lower_bounds'][0])
    pp[:, PP_HB1:PP_HB1 + 4] = fmj(inp['hgrn_lower_bounds'][1])
    pm = np.zeros((128, PM_N), np.float32)
    pm[0:64, PM_W2:PM_W2 + 256] = inp['rwkv_w2'][l]
    pm[64:128, PM_A2:PM_A2 + 256] = inp['rwkv_a2'][l]
    pm[:, PM_G2:PM_G2 + 256] = inp['rwkv_g2'][l]
    if l == 1:
        pm[:, PM_V1:PM_V1 + 64] = inp['rwkv_v1'][0].reshape(2, 128, 32).transpose(1, 0, 2).reshape(128, 64)
        pm[0:32, PM_V2:PM_V2 + 256] = inp['rwkv_v2'][0]
    ident = np.eye(128).astype(ml_dtypes.bfloat16)
    blk = np.kron(np.eye(2), np.ones((64, 64))).astype(ml_dtypes.bfloat16)
    maps = []
    w = np.ascontiguousarray(inp['w_in'][l])
    for c in range(8):
        m = {"h": np.ascontiguousarray(h_full[c * A_NT:(c + 1) * A_NT]),
             "hh": np.ascontiguousarray(h_full[c * A_NT - 128:c * A_NT]) if c > 0 else np.zeros((128, 1024), np.float32),
             "w_in": w, "pp": pp, "pm": pm, "ident": ident, "blk64": blk}
        if l == 1:
            m["vfirst"] = np.ascontiguousarray(vfirst_full[:, c * A_NT:(c + 1) * A_NT])
        maps.append(m)
    return maps


NOWN = 8192
NHALO = 2048
NTOT = NOWN + NHALO
PATTERNS = (1, 4, 16)


def emit_sw(P):
    qT_d = P.dram("sw_qT", [64, NOWN], F32, "ExternalInput")
    kT_d = P.dram("sw_kT", [64, NTOT], F32, "ExternalInput")
    v_d = P.dram("sw_v", [NTOT, 64], F32, "ExternalInput")
    msk_d = P.dram("sw_mask", [128, 512], BF16, "ExternalInput")
    id_d = P.dram("sw_ident", [128, 128], BF16, "ExternalInput")
    flag_d = P.dram("sw_flag", [128, 1], F32, "ExternalInput")
    o_d = P.dram("sw_o", [64, NOWN], F32, "ExternalOutput")

    q32 = P.sbuf("q32", [64, NOWN], F32)
    k32 = P.sbuf("k32", [64, NTOT], F32)
    qd = P.sbuf("qd", [64, NOWN], BF16)
    kd = P.sbuf("kd", [64, NTOT], BF16)
    vst = P.sbuf("vst", [128, 85 * 64], F32)
    vaug = P.sbuf("vaug", [128, 85, 65], BF16)
    acc = P.sbuf("acc", [65, NOWN], F32)
    msk = P.sbuf("msk", [128, 512], BF16)
    ident = P.sbuf("identsw", [128, 128], BF16)
    flag = P.sbuf("flag", [128, 1], F32)
    ones = P.sbuf("ones", [65, 64], F32)
    PT = [P.sbuf("PT%d" % i, [128, 512], BF16) for i in range(3)]
    ost = [P.sbuf("ost%d" % i, [64, 512], F32) for i in range(2)]
    rz = [P.sbuf("rz%d" % i, [64, 512], F32) for i in range(2)]
    psS = [P.psum("psS%d" % i, [128, 512], F32) for i in range(3)]
    psN = [P.psum("psN%d" % i, [128, 512], F32) for i in range(3)]

    P.dma("sp", msk, msk[:], msk_d, msk_d[:])
    P.dma("sp", ident, ident[:], id_d, id_d[:])
    P.dma("sp", flag, flag[:], flag_d, flag_d[:])
    for i in range(4):
        P.dma("sp" if i % 2 == 0 else "act", q32, q32[:, i * 2048:(i + 1) * 2048], qT_d, qT_d[:, i * 2048:(i + 1) * 2048],
              disjoint=True)
    for i in range(5):
        P.dma("act" if i % 2 == 0 else "sp", k32, k32[:, i * 2048:(i + 1) * 2048], kT_d, kT_d[:, i * 2048:(i + 1) * 2048],
              disjoint=True)
    P.op("dve", lambda e: e.memset(ones[:], 1.0), [], [ones])

    si = [0]
    for pi, D in enumerate(PATTERNS):
        nb = NOWN // (128 * D)
        nbt = nb + 1
        LQ = NOWN // D
        LK = LQ + 128
        koff = NHALO - 128 * D
        if D == 1:
            P.op("dve", lambda e: e.tensor_copy(out=qd[:, 0:NOWN], in_=q32[:, :]), [q32], [qd])
            P.op("pool", lambda e: e.tensor_copy(out=kd[:, 0:LK], in_=k32[:, koff:koff + LK]), [k32], [kd])
        else:
            P.op("dve", lambda e: e.tensor_copy(out=qd[:, 0:NOWN].rearrange("p (r l) -> p r l", r=D),
                                                in_=q32[:, :].rearrange("p (l r) -> p r l", r=D)), [q32], [qd])
            P.op("pool", lambda e: e.tensor_copy(out=kd[:, 0:D * LK].rearrange("p (r l) -> p r l", r=D),
                                                 in_=k32[:, koff:koff + D * LK].rearrange("p (l r) -> p r l", r=D)), [k32], [kd])
        vsrc = v_d[koff:koff + nbt * 128 * D, :].rearrange("(bb i r) c -> i r bb c", i=128, r=D)
        vv = vst[:, 0:D * nbt * 64].rearrange("p (r bb c) -> p r bb c", r=D, bb=nbt)
        for r in range(D):
            P.dma("sp" if r % 2 == 0 else "act", vst, vv[:, r], v_d, vsrc[:, r], disjoint=(r > 0))
        va = vaug[:, 0:D * nbt, :]
        P.op("dve", lambda e: e.memset(va[:, :, 64:65], 1.0), [], [vaug])
        P.op("dve", lambda e: e.tensor_copy(out=va[:, :, 0:64], in_=vst[:, 0:D * nbt * 64].rearrange("p (n c) -> p n c", c=64)),
             [vst], [vaug])
        va4 = va.rearrange("p (r bb) c -> p r bb c", r=D)
        P.op("dve", lambda e: e.tensor_scalar(out=va4[:, :, 0, :], in0=va4[:, :, 0, :], scalar1=flag[:, 0:1], scalar2=None,
                                              op0=ALU.mult), [vaug, flag], [vaug])
        for r in range(D):
            for b0 in range(0, nb, 4):
                pn = psN[si[0] % 3]
                pts = []
                for pr in range(2):
                    p_s = psS[(2 * si[0] + pr) % 3]
                    pt = PT[(2 * si[0] + pr) % 3]
                    P.op("pe", lambda e: e.matmul(p_s[:], lhsT=ident[:], rhs=msk[:], start=True, stop=False), [ident, msk], [p_s])
                    for j in range(2):
                        b = b0 + 2 * pr + j
                        qb = qd[:, r * LQ + b * 128: r * LQ + (b + 1) * 128]
                        for kb in range(2):
                            kblk = kd[:, r * LK + (b + kb) * 128: r * LK + (b + kb + 1) * 128]
                            P.op("pe", lambda e: e.matmul(p_s[:, (2 * j + kb) * 128:(2 * j + kb + 1) * 128], lhsT=kblk, rhs=qb,
                                                          start=False, stop=True), [kd, qd], [p_s])
                    P.op("act", lambda e: e.activation(out=pt[:], in_=p_s[:], func=AF.Exp, scale=0.125), [p_s], [pt])
                    pts.append(pt)
                for pr in range(2):
                    for j in range(2):
                        b = b0 + 2 * pr + j
                        for kb in range(2):
                            P.op("pe", lambda e: e.matmul(pn[0:65, (2 * pr + j) * 128:(2 * pr + j + 1) * 128],
                                                          lhsT=vaug[:, r * nbt + b + kb, :],
                                                          rhs=pts[pr][:, (2 * j + kb) * 128:(2 * j + kb + 1) * 128],
                                                          start=(kb == 0), stop=(kb == 1)), [vaug, pts[pr]], [pn])
                tstart = r + D * 128 * b0
                av = acc[:, tstart: tstart + 512 * D] if D == 1 else \
                    acc[:, D * 128 * b0: D * 128 * b0 + 512 * D].rearrange("p (l r) -> p r l", r=D)[:, r, :]
                if pi == 0:
                    P.op("dve", lambda e: e.tensor_copy(out=av, in_=pn[0:65, :]), [pn], [acc])
                else:
                    P.op("dve", lambda e: e.tensor_tensor(out=av, in0=av, in1=pn[0:65, :], op=ALU.add), [pn, acc], [acc])
                si[0] += 1
    for i in range(NOWN // 512):
        pz = psS[i % 3]
        P.op("pe", lambda e: e.matmul(pz[0:64, :], lhsT=ones[64:65, :], rhs=acc[64:65, i * 512:(i + 1) * 512], start=True, stop=True),
             [ones, acc], [pz])
        rzi = rz[i % 2]; o_ = ost[i % 2]
        P.op("dve", lambda e: e.reciprocal(out=rzi[:], in_=pz[0:64, :]), [pz], [rzi])
        P.op("pool", lambda e: e.tensor_tensor(out=o_[:], in0=acc[0:64, i * 512:(i + 1) * 512], in1=rzi[:], op=ALU.mult), [acc, rzi], [o_])
        P.dma("sp" if i % 2 == 0 else "act", o_d, o_d[:, i * 512:(i + 1) * 512], o_, o_[:], owner=o_, disjoint=True)
    return [o_d]


def sw_consts():
    j = np.arange(128)[:, None]
    i = np.arange(128)[None, :]
    mp = np.where(j >= i, 0.0, -30000.0)
    mo = np.where(j <= i, 0.0, -30000.0)
    m = np.concatenate([mp, mo, mp, mo], axis=1).astype(ml_dtypes.bfloat16)
    return {"sw_mask": m, "sw_ident": np.eye(128).astype(ml_dtypes.bfloat16)}


def sw_inputs(core, bq_fm, bk_fm, bv_tm):
    hd, s = core // 2, core % 2
    t0 = s * NOWN
    qT = np.ascontiguousarray(bq_fm[hd * 64:(hd + 1) * 64, t0:t0 + NOWN])
    kT = np.zeros((64, NTOT), np.float32)
    v = np.zeros((NTOT, 64), np.float32)
    kT[:, NHALO:] = bk_fm[hd * 64:(hd + 1) * 64, t0:t0 + NOWN]
    v[NHALO:] = bv_tm[t0:t0 + NOWN, hd * 64:(hd + 1) * 64]
    if s > 0:
        kT[:, :NHALO] = bk_fm[hd * 64:(hd + 1) * 64, t0 - NHALO:t0]
        v[:NHALO] = bv_tm[t0 - NHALO:t0, hd * 64:(hd + 1) * 64]
    m = {"sw_qT": qT, "sw_kT": kT, "sw_v": v, "sw_flag": np.full((128, 1), 1.0 if s > 0 else 0.0, np.float32)}
    m.update(sw_consts())
    return m


HG_SEQ = 16384
HG_ST = 2048
HG_NST = HG_SEQ // HG_ST
HG_CL = 40.0


def emit_hgrn(P):
    q_d = P.dram("hg_q", [128, HG_SEQ], F32, "ExternalInput")
    lf_d = P.dram("hg_lf", [128, HG_SEQ], F32, "ExternalInput")
    i_d = P.dram("hg_i", [HG_SEQ, 64], F32, "ExternalInput")
    rm_d = P.dram("hg_rmask", [128, HG_ST], F32, "ExternalInput")
    cm_d = P.dram("hg_cmask", [128, 128], F32, "ExternalInput")
    id_d = P.dram("hg_ident", [128, 128], BF16, "ExternalInput")
    o_d = P.dram("hg_o", [64, HG_SEQ], F32, "ExternalOutput")

    qs = P.sbuf("hqs", [128, HG_ST], F32)
    lf = P.sbuf("hlf", [128, HG_ST], F32)
    bb = P.sbuf("hb", [128, HG_ST], F32)
    kf = P.sbuf("hkf", [128, HG_ST], F32)
    t1 = P.sbuf("ht1", [128, HG_ST], F32)
    t2 = P.sbuf("ht2", [128, HG_ST], F32)
    dch = P.sbuf("hdch", [128, 32], F32)
    Qt = P.sbuf("hQt", [128, HG_ST], BF16)
    Kt = P.sbuf("hKt", [128, HG_ST], BF16)
    Qh = P.sbuf("hQh", [128, HG_ST], BF16)
    Kh = P.sbuf("hKh", [128, HG_ST], BF16)
    rmask = P.sbuf("hrmask", [128, HG_ST], F32)
    cmask = P.sbuf("hcmask", [128, 128], F32)
    ident = P.sbuf("hident", [128, 128], BF16)
    v32 = P.sbuf("hv32", [128, 16, 64], F32)
    vb = P.sbuf("hvb", [128, 16, 64], BF16)
    KhT = [P.sbuf("hKhT%d" % i, [128, 128], BF16) for i in range(2)]
    Am = [P.sbuf("hAm%d" % i, [128, 128], BF16) for i in range(2)]
    S32 = P.sbuf("hS32", [128, 64], F32)
    Sb = [P.sbuf("hSb%d" % i, [128, 64], BF16) for i in range(2)]
    ost = P.sbuf("host", [64, HG_ST], F32)
    psT = [P.psum("hpsT%d" % i, [128, 128], BF16) for i in range(2)]
    psA = [P.psum("hpsA%d" % i, [128, 128], F32) for i in range(2)]
    psO = [P.psum("hpsO%d" % i, [128, 128], F32) for i in range(2)]
    psU = [P.psum("hpsU%d" % i, [128, 64], F32) for i in range(2)]

    P.dma("sp", rmask, rmask[:], rm_d, rm_d[:])
    P.dma("sp", cmask, cmask[:], cm_d, cm_d[:])
    P.dma("sp", ident, ident[:], id_d, id_d[:])
    P.op("dve", lambda e: e.memset(S32[:], 0.0), [], [S32])
    P.op("dve", lambda e: e.memset(Sb[0][:], 0.0), [], [Sb[0]])
    sbi = 0
    pc = 0
    b3 = bb[:, :].rearrange("p (n c) -> p n c", c=64)
    for st in range(HG_NST):
        t0 = st * HG_ST
        P.dma("sp", qs, qs[:], q_d, q_d[:, t0:t0 + HG_ST])
        P.dma("act", lf, lf[:], lf_d, lf_d[:, t0:t0 + HG_ST])
        P.dma("pool", v32, v32[:], i_d, i_d[t0:t0 + HG_ST, :].rearrange("(n i) c -> i n c", i=128))
        P.op("pool", lambda e: e.tensor_copy(out=vb[:], in_=v32[:]), [v32], [vb])
        P.op("act", lambda e: e.activation(out=kf[:], in_=lf[:], func=AF.Exp), [lf], [kf])
        P.op("pool", lambda e: e.tensor_scalar(out=kf[:], in0=kf[:], scalar1=-1.0, scalar2=1.0, op0=ALU.mult, op1=ALU.add), [kf], [kf])
        P.op("dve", lambda e: e.tensor_tensor_scan(out=bb[:], data0=rmask[:], data1=lf[:], initial=0.0, op0=ALU.mult, op1=ALU.add),
             [rmask, lf], [bb])
        bm = b3[:, :, 31:32].to_broadcast([128, 32, 64])
        bl = b3[:, :, 63:64].to_broadcast([128, 32, 64])
        t1v = t1[:, :].rearrange("p (n c) -> p n c", c=64)
        t2v = t2[:, :].rearrange("p (n c) -> p n c", c=64)
        P.op("dve", lambda e: e.tensor_tensor(out=t1v, in0=b3, in1=bm, op=ALU.subtract), [bb], [t1])
        P.op("pool", lambda e: e.tensor_scalar(out=t2[:], in0=t1[:], scalar1=-1.0, scalar2=HG_CL, op0=ALU.mult, op1=ALU.min), [t1], [t2])
        P.op("dve", lambda e: e.tensor_scalar(out=t1[:], in0=t1[:], scalar1=HG_CL, scalar2=None, op0=ALU.min), [t1], [t1])
        P.op("act", lambda e: e.activation(out=t1[:], in_=t1[:], func=AF.Exp), [t1], [t1])
        P.op("act", lambda e: e.activation(out=t2[:], in_=t2[:], func=AF.Exp), [t2], [t2])
        P.op("dve", lambda e: e.tensor_tensor(out=Qt[:], in0=qs[:], in1=t1[:], op=ALU.mult), [qs, t1], [Qt])
        P.op("pool", lambda e: e.tensor_tensor(out=Kt[:], in0=kf[:], in1=t2[:], op=ALU.mult), [kf, t2], [Kt])
        P.op("act", lambda e: e.activation(out=t1[:], in_=bb[:], func=AF.Exp), [bb], [t1])
        P.op("dve", lambda e: e.tensor_tensor(out=Qh[:], in0=qs[:], in1=t1[:], op=ALU.mult), [qs, t1], [Qh])
        P.op("dve", lambda e: e.tensor_tensor(out=t2v, in0=bl, in1=b3, op=ALU.subtract), [bb], [t2])
        P.op("act", lambda e: e.activation(out=t2[:], in_=t2[:], func=AF.Exp), [t2], [t2])
        P.op("pool", lambda e: e.tensor_tensor(out=Kh[:], in0=kf[:], in1=t2[:], op=ALU.mult), [kf, t2], [Kh])
        P.op("act", lambda e: e.activation(out=dch[:], in_=b3[:, :, 63], func=AF.Exp), [bb], [dch])
        for pr in range(16):
            c0 = pr * 128
            kh_t = KhT[pc % 2]; am = Am[pc % 2]; p_t = psT[pc % 2]; p_a = psA[pc % 2]; p_o = psO[pc % 2]
            pc += 1
            P.op("pe", lambda e: e.transpose(out=p_t[:], in_=Kh[:, c0:c0 + 128], identity=ident[:]), [Kh, ident], [p_t])
            P.op("act", lambda e: e.activation(out=kh_t[:], in_=p_t[:], func=AF.Copy), [p_t], [kh_t])
            P.op("pe", lambda e: e.matmul(p_a[:], lhsT=Kt[:, c0:c0 + 128], rhs=Qt[:, c0:c0 + 128], start=True, stop=True), [Kt, Qt], [p_a])
            P.op("dve", lambda e: e.tensor_tensor(out=am[:], in0=p_a[:], in1=cmask[:], op=ALU.mult), [p_a, cmask], [am])
            for ch in range(2):
                r0 = ch * 64
                s_cur = Sb[sbi % 2]; s_nxt = Sb[(sbi + 1) % 2]; p_u = psU[sbi % 2]
                sbi += 1
                P.op("pe", lambda e: e.matmul(p_o[0:64, r0:r0 + 64], lhsT=vb[r0:r0 + 64, pr, :], rhs=am[r0:r0 + 64, r0:r0 + 64],
                                              start=True, stop=False), [vb, am], [p_o])
                P.op("pe", lambda e: e.matmul(p_o[0:64, r0:r0 + 64], lhsT=s_cur[:], rhs=Qh[:, c0 + r0:c0 + r0 + 64],
                                              start=False, stop=True), [s_cur, Qh], [p_o])
                P.op("pe", lambda e: e.matmul(p_u[:], lhsT=kh_t[r0:r0 + 64, :], rhs=vb[r0:r0 + 64, pr, :], start=True, stop=True),
                     [kh_t, vb], [p_u])
                cidx = pr * 2 + ch
                P.op("dve", lambda e: e.scalar_tensor_tensor(out=S32[:], in0=S32[:], scalar=dch[:, cidx:cidx + 1], in1=p_u[:],
                                                             op0=ALU.mult, op1=ALU.add), [S32, dch, p_u], [S32])
                P.op("pool", lambda e: e.tensor_copy(out=s_nxt[:], in_=S32[:]), [S32], [s_nxt])
            P.op("act", lambda e: e.activation(out=ost[:, c0:c0 + 128], in_=p_o[0:64, :], func=AF.Copy), [p_o], [ost])
        P.dma("sp", o_d, o_d[:, t0:t0 + HG_ST], ost, ost[:], owner=ost, disjoint=True)
    return [o_d]


def hg_consts():
    t = np.arange(HG_ST)
    rm = np.tile(((t % 64) != 0).astype(np.float32)[None, :], (128, 1))
    j = np.arange(128)[:, None]; i = np.arange(128)[None, :]
    cmk = ((j // 64 == i // 64) & (j <= i)).astype(np.float32)
    return {"hg_rmask": rm, "hg_cmask": cmk, "hg_ident": np.eye(128).astype(ml_dtypes.bfloat16)}


def hg_inputs(core, qs_fm, lf_fm, i_tm):
    hd, vh = core // 2, core % 2
    m = {"hg_q": np.ascontiguousarray(qs_fm[hd * 128:(hd + 1) * 128]),
         "hg_lf": np.ascontiguousarray(lf_fm[hd * 128:(hd + 1) * 128]),
         "hg_i": np.ascontiguousarray(i_tm[:, hd * 128 + vh * 64: hd * 128 + (vh + 1) * 64])}
    m.update(hg_consts())
    return m


RW_SEQ = 16384
RW_ST = 2048
RW_NST = RW_SEQ // RW_ST
RW_C = 128
RW_NCH = RW_ST // RW_C


def emit_rwkv(P, KCH=2, NSET=4):
    r_d = P.dram("rw_r", [64, RW_SEQ], F32, "ExternalInput")
    kp_d = P.dram("rw_kp", [64, RW_SEQ], F32, "ExternalInput")
    kk_d = P.dram("rw_kk", [64, RW_SEQ], F32, "ExternalInput")
    a_d = P.dram("rw_a", [64, RW_SEQ], F32, "ExternalInput")
    ld_d = P.dram("rw_ld", [64, RW_SEQ], F32, "ExternalInput")
    v_d = P.dram("rw_v", [32, RW_SEQ], F32, "ExternalInput")
    rm_d = P.dram("rw_rmask", [64, RW_ST], F32, "ExternalInput")
    mk_d = P.dram("rw_masks", [128, 4 * 128], F32, "ExternalInput")
    id_d = P.dram("rw_ident", [128, 128], BF16, "ExternalInput")
    y_d = P.dram("rw_y", [32, RW_SEQ], F32, "ExternalOutput")

    def sb(name, shape, dt=F32):
        return P.sbuf("rws_" + name, shape, dt)
    r_s, kp_s, kk_s, a_s, ld_s = [sb(n, [64, RW_ST]) for n in ("r", "kp", "kk", "a", "ld")]
    v_s = sb("v", [32, RW_ST])
    G = sb("G", [64, RW_ST]); x1 = sb("x1", [64, RW_ST]); x2 = sb("x2", [64, RW_ST]); x3 = sb("x3", [64, RW_ST])
    dC = [sb("dC%d" % i, [64, RW_NCH]) for i in range(2)]
    AR = [sb("AR%d" % i, [64, RW_NCH, 2, RW_C], BF16) for i in range(2)]
    Bt = [sb("Bt%d" % i, [64, RW_ST], BF16) for i in range(2)]
    Kt = [sb("Kt%d" % i, [64, RW_ST], BF16) for i in range(2)]
    Bh = [sb("Bh%d" % i, [64, RW_ST], BF16) for i in range(2)]
    Kh = [sb("Kh%d" % i, [64, RW_ST], BF16) for i in range(2)]
    vb = [sb("vb%d" % i, [32, RW_ST], BF16) for i in range(2)]
    yst = [sb("yst%d" % i, [32, RW_ST]) for i in range(2)]
    rmask = sb("rmask", [64, RW_ST])
    masks = sb("masks", [128, 4 * 128])
    ident = sb("ident", [128, 128], BF16)
    tok = [sb("tok%d" % i, [128, 160], BF16) for i in range(NSET)]
    XTb = [sb("XTb%d" % i, [128, 128], BF16) for i in range(NSET)]
    Arb = [sb("Arb%d" % i, [128, 128], BF16) for i in range(NSET)]
    Ak = [sb("Ak%d" % i, [128, 256], BF16) for i in range(NSET)]
    Mf = [[sb("Mf%d_%d" % (k, i), [128, 128]) for i in range(2)] for k in range(KCH)]
    Nf = [[sb("Nf%d_%d" % (k, i), [128, 128]) for i in range(2)] for k in range(KCH)]
    XT = [[sb("XT%d_%d" % (k, i), [128, 128]) for i in range(2)] for k in range(KCH)]
    H32 = sb("H32", [64, 32])
    Hb = [sb("Hb%d" % i, [64, 32], BF16) for i in range(2)]
    Wb = [sb("Wb%d" % i, [128, 32], BF16) for i in range(2)]
    Ub = [sb("Ub%d" % i, [128, 32], BF16) for i in range(2)]
    bT = P.psum("rw_bT", [128, 1024], BF16)
    bA = [P.psum("rw_bA%d" % k, [128, 512], F32) for k in range(KCH)]
    bC1 = [P.psum("rw_bC1_%d" % k, [128, 512], F32) for k in range(KCH)]
    bC2 = [P.psum("rw_bC2_%d" % k, [128, 512], F32) for k in range(KCH)]
    bS = P.psum("rw_bS", [128, 512], F32)

    mSU = masks[:, 0:128]; mIU = masks[:, 128:256]; mSL = masks[:, 256:384]; mI = masks[:, 384:512]
    P.dma("sp", rmask, rmask[:], rm_d, rm_d[:])
    P.dma("sp", masks, masks[:], mk_d, mk_d[:])
    P.dma("sp", ident, ident[:], id_d, id_d[:])
    P.op("dve", lambda e: e.memset(H32[:], 0.0), [], [H32])
    P.op("dve", lambda e: e.memset(Hb[0][:], 0.0), [], [Hb[0]])
    G3 = G[:, :].rearrange("p (n c) -> p n c", c=RW_C)
    x13 = x1[:, :].rearrange("p (n c) -> p n c", c=RW_C)
    x23 = x2[:, :].rearrange("p (n c) -> p n c", c=RW_C)
    NCHUNK = RW_NST * RW_NCH
    state = {"prep_done": 0, "inv_started": 0, "inv_done": set(), "state_done": 0}

    def prep_thread():
        for st in range(RW_NST):
            while state["inv_started"] < RW_NCH * st - 10:
                yield
            t0 = st * RW_ST
            b = st % 2
            for i, (s_, d_) in enumerate(((r_s, r_d), (kp_s, kp_d), (kk_s, kk_d), (a_s, a_d), (ld_s, ld_d))):
                P.dma(("sp", "act", "pool")[i % 3], s_, s_[:], d_, d_[:, t0:t0 + RW_ST])
            P.dma("sp", v_s, v_s[:], v_d, v_d[:, t0:t0 + RW_ST])
            yield
            P.op("pool", lambda e: e.tensor_copy(out=vb[b][:], in_=v_s[:]), [v_s], [vb[b]])
            P.op("dve", lambda e: e.tensor_tensor_scan(out=G[:], data0=rmask[:], data1=ld_s[:], initial=0.0, op0=ALU.mult, op1=ALU.add),
                 [rmask, ld_s], [G])
            yield
            Gl = G3[:, :, RW_C - 1:RW_C].to_broadcast([64, RW_NCH, RW_C])
            P.op("dve", lambda e: e.tensor_tensor(out=x1[:], in0=G[:], in1=ld_s[:], op=ALU.subtract), [G, ld_s], [x1])
            P.op("act", lambda e: e.activation(out=x1[:], in_=x1[:], func=AF.Exp), [x1], [x1])
            yield
            P.op("dve", lambda e: e.scalar_tensor_tensor(out=AR[b][:, :, 0, :], in0=x13, scalar=-1.0,
                                                         in1=kk_s[:, :].rearrange("p (n c) -> p n c", c=RW_C),
                                                         op0=ALU.mult, op1=ALU.mult), [x1, kk_s], [AR[b]])
            P.op("act", lambda e: e.activation(out=x2[:], in_=G[:], func=AF.Exp), [G], [x2])
            yield
            P.op("pool", lambda e: e.tensor_tensor(out=AR[b][:, :, 1, :], in0=x23, in1=r_s[:, :].rearrange("p (n c) -> p n c", c=RW_C),
                                                   op=ALU.mult), [x2, r_s], [AR[b]])
            P.op("pool", lambda e: e.tensor_tensor(out=x3[:], in0=kk_s[:], in1=a_s[:], op=ALU.mult), [kk_s, a_s], [x3])
            P.op("act", lambda e: e.activation(out=x1[:], in_=G[:], func=AF.Exp, scale=-1.0), [G], [x1])
            yield
            P.op("dve", lambda e: e.tensor_tensor(out=Bt[b][:], in0=x3[:], in1=x1[:], op=ALU.mult), [x3, x1], [Bt[b]])
            P.op("pool", lambda e: e.tensor_tensor(out=Kt[b][:], in0=kp_s[:], in1=x1[:], op=ALU.mult), [kp_s, x1], [Kt[b]])
            yield
            P.op("dve", lambda e: e.tensor_tensor(out=x23, in0=Gl, in1=G3, op=ALU.subtract), [G], [x2])
            P.op("act", lambda e: e.activation(out=x2[:], in_=x2[:], func=AF.Exp), [x2], [x2])
            yield
            P.op("dve", lambda e: e.tensor_tensor(out=Bh[b][:], in0=x3[:], in1=x2[:], op=ALU.mult), [x3, x2], [Bh[b]])
            P.op("pool", lambda e: e.tensor_tensor(out=Kh[b][:], in0=kp_s[:], in1=x2[:], op=ALU.mult), [kp_s, x2], [Kh[b]])
            P.op("act", lambda e: e.activation(out=dC[b][:], in_=G3[:, :, RW_C - 1], func=AF.Exp), [G], [dC[b]])
            state["prep_done"] = st + 1
            yield

    def inv_thread(k):
        for gn in range(k, NCHUNK, KCH):
            st, n = gn // RW_NCH, gn % RW_NCH
            while state["prep_done"] <= st or state["state_done"] < gn - (NSET - 1):
                yield
            state["inv_started"] = max(state["inv_started"], gn)
            b = st % 2
            c0 = n * RW_C
            s_ = gn % NSET
            tk, xtb, arb, ak = tok[s_], XTb[s_], Arb[s_], Ak[s_]
            pT = bT; BA = bA[k]; B1 = bC1[k]; B2 = bC2[k]
            P.op("pe", lambda e: e.transpose(out=pT[:, k * 160:k * 160 + 64], in_=Bh[b][:, c0:c0 + RW_C], identity=ident[0:64, 0:64]), [Bh[b], ident], [pT])
            P.op("pe", lambda e: e.transpose(out=pT[:, k * 160 + 64:k * 160 + 128], in_=Kh[b][:, c0:c0 + RW_C], identity=ident[0:64, 0:64]), [Kh[b], ident], [pT])
            P.op("pe", lambda e: e.transpose(out=pT[:, k * 160 + 128:k * 160 + 160], in_=vb[b][:, c0:c0 + RW_C], identity=ident[0:32, 0:32]), [vb[b], ident], [pT])
            arv = AR[b][:, n, :, :].rearrange("p a c -> p (a c)")
            P.op("pe", lambda e: e.matmul(BA[:, 0:256], lhsT=Bt[b][:, c0:c0 + RW_C], rhs=arv, start=True, stop=True), [Bt[b], AR[b]], [BA])
            P.op("pe", lambda e: e.matmul(BA[:, 256:512], lhsT=Kt[b][:, c0:c0 + RW_C], rhs=arv, start=True, stop=True), [Kt[b], AR[b]], [BA])
            P.op("pe", lambda e: e.matmul(B2[:, 128:256], lhsT=AR[b][:, n, 0, :], rhs=Bt[b][:, c0:c0 + RW_C], start=True, stop=True), [AR[b], Bt[b]], [B2])
            yield
            mf, nf, xt = Mf[k][0], Nf[k][0], XT[k][0]
            P.op("act", lambda e: e.activation(out=tk[:], in_=pT[:, k * 160:(k + 1) * 160], func=AF.Copy), [pT], [tk])
            P.op("dve", lambda e: e.tensor_tensor(out=mf[:], in0=BA[:, 0:128], in1=mSU, op=ALU.mult), [BA, masks], [mf])
            P.op("dve", lambda e: e.tensor_tensor(out=nf[:], in0=B2[:, 128:256], in1=mSL, op=ALU.mult), [B2, masks], [nf])
            P.op("pool", lambda e: e.tensor_tensor(out=xt[:], in0=mf[:], in1=mI, op=ALU.add), [mf, masks], [xt])
            yield
            P.op("dve", lambda e: e.tensor_tensor(out=arb[:], in0=BA[:, 128:256], in1=mIU, op=ALU.mult), [BA, masks], [arb])
            P.op("dve", lambda e: e.tensor_tensor(out=ak[:], in0=BA[:, 256:512], in1=masks[:, 0:256], op=ALU.mult), [BA, masks], [ak])
            cur = 0
            for it in range(6):
                mo, no = Mf[k][cur], Nf[k][cur]
                mn, nn = Mf[k][1 - cur], Nf[k][1 - cur]
                xo, xn = XT[k][cur], XT[k][1 - cur]
                P.op("pe", lambda e: e.matmul(B1[:, 0:128], lhsT=mo[:], rhs=no[:], start=True, stop=True), [mo, no], [B1])
                if it < 5:
                    P.op("pe", lambda e: e.matmul(B2[:, 128:256], lhsT=no[:], rhs=mo[:], start=True, stop=True), [mo, no], [B2])
                yield
                P.op("act", lambda e: e.activation(out=nn[:], in_=B1[:, 0:128], func=AF.Copy), [B1], [nn])
                if it < 5:
                    P.op("dve", lambda e: e.tensor_copy(out=mn[:], in_=B2[:, 128:256]), [B2], [mn])
                yield
                P.op("pe", lambda e: e.matmul(BA[:, 0:128], lhsT=nn[:], rhs=xo[:], start=True, stop=True), [nn, xo], [BA])
                P.op("dve", lambda e: e.tensor_tensor(out=xn[:], in0=BA[:, 0:128], in1=xo[:], op=ALU.add), [BA, xo], [xn])
                cur = 1 - cur
            P.op("pool", lambda e: e.tensor_copy(out=xtb[:], in_=XT[k][cur][:]), [XT[k][cur]], [xtb])
            state["inv_done"].add(gn)
            yield

    def state_thread():
        hbi = 0
        for gn in range(NCHUNK):
            while gn not in state["inv_done"]:
                yield
            st, n = gn // RW_NCH, gn % RW_NCH
            b = st % 2
            c0 = n * RW_C
            s_ = gn % NSET
            tk, xtb, arb, ak = tok[s_], XTb[s_], Arb[s_], Ak[s_]
            hb_cur = Hb[hbi % 2]; hb_nxt = Hb[(hbi + 1) % 2]; wb = Wb[hbi % 2]; ub = Ub[hbi % 2]
            hbi += 1
            P.op("pe", lambda e: e.matmul(bS[:, 0:32], lhsT=AR[b][:, n, 0, :], rhs=hb_cur[:], start=True, stop=False), [AR[b], hb_cur], [bS])
            P.op("pe", lambda e: e.matmul(bS[:, 0:32], lhsT=ak[:, 0:128], rhs=tk[:, 128:160], start=False, stop=True), [ak, tk], [bS])
            P.op("act", lambda e: e.activation(out=wb[:], in_=bS[:, 0:32], func=AF.Copy), [bS], [wb])
            yield
            P.op("pe", lambda e: e.matmul(bS[:, 32:64], lhsT=xtb[:], rhs=wb[:], start=True, stop=True), [xtb, wb], [bS])
            P.op("act", lambda e: e.activation(out=ub[:], in_=bS[:, 32:64], func=AF.Copy), [bS], [ub])
            yield
            P.op("pe", lambda e: e.matmul(bS[0:64, 64:96], lhsT=tk[:, 0:64], rhs=ub[:], start=True, stop=False), [tk, ub], [bS])
            P.op("pe", lambda e: e.matmul(bS[0:64, 64:96], lhsT=tk[:, 64:128], rhs=tk[:, 128:160], start=False, stop=True), [tk], [bS])
            P.op("pe", lambda e: e.matmul(bS[0:32, 128:256], lhsT=hb_cur[:], rhs=AR[b][:, n, 1, :], start=True, stop=False), [hb_cur, AR[b]], [bS])
            P.op("pe", lambda e: e.matmul(bS[0:32, 128:256], lhsT=ub[:], rhs=arb[:], start=False, stop=False), [ub, arb], [bS])
            P.op("pe", lambda e: e.matmul(bS[0:32, 128:256], lhsT=tk[:, 128:160], rhs=ak[:, 128:256], start=False, stop=True), [tk, ak], [bS])
            P.op("dve", lambda e: e.scalar_tensor_tensor(out=H32[:], in0=H32[:], scalar=dC[b][:, n:n + 1], in1=bS[0:64, 64:96],
                                                         op0=ALU.mult, op1=ALU.add), [H32, dC[b], bS], [H32])
            P.op("pool", lambda e: e.tensor_copy(out=hb_nxt[:], in_=H32[:]), [H32], [hb_nxt])
            P.op("act", lambda e: e.activation(out=yst[b][:, c0:c0 + RW_C], in_=bS[0:32, 128:256], func=AF.Copy), [bS], [yst[b]])
            state["state_done"] = gn + 1
            if n == RW_NCH - 1:
                P.dma("sp", y_d, y_d[:, st * RW_ST:(st + 1) * RW_ST], yst[b], yst[b][:], owner=yst[b], disjoint=True)
            yield

    threads = [prep_thread()] + [inv_thread(k) for k in range(KCH)] + [state_thread()]
    guard = 0
    while threads:
        for g in list(threads):
            try:
                next(g)
            except StopIteration:
                threads.remove(g)
        guard += 1
        assert guard < 200000, "scheduler stuck"
    return [y_d]


def rw_consts():
    t = np.arange(RW_ST)
    rm = np.tile(((t % RW_C) != 0).astype(np.float32)[None, :], (64, 1))
    j = np.arange(128)[:, None]; i = np.arange(128)[None, :]
    su = (i > j).astype(np.float32); iu = (i >= j).astype(np.float32); sl = (j > i).astype(np.float32)
    masks = np.concatenate([su, iu, sl, np.eye(128, dtype=np.float32)], axis=1)
    return {"rw_rmask": rm, "rw_masks": masks, "rw_ident": np.eye(128).astype(ml_dtypes.bfloat16)}


def rw_inputs(core, fmr):
    hd, vh = core // 2, core % 2
    m = {"rw_" + n: np.ascontiguousarray(fmr[n][hd * 64:(hd + 1) * 64]) for n in ("r", "kp", "kk", "a", "ld")}
    m["rw_v"] = np.ascontiguousarray(fmr["v"][hd * 64 + vh * 32: hd * 64 + (vh + 1) * 32])
    m.update(rw_consts())
    return m


C1_NT = 2048
C1_TG = 512
C1_NG = C1_NT // C1_TG
CF_HGO, CF_GS, CF_SWO, CF_RWY, CF_R, CF_KP, CF_V, CF_G = 0, 512, 1024, 1280, 1536, 1792, 2048, 2304
CF_ROWS = 2560
CP_GN, CP_LNW, CP_LNB, CP_RK, CP_N = 0, 4, 6, 8, 10


def emit_c1(P, layer):
    h_d = P.dram("c1_h", [C1_NT, 1024], F32, "ExternalInput")
    cf_d = P.dram("c1_cf", [CF_ROWS, C1_NT], F32, "ExternalInput")
    wo_d = P.dram("c1_wout", [1024, 1024], F32, "ExternalInput")
    cp_d = P.dram("c1_cp", [128, CP_N], F32, "ExternalInput")
    k_d = P.dram("c1_consts", [128, 256], F32, "ExternalInput")
    hm_d = P.dram("c1_hmid", [C1_NT, 1024], F32, "ExternalOutput")

    def sb(name, shape, dt=F32):
        return P.sbuf("c1s_" + name, shape, dt)
    wob = sb("wob", [128, 8, 1024], BF16)
    wst = [sb("wst%d" % i, [128, 1024]) for i in range(2)]
    cp = sb("cp", [128, CP_N])
    kc = sb("kc", [128, 256])
    eps1 = sb("eps1", [128, 1]); eps2 = sb("eps2", [128, 1])
    oT = [sb("oT%d" % i, [128, 8, C1_TG], BF16) for i in range(2)]
    fin = [sb("fin%d" % i, [128, C1_TG]) for i in range(6)]
    tmp = [sb("tmp%d" % i, [128, C1_TG]) for i in range(6)]
    hin = [sb("hin%d" % i, [128, 1024]) for i in range(2)]
    ps = [P.psum("c1_ps%d" % i, [128, 512], F32) for i in range(4)]
    pd = [P.psum("c1_pd%d" % i, [128, 1024], F32) for i in range(2)]

    ones_m = kc[:, 0:128]; blk_m = kc[:, 128:256]
    P.dma("sp", cp, cp[:], cp_d, cp_d[:])
    P.dma("sp", kc, kc[:], k_d, k_d[:])
    P.op("dve", lambda e: e.memset(eps1[:], 1e-6), [], [eps1])
    P.op("dve", lambda e: e.memset(eps2[:], 64e-5), [], [eps2])
    for k in range(8):
        s_ = wst[k % 2]
        P.dma("sp" if k % 2 == 0 else "act", s_, s_[:], wo_d, wo_d[k * 128:(k + 1) * 128, :])
        P.op("pool", lambda e: e.tensor_copy(out=wob[:, k, :], in_=s_[:]), [s_], [wob])

    fi = [0]; ti = [0]; pi = [0]; qi = [0]

    def load(row0, t0):
        f = fin[fi[0] % 6]; fi[0] += 1
        q = ("sp", "act", "pool")[qi[0] % 3]; qi[0] += 1
        P.dma(q, f, f[:], cf_d, cf_d[row0:row0 + 128, t0:t0 + C1_TG])
        return f

    def T_():
        t = tmp[ti[0] % 6]; ti[0] += 1
        return t

    def PS():
        p = ps[pi[0] % 4]; pi[0] += 1
        return p

    for g in range(C1_NG):
        t0 = g * C1_TG
        ot = oT[g % 2]
        for c in range(4):
            o = load(CF_HGO + c * 128, t0); gs = load(CF_GS + c * 128, t0)
            sq = T_(); p_ = PS(); rs = T_()
            P.op("pool", lambda e: e.tensor_tensor(out=sq[:], in0=o[:], in1=o[:], op=ALU.mult), [o], [sq])
            P.op("pe", lambda e: e.matmul(p_[:], lhsT=ones_m, rhs=sq[:], start=True, stop=True), [kc, sq], [p_])
            P.op("act", lambda e: e.activation(out=rs[:], in_=p_[:], func=AF.Sqrt, bias=eps1[:, 0:1]), [p_, eps1], [rs])
            P.op("dve", lambda e: e.reciprocal(out=rs[:], in_=rs[:]), [rs], [rs])
            P.op("dve", lambda e: e.scalar_tensor_tensor(out=rs[:], in0=rs[:], scalar=cp[:, CP_GN + c:CP_GN + c + 1], in1=o[:],
                                                         op0=ALU.mult, op1=ALU.mult), [rs, cp, o], [rs])
            P.op("pool", lambda e: e.tensor_tensor(out=ot[:, c, :], in0=rs[:], in1=gs[:], op=ALU.mult), [rs, gs], [ot])
        for c in range(2):
            o = load(CF_SWO + c * 128, t0)
            P.op("pool", lambda e: e.tensor_copy(out=ot[:, 4 + c, :], in_=o[:]), [o], [ot])
        for c in range(2):
            y = load(CF_RWY + c * 128, t0); r_ = load(CF_R + c * 128, t0); kp = load(CF_KP + c * 128, t0)
            v_ = load(CF_V + c * 128, t0); g_ = load(CF_G + c * 128, t0)
            pm = PS(); pq = PS(); pb = PS()
            ysq = T_(); mean = T_(); var = T_(); rk = T_()
            P.op("pe", lambda e: e.matmul(pm[:], lhsT=blk_m, rhs=y[:], start=True, stop=True), [kc, y], [pm])
            P.op("pool", lambda e: e.tensor_tensor(out=ysq[:], in0=y[:], in1=y[:], op=ALU.mult), [y], [ysq])
            P.op("pe", lambda e: e.matmul(pq[:], lhsT=blk_m, rhs=ysq[:], start=True, stop=True), [kc, ysq], [pq])
            P.op("act", lambda e: e.activation(out=mean[:], in_=pm[:], func=AF.Copy), [pm], [mean])
            P.op("pool", lambda e: e.tensor_tensor(out=var[:], in0=mean[:], in1=mean[:], op=ALU.mult), [mean], [var])
            P.op("dve", lambda e: e.tensor_tensor(out=var[:], in0=pq[:], in1=var[:], op=ALU.subtract), [pq, var], [var])
            P.op("act", lambda e: e.activation(out=var[:], in_=var[:], func=AF.Sqrt, bias=eps2[:, 0:1]), [var, eps2], [var])
            P.op("dve", lambda e: e.reciprocal(out=var[:], in_=var[:]), [var], [var])
            P.op("dve", lambda e: e.tensor_tensor(out=mean[:], in0=y[:], in1=mean[:], op=ALU.subtract), [y, mean], [mean])
            P.op("dve", lambda e: e.tensor_tensor(out=mean[:], in0=mean[:], in1=var[:], op=ALU.mult), [mean, var], [mean])
            P.op("dve", lambda e: e.tensor_scalar(out=mean[:], in0=mean[:], scalar1=cp[:, CP_LNW + c:CP_LNW + c + 1],
                                                   scalar2=cp[:, CP_LNB + c:CP_LNB + c + 1], op0=ALU.mult, op1=ALU.add), [mean, cp], [mean])
            P.op("dve", lambda e: e.scalar_tensor_tensor(out=rk[:], in0=r_[:], scalar=cp[:, CP_RK + c:CP_RK + c + 1], in1=kp[:],
                                                         op0=ALU.mult, op1=ALU.mult), [r_, cp, kp], [rk])
            P.op("pe", lambda e: e.matmul(pb[:], lhsT=blk_m, rhs=rk[:], start=True, stop=True), [kc, rk], [pb])
            P.op("dve", lambda e: e.scalar_tensor_tensor(out=rk[:], in0=pb[:], scalar=64.0, in1=v_[:], op0=ALU.mult, op1=ALU.mult),
                 [pb, v_], [rk])
            P.op("pool", lambda e: e.tensor_tensor(out=mean[:], in0=mean[:], in1=rk[:], op=ALU.add), [mean, rk], [mean])
            P.op("pool", lambda e: e.tensor_tensor(out=ot[:, 6 + c, :], in0=mean[:], in1=g_[:], op=ALU.mult), [mean, g_], [ot])
        for t in range(4):
            hi = hin[t % 2]; p_d = pd[t % 2]
            r0 = t0 + t * 128
            P.dma("sp", hi, hi[:], h_d, h_d[r0:r0 + 128, :])
            for half in range(2):
                for c in range(8):
                    P.op("pe", lambda e: e.matmul(p_d[:, half * 512:(half + 1) * 512], lhsT=ot[:, c, t * 128:(t + 1) * 128],
                                                  rhs=wob[:, c, half * 512:(half + 1) * 512], start=(c == 0), stop=(c == 7)),
                         [ot, wob], [p_d])
            for half in range(2):
                P.op("dve", lambda e: e.tensor_tensor(out=hi[:, half * 512:(half + 1) * 512], in0=hi[:, half * 512:(half + 1) * 512],
                                                      in1=p_d[:, half * 512:(half + 1) * 512], op=ALU.add), [hi, p_d], [hi])
            P.dma("act", hm_d, hm_d[r0:r0 + 128, :], hi, hi[:], owner=hi, disjoint=True)
    return [hm_d]


def c1_inputs(layer, inp, core, h_full, cf_full):
    l = layer
    fmj = lambda v: np.ascontiguousarray(v.reshape(-1, 128).T)
    cp = np.zeros((128, CP_N), np.float32)
    cp[:, CP_GN:CP_GN + 4] = fmj(inp['hgrn_gnorm_g'][l])
    cp[:, CP_LNW:CP_LNW + 2] = fmj(inp['rwkv_ln_w'][l])
    cp[:, CP_LNB:CP_LNB + 2] = fmj(inp['rwkv_ln_b'][l])
    cp[:, CP_RK:CP_RK + 2] = fmj(inp['rwkv_r_k'][l].reshape(-1))
    kc = np.concatenate([np.full((128, 128), 1.0 / 128), np.kron(np.eye(2), np.full((64, 64), 1.0 / 64))], axis=1).astype(np.float32)
    return {"c1_h": np.ascontiguousarray(h_full[core * C1_NT:(core + 1) * C1_NT]),
            "c1_cf": np.ascontiguousarray(cf_full[:, core * C1_NT:(core + 1) * C1_NT]),
            "c1_wout": np.ascontiguousarray(inp['w_out'][l]), "c1_cp": cp, "c1_consts": kc}


C2_NT = 2048
C2_GT = 256
C2_NGR = C2_NT // C2_GT
C2_DFF = 2816
C2_NJ = C2_DFF // 128


def norm_transpose(P, src_dram_t, src_ap, hres, hnb, st, junk, epsc, ident, pst, dstT, col0, dma_q="sp"):
    P.dma(dma_q, hres, hres[:], src_dram_t, src_ap)
    P.op("act", lambda e: e.activation(out=junk[:], in_=hres[:], func=AF.Square, accum_out=st[:, 0:1]), [hres], [junk, st])
    P.op("act", lambda e: e.activation(out=st[:, 1:2], in_=st[:, 0:1], func=AF.Sqrt, scale=1.0 / 1024, bias=epsc[:, 0:1]), [st, epsc], [st])
    P.op("dve", lambda e: e.reciprocal(out=st[:, 2:3], in_=st[:, 1:2]), [st], [st])
    P.op("dve", lambda e: e.tensor_scalar(out=hnb[:], in0=hres[:], scalar1=st[:, 2:3], scalar2=None, op0=ALU.mult), [hres, st], [hnb])
    for c in range(8):
        P.op("pe", lambda e: e.transpose(out=pst[:, c * 128:(c + 1) * 128], in_=hnb[:, c * 128:(c + 1) * 128], identity=ident[:]),
             [hnb, ident], [pst])
    P.op("act", lambda e: e.activation(out=dstT[:, :, col0:col0 + 128], in_=pst[:].rearrange("p (c t) -> p c t", c=8), func=AF.Copy),
         [pst], [dstT])


def emit_c2(P):
    h_d = P.dram("c2_h", [C2_NT, 1024], F32, "ExternalInput")
    hh_d = P.dram("c2_hh", [128, 1024], F32, "ExternalInput")
    up_d = P.dram("c2_up", [1024, 2 * C2_DFF], F32, "ExternalInput")
    dn_d = P.dram("c2_dn", [C2_DFF, 1024], F32, "ExternalInput")
    pp_d = P.dram("c2_pp", [128, 8 + 44 * 4], F32, "ExternalInput")
    id_d = P.dram("c2_ident", [128, 128], BF16, "ExternalInput")
    o_d = P.dram("c2_out", [C2_NT, 1024], F32, "ExternalOutput")

    def sb(name, shape, dt=F32):
        return P.sbuf("c2s_" + name, shape, dt)
    upb = [sb("upb%d" % k, [128, 2 * C2_DFF], BF16) for k in range(8)]
    dnb = sb("dnb", [128, C2_NJ, 1024], BF16)
    pp = sb("pp", [128, 8 + 44 * 4])
    ident = sb("ident", [128, 128], BF16)
    epsc = sb("epsc", [128, 1])
    hres = [sb("hres%d" % i, [128, 1024]) for i in range(2)]
    hnb = [sb("hnb%d" % i, [128, 1024], BF16) for i in range(2)]
    st = [sb("st%d" % i, [128, 4]) for i in range(2)]
    junk = sb("junk", [128, 1024])
    hnT = sb("hnT", [128, 8, C2_GT], BF16)
    ug = [sb("ug%d" % i, [128, C2_GT + 2]) for i in range(2)]
    uv = [sb("uv%d" % i, [128, C2_GT + 2]) for i in range(2)]
    tg = [sb("tg%d" % i, [128, C2_GT]) for i in range(2)]
    tv = [sb("tv%d" % i, [128, C2_GT]) for i in range(2)]
    actT = sb("actT", [128, C2_NJ, C2_GT], BF16)
    uprev = sb("uprev", [128, 44, 2])
    pst = P.psum("c2_pst", [128, 1024], BF16)
    pu = [P.psum("c2_pu%d" % i, [128, 512], F32) for i in range(3)]
    pd = [P.psum("c2_pd%d" % i, [128, 512], F32) for i in range(4)]

    P.dma("sp", pp, pp[:], pp_d, pp_d[:])
    P.dma("sp", ident, ident[:], id_d, id_d[:])
    P.op("dve", lambda e: e.memset(epsc[:], 1e-6), [], [epsc])
    wi = 0
    for k in range(8):
        for c0 in range(0, 2 * C2_DFF, 1024):
            w_ = min(1024, 2 * C2_DFF - c0)
            s_ = hres[wi % 2]; wi += 1
            P.dma("sp" if wi % 2 == 0 else "act", s_, s_[:, 0:w_], up_d, up_d[k * 128:(k + 1) * 128, c0:c0 + w_])
            P.op("dve", lambda e: e.tensor_scalar(out=upb[k][:, c0:c0 + w_], in0=s_[:, 0:w_], scalar1=pp[:, k:k + 1], scalar2=None,
                                                  op0=ALU.mult), [s_, pp], [upb[k]])
    for j in range(C2_NJ):
        s_ = hres[wi % 2]; wi += 1
        P.dma("sp" if wi % 2 == 0 else "act", s_, s_[:], dn_d, dn_d[j * 128:(j + 1) * 128, :])
        P.op("pool", lambda e: e.tensor_copy(out=dnb[:, j, :], in_=s_[:]), [s_], [dnb])

    def cw(c, i):
        o = 8 + c * 4 + i
        return pp[:, o:o + 1]

    norm_transpose(P, hh_d, hh_d[:], hres[0], hnb[0], st[0], junk, epsc, ident, pst, hnT, 0)
    for c in range(44):
        p_ = pu[c % 3]
        for k in range(8):
            P.op("pe", lambda e: e.matmul(p_[:, 0:2], lhsT=upb[k][:, c * 128:(c + 1) * 128], rhs=hnT[:, k, 126:128],
                                          start=(k == 0), stop=(k == 7)), [upb[k], hnT], [p_])
        P.op("act", lambda e: e.activation(out=uprev[:, c, :], in_=p_[:, 0:2], func=AF.Copy), [p_], [uprev])

    ui = 0
    for g in range(C2_NGR):
        r0 = g * C2_GT
        for t in range(2):
            norm_transpose(P, h_d, h_d[r0 + t * 128:r0 + (t + 1) * 128, :], hres[t], hnb[t], st[t], junk, epsc, ident, pst, hnT, t * 128)
        for j in range(C2_NJ):
            cg, cv = j, C2_NJ + j
            p_ = pu[ui % 3]; u_g = ug[ui % 2]; u_v = uv[ui % 2]; t_g = tg[ui % 2]; t_v = tv[ui % 2]
            ui += 1
            for k in range(8):
                P.op("pe", lambda e: e.matmul(p_[:, 0:C2_GT], lhsT=upb[k][:, cg * 128:(cg + 1) * 128], rhs=hnT[:, k, :],
                                              start=(k == 0), stop=(k == 7)), [upb[k], hnT], [p_])
            for k in range(8):
                P.op("pe", lambda e: e.matmul(p_[:, C2_GT:2 * C2_GT], lhsT=upb[k][:, cv * 128:(cv + 1) * 128], rhs=hnT[:, k, :],
                                              start=(k == 0), stop=(k == 7)), [upb[k], hnT], [p_])
            P.op("pool", lambda e: e.tensor_copy(out=u_g[:, 0:2], in_=uprev[:, cg, :]), [uprev], [u_g])
            P.op("pool", lambda e: e.tensor_copy(out=u_v[:, 0:2], in_=uprev[:, cv, :]), [uprev], [u_v])
            P.op("act", lambda e: e.activation(out=u_g[:, 2:C2_GT + 2], in_=p_[:, 0:C2_GT], func=AF.Copy), [p_], [u_g])
            P.op("act", lambda e: e.activation(out=u_v[:, 2:C2_GT + 2], in_=p_[:, C2_GT:2 * C2_GT], func=AF.Copy), [p_], [u_v])
            P.op("pool", lambda e: e.tensor_copy(out=uprev[:, cg, :], in_=u_g[:, C2_GT:C2_GT + 2]), [u_g], [uprev])
            P.op("pool", lambda e: e.tensor_copy(out=uprev[:, cv, :], in_=u_v[:, C2_GT:C2_GT + 2]), [u_v], [uprev])
            for (u_, t_, c_) in ((u_g, t_g, cg), (u_v, t_v, cv)):
                P.op("dve", lambda e: e.tensor_scalar(out=t_[:], in0=u_[:, 2:C2_GT + 2], scalar1=cw(c_, 2), scalar2=cw(c_, 3),
                                                      op0=ALU.mult, op1=ALU.add), [u_, pp], [t_])
                P.op("dve", lambda e: e.scalar_tensor_tensor(out=t_[:], in0=u_[:, 1:C2_GT + 1], scalar=cw(c_, 1), in1=t_[:],
                                                             op0=ALU.mult, op1=ALU.add), [u_, pp, t_], [t_])
                P.op("dve", lambda e: e.scalar_tensor_tensor(out=t_[:], in0=u_[:, 0:C2_GT], scalar=cw(c_, 0), in1=t_[:],
                                                             op0=ALU.mult, op1=ALU.add), [u_, pp, t_], [t_])
            P.op("act", lambda e: e.activation(out=t_g[:], in_=t_g[:], func=AF.Silu), [t_g], [t_g])
            P.op("pool", lambda e: e.tensor_tensor(out=actT[:, j, :], in0=t_g[:], in1=t_v[:], op=ALU.mult), [t_g, t_v], [actT])
        for t in range(2):
            for half in range(2):
                p_d = pd[(2 * t + half) % 4]
                for j in range(C2_NJ):
                    P.op("pe", lambda e: e.matmul(p_d[:], lhsT=actT[:, j, t * 128:(t + 1) * 128], rhs=dnb[:, j, half * 512:(half + 1) * 512],
                                                  start=(j == 0), stop=(j == C2_NJ - 1)), [actT, dnb], [p_d])
                P.op("dve", lambda e: e.tensor_tensor(out=hres[t][:, half * 512:(half + 1) * 512], in0=hres[t][:, half * 512:(half + 1) * 512],
                                                      in1=p_d[:], op=ALU.add), [hres[t], p_d], [hres[t]])
            P.dma("act", o_d, o_d[r0 + t * 128:r0 + (t + 1) * 128, :], hres[t], hres[t][:], owner=hres[t], disjoint=True)
    return [o_d]


def c2_inputs(layer, inp, core, hmid_full):
    l = layer
    fmj = lambda v: np.ascontiguousarray(v.reshape(-1, 128).T)
    pp = np.zeros((128, 8 + 44 * 4), np.float32)
    pp[:, 0:8] = fmj(inp['norm_ffn_g'][l])
    cwb = np.concatenate([inp['ffn_conv_w'][l], inp['ffn_conv_b'][l][None]], axis=0)
    pp[:, 8:] = cwb.reshape(4, 44, 128).transpose(2, 1, 0).reshape(128, 176)
    return {"c2_h": np.ascontiguousarray(hmid_full[core * C2_NT:(core + 1) * C2_NT]),
            "c2_hh": np.ascontiguousarray(hmid_full[core * C2_NT - 128:core * C2_NT]) if core > 0 else np.zeros((128, 1024), np.float32),
            "c2_up": np.ascontiguousarray(inp['ffn_up'][l]), "c2_dn": np.ascontiguousarray(inp['ffn_down'][l]),
            "c2_pp": pp, "c2_ident": np.eye(128).astype(ml_dtypes.bfloat16)}


C3_NT = 2048


def emit_c3(P, final):
    h_d = P.dram("c3_h", [C3_NT, 1024], F32, "ExternalInput")
    pT_d = P.dram("c3_pT", [256, C3_NT], F32, "ExternalInput")
    gt_d = P.dram("c3_gate", [1024, 1024], F32, "ExternalInput")
    pj_d = P.dram("c3_proj", [256, 1024], F32, "ExternalInput")
    pp_d = P.dram("c3_pp", [128, 8], F32, "ExternalInput")
    id_d = P.dram("c3_ident", [128, 128], BF16, "ExternalInput")
    if final:
        gf_d = P.dram("c3_gfin", [128, 1024], F32, "ExternalInput")
    o_d = P.dram("c3_out", [C3_NT, 1024], F32, "ExternalOutput")

    def sb(name, shape, dt=F32):
        return P.sbuf("c3s_" + name, shape, dt)
    gtb = sb("gtb", [128, 8, 1024], BF16)
    pjb = sb("pjb", [128, 2, 1024], BF16)
    wst = [sb("wst%d" % i, [128, 1024]) for i in range(2)]
    pp = sb("pp", [128, 8])
    ident = sb("ident", [128, 128], BF16)
    epsc = sb("epsc", [128, 1])
    p32 = sb("p32", [128, 2, C3_NT])
    pTb = sb("pTb", [128, 2, C3_NT], BF16)
    hres = [sb("hres%d" % i, [128, 1024]) for i in range(2)]
    hnb = [sb("hnb%d" % i, [128, 1024], BF16) for i in range(2)]
    st = [sb("st%d" % i, [128, 4]) for i in range(2)]
    st2 = [sb("st2%d" % i, [128, 4]) for i in range(2)]
    junk = sb("junk", [128, 1024])
    hnT = [sb("hnT%d" % i, [128, 8, 128], BF16) for i in range(2)]
    sig = [sb("sig%d" % i, [128, 1024]) for i in range(2)]
    gfin = sb("gfin", [128, 1024]) if final else None
    pst = P.psum("c3_pst", [128, 1024], BF16)
    pg = [P.psum("c3_pg%d" % i, [128, 512], F32) for i in range(4)]
    pq = [P.psum("c3_pq%d" % i, [128, 512], F32) for i in range(2)]

    P.dma("sp", pp, pp[:], pp_d, pp_d[:])
    P.dma("sp", ident, ident[:], id_d, id_d[:])
    if final:
        P.dma("sp", gfin, gfin[:], gf_d, gf_d[:])
    P.op("dve", lambda e: e.memset(epsc[:], 1e-6), [], [epsc])
    for k in range(8):
        s_ = wst[k % 2]
        P.dma("sp" if k % 2 == 0 else "act", s_, s_[:], gt_d, gt_d[k * 128:(k + 1) * 128, :])
        P.op("dve", lambda e: e.tensor_scalar(out=gtb[:, k, :], in0=s_[:], scalar1=pp[:, k:k + 1], scalar2=None, op0=ALU.mult),
             [s_, pp], [gtb])
    for c in range(2):
        s_ = wst[c % 2]
        P.dma("sp" if c % 2 == 0 else "act", s_, s_[:], pj_d, pj_d[c * 128:(c + 1) * 128, :])
        P.op("pool", lambda e: e.tensor_copy(out=pjb[:, c, :], in_=s_[:]), [s_], [pjb])
    for c in range(2):
        P.dma("pool", p32, p32[:, c, :], pT_d, pT_d[c * 128:(c + 1) * 128, :], disjoint=(c > 0))
    P.op("pool", lambda e: e.tensor_copy(out=pTb[:], in_=p32[:]), [p32], [pTb])

    for t in range(C3_NT // 128):
        i = t % 2
        r0 = t * 128
        hr = hres[i]; hT = hnT[i]; sg = sig[i]
        norm_transpose(P, h_d, h_d[r0:r0 + 128, :], hr, hnb[i], st[i], junk, epsc, ident, pst, hT, 0)
        for half in range(2):
            p_g = pg[(2 * t + half) % 4]; p_q = pq[half]
            for k in range(8):
                P.op("pe", lambda e: e.matmul(p_g[:], lhsT=hT[:, k, :], rhs=gtb[:, k, half * 512:(half + 1) * 512],
                                              start=(k == 0), stop=(k == 7)), [hT, gtb], [p_g])
            for c in range(2):
                P.op("pe", lambda e: e.matmul(p_q[:], lhsT=pTb[:, c, r0:r0 + 128], rhs=pjb[:, c, half * 512:(half + 1) * 512],
                                              start=(c == 0), stop=(c == 1)), [pTb, pjb], [p_q])
            hs = slice(half * 512, (half + 1) * 512)
            P.op("act", lambda e: e.activation(out=sg[:, hs], in_=p_g[:], func=AF.Sigmoid), [p_g], [sg])
            P.op("dve", lambda e: e.tensor_tensor(out=sg[:, hs], in0=sg[:, hs], in1=p_q[:], op=ALU.mult), [sg, p_q], [sg])
            P.op("pool", lambda e: e.tensor_tensor(out=hr[:, hs], in0=hr[:, hs], in1=sg[:, hs], op=ALU.add), [hr, sg], [hr])
        if final:
            s2 = st2[i]
            P.op("act", lambda e: e.activation(out=junk[:], in_=hr[:], func=AF.Square, accum_out=s2[:, 0:1]), [hr], [junk, s2])
            P.op("act", lambda e: e.activation(out=s2[:, 1:2], in_=s2[:, 0:1], func=AF.Sqrt, scale=1.0 / 1024, bias=epsc[:, 0:1]),
                 [s2, epsc], [s2])
            P.op("dve", lambda e: e.reciprocal(out=s2[:, 2:3], in_=s2[:, 1:2]), [s2], [s2])
            P.op("dve", lambda e: e.scalar_tensor_tensor(out=hr[:], in0=hr[:], scalar=s2[:, 2:3], in1=gfin[:], op0=ALU.mult, op1=ALU.mult),
                 [hr, s2, gfin], [hr])
        P.dma("act", o_d, o_d[r0:r0 + 128, :], hr, hr[:], owner=hr, disjoint=True)
    return [o_d]


def c3_inputs(layer, inp, core, hffn_full, final):
    l = layer
    fmj = lambda v: np.ascontiguousarray(v.reshape(-1, 128).T)
    m = {"c3_h": np.ascontiguousarray(hffn_full[core * C3_NT:(core + 1) * C3_NT]),
         "c3_pT": np.ascontiguousarray(inp['p'][l, 0, core * C3_NT:(core + 1) * C3_NT, :].T),
         "c3_gate": np.ascontiguousarray(inp['ple_gate'][l]), "c3_proj": np.ascontiguousarray(inp['ple_proj'][l]),
         "c3_pp": fmj(inp['norm_ple_g'][l]), "c3_ident": np.eye(128).astype(ml_dtypes.bfloat16)}
    if final:
        m["c3_gfin"] = np.ascontiguousarray(np.broadcast_to(inp['final_norm_g'][None, :], (128, 1024))).astype(np.float32)
    return m


def _launch(build, maps):
    nc = bass.Bass("TRN2", target_bir_lowering=False)
    P = Prog(nc)
    outs = build(P)
    P.final_wait("sp", outs)
    P.emit()
    res = run_bass_kernel_spmd(nc, maps, core_ids=list(range(8)))
    return res.results


def kernel(**inputs):
    inp = {k: np.asarray(v) for k, v in inputs.items()}
    S = 16384
    h = np.ascontiguousarray(inp['x'][0], dtype=np.float32)
    vfirst = None
    for l in range(2):
        nc, P = build_A(l)
        res = run_bass_kernel_spmd(nc, host_inputs_A(l, inp, h, vfirst), core_ids=list(range(8))).results
        fm = np.concatenate([r["fm"] for r in res], axis=1)
        tm = np.concatenate([r["tm"] for r in res], axis=0)
        del res
        if l == 0:
            vfirst = np.ascontiguousarray(fm[FM_V:FM_V + 256])
        res = _launch(emit_sw, [sw_inputs(c, fm[FM_BQ:FM_BQ + 256], fm[FM_BK:FM_BK + 256], tm[:, 512:768]) for c in range(8)])
        cf = np.empty((CF_ROWS, S), np.float32)
        for c in range(8):
            hd, s = c // 2, c % 2
            cf[CF_SWO + hd * 64:CF_SWO + (hd + 1) * 64, s * NOWN:(s + 1) * NOWN] = res[c]["sw_o"]
        res = _launch(emit_hgrn, [hg_inputs(c, fm[FM_QS:FM_QS + 512], fm[FM_LF:FM_LF + 512], tm[:, 0:512]) for c in range(8)])
        for c in range(8):
            hd, vh = c // 2, c % 2
            cf[CF_HGO + hd * 128 + vh * 64:CF_HGO + hd * 128 + (vh + 1) * 64] = res[c]["hg_o"]
        fmr = {"r": fm[FM_R:FM_R + 256], "kp": fm[FM_KP:FM_KP + 256], "kk": fm[FM_KK:FM_KK + 256],
               "a": fm[FM_A:FM_A + 256], "ld": fm[FM_LD:FM_LD + 256], "v": fm[FM_V:FM_V + 256]}
        res = _launch(emit_rwkv, [rw_inputs(c, fmr) for c in range(8)])
        for c in range(8):
            hd, vh = c // 2, c % 2
            cf[CF_RWY + hd * 64 + vh * 32:CF_RWY + hd * 64 + (vh + 1) * 32] = res[c]["rw_y"]
        cf[CF_GS:CF_GS + 512] = fm[FM_GS:FM_GS + 512]
        cf[CF_R:CF_R + 256] = fm[FM_R:FM_R + 256]
        cf[CF_KP:CF_KP + 256] = fm[FM_KP:FM_KP + 256]
        cf[CF_V:CF_V + 256] = fm[FM_V:FM_V + 256]
        cf[CF_G:CF_G + 256] = fm[FM_G:FM_G + 256]
        del fm, tm, fmr
        res = _launch(lambda P: emit_c1(P, l), [c1_inputs(l, inp, c, h, cf) for c in range(8)])
        hmid = np.concatenate([r["c1_hmid"] for r in res], axis=0)
        del cf
        res = _launch(emit_c2, [c2_inputs(l, inp, c, hmid) for c in range(8)])
        hffn = np.concatenate([r["c2_out"] for r in res], axis=0)
        final = (l == 1)
        res = _launch(lambda P: emit_c3(P, final), [c3_inputs(l, inp, c, hffn, final) for c in range(8)])
        h = np.concatenate([r["c3_out"] for r in res], axis=0)
    return h[None].astype(np.float32)
```

```python
import numpy as np
import ml_dtypes
from concourse.bass_utils import run_bass_kernel_spmd


import concourse.bass as bass
import concourse.mybir as mybir

F32 = mybir.dt.float32
BF16 = mybir.dt.bfloat16
AF = mybir.ActivationFunctionType
ALU = mybir.AluOpType
AX = mybir.AxisListType

ENGS = ("pe", "act", "dve", "pool", "sp")


class T:
    __slots__ = ("name", "h", "last_w", "readers", "sem", "cnt", "excl")

    def __init__(self, name, h):
        self.name = name
        self.h = h
        self.last_w = {}
        self.readers = []
        self.sem = None
        self.cnt = 0
        self.excl = False

    def __getitem__(self, idx):
        return self.h[idx]


class _Rec:
    def __getattr__(self, name):
        def f(*a, **k):
            self.call = (name, a, k)
        return f


def _eager(fn):
    r = _Rec()
    fn(r)
    name, a, k = r.call
    return lambda e: getattr(e, name)(*a, **k)


class Prog:
    def __init__(self, nc):
        self.nc = nc
        self.ops = {e: [] for e in ENGS}
        self.count = {e: 0 for e in ENGS}
        self.waited = {e: {} for e in ENGS}
        self.dma_sems = []
        self.ctx = []
        self.ntiles = 0

    def sbuf(self, name, shape, dt):
        g = self.nc.sbuf_tensor(name, list(shape), dt)
        h = g.__enter__()
        self.ctx.append(g)
        return T(name, h)

    def psum(self, name, shape, dt=F32):
        g = self.nc.psum_tensor(name, list(shape), dt)
        h = g.__enter__()
        self.ctx.append(g)
        t = T(name, h)
        t.excl = True
        return t

    def dram(self, name, shape, dt, kind="Internal"):
        h = self.nc.dram_tensor(name, list(shape), dt, kind=kind)
        return T(name, h.ap() if hasattr(h, "ap") else h)

    def view(self, name, h):
        return T(name, h)

    def _deps(self, eng, reads, writes):
        deps = []
        for t in reads:
            for ev in t.last_w.items():
                deps.append((ev, "raw"))
            if t.excl:
                for r in t.readers:
                    if r[0] != eng:
                        deps.append((r, "rar"))
        for t in writes:
            if not getattr(self, "_disjoint", False):
                for ev in t.last_w.items():
                    deps.append((ev, "waw"))
            for r in t.readers:
                deps.append((r, "war"))
        out = {}
        for (key, val), kind in deps:
            if key == eng:
                if eng == "pe":
                    continue
            if out.get(key, 0) < val:
                out[key] = val
        res = []
        w = self.waited[eng]
        for key, val in out.items():
            if w.get(key, 0) >= val:
                continue
            w[key] = val
            res.append((key, val))
        return res

    def op(self, eng, fn, reads=(), writes=()):
        waits = self._deps(eng, reads, writes)
        self.count[eng] += 1
        ev = (eng, self.count[eng])
        for t in reads:
            t.readers.append(ev)
        for t in writes:
            t.last_w = {ev[0]: ev[1]}
            t.readers = []
        self.ops[eng].append((waits, _eager(fn), None))

    def dma(self, eng, out_t, out_ap, in_t, in_ap, owner=None, disjoint=False, **kw):
        self._disjoint = disjoint
        waits = self._deps(eng, [in_t], [out_t])
        self._disjoint = False
        ow = owner if owner is not None else out_t
        if ow.sem is None:
            g = self.nc.semaphore("ds%d" % len(self.dma_sems))
            ow.sem = g.__enter__()
            self.ctx.append(g)
            self.dma_sems.append(ow.sem)
        ow.cnt += 16
        ev = (ow.sem, ow.cnt)
        in_t.readers.append(ev)
        if disjoint:
            out_t.last_w[ev[0]] = ev[1]
        else:
            out_t.last_w = {ev[0]: ev[1]}
            out_t.readers = []

        def fn(e, out_ap=out_ap, in_ap=in_ap, kw=kw):
            return e.dma_start(out=out_ap, in_=in_ap, **kw)
        self.ops[eng].append((waits, fn, ow.sem))

    def final_wait(self, eng, tiles):
        waits = self._deps(eng, tiles, [])
        self.ops[eng].append((waits, None, None))

    def emit(self):
        nc = self.nc
        esem = {}
        for e in ENGS:
            g = nc.semaphore("es_" + e)
            esem[e] = g.__enter__()
            self.ctx.append(g)
        engobj = {"pe": "tensor", "act": "scalar", "dve": "vector", "pool": "gpsimd", "sp": "sync"}

        def run(e, eng):
            for waits, fn, dsem in self.ops[e]:
                for key, val in waits:
                    s = esem[key] if isinstance(key, str) else key
                    eng.wait_ge(s, val)
                if fn is None:
                    continue
                ins = fn(eng)
                if dsem is not None:
                    ins.then_inc(dsem, 16)
                else:
                    ins.then_inc(esem[e], 1)

        with nc.Block() as block:
            for e in ENGS:
                if not self.ops[e]:
                    continue
                getattr(block, engobj[e])(lambda eng, e=e: run(e, eng))

    def close(self):
        for g in reversed(self.ctx):
            g.__exit__(None, None, None)
        self.ctx = []


A_NT = 2048
A_TG = 512
A_NG = A_NT // A_TG
FM_QS, FM_LF, FM_GS, FM_BQ, FM_BK = 0, 512, 1024, 1536, 1792
FM_R, FM_KP, FM_KK, FM_A, FM_LD, FM_V, FM_G = [2048 + 256 * i for i in range(7)]
FM_ROWS = 3840
PP_GMIX, PP_MU, PP_W0, PP_A0, PP_KK, PP_KA, PP_V0, PP_HB0, PP_HB1, PP_N = 0, 8, 16, 18, 20, 22, 24, 26, 30, 34
PM_W2, PM_A2, PM_G2, PM_V1, PM_V2, PM_N = 0, 256, 512, 768, 832, 1088


def build_A(layer):
    nc = bass.Bass("TRN2", target_bir_lowering=False)
    P = Prog(nc)
    h = P.dram("h", [A_NT, 1024], F32, "ExternalInput")
    hh = P.dram("hh", [128, 1024], F32, "ExternalInput")
    w_in = P.dram("w_in", [1024, 3840], F32, "ExternalInput")
    pp_d = P.dram("pp", [128, PP_N], F32, "ExternalInput")
    pm_d = P.dram("pm", [128, PM_N], F32, "ExternalInput")
    id_d = P.dram("ident", [128, 128], BF16, "ExternalInput")
    blk_d = P.dram("blk64", [128, 128], BF16, "ExternalInput")
    if layer == 1:
        vf_d = P.dram("vfirst", [256, A_NT], F32, "ExternalInput")
    fm = P.dram("fm", [FM_ROWS, A_NT], F32, "ExternalOutput")
    tm = P.dram("tm", [A_NT, 768], F32, "ExternalOutput")

    wbf = [P.sbuf("wbf%d" % k, [128, 3840], BF16) for k in range(8)]
    wst = [P.sbuf("wst%d" % i, [128, 1920], F32) for i in range(2)]
    pp = P.sbuf("pp_s", [128, PP_N], F32)
    pm32 = P.sbuf("pm32", [128, PM_N], F32)
    pm = P.sbuf("pm_s", [128, PM_N], BF16)
    ident = P.sbuf("ident_s", [128, 128], BF16)
    blk = P.sbuf("blk_s", [128, 128], BF16)
    hin = [P.sbuf("hin%d" % i, [128, 1024], F32) for i in range(2)]
    hsq = P.sbuf("hsq", [128, 1024], F32)
    hnb = [P.sbuf("hnb%d" % i, [128, 1024], BF16) for i in range(2)]
    st = [P.sbuf("st%d" % i, [128, 4], F32) for i in range(2)]
    hnT = [P.sbuf("hnT%d" % i, [128, 8, A_TG], BF16) for i in range(2)]
    gb = P.sbuf("gb", [128, 8, 128], F32)
    CB = [P.sbuf("CB%d" % i, [128, 8, A_TG + 1], F32) for i in range(2)]
    stg = [P.sbuf("stg%d" % i, [128, A_TG], F32) for i in range(6)]
    stt = [P.sbuf("stt%d" % i, [128, 768], F32) for i in range(2)]
    cm = P.sbuf("cm", [128, 8, A_TG], F32)
    tmpA = [P.sbuf("tmpA%d" % i, [128, A_TG], F32) for i in range(4)]
    tb = [P.sbuf("tb%d" % i, [128, A_TG], BF16) for i in range(4)]
    lbc = P.sbuf("lbc", [128, 8], F32)
    kac = P.sbuf("kac", [128, 2], F32)
    epsc = P.sbuf("epsc", [128, 1], F32)
    vfs = P.sbuf("vfs", [128, 2, A_TG], F32) if layer == 1 else None
    ps = [P.psum("ps%d" % i, [128, 512], F32) for i in range(6)]
    pst = P.psum("pst", [128, 1024], BF16)
    psm = P.psum("psm", [128, 512], F32)

    P.dma("sp", pp, pp[:], pp_d, pp_d[:])
    P.dma("sp", pm32, pm32[:], pm_d, pm_d[:])
    P.dma("sp", ident, ident[:], id_d, id_d[:])
    P.dma("sp", blk, blk[:], blk_d, blk_d[:])
    P.op("dve", lambda e: e.tensor_copy(out=pm[:], in_=pm32[:]), [pm32], [pm])
    P.op("dve", lambda e: e.memset(epsc[:], 1e-6), [], [epsc])
    for c in range(8):
        P.op("dve", lambda e, c=c: e.memset(gb[:, c, :], 1.0), [], [gb])
    for c in range(8):
        P.op("dve", lambda e, c=c: e.tensor_scalar(out=gb[:, c, :], in0=gb[:, c, :], scalar1=pp[:, PP_GMIX + c:PP_GMIX + c + 1],
                                                    scalar2=None, op0=ALU.mult), [gb, pp], [gb])
    P.op("dve", lambda e: e.tensor_scalar(out=kac[:], in0=pp[:, PP_KA:PP_KA + 2], scalar1=-1.0, scalar2=1.0,
                                          op0=ALU.mult, op1=ALU.add), [pp], [kac])
    if layer == 1:
        P.op("dve", lambda e: e.tensor_tensor(out=lbc[:, 0:4], in0=pp[:, PP_HB1:PP_HB1 + 4], in1=pp[:, PP_HB0:PP_HB0 + 4],
                                              op=ALU.subtract), [pp], [lbc])
        P.op("act", lambda e: e.activation(out=lbc[:, 0:4], in_=lbc[:, 0:4], func=AF.Sigmoid), [lbc], [lbc])
        P.op("dve", lambda e: e.tensor_scalar(out=lbc[:, 4:8], in0=lbc[:, 0:4], scalar1=-1.0, scalar2=1.0,
                                              op0=ALU.mult, op1=ALU.add), [lbc], [lbc])
    for k in range(8):
        for hf in range(2):
            s_ = wst[hf]
            P.dma("sp" if hf == 0 else "act", s_, s_[:], w_in, w_in[k * 128:(k + 1) * 128, hf * 1920:(hf + 1) * 1920])
            P.op("pool", lambda e, k=k, s_=s_, hf=hf: e.tensor_copy(out=wbf[k][:, hf * 1920:(hf + 1) * 1920], in_=s_[:]), [s_], [wbf[k]])

    outq = ["sp", "act", "pool"]
    oq = [0]

    def out_dma(dst_t, dst_ap, src_t, src_ap):
        q = outq[oq[0] % 3]
        oq[0] += 1
        P.dma(q, dst_t, dst_ap, src_t, src_ap, owner=src_t, disjoint=True)

    tcount = [0]

    def norm_tile(src_ap, dstT, col0):
        i = tcount[0] % 2
        tcount[0] += 1
        hi, hb, s_ = hin[i], hnb[i], st[i]
        P.dma("sp", hi, hi[:], h, src_ap)
        P.op("act", lambda e: e.activation(out=hsq[:], in_=hi[:], func=AF.Square, accum_out=s_[:, 0:1]), [hi], [hsq, s_])
        P.op("act", lambda e: e.activation(out=s_[:, 1:2], in_=s_[:, 0:1], func=AF.Sqrt, scale=1.0 / 1024, bias=epsc[:, 0:1]),
             [s_, epsc], [s_])
        P.op("dve", lambda e: e.reciprocal(out=s_[:, 2:3], in_=s_[:, 1:2]), [s_], [s_])
        P.op("dve", lambda e: e.tensor_scalar(out=hb[:], in0=hi[:], scalar1=s_[:, 2:3], scalar2=None, op0=ALU.mult),
             [hi, s_], [hb])
        for c in range(8):
            P.op("pe", lambda e, c=c: e.transpose(out=pst[:, c * 128:(c + 1) * 128], in_=hb[:, c * 128:(c + 1) * 128],
                                                   identity=ident[:]), [hb, ident], [pst])
        P.op("dve", lambda e: e.tensor_tensor(out=dstT[:, :, col0:col0 + 128],
                                              in0=pst[:].rearrange("p (c t) -> p c t", c=8), in1=gb[:], op=ALU.mult),
             [pst, gb], [dstT])

    def mm_fm(dst_ps, cc, src):
        for k in range(8):
            P.op("pe", lambda e, k=k: e.matmul(dst_ps[:], lhsT=wbf[k][:, cc * 128:(cc + 1) * 128], rhs=src[:, k, :],
                                                start=(k == 0), stop=(k == 7)), [wbf[k], src], [dst_ps])

    hT_h = hnT[1]
    P.hsrc = hh
    i0 = tcount[0]
    hi, hb, s_ = hin[0], hnb[0], st[0]
    tcount[0] += 1
    P.dma("sp", hi, hi[:], hh, hh[:])
    P.op("act", lambda e: e.activation(out=hsq[:], in_=hi[:], func=AF.Square, accum_out=s_[:, 0:1]), [hi], [hsq, s_])
    P.op("act", lambda e: e.activation(out=s_[:, 1:2], in_=s_[:, 0:1], func=AF.Sqrt, scale=1.0 / 1024, bias=epsc[:, 0:1]),
         [s_, epsc], [s_])
    P.op("dve", lambda e: e.reciprocal(out=s_[:, 2:3], in_=s_[:, 1:2]), [s_], [s_])
    P.op("dve", lambda e: e.tensor_scalar(out=hb[:], in0=hi[:], scalar1=s_[:, 2:3], scalar2=None, op0=ALU.mult), [hi, s_], [hb])
    for c in range(8):
        P.op("pe", lambda e, c=c: e.transpose(out=pst[:, c * 128:(c + 1) * 128], in_=hb[:, c * 128:(c + 1) * 128],
                                               identity=ident[:]), [hb, ident], [pst])
    P.op("dve", lambda e: e.tensor_tensor(out=hT_h[:, :, 0:128], in0=pst[:].rearrange("p (c t) -> p c t", c=8),
                                          in1=gb[:], op=ALU.mult), [pst, gb], [hT_h])
    for c8 in range(8):
        cc = 22 + c8
        pz = ps[c8 % 6]
        for k in range(8):
            P.op("pe", lambda e, k=k, cc=cc, pz=pz: e.matmul(pz[:, 0:128], lhsT=wbf[k][:, cc * 128:(cc + 1) * 128],
                                                              rhs=hT_h[:, k, 0:128], start=(k == 0), stop=(k == 7)),
                 [wbf[k], hT_h], [pz])
        P.op("act", lambda e, c8=c8, pz=pz: e.activation(out=CB[0][:, c8, 0:1], in_=pz[:, 127:128], func=AF.Copy), [pz], [CB[0]])

    sti = [0]

    def stage():
        s_ = stg[sti[0] % 6]
        sti[0] += 1
        return s_

    psi = [0]

    def nps():
        p_ = ps[psi[0] % 6]
        psi[0] += 1
        return p_

    for g in range(A_NG):
        hT = hnT[g % 2]
        cb = CB[g % 2]
        cbn = CB[(g + 1) % 2]
        t0 = g * A_TG
        for t in range(4):
            norm_tile(h[t0 + t * 128:t0 + (t + 1) * 128, :], hT, t * 128)
        for c in range(4):
            pz = nps(); mm_fm(pz, c, hT); s_ = stage()
            P.op("act", lambda e, pz=pz, s_=s_: e.activation(out=s_[:], in_=pz[:], func=AF.Silu), [pz], [s_])
            out_dma(fm, fm[FM_QS + c * 128:FM_QS + (c + 1) * 128, t0:t0 + A_TG], s_, s_[:])
        for c in range(4):
            pz = nps(); mm_fm(pz, 4 + c, hT); s_ = stage()
            P.op("act", lambda e, pz=pz, s_=s_: e.activation(out=s_[:], in_=pz[:], func=AF.Sigmoid), [pz], [s_])
            if layer == 1:
                P.op("dve", lambda e, s_=s_, c=c: e.tensor_scalar(out=s_[:], in0=s_[:], scalar1=lbc[:, 4 + c:5 + c],
                                                                    scalar2=lbc[:, c:c + 1], op0=ALU.mult, op1=ALU.add),
                     [s_, lbc], [s_])
            P.op("act", lambda e, s_=s_: e.activation(out=s_[:], in_=s_[:], func=AF.Ln), [s_], [s_])
            out_dma(fm, fm[FM_LF + c * 128:FM_LF + (c + 1) * 128, t0:t0 + A_TG], s_, s_[:])
        for c in range(4):
            pz = nps(); mm_fm(pz, 12 + c, hT); s_ = stage()
            P.op("act", lambda e, pz=pz, s_=s_: e.activation(out=s_[:], in_=pz[:], func=AF.Silu), [pz], [s_])
            out_dma(fm, fm[FM_GS + c * 128:FM_GS + (c + 1) * 128, t0:t0 + A_TG], s_, s_[:])
        for c in range(4):
            pz = nps(); mm_fm(pz, 16 + c, hT); s_ = stage()
            P.op("dve", lambda e, pz=pz, s_=s_: e.tensor_copy(out=s_[:], in_=pz[:]), [pz], [s_])
            out_dma(fm, fm[FM_BQ + c * 128:FM_BQ + (c + 1) * 128, t0:t0 + A_TG], s_, s_[:])
        for t in range(4):
            pz = nps(); pz2 = nps(); s_ = stt[t % 2]
            for k in range(8):
                P.op("pe", lambda e, k=k, t=t, pz=pz: e.matmul(pz[:], lhsT=hT[:, k, t * 128:(t + 1) * 128],
                                                                rhs=wbf[k][:, 1024:1536], start=(k == 0), stop=(k == 7)),
                     [wbf[k], hT], [pz])
            for k in range(8):
                P.op("pe", lambda e, k=k, t=t, pz2=pz2: e.matmul(pz2[:, 0:256], lhsT=hT[:, k, t * 128:(t + 1) * 128],
                                                                  rhs=wbf[k][:, 2560:2816], start=(k == 0), stop=(k == 7)),
                     [wbf[k], hT], [pz2])
            P.op("dve", lambda e, pz=pz, s_=s_: e.tensor_copy(out=s_[:, 0:512], in_=pz[:]), [pz], [s_])
            P.op("act", lambda e, pz2=pz2, s_=s_: e.activation(out=s_[:, 512:768], in_=pz2[:, 0:256], func=AF.Copy), [pz2], [s_])
            out_dma(tm, tm[t0 + t * 128:t0 + (t + 1) * 128, :], s_, s_[:])
        for c8 in range(8):
            pz = nps(); mm_fm(pz, 22 + c8, hT)
            if c8 % 2 == 0:
                P.op("dve", lambda e, pz=pz, c8=c8: e.tensor_copy(out=cb[:, c8, 1:A_TG + 1], in_=pz[:]), [pz], [cb])
            else:
                P.op("act", lambda e, pz=pz, c8=c8: e.activation(out=cb[:, c8, 1:A_TG + 1], in_=pz[:], func=AF.Copy), [pz], [cb])
        P.op("pool", lambda e: e.tensor_copy(out=cbn[:, :, 0:1], in_=cb[:, :, A_TG:A_TG + 1]), [cb], [cbn])
        for c8 in range(8):
            ta = tmpA[c8 % 2]
            eng = "dve"
            P.op(eng, lambda e, c8=c8, ta=ta: e.tensor_tensor(out=ta[:], in0=cb[:, c8, 0:A_TG], in1=cb[:, c8, 1:A_TG + 1],
                                                              op=ALU.subtract), [cb], [ta])
            P.op(eng, lambda e, c8=c8, ta=ta: e.scalar_tensor_tensor(out=cm[:, c8, :], in0=ta[:], scalar=pp[:, PP_MU + c8:PP_MU + c8 + 1],
                                                                     in1=cb[:, c8, 1:A_TG + 1], op0=ALU.mult, op1=ALU.add),
                 [ta, pp, cb], [cm])
        for c in range(2):
            out_dma(fm, fm[FM_R + c * 128:FM_R + (c + 1) * 128, t0:t0 + A_TG], cm, cm[:, c, :])
        P.op("act", lambda e: e.activation(out=tb[0][0:64, :], in_=cm[0:64, 6, :], func=AF.Tanh), [cm], [tb[0]])
        P.op("dve", lambda e: e.tensor_copy(out=tb[0][64:128, :], in_=cm[64:128, 6, :]), [cm], [tb[0]])
        P.op("act", lambda e: e.activation(out=tb[1][:], in_=cm[:, 7, :], func=AF.Sigmoid), [cm], [tb[1]])
        E05 = float(np.exp(-0.5))
        for c in range(2):
            pz = nps(); s_ = stage()
            P.op("pe", lambda e, pz=pz, c=c: e.matmul(pz[:], lhsT=pm[0:64, PM_W2 + c * 128:PM_W2 + (c + 1) * 128],
                                                       rhs=tb[0][0:64, :], start=True, stop=True), [pm, tb[0]], [pz])
            P.op("act", lambda e, pz=pz, s_=s_, c=c: e.activation(out=s_[:], in_=pz[:], func=AF.Sigmoid,
                                                                   bias=pp[:, PP_W0 + c:PP_W0 + c + 1]), [pz, pp], [s_])
            P.op("dve", lambda e, s_=s_: e.tensor_scalar(out=s_[:], in0=s_[:], scalar1=-E05, scalar2=None, op0=ALU.mult), [s_], [s_])
            out_dma(fm, fm[FM_LD + c * 128:FM_LD + (c + 1) * 128, t0:t0 + A_TG], s_, s_[:])
        a_t = [tmpA[2], tmpA[3]]
        for c in range(2):
            pz = nps()
            P.op("pe", lambda e, pz=pz, c=c: e.matmul(pz[:], lhsT=pm[64:128, PM_A2 + c * 128:PM_A2 + (c + 1) * 128],
                                                       rhs=tb[0][64:128, :], start=True, stop=True), [pm, tb[0]], [pz])
            P.op("act", lambda e, pz=pz, c=c: e.activation(out=a_t[c][:], in_=pz[:], func=AF.Sigmoid,
                                                           bias=pp[:, PP_A0 + c:PP_A0 + c + 1]), [pz, pp], [a_t[c]])
            out_dma(fm, fm[FM_A + c * 128:FM_A + (c + 1) * 128, t0:t0 + A_TG], a_t[c], a_t[c][:])
        for c in range(2):
            pz = nps(); s_ = stage()
            P.op("pe", lambda e, pz=pz, c=c: e.matmul(pz[:], lhsT=pm[:, PM_G2 + c * 128:PM_G2 + (c + 1) * 128],
                                                       rhs=tb[1][:], start=True, stop=True), [pm, tb[1]], [pz])
            P.op("dve", lambda e, pz=pz, s_=s_: e.tensor_copy(out=s_[:], in_=pz[:]), [pz], [s_])
            out_dma(fm, fm[FM_G + c * 128:FM_G + (c + 1) * 128, t0:t0 + A_TG], s_, s_[:])
        if layer == 1:
            P.dma("sp", vfs, vfs[:], vf_d, vf_d[:, t0:t0 + A_TG].rearrange("(c p) t -> p c t", p=128))
            for c in range(2):
                P.op("dve", lambda e, c=c: e.tensor_copy(out=tb[2 + c][:], in_=cm[:, 4 + c, :]), [cm], [tb[2 + c]])
            for c in range(2):
                P.op("pe", lambda e, c=c: e.matmul(psm[0:32, :], lhsT=pm[:, PM_V1 + c * 32:PM_V1 + (c + 1) * 32],
                                                   rhs=tb[2 + c][:], start=(c == 0), stop=(c == 1)), [pm, tb[2 + c]], [psm])
            P.op("dve", lambda e: e.tensor_copy(out=tb[1][0:32, :], in_=psm[0:32, :]), [psm], [tb[1]])
            for c in range(2):
                pz = nps(); ta = tmpA[c]
                P.op("pe", lambda e, pz=pz, c=c: e.matmul(pz[:], lhsT=pm[0:32, PM_V2 + c * 128:PM_V2 + (c + 1) * 128],
                                                           rhs=tb[1][0:32, :], start=True, stop=True), [pm, tb[1]], [pz])
                P.op("act", lambda e, pz=pz, c=c, ta=ta: e.activation(out=ta[:], in_=pz[:], func=AF.Sigmoid,
                                                                        bias=pp[:, PP_V0 + c:PP_V0 + c + 1]), [pz, pp], [ta])
                s_ = stage()
                P.op("dve", lambda e, c=c, s_=s_: e.tensor_tensor(out=s_[:], in0=vfs[:, c, :], in1=cm[:, 4 + c, :],
                                                                   op=ALU.subtract), [vfs, cm], [s_])
                P.op("dve", lambda e, s_=s_, ta=ta: e.tensor_tensor(out=s_[:], in0=s_[:], in1=ta[:], op=ALU.mult), [s_, ta], [s_])
                P.op("dve", lambda e, s_=s_, c=c: e.tensor_tensor(out=s_[:], in0=s_[:], in1=cm[:, 4 + c, :], op=ALU.add),
                     [s_, cm], [s_])
                out_dma(fm, fm[FM_V + c * 128:FM_V + (c + 1) * 128, t0:t0 + A_TG], s_, s_[:])
        else:
            for c in range(2):
                out_dma(fm, fm[FM_V + c * 128:FM_V + (c + 1) * 128, t0:t0 + A_TG], cm, cm[:, 4 + c, :])
        for c in range(2):
            kx = tmpA[c]; s_ = stage(); s2 = stage(); pz = nps()
            P.op("dve", lambda e, c=c, kx=kx: e.tensor_scalar(out=kx[:], in0=cm[:, 2 + c, :], scalar1=pp[:, PP_KK + c:PP_KK + c + 1],
                                                               scalar2=None, op0=ALU.mult), [cm, pp], [kx])
            P.op("pool", lambda e, c=c, kx=kx: e.tensor_tensor(out=tb[2 + c][:], in0=kx[:], in1=kx[:], op=ALU.mult), [kx], [tb[2 + c]])
            P.op("pe", lambda e, pz=pz, c=c: e.matmul(pz[:], lhsT=blk[:], rhs=tb[2 + c][:], start=True, stop=True),
                 [blk, tb[2 + c]], [pz])
            P.op("act", lambda e, pz=pz, s_=s_: e.activation(out=s_[:], in_=pz[:], func=AF.Sqrt), [pz], [s_])
            P.op("dve", lambda e, s_=s_: e.tensor_scalar(out=s_[:], in0=s_[:], scalar1=1e-12, scalar2=None, op0=ALU.max), [s_], [s_])
            P.op("dve", lambda e, s_=s_: e.reciprocal(out=s_[:], in_=s_[:]), [s_], [s_])
            P.op("dve", lambda e, s_=s_, kx=kx: e.tensor_tensor(out=s_[:], in0=s_[:], in1=kx[:], op=ALU.mult), [s_, kx], [s_])
            out_dma(fm, fm[FM_KK + c * 128:FM_KK + (c + 1) * 128, t0:t0 + A_TG], s_, s_[:])
            P.op("dve", lambda e, s2=s2, c=c: e.tensor_scalar(out=s2[:], in0=a_t[c][:], scalar1=pp[:, PP_KA + c:PP_KA + c + 1],
                                                               scalar2=kac[:, c:c + 1], op0=ALU.mult, op1=ALU.add),
                 [a_t[c], pp, kac], [s2])
            P.op("dve", lambda e, s2=s2, c=c: e.tensor_tensor(out=s2[:], in0=s2[:], in1=cm[:, 2 + c, :], op=ALU.mult), [s2, cm], [s2])
            out_dma(fm, fm[FM_KP + c * 128:FM_KP + (c + 1) * 128, t0:t0 + A_TG], s2, s2[:])

    P.final_wait("sp", [fm, tm])
    P.emit()
    return nc, P


def host_inputs_A(layer, inp, h_full, vfirst_full=None):
    l = layer
    pp = np.zeros((128, PP_N), np.float32)
    fmj = lambda v: np.ascontiguousarray(v.reshape(-1, 128).T)
    pp[:, PP_GMIX:PP_GMIX + 8] = fmj(inp['norm_mix_g'][l])
    pp[:, PP_MU:PP_MU + 8] = fmj(inp['rwkv_mu'][l])
    pp[:, PP_W0:PP_W0 + 2] = fmj(inp['rwkv_w0'][l])
    pp[:, PP_A0:PP_A0 + 2] = fmj(inp['rwkv_a0'][l])
    pp[:, PP_KK:PP_KK + 2] = fmj(inp['rwkv_k_k'][l])
    pp[:, PP_KA:PP_KA + 2] = fmj(inp['rwkv_k_a'][l])
    if l == 1:
        pp[:, PP_V0:PP_V0 + 2] = fmj(inp['rwkv_v0'][0])
    pp[:, PP_HB0:PP_HB0 + 4] = fmj(inp['hgrn_lower_bounds'][0])
    pp[:, PP_HB1:PP_HB1 + 4] = fmj(inp['hgrn_lower_bounds'][1])
    pm = np.zeros((128, PM_N), np.float32)
    pm[0:64, PM_W2:PM_W2 + 256] = inp['rwkv_w2'][l]
    pm[64:128, PM_A2:PM_A2 + 256] = inp['rwkv_a2'][l]
    pm[:, PM_G2:PM_G2 + 256] = inp['rwkv_g2'][l]
    if l == 1:
        pm[:, PM_V1:PM_V1 + 64] = inp['rwkv_v1'][0].reshape(2, 128, 32).transpose(1, 0, 2).reshape(128, 64)
        pm[0:32, PM_V2:PM_V2 + 256] = inp['rwkv_v2'][0]
    ident = np.eye(128).astype(ml_dtypes.bfloat16)
    blk = np.kron(np.eye(2), np.ones((64, 64))).astype(ml_dtypes.bfloat16)
    maps = []
    w = np.ascontiguousarray(inp['w_in'][l])
    for c in range(8):
        m = {"h": np.ascontiguousarray(h_full[c * A_NT:(c + 1) * A_NT]),
             "hh": np.ascontiguousarray(h_full[c * A_NT - 128:c * A_NT]) if c > 0 else np.zeros((128, 1024), np.float32),
             "w_in": w, "pp": pp, "pm": pm, "ident": ident, "blk64": blk}
        if l == 1:
            m["vfirst"] = np.ascontiguousarray(vfirst_full[:, c * A_NT:(c + 1) * A_NT])
        maps.append(m)
    return maps


NOWN = 8192
NHALO = 2048
NTOT = NOWN + NHALO
PATTERNS = (1, 4, 16)


def emit_sw(P):
    qT_d = P.dram("sw_qT", [64, NOWN], F32, "ExternalInput")
    kT_d = P.dram("sw_kT", [64, NTOT], F32, "ExternalInput")
    v_d = P.dram("sw_v", [NTOT, 64], F32, "ExternalInput")
    msk_d = P.dram("sw_mask", [128, 512], BF16, "ExternalInput")
    id_d = P.dram("sw_ident", [128, 128], BF16, "ExternalInput")
    flag_d = P.dram("sw_flag", [128, 1], F32, "ExternalInput")
    o_d = P.dram("sw_o", [64, NOWN], F32, "ExternalOutput")

    q32 = P.sbuf("q32", [64, NOWN], F32)
    k32 = P.sbuf("k32", [64, NTOT], F32)
    qd = P.sbuf("qd", [64, NOWN], BF16)
    kd = P.sbuf("kd", [64, NTOT], BF16)
    vst = P.sbuf("vst", [128, 85 * 64], F32)
    vaug = P.sbuf("vaug", [128, 85, 65], BF16)
    acc = P.sbuf("acc", [65, NOWN], F32)
    msk = P.sbuf("msk", [128, 512], BF16)
    ident = P.sbuf("identsw", [128, 128], BF16)
    flag = P.sbuf("flag", [128, 1], F32)
    ones = P.sbuf("ones", [65, 64], F32)
    PT = [P.sbuf("PT%d" % i, [128, 512], BF16) for i in range(3)]
    ost = [P.sbuf("ost%d" % i, [64, 512], F32) for i in range(2)]
    rz = [P.sbuf("rz%d" % i, [64, 512], F32) for i in range(2)]
    psS = [P.psum("psS%d" % i, [128, 512], F32) for i in range(3)]
    psN = [P.psum("psN%d" % i, [128, 512], F32) for i in range(3)]

    P.dma("sp", msk, msk[:], msk_d, msk_d[:])
    P.dma("sp", ident, ident[:], id_d, id_d[:])
    P.dma("sp", flag, flag[:], flag_d, flag_d[:])
    for i in range(4):
        P.dma("sp" if i % 2 == 0 else "act", q32, q32[:, i * 2048:(i + 1) * 2048], qT_d, qT_d[:, i * 2048:(i + 1) * 2048],
              disjoint=True)
    for i in range(5):
        P.dma("act" if i % 2 == 0 else "sp", k32, k32[:, i * 2048:(i + 1) * 2048], kT_d, kT_d[:, i * 2048:(i + 1) * 2048],
              disjoint=True)
    P.op("dve", lambda e: e.memset(ones[:], 1.0), [], [ones])

    si = [0]
    for pi, D in enumerate(PATTERNS):
        nb = NOWN // (128 * D)
        nbt = nb + 1
        LQ = NOWN // D
        LK = LQ + 128
        koff = NHALO - 128 * D
        if D == 1:
            P.op("dve", lambda e: e.tensor_copy(out=qd[:, 0:NOWN], in_=q32[:, :]), [q32], [qd])
            P.op("pool", lambda e: e.tensor_copy(out=kd[:, 0:LK], in_=k32[:, koff:koff + LK]), [k32], [kd])
        else:
            P.op("dve", lambda e: e.tensor_copy(out=qd[:, 0:NOWN].rearrange("p (r l) -> p r l", r=D),
                                                in_=q32[:, :].rearrange("p (l r) -> p r l", r=D)), [q32], [qd])
            P.op("pool", lambda e: e.tensor_copy(out=kd[:, 0:D * LK].rearrange("p (r l) -> p r l", r=D),
                                                 in_=k32[:, koff:koff + D * LK].rearrange("p (l r) -> p r l", r=D)), [k32], [kd])
        vsrc = v_d[koff:koff + nbt * 128 * D, :].rearrange("(bb i r) c -> i r bb c", i=128, r=D)
        vv = vst[:, 0:D * nbt * 64].rearrange("p (r bb c) -> p r bb c", r=D, bb=nbt)
        for r in range(D):
            P.dma("sp" if r % 2 == 0 else "act", vst, vv[:, r], v_d, vsrc[:, r], disjoint=(r > 0))
        va = vaug[:, 0:D * nbt, :]
        P.op("dve", lambda e: e.memset(va[:, :, 64:65], 1.0), [], [vaug])
        P.op("dve", lambda e: e.tensor_copy(out=va[:, :, 0:64], in_=vst[:, 0:D * nbt * 64].rearrange("p (n c) -> p n c", c=64)),
             [vst], [vaug])
        va4 = va.rearrange("p (r bb) c -> p r bb c", r=D)
        P.op("dve", lambda e: e.tensor_scalar(out=va4[:, :, 0, :], in0=va4[:, :, 0, :], scalar1=flag[:, 0:1], scalar2=None,
                                              op0=ALU.mult), [vaug, flag], [vaug])
        for r in range(D):
            for b0 in range(0, nb, 4):
                pn = psN[si[0] % 3]
                pts = []
                for pr in range(2):
                    p_s = psS[(2 * si[0] + pr) % 3]
                    pt = PT[(2 * si[0] + pr) % 3]
                    P.op("pe", lambda e: e.matmul(p_s[:], lhsT=ident[:], rhs=msk[:], start=True, stop=False), [ident, msk], [p_s])
                    for j in range(2):
                        b = b0 + 2 * pr + j
                        qb = qd[:, r * LQ + b * 128: r * LQ + (b + 1) * 128]
                        for kb in range(2):
                            kblk = kd[:, r * LK + (b + kb) * 128: r * LK + (b + kb + 1) * 128]
                            P.op("pe", lambda e: e.matmul(p_s[:, (2 * j + kb) * 128:(2 * j + kb + 1) * 128], lhsT=kblk, rhs=qb,
                                                          start=False, stop=True), [kd, qd], [p_s])
                    P.op("act", lambda e: e.activation(out=pt[:], in_=p_s[:], func=AF.Exp, scale=0.125), [p_s], [pt])
                    pts.append(pt)
                for pr in range(2):
                    for j in range(2):
                        b = b0 + 2 * pr + j
                        for kb in range(2):
                            P.op("pe", lambda e: e.matmul(pn[0:65, (2 * pr + j) * 128:(2 * pr + j + 1) * 128],
                                                          lhsT=vaug[:, r * nbt + b + kb, :],
                                                          rhs=pts[pr][:, (2 * j + kb) * 128:(2 * j + kb + 1) * 128],
                                                          start=(kb == 0), stop=(kb == 1)), [vaug, pts[pr]], [pn])
                tstart = r + D * 128 * b0
                av = acc[:, tstart: tstart + 512 * D] if D == 1 else \
                    acc[:, D * 128 * b0: D * 128 * b0 + 512 * D].rearrange("p (l r) -> p r l", r=D)[:, r, :]
                if pi == 0:
                    P.op("dve", lambda e: e.tensor_copy(out=av, in_=pn[0:65, :]), [pn], [acc])
                else:
                    P.op("dve", lambda e: e.tensor_tensor(out=av, in0=av, in1=pn[0:65, :], op=ALU.add), [pn, acc], [acc])
                si[0] += 1
    for i in range(NOWN // 512):
        pz = psS[i % 3]
        P.op("pe", lambda e: e.matmul(pz[0:64, :], lhsT=ones[64:65, :], rhs=acc[64:65, i * 512:(i + 1) * 512], start=True, stop=True),
             [ones, acc], [pz])
        rzi = rz[i % 2]; o_ = ost[i % 2]
        P.op("dve", lambda e: e.reciprocal(out=rzi[:], in_=pz[0:64, :]), [pz], [rzi])
        P.op("pool", lambda e: e.tensor_tensor(out=o_[:], in0=acc[0:64, i * 512:(i + 1) * 512], in1=rzi[:], op=ALU.mult), [acc, rzi], [o_])
        P.dma("sp" if i % 2 == 0 else "act", o_d, o_d[:, i * 512:(i + 1) * 512], o_, o_[:], owner=o_, disjoint=True)
    return [o_d]


def sw_consts():
    j = np.arange(128)[:, None]
    i = np.arange(128)[None, :]
    mp = np.where(j >= i, 0.0, -30000.0)
    mo = np.where(j <= i, 0.0, -30000.0)
    m = np.concatenate([mp, mo, mp, mo], axis=1).astype(ml_dtypes.bfloat16)
    return {"sw_mask": m, "sw_ident": np.eye(128).astype(ml_dtypes.bfloat16)}


def sw_inputs(core, bq_fm, bk_fm, bv_tm):
    hd, s = core // 2, core % 2
    t0 = s * NOWN
    qT = np.ascontiguousarray(bq_fm[hd * 64:(hd + 1) * 64, t0:t0 + NOWN])
    kT = np.zeros((64, NTOT), np.float32)
    v = np.zeros((NTOT, 64), np.float32)
    kT[:, NHALO:] = bk_fm[hd * 64:(hd + 1) * 64, t0:t0 + NOWN]
    v[NHALO:] = bv_tm[t0:t0 + NOWN, hd * 64:(hd + 1) * 64]
    if s > 0:
        kT[:, :NHALO] = bk_fm[hd * 64:(hd + 1) * 64, t0 - NHALO:t0]
        v[:NHALO] = bv_tm[t0 - NHALO:t0, hd * 64:(hd + 1) * 64]
    m = {"sw_qT": qT, "sw_kT": kT, "sw_v": v, "sw_flag": np.full((128, 1), 1.0 if s > 0 else 0.0, np.float32)}
    m.update(sw_consts())
    return m


HG_SEQ = 16384
HG_ST = 2048
HG_NST = HG_SEQ // HG_ST
HG_CL = 40.0


def emit_hgrn(P):
    q_d = P.dram("hg_q", [128, HG_SEQ], F32, "ExternalInput")
    lf_d = P.dram("hg_lf", [128, HG_SEQ], F32, "ExternalInput")
    i_d = P.dram("hg_i", [HG_SEQ, 64], F32, "ExternalInput")
    rm_d = P.dram("hg_rmask", [128, HG_ST], F32, "ExternalInput")
    cm_d = P.dram("hg_cmask", [128, 128], F32, "ExternalInput")
    id_d = P.dram("hg_ident", [128, 128], BF16, "ExternalInput")
    o_d = P.dram("hg_o", [64, HG_SEQ], F32, "ExternalOutput")

    qs = P.sbuf("hqs", [128, HG_ST], F32)
    lf = P.sbuf("hlf", [128, HG_ST], F32)
    bb = P.sbuf("hb", [128, HG_ST], F32)
    kf = P.sbuf("hkf", [128, HG_ST], F32)
    t1 = P.sbuf("ht1", [128, HG_ST], F32)
    t2 = P.sbuf("ht2", [128, HG_ST], F32)
    dch = P.sbuf("hdch", [128, 32], F32)
    Qt = P.sbuf("hQt", [128, HG_ST], BF16)
    Kt = P.sbuf("hKt", [128, HG_ST], BF16)
    Qh = P.sbuf("hQh", [128, HG_ST], BF16)
    Kh = P.sbuf("hKh", [128, HG_ST], BF16)
    rmask = P.sbuf("hrmask", [128, HG_ST], F32)
    cmask = P.sbuf("hcmask", [128, 128], F32)
    ident = P.sbuf("hident", [128, 128], BF16)
    v32 = P.sbuf("hv32", [128, 16, 64], F32)
    vb = P.sbuf("hvb", [128, 16, 64], BF16)
    KhT = [P.sbuf("hKhT%d" % i, [128, 128], BF16) for i in range(2)]
    Am = [P.sbuf("hAm%d" % i, [128, 128], BF16) for i in range(2)]
    S32 = P.sbuf("hS32", [128, 64], F32)
    Sb = [P.sbuf("hSb%d" % i, [128, 64], BF16) for i in range(2)]
    ost = P.sbuf("host", [64, HG_ST], F32)
    psT = [P.psum("hpsT%d" % i, [128, 128], BF16) for i in range(2)]
    psA = [P.psum("hpsA%d" % i, [128, 128], F32) for i in range(2)]
    psO = [P.psum("hpsO%d" % i, [128, 128], F32) for i in range(2)]
    psU = [P.psum("hpsU%d" % i, [128, 64], F32) for i in range(2)]

    P.dma("sp", rmask, rmask[:], rm_d, rm_d[:])
    P.dma("sp", cmask, cmask[:], cm_d, cm_d[:])
    P.dma("sp", ident, ident[:], id_d, id_d[:])
    P.op("dve", lambda e: e.memset(S32[:], 0.0), [], [S32])
    P.op("dve", lambda e: e.memset(Sb[0][:], 0.0), [], [Sb[0]])
    sbi = 0
    pc = 0
    b3 = bb[:, :].rearrange("p (n c) -> p n c", c=64)
    for st in range(HG_NST):
        t0 = st * HG_ST
        P.dma("sp", qs, qs[:], q_d, q_d[:, t0:t0 + HG_ST])
        P.dma("act", lf, lf[:], lf_d, lf_d[:, t0:t0 + HG_ST])
        P.dma("pool", v32, v32[:], i_d, i_d[t0:t0 + HG_ST, :].rearrange("(n i) c -> i n c", i=128))
        P.op("pool", lambda e: e.tensor_copy(out=vb[:], in_=v32[:]), [v32], [vb])
        P.op("act", lambda e: e.activation(out=kf[:], in_=lf[:], func=AF.Exp), [lf], [kf])
        P.op("pool", lambda e: e.tensor_scalar(out=kf[:], in0=kf[:], scalar1=-1.0, scalar2=1.0, op0=ALU.mult, op1=ALU.add), [kf], [kf])
        P.op("dve", lambda e: e.tensor_tensor_scan(out=bb[:], data0=rmask[:], data1=lf[:], initial=0.0, op0=ALU.mult, op1=ALU.add),
             [rmask, lf], [bb])
        bm = b3[:, :, 31:32].to_broadcast([128, 32, 64])
        bl = b3[:, :, 63:64].to_broadcast([128, 32, 64])
        t1v = t1[:, :].rearrange("p (n c) -> p n c", c=64)
        t2v = t2[:, :].rearrange("p (n c) -> p n c", c=64)
        P.op("dve", lambda e: e.tensor_tensor(out=t1v, in0=b3, in1=bm, op=ALU.subtract), [bb], [t1])
        P.op("pool", lambda e: e.tensor_scalar(out=t2[:], in0=t1[:], scalar1=-1.0, scalar2=HG_CL, op0=ALU.mult, op1=ALU.min), [t1], [t2])
        P.op("dve", lambda e: e.tensor_scalar(out=t1[:], in0=t1[:], scalar1=HG_CL, scalar2=None, op0=ALU.min), [t1], [t1])
        P.op("act", lambda e: e.activation(out=t1[:], in_=t1[:], func=AF.Exp), [t1], [t1])
        P.op("act", lambda e: e.activation(out=t2[:], in_=t2[:], func=AF.Exp), [t2], [t2])
        P.op("dve", lambda e: e.tensor_tensor(out=Qt[:], in0=qs[:], in1=t1[:], op=ALU.mult), [qs, t1], [Qt])
        P.op("pool", lambda e: e.tensor_tensor(out=Kt[:], in0=kf[:], in1=t2[:], op=ALU.mult), [kf, t2], [Kt])
        P.op("act", lambda e: e.activation(out=t1[:], in_=bb[:], func=AF.Exp), [bb], [t1])
        P.op("dve", lambda e: e.tensor_tensor(out=Qh[:], in0=qs[:], in1=t1[:], op=ALU.mult), [qs, t1], [Qh])
        P.op("dve", lambda e: e.tensor_tensor(out=t2v, in0=bl, in1=b3, op=ALU.subtract), [bb], [t2])
        P.op("act", lambda e: e.activation(out=t2[:], in_=t2[:], func=AF.Exp), [t2], [t2])
        P.op("pool", lambda e: e.tensor_tensor(out=Kh[:], in0=kf[:], in1=t2[:], op=ALU.mult), [kf, t2], [Kh])
        P.op("act", lambda e: e.activation(out=dch[:], in_=b3[:, :, 63], func=AF.Exp), [bb], [dch])
        for pr in range(16):
            c0 = pr * 128
            kh_t = KhT[pc % 2]; am = Am[pc % 2]; p_t = psT[pc % 2]; p_a = psA[pc % 2]; p_o = psO[pc % 2]
            pc += 1
            P.op("pe", lambda e: e.transpose(out=p_t[:], in_=Kh[:, c0:c0 + 128], identity=ident[:]), [Kh, ident], [p_t])
            P.op("act", lambda e: e.activation(out=kh_t[:], in_=p_t[:], func=AF.Copy), [p_t], [kh_t])
            P.op("pe", lambda e: e.matmul(p_a[:], lhsT=Kt[:, c0:c0 + 128], rhs=Qt[:, c0:c0 + 128], start=True, stop=True), [Kt, Qt], [p_a])
            P.op("dve", lambda e: e.tensor_tensor(out=am[:], in0=p_a[:], in1=cmask[:], op=ALU.mult), [p_a, cmask], [am])
            for ch in range(2):
                r0 = ch * 64
                s_cur = Sb[sbi % 2]; s_nxt = Sb[(sbi + 1) % 2]; p_u = psU[sbi % 2]
                sbi += 1
                P.op("pe", lambda e: e.matmul(p_o[0:64, r0:r0 + 64], lhsT=vb[r0:r0 + 64, pr, :], rhs=am[r0:r0 + 64, r0:r0 + 64],
                                              start=True, stop=False), [vb, am], [p_o])
                P.op("pe", lambda e: e.matmul(p_o[0:64, r0:r0 + 64], lhsT=s_cur[:], rhs=Qh[:, c0 + r0:c0 + r0 + 64],
                                              start=False, stop=True), [s_cur, Qh], [p_o])
                P.op("pe", lambda e: e.matmul(p_u[:], lhsT=kh_t[r0:r0 + 64, :], rhs=vb[r0:r0 + 64, pr, :], start=True, stop=True),
                     [kh_t, vb], [p_u])
                cidx = pr * 2 + ch
                P.op("dve", lambda e: e.scalar_tensor_tensor(out=S32[:], in0=S32[:], scalar=dch[:, cidx:cidx + 1], in1=p_u[:],
                                                             op0=ALU.mult, op1=ALU.add), [S32, dch, p_u], [S32])
                P.op("pool", lambda e: e.tensor_copy(out=s_nxt[:], in_=S32[:]), [S32], [s_nxt])
            P.op("act", lambda e: e.activation(out=ost[:, c0:c0 + 128], in_=p_o[0:64, :], func=AF.Copy), [p_o], [ost])
        P.dma("sp", o_d, o_d[:, t0:t0 + HG_ST], ost, ost[:], owner=ost, disjoint=True)
    return [o_d]


def hg_consts():
    t = np.arange(HG_ST)
    rm = np.tile(((t % 64) != 0).astype(np.float32)[None, :], (128, 1))
    j = np.arange(128)[:, None]; i = np.arange(128)[None, :]
    cmk = ((j // 64 == i // 64) & (j <= i)).astype(np.float32)
    return {"hg_rmask": rm, "hg_cmask": cmk, "hg_ident": np.eye(128).astype(ml_dtypes.bfloat16)}


def hg_inputs(core, qs_fm, lf_fm, i_tm):
    hd, vh = core // 2, core % 2
    m = {"hg_q": np.ascontiguousarray(qs_fm[hd * 128:(hd + 1) * 128]),
         "hg_lf": np.ascontiguousarray(lf_fm[hd * 128:(hd + 1) * 128]),
         "hg_i": np.ascontiguousarray(i_tm[:, hd * 128 + vh * 64: hd * 128 + (vh + 1) * 64])}
    m.update(hg_consts())
    return m


RW_SEQ = 16384
RW_ST = 2048
RW_NST = RW_SEQ // RW_ST
RW_C = 128
RW_NCH = RW_ST // RW_C


def emit_rwkv(P, KCH=2, NSET=4, CHDT=BF16):
    r_d = P.dram("rw_r", [64, RW_SEQ], F32, "ExternalInput")
    kp_d = P.dram("rw_kp", [64, RW_SEQ], F32, "ExternalInput")
    kk_d = P.dram("rw_kk", [64, RW_SEQ], F32, "ExternalInput")
    a_d = P.dram("rw_a", [64, RW_SEQ], F32, "ExternalInput")
    ld_d = P.dram("rw_ld", [64, RW_SEQ], F32, "ExternalInput")
    v_d = P.dram("rw_v", [32, RW_SEQ], F32, "ExternalInput")
    rm_d = P.dram("rw_rmask", [64, RW_ST], F32, "ExternalInput")
    mk_d = P.dram("rw_masks", [128, 4 * 128], F32, "ExternalInput")
    id_d = P.dram("rw_ident", [128, 128], BF16, "ExternalInput")
    y_d = P.dram("rw_y", [32, RW_SEQ], F32, "ExternalOutput")

    def sb(name, shape, dt=F32):
        return P.sbuf("rws_" + name, shape, dt)
    r_s, kp_s, kk_s, a_s, ld_s = [sb(n, [64, RW_ST]) for n in ("r", "kp", "kk", "a", "ld")]
    v_s = sb("v", [32, RW_ST])
    G = sb("G", [64, RW_ST]); x1 = sb("x1", [64, RW_ST]); x2 = sb("x2", [64, RW_ST]); x3 = sb("x3", [64, RW_ST])
    dC = [sb("dC%d" % i, [64, RW_NCH]) for i in range(2)]
    AR = [sb("AR%d" % i, [64, RW_NCH, 2, RW_C], BF16) for i in range(2)]
    Bt = [sb("Bt%d" % i, [64, RW_ST], BF16) for i in range(2)]
    Kt = [sb("Kt%d" % i, [64, RW_ST], BF16) for i in range(2)]
    Bh = [sb("Bh%d" % i, [64, RW_ST], BF16) for i in range(2)]
    Kh = [sb("Kh%d" % i, [64, RW_ST], BF16) for i in range(2)]
    vb = [sb("vb%d" % i, [32, RW_ST], BF16) for i in range(2)]
    yst = [sb("yst%d" % i, [32, RW_ST]) for i in range(2)]
    rmask = sb("rmask", [64, RW_ST])
    masks = sb("masks", [128, 4 * 128])
    ident = sb("ident", [128, 128], BF16)
    tok = [sb("tok%d" % i, [128, 160], BF16) for i in range(NSET)]
    XTb = [sb("XTb%d" % i, [128, 128], BF16) for i in range(NSET)]
    Arb = [sb("Arb%d" % i, [128, 128], BF16) for i in range(NSET)]
    Ak = [sb("Ak%d" % i, [128, 256], BF16) for i in range(NSET)]
    Mf = [[sb("Mf%d_%d" % (k, i), [128, 128], CHDT) for i in range(2)] for k in range(KCH)]
    Nf = [[sb("Nf%d_%d" % (k, i), [128, 128], CHDT) for i in range(2)] for k in range(KCH)]
    XT = [[sb("XT%d_%d" % (k, i), [128, 128], CHDT) for i in range(2)] for k in range(KCH)]
    H32 = sb("H32", [64, 32])
    Hb = [sb("Hb%d" % i, [64, 32], BF16) for i in range(2)]
    Wb = [sb("Wb%d" % i, [128, 32], BF16) for i in range(2)]
    Ub = [sb("Ub%d" % i, [128, 32], BF16) for i in range(2)]
    bT = P.psum("rw_bT", [128, 1024], BF16)
    bA = [P.psum("rw_bA%d" % k, [128, 512], F32) for k in range(KCH)]
    bC1 = [P.psum("rw_bC1_%d" % k, [128, 512], F32) for k in range(KCH)]
    bC2 = [P.psum("rw_bC2_%d" % k, [128, 512], F32) for k in range(KCH)]
    bS = P.psum("rw_bS", [128, 512], F32)

    mSU = masks[:, 0:128]; mIU = masks[:, 128:256]; mSL = masks[:, 256:384]; mI = masks[:, 384:512]
    P.dma("sp", rmask, rmask[:], rm_d, rm_d[:])
    P.dma("sp", masks, masks[:], mk_d, mk_d[:])
    P.dma("sp", ident, ident[:], id_d, id_d[:])
    P.op("dve", lambda e: e.memset(H32[:], 0.0), [], [H32])
    P.op("dve", lambda e: e.memset(Hb[0][:], 0.0), [], [Hb[0]])
    G3 = G[:, :].rearrange("p (n c) -> p n c", c=RW_C)
    x13 = x1[:, :].rearrange("p (n c) -> p n c", c=RW_C)
    x23 = x2[:, :].rearrange("p (n c) -> p n c", c=RW_C)
    NCHUNK = RW_NST * RW_NCH
    state = {"prep_done": 0, "inv_started": 0, "inv_done": set(), "state_done": 0}

    def prep_thread():
        for st in range(RW_NST):
            while state["inv_started"] < RW_NCH * st - 10:
                yield
            t0 = st * RW_ST
            b = st % 2
            for i, (s_, d_) in enumerate(((r_s, r_d), (kp_s, kp_d), (kk_s, kk_d), (a_s, a_d), (ld_s, ld_d))):
                P.dma(("sp", "act", "pool")[i % 3], s_, s_[:], d_, d_[:, t0:t0 + RW_ST])
            P.dma("sp", v_s, v_s[:], v_d, v_d[:, t0:t0 + RW_ST])
            yield
            P.op("pool", lambda e: e.tensor_copy(out=vb[b][:], in_=v_s[:]), [v_s], [vb[b]])
            P.op("dve", lambda e: e.tensor_tensor_scan(out=G[:], data0=rmask[:], data1=ld_s[:], initial=0.0, op0=ALU.mult, op1=ALU.add),
                 [rmask, ld_s], [G])
            yield
            Gl = G3[:, :, RW_C - 1:RW_C].to_broadcast([64, RW_NCH, RW_C])
            P.op("dve", lambda e: e.tensor_tensor(out=x1[:], in0=G[:], in1=ld_s[:], op=ALU.subtract), [G, ld_s], [x1])
            P.op("act", lambda e: e.activation(out=x1[:], in_=x1[:], func=AF.Exp), [x1], [x1])
            yield
            P.op("dve", lambda e: e.scalar_tensor_tensor(out=AR[b][:, :, 0, :], in0=x13, scalar=-1.0,
                                                         in1=kk_s[:, :].rearrange("p (n c) -> p n c", c=RW_C),
                                                         op0=ALU.mult, op1=ALU.mult), [x1, kk_s], [AR[b]])
            P.op("act", lambda e: e.activation(out=x2[:], in_=G[:], func=AF.Exp), [G], [x2])
            yield
            P.op("pool", lambda e: e.tensor_tensor(out=AR[b][:, :, 1, :], in0=x23, in1=r_s[:, :].rearrange("p (n c) -> p n c", c=RW_C),
                                                   op=ALU.mult), [x2, r_s], [AR[b]])
            P.op("pool", lambda e: e.tensor_tensor(out=x3[:], in0=kk_s[:], in1=a_s[:], op=ALU.mult), [kk_s, a_s], [x3])
            P.op("act", lambda e: e.activation(out=x1[:], in_=G[:], func=AF.Exp, scale=-1.0), [G], [x1])
            yield
            P.op("dve", lambda e: e.tensor_tensor(out=Bt[b][:], in0=x3[:], in1=x1[:], op=ALU.mult), [x3, x1], [Bt[b]])
            P.op("pool", lambda e: e.tensor_tensor(out=Kt[b][:], in0=kp_s[:], in1=x1[:], op=ALU.mult), [kp_s, x1], [Kt[b]])
            yield
            P.op("dve", lambda e: e.tensor_tensor(out=x23, in0=Gl, in1=G3, op=ALU.subtract), [G], [x2])
            P.op("act", lambda e: e.activation(out=x2[:], in_=x2[:], func=AF.Exp), [x2], [x2])
            yield
            P.op("dve", lambda e: e.tensor_tensor(out=Bh[b][:], in0=x3[:], in1=x2[:], op=ALU.mult), [x3, x2], [Bh[b]])
            P.op("pool", lambda e: e.tensor_tensor(out=Kh[b][:], in0=kp_s[:], in1=x2[:], op=ALU.mult), [kp_s, x2], [Kh[b]])
            P.op("act", lambda e: e.activation(out=dC[b][:], in_=G3[:, :, RW_C - 1], func=AF.Exp), [G], [dC[b]])
            state["prep_done"] = st + 1
            yield

    def inv_thread(k):
        for gn in range(k, NCHUNK, KCH):
            st, n = gn // RW_NCH, gn % RW_NCH
            while state["prep_done"] <= st or state["state_done"] < gn - (NSET - 1):
                yield
            state["inv_started"] = max(state["inv_started"], gn)
            b = st % 2
            c0 = n * RW_C
            s_ = gn % NSET
            tk, xtb, arb, ak = tok[s_], XTb[s_], Arb[s_], Ak[s_]
            pT = bT; BA = bA[k]; B1 = bC1[k]; B2 = bC2[k]
            P.op("pe", lambda e: e.transpose(out=pT[:, k * 160:k * 160 + 64], in_=Bh[b][:, c0:c0 + RW_C], identity=ident[0:64, 0:64]), [Bh[b], ident], [pT])
            P.op("pe", lambda e: e.transpose(out=pT[:, k * 160 + 64:k * 160 + 128], in_=Kh[b][:, c0:c0 + RW_C], identity=ident[0:64, 0:64]), [Kh[b], ident], [pT])
            P.op("pe", lambda e: e.transpose(out=pT[:, k * 160 + 128:k * 160 + 160], in_=vb[b][:, c0:c0 + RW_C], identity=ident[0:32, 0:32]), [vb[b], ident], [pT])
            arv = AR[b][:, n, :, :].rearrange("p a c -> p (a c)")
            P.op("pe", lambda e: e.matmul(BA[:, 0:256], lhsT=Bt[b][:, c0:c0 + RW_C], rhs=arv, start=True, stop=True), [Bt[b], AR[b]], [BA])
            P.op("pe", lambda e: e.matmul(BA[:, 256:512], lhsT=Kt[b][:, c0:c0 + RW_C], rhs=arv, start=True, stop=True), [Kt[b], AR[b]], [BA])
            P.op("pe", lambda e: e.matmul(B2[:, 128:256], lhsT=AR[b][:, n, 0, :], rhs=Bt[b][:, c0:c0 + RW_C], start=True, stop=True), [AR[b], Bt[b]], [B2])
            yield
            mf, nf, xt = Mf[k][0], Nf[k][0], XT[k][0]
            P.op("act", lambda e: e.activation(out=tk[:], in_=pT[:, k * 160:(k + 1) * 160], func=AF.Copy), [pT], [tk])
            P.op("dve", lambda e: e.tensor_tensor(out=mf[:], in0=BA[:, 0:128], in1=mSU, op=ALU.mult), [BA, masks], [mf])
            P.op("dve", lambda e: e.tensor_tensor(out=nf[:], in0=B2[:, 128:256], in1=mSL, op=ALU.mult), [B2, masks], [nf])
            P.op("pool", lambda e: e.tensor_tensor(out=xt[:], in0=mf[:], in1=mI, op=ALU.add), [mf, masks], [xt])
            yield
            P.op("dve", lambda e: e.tensor_tensor(out=arb[:], in0=BA[:, 128:256], in1=mIU, op=ALU.mult), [BA, masks], [arb])
            P.op("dve", lambda e: e.tensor_tensor(out=ak[:], in0=BA[:, 256:512], in1=masks[:, 0:256], op=ALU.mult), [BA, masks], [ak])
            cur = 0
            for it in range(6):
                mo, no = Mf[k][cur], Nf[k][cur]
                mn, nn = Mf[k][1 - cur], Nf[k][1 - cur]
                xo, xn = XT[k][cur], XT[k][1 - cur]
                P.op("pe", lambda e: e.matmul(B1[:, 0:128], lhsT=mo[:], rhs=no[:], start=True, stop=True), [mo, no], [B1])
                if it < 5:
                    P.op("pe", lambda e: e.matmul(B2[:, 128:256], lhsT=no[:], rhs=mo[:], start=True, stop=True), [mo, no], [B2])
                yield
                P.op("act", lambda e: e.activation(out=nn[:], in_=B1[:, 0:128], func=AF.Copy), [B1], [nn])
                if it < 5:
                    P.op("dve", lambda e: e.tensor_copy(out=mn[:], in_=B2[:, 128:256]), [B2], [mn])
                yield
                P.op("pe", lambda e: e.matmul(BA[:, 0:128], lhsT=nn[:], rhs=xo[:], start=True, stop=True), [nn, xo], [BA])
                P.op("dve", lambda e: e.tensor_tensor(out=xn[:], in0=BA[:, 0:128], in1=xo[:], op=ALU.add), [BA, xo], [xn])
                cur = 1 - cur
            P.op("pool", lambda e: e.tensor_copy(out=xtb[:], in_=XT[k][cur][:]), [XT[k][cur]], [xtb])
            state["inv_done"].add(gn)
            yield

    def state_thread():
        hbi = 0
        for gn in range(NCHUNK):
            while gn not in state["inv_done"]:
                yield
            st, n = gn // RW_NCH, gn % RW_NCH
            b = st % 2
            c0 = n * RW_C
            s_ = gn % NSET
            tk, xtb, arb, ak = tok[s_], XTb[s_], Arb[s_], Ak[s_]
            hb_cur = Hb[hbi % 2]; hb_nxt = Hb[(hbi + 1) % 2]; wb = Wb[hbi % 2]; ub = Ub[hbi % 2]
            hbi += 1
            P.op("pe", lambda e: e.matmul(bS[:, 0:32], lhsT=AR[b][:, n, 0, :], rhs=hb_cur[:], start=True, stop=False), [AR[b], hb_cur], [bS])
            P.op("pe", lambda e: e.matmul(bS[:, 0:32], lhsT=ak[:, 0:128], rhs=tk[:, 128:160], start=False, stop=True), [ak, tk], [bS])
            P.op("act", lambda e: e.activation(out=wb[:], in_=bS[:, 0:32], func=AF.Copy), [bS], [wb])
            yield
            P.op("pe", lambda e: e.matmul(bS[:, 32:64], lhsT=xtb[:], rhs=wb[:], start=True, stop=True), [xtb, wb], [bS])
            P.op("act", lambda e: e.activation(out=ub[:], in_=bS[:, 32:64], func=AF.Copy), [bS], [ub])
            yield
            P.op("pe", lambda e: e.matmul(bS[0:64, 64:96], lhsT=tk[:, 0:64], rhs=ub[:], start=True, stop=False), [tk, ub], [bS])
            P.op("pe", lambda e: e.matmul(bS[0:64, 64:96], lhsT=tk[:, 64:128], rhs=tk[:, 128:160], start=False, stop=True), [tk], [bS])
            P.op("pe", lambda e: e.matmul(bS[0:32, 128:256], lhsT=hb_cur[:], rhs=AR[b][:, n, 1, :], start=True, stop=False), [hb_cur, AR[b]], [bS])
            P.op("pe", lambda e: e.matmul(bS[0:32, 128:256], lhsT=ub[:], rhs=arb[:], start=False, stop=False), [ub, arb], [bS])
            P.op("pe", lambda e: e.matmul(bS[0:32, 128:256], lhsT=tk[:, 128:160], rhs=ak[:, 128:256], start=False, stop=True), [tk, ak], [bS])
            P.op("dve", lambda e: e.scalar_tensor_tensor(out=H32[:], in0=H32[:], scalar=dC[b][:, n:n + 1], in1=bS[0:64, 64:96],
                                                         op0=ALU.mult, op1=ALU.add), [H32, dC[b], bS], [H32])
            P.op("pool", lambda e: e.tensor_copy(out=hb_nxt[:], in_=H32[:]), [H32], [hb_nxt])
            P.op("act", lambda e: e.activation(out=yst[b][:, c0:c0 + RW_C], in_=bS[0:32, 128:256], func=AF.Copy), [bS], [yst[b]])
            state["state_done"] = gn + 1
            if n == RW_NCH - 1:
                P.dma("sp", y_d, y_d[:, st * RW_ST:(st + 1) * RW_ST], yst[b], yst[b][:], owner=yst[b], disjoint=True)
            yield

    threads = [prep_thread()] + [inv_thread(k) for k in range(KCH)] + [state_thread()]
    guard = 0
    while threads:
        for g in list(threads):
            try:
                next(g)
            except StopIteration:
                threads.remove(g)
        guard += 1
        assert guard < 200000, "scheduler stuck"
    return [y_d]


def rw_consts():
    t = np.arange(RW_ST)
    rm = np.tile(((t % RW_C) != 0).astype(np.float32)[None, :], (64, 1))
    j = np.arange(128)[:, None]; i = np.arange(128)[None, :]
    su = (i > j).astype(np.float32); iu = (i >= j).astype(np.float32); sl = (j > i).astype(np.float32)
    masks = np.concatenate([su, iu, sl, np.eye(128, dtype=np.float32)], axis=1)
    return {"rw_rmask": rm, "rw_masks": masks, "rw_ident": np.eye(128).astype(ml_dtypes.bfloat16)}


def rw_inputs(core, fmr):
    hd, vh = core // 2, core % 2
    m = {"rw_" + n: np.ascontiguousarray(fmr[n][hd * 64:(hd + 1) * 64]) for n in ("r", "kp", "kk", "a", "ld")}
    m["rw_v"] = np.ascontiguousarray(fmr["v"][hd * 64 + vh * 32: hd * 64 + (vh + 1) * 32])
    m.update(rw_consts())
    return m


C1_NT = 2048
C1_TG = 512
C1_NG = C1_NT // C1_TG
CF_HGO, CF_GS, CF_SWO, CF_RWY, CF_R, CF_KP, CF_V, CF_G = 0, 512, 1024, 1280, 1536, 1792, 2048, 2304
CF_ROWS = 2560
CP_GN, CP_LNW, CP_LNB, CP_RK, CP_N = 0, 4, 6, 8, 10


def emit_c1(P, layer):
    h_d = P.dram("c1_h", [C1_NT, 1024], F32, "ExternalInput")
    cf_d = P.dram("c1_cf", [CF_ROWS, C1_NT], F32, "ExternalInput")
    wo_d = P.dram("c1_wout", [1024, 1024], F32, "ExternalInput")
    cp_d = P.dram("c1_cp", [128, CP_N], F32, "ExternalInput")
    k_d = P.dram("c1_consts", [128, 256], F32, "ExternalInput")
    hm_d = P.dram("c1_hmid", [C1_NT, 1024], F32, "ExternalOutput")

    def sb(name, shape, dt=F32):
        return P.sbuf("c1s_" + name, shape, dt)
    wob = sb("wob", [128, 8, 1024], BF16)
    wst = [sb("wst%d" % i, [128, 1024]) for i in range(2)]
    cp = sb("cp", [128, CP_N])
    kc = sb("kc", [128, 256])
    eps1 = sb("eps1", [128, 1]); eps2 = sb("eps2", [128, 1])
    oT = [sb("oT%d" % i, [128, 8, C1_TG], BF16) for i in range(2)]
    fin = [sb("fin%d" % i, [128, C1_TG]) for i in range(6)]
    tmp = [sb("tmp%d" % i, [128, C1_TG]) for i in range(6)]
    hin = [sb("hin%d" % i, [128, 1024]) for i in range(2)]
    ps = [P.psum("c1_ps%d" % i, [128, 512], F32) for i in range(4)]
    pd = [P.psum("c1_pd%d" % i, [128, 1024], F32) for i in range(2)]

    ones_m = kc[:, 0:128]; blk_m = kc[:, 128:256]
    P.dma("sp", cp, cp[:], cp_d, cp_d[:])
    P.dma("sp", kc, kc[:], k_d, k_d[:])
    P.op("dve", lambda e: e.memset(eps1[:], 1e-6), [], [eps1])
    P.op("dve", lambda e: e.memset(eps2[:], 64e-5), [], [eps2])
    for k in range(8):
        s_ = wst[k % 2]
        P.dma("sp" if k % 2 == 0 else "act", s_, s_[:], wo_d, wo_d[k * 128:(k + 1) * 128, :])
        P.op("pool", lambda e: e.tensor_copy(out=wob[:, k, :], in_=s_[:]), [s_], [wob])

    fi = [0]; ti = [0]; pi = [0]; qi = [0]

    def load(row0, t0):
        f = fin[fi[0] % 6]; fi[0] += 1
        q = ("sp", "act", "pool")[qi[0] % 3]; qi[0] += 1
        P.dma(q, f, f[:], cf_d, cf_d[row0:row0 + 128, t0:t0 + C1_TG])
        return f

    def T_():
        t = tmp[ti[0] % 6]; ti[0] += 1
        return t

    def PS():
        p = ps[pi[0] % 4]; pi[0] += 1
        return p

    for g in range(C1_NG):
        t0 = g * C1_TG
        ot = oT[g % 2]
        for c in range(4):
            o = load(CF_HGO + c * 128, t0); gs = load(CF_GS + c * 128, t0)
            sq = T_(); p_ = PS(); rs = T_()
            P.op("pool", lambda e: e.tensor_tensor(out=sq[:], in0=o[:], in1=o[:], op=ALU.mult), [o], [sq])
            P.op("pe", lambda e: e.matmul(p_[:], lhsT=ones_m, rhs=sq[:], start=True, stop=True), [kc, sq], [p_])
            P.op("act", lambda e: e.activation(out=rs[:], in_=p_[:], func=AF.Sqrt, bias=eps1[:, 0:1]), [p_, eps1], [rs])
            P.op("dve", lambda e: e.reciprocal(out=rs[:], in_=rs[:]), [rs], [rs])
            P.op("dve", lambda e: e.scalar_tensor_tensor(out=rs[:], in0=rs[:], scalar=cp[:, CP_GN + c:CP_GN + c + 1], in1=o[:],
                                                         op0=ALU.mult, op1=ALU.mult), [rs, cp, o], [rs])
            P.op("pool", lambda e: e.tensor_tensor(out=ot[:, c, :], in0=rs[:], in1=gs[:], op=ALU.mult), [rs, gs], [ot])
        for c in range(2):
            o = load(CF_SWO + c * 128, t0)
            P.op("pool", lambda e: e.tensor_copy(out=ot[:, 4 + c, :], in_=o[:]), [o], [ot])
        for c in range(2):
            y = load(CF_RWY + c * 128, t0); r_ = load(CF_R + c * 128, t0); kp = load(CF_KP + c * 128, t0)
            v_ = load(CF_V + c * 128, t0); g_ = load(CF_G + c * 128, t0)
            pm = PS(); pq = PS(); pb = PS()
            ysq = T_(); mean = T_(); var = T_(); rk = T_()
            P.op("pe", lambda e: e.matmul(pm[:], lhsT=blk_m, rhs=y[:], start=True, stop=True), [kc, y], [pm])
            P.op("pool", lambda e: e.tensor_tensor(out=ysq[:], in0=y[:], in1=y[:], op=ALU.mult), [y], [ysq])
            P.op("pe", lambda e: e.matmul(pq[:], lhsT=blk_m, rhs=ysq[:], start=True, stop=True), [kc, ysq], [pq])
            P.op("act", lambda e: e.activation(out=mean[:], in_=pm[:], func=AF.Copy), [pm], [mean])
            P.op("pool", lambda e: e.tensor_tensor(out=var[:], in0=mean[:], in1=mean[:], op=ALU.mult), [mean], [var])
            P.op("dve", lambda e: e.tensor_tensor(out=var[:], in0=pq[:], in1=var[:], op=ALU.subtract), [pq, var], [var])
            P.op("act", lambda e: e.activation(out=var[:], in_=var[:], func=AF.Sqrt, bias=eps2[:, 0:1]), [var, eps2], [var])
            P.op("dve", lambda e: e.reciprocal(out=var[:], in_=var[:]), [var], [var])
            P.op("dve", lambda e: e.tensor_tensor(out=mean[:], in0=y[:], in1=mean[:], op=ALU.subtract), [y, mean], [mean])
            P.op("dve", lambda e: e.tensor_tensor(out=mean[:], in0=mean[:], in1=var[:], op=ALU.mult), [mean, var], [mean])
            P.op("dve", lambda e: e.tensor_scalar(out=mean[:], in0=mean[:], scalar1=cp[:, CP_LNW + c:CP_LNW + c + 1],
                                                   scalar2=cp[:, CP_LNB + c:CP_LNB + c + 1], op0=ALU.mult, op1=ALU.add), [mean, cp], [mean])
            P.op("dve", lambda e: e.scalar_tensor_tensor(out=rk[:], in0=r_[:], scalar=cp[:, CP_RK + c:CP_RK + c + 1], in1=kp[:],
                                                         op0=ALU.mult, op1=ALU.mult), [r_, cp, kp], [rk])
            P.op("pe", lambda e: e.matmul(pb[:], lhsT=blk_m, rhs=rk[:], start=True, stop=True), [kc, rk], [pb])
            P.op("dve", lambda e: e.scalar_tensor_tensor(out=rk[:], in0=pb[:], scalar=64.0, in1=v_[:], op0=ALU.mult, op1=ALU.mult),
                 [pb, v_], [rk])
            P.op("pool", lambda e: e.tensor_tensor(out=mean[:], in0=mean[:], in1=rk[:], op=ALU.add), [mean, rk], [mean])
            P.op("pool", lambda e: e.tensor_tensor(out=ot[:, 6 + c, :], in0=mean[:], in1=g_[:], op=ALU.mult), [mean, g_], [ot])
        for t in range(4):
            hi = hin[t % 2]; p_d = pd[t % 2]
            r0 = t0 + t * 128
            P.dma("sp", hi, hi[:], h_d, h_d[r0:r0 + 128, :])
            for half in range(2):
                for c in range(8):
                    P.op("pe", lambda e: e.matmul(p_d[:, half * 512:(half + 1) * 512], lhsT=ot[:, c, t * 128:(t + 1) * 128],
                                                  rhs=wob[:, c, half * 512:(half + 1) * 512], start=(c == 0), stop=(c == 7)),
                         [ot, wob], [p_d])
            for half in range(2):
                P.op("dve", lambda e: e.tensor_tensor(out=hi[:, half * 512:(half + 1) * 512], in0=hi[:, half * 512:(half + 1) * 512],
                                                      in1=p_d[:, half * 512:(half + 1) * 512], op=ALU.add), [hi, p_d], [hi])
            P.dma("act", hm_d, hm_d[r0:r0 + 128, :], hi, hi[:], owner=hi, disjoint=True)
    return [hm_d]


def c1_inputs(layer, inp, core, h_full, cf_full):
    l = layer
    fmj = lambda v: np.ascontiguousarray(v.reshape(-1, 128).T)
    cp = np.zeros((128, CP_N), np.float32)
    cp[:, CP_GN:CP_GN + 4] = fmj(inp['hgrn_gnorm_g'][l])
    cp[:, CP_LNW:CP_LNW + 2] = fmj(inp['rwkv_ln_w'][l])
    cp[:, CP_LNB:CP_LNB + 2] = fmj(inp['rwkv_ln_b'][l])
    cp[:, CP_RK:CP_RK + 2] = fmj(inp['rwkv_r_k'][l].reshape(-1))
    kc = np.concatenate([np.full((128, 128), 1.0 / 128), np.kron(np.eye(2), np.full((64, 64), 1.0 / 64))], axis=1).astype(np.float32)
    return {"c1_h": np.ascontiguousarray(h_full[core * C1_NT:(core + 1) * C1_NT]),
            "c1_cf": np.ascontiguousarray(cf_full[:, core * C1_NT:(core + 1) * C1_NT]),
            "c1_wout": np.ascontiguousarray(inp['w_out'][l]), "c1_cp": cp, "c1_consts": kc}


C2_NT = 2048
C2_GT = 256
C2_NGR = C2_NT // C2_GT
C2_DFF = 2816
C2_NJ = C2_DFF // 128


def norm_transpose(P, src_dram_t, src_ap, hres, hnb, st, junk, epsc, ident, pst, dstT, col0, dma_q="sp"):
    P.dma(dma_q, hres, hres[:], src_dram_t, src_ap)
    P.op("act", lambda e: e.activation(out=junk[:], in_=hres[:], func=AF.Square, accum_out=st[:, 0:1]), [hres], [junk, st])
    P.op("act", lambda e: e.activation(out=st[:, 1:2], in_=st[:, 0:1], func=AF.Sqrt, scale=1.0 / 1024, bias=epsc[:, 0:1]), [st, epsc], [st])
    P.op("dve", lambda e: e.reciprocal(out=st[:, 2:3], in_=st[:, 1:2]), [st], [st])
    P.op("dve", lambda e: e.tensor_scalar(out=hnb[:], in0=hres[:], scalar1=st[:, 2:3], scalar2=None, op0=ALU.mult), [hres, st], [hnb])
    for c in range(8):
        P.op("pe", lambda e: e.transpose(out=pst[:, c * 128:(c + 1) * 128], in_=hnb[:, c * 128:(c + 1) * 128], identity=ident[:]),
             [hnb, ident], [pst])
    P.op("act", lambda e: e.activation(out=dstT[:, :, col0:col0 + 128], in_=pst[:].rearrange("p (c t) -> p c t", c=8), func=AF.Copy),
         [pst], [dstT])


def emit_c2(P):
    h_d = P.dram("c2_h", [C2_NT, 1024], F32, "ExternalInput")
    hh_d = P.dram("c2_hh", [128, 1024], F32, "ExternalInput")
    up_d = P.dram("c2_up", [1024, 2 * C2_DFF], F32, "ExternalInput")
    dn_d = P.dram("c2_dn", [C2_DFF, 1024], F32, "ExternalInput")
    pp_d = P.dram("c2_pp", [128, 8 + 44 * 4], F32, "ExternalInput")
    id_d = P.dram("c2_ident", [128, 128], BF16, "ExternalInput")
    o_d = P.dram("c2_out", [C2_NT, 1024], F32, "ExternalOutput")

    def sb(name, shape, dt=F32):
        return P.sbuf("c2s_" + name, shape, dt)
    upb = [sb("upb%d" % k, [128, 2 * C2_DFF], BF16) for k in range(8)]
    dnb = sb("dnb", [128, C2_NJ, 1024], BF16)
    pp = sb("pp", [128, 8 + 44 * 4])
    ident = sb("ident", [128, 128], BF16)
    epsc = sb("epsc", [128, 1])
    hres = [sb("hres%d" % i, [128, 1024]) for i in range(2)]
    hnb = [sb("hnb%d" % i, [128, 1024], BF16) for i in range(2)]
    st = [sb("st%d" % i, [128, 4]) for i in range(2)]
    junk = sb("junk", [128, 1024])
    hnT = sb("hnT", [128, 8, C2_GT], BF16)
    ug = [sb("ug%d" % i, [128, C2_GT + 2]) for i in range(2)]
    uv = [sb("uv%d" % i, [128, C2_GT + 2]) for i in range(2)]
    tg = [sb("tg%d" % i, [128, C2_GT]) for i in range(2)]
    tv = [sb("tv%d" % i, [128, C2_GT]) for i in range(2)]
    actT = sb("actT", [128, C2_NJ, C2_GT], BF16)
    uprev = sb("uprev", [128, 44, 2])
    pst = P.psum("c2_pst", [128, 1024], BF16)
    pu = [P.psum("c2_pu%d" % i, [128, 512], F32) for i in range(3)]
    pd = [P.psum("c2_pd%d" % i, [128, 512], F32) for i in range(4)]

    P.dma("sp", pp, pp[:], pp_d, pp_d[:])
    P.dma("sp", ident, ident[:], id_d, id_d[:])
    P.op("dve", lambda e: e.memset(epsc[:], 1e-6), [], [epsc])
    wi = 0
    for k in range(8):
        for c0 in range(0, 2 * C2_DFF, 1024):
            w_ = min(1024, 2 * C2_DFF - c0)
            s_ = hres[wi % 2]; wi += 1
            P.dma("sp" if wi % 2 == 0 else "act", s_, s_[:, 0:w_], up_d, up_d[k * 128:(k + 1) * 128, c0:c0 + w_])
            P.op("dve", lambda e: e.tensor_scalar(out=upb[k][:, c0:c0 + w_], in0=s_[:, 0:w_], scalar1=pp[:, k:k + 1], scalar2=None,
                                                  op0=ALU.mult), [s_, pp], [upb[k]])
    for j in range(C2_NJ):
        s_ = hres[wi % 2]; wi += 1
        P.dma("sp" if wi % 2 == 0 else "act", s_, s_[:], dn_d, dn_d[j * 128:(j + 1) * 128, :])
        P.op("pool", lambda e: e.tensor_copy(out=dnb[:, j, :], in_=s_[:]), [s_], [dnb])

    def cw(c, i):
        o = 8 + c * 4 + i
        return pp[:, o:o + 1]

    norm_transpose(P, hh_d, hh_d[:], hres[0], hnb[0], st[0], junk, epsc, ident, pst, hnT, 0)
    for c in range(44):
        p_ = pu[c % 3]
        for k in range(8):
            P.op("pe", lambda e: e.matmul(p_[:, 0:2], lhsT=upb[k][:, c * 128:(c + 1) * 128], rhs=hnT[:, k, 126:128],
                                          start=(k == 0), stop=(k == 7)), [upb[k], hnT], [p_])
        P.op("act", lambda e: e.activation(out=uprev[:, c, :], in_=p_[:, 0:2], func=AF.Copy), [p_], [uprev])

    ui = 0
    for g in range(C2_NGR):
        r0 = g * C2_GT
        for t in range(2):
            norm_transpose(P, h_d, h_d[r0 + t * 128:r0 + (t + 1) * 128, :], hres[t], hnb[t], st[t], junk, epsc, ident, pst, hnT, t * 128)
        for j in range(C2_NJ):
            cg, cv = j, C2_NJ + j
            p_ = pu[ui % 3]; u_g = ug[ui % 2]; u_v = uv[ui % 2]; t_g = tg[ui % 2]; t_v = tv[ui % 2]
            ui += 1
            for k in range(8):
                P.op("pe", lambda e: e.matmul(p_[:, 0:C2_GT], lhsT=upb[k][:, cg * 128:(cg + 1) * 128], rhs=hnT[:, k, :],
                                              start=(k == 0), stop=(k == 7)), [upb[k], hnT], [p_])
            for k in range(8):
                P.op("pe", lambda e: e.matmul(p_[:, C2_GT:2 * C2_GT], lhsT=upb[k][:, cv * 128:(cv + 1) * 128], rhs=hnT[:, k, :],
                                              start=(k == 0), stop=(k == 7)), [upb[k], hnT], [p_])
            P.op("pool", lambda e: e.tensor_copy(out=u_g[:, 0:2], in_=uprev[:, cg, :]), [uprev], [u_g])
            P.op("pool", lambda e: e.tensor_copy(out=u_v[:, 0:2], in_=uprev[:, cv, :]), [uprev], [u_v])
            P.op("act", lambda e: e.activation(out=u_g[:, 2:C2_GT + 2], in_=p_[:, 0:C2_GT], func=AF.Copy), [p_], [u_g])
            P.op("act", lambda e: e.activation(out=u_v[:, 2:C2_GT + 2], in_=p_[:, C2_GT:2 * C2_GT], func=AF.Copy), [p_], [u_v])
            P.op("pool", lambda e: e.tensor_copy(out=uprev[:, cg, :], in_=u_g[:, C2_GT:C2_GT + 2]), [u_g], [uprev])
            P.op("pool", lambda e: e.tensor_copy(out=uprev[:, cv, :], in_=u_v[:, C2_GT:C2_GT + 2]), [u_v], [uprev])
            for (u_, t_, c_) in ((u_g, t_g, cg), (u_v, t_v, cv)):
                P.op("dve", lambda e: e.tensor_scalar(out=t_[:], in0=u_[:, 2:C2_GT + 2], scalar1=cw(c_, 2), scalar2=cw(c_, 3),
                                                      op0=ALU.mult, op1=ALU.add), [u_, pp], [t_])
                P.op("dve", lambda e: e.scalar_tensor_tensor(out=t_[:], in0=u_[:, 1:C2_GT + 1], scalar=cw(c_, 1), in1=t_[:],
                                                             op0=ALU.mult, op1=ALU.add), [u_, pp, t_], [t_])
                P.op("dve", lambda e: e.scalar_tensor_tensor(out=t_[:], in0=u_[:, 0:C2_GT], scalar=cw(c_, 0), in1=t_[:],
                                                             op0=ALU.mult, op1=ALU.add), [u_, pp, t_], [t_])
            P.op("act", lambda e: e.activation(out=t_g[:], in_=t_g[:], func=AF.Silu), [t_g], [t_g])
            P.op("pool", lambda e: e.tensor_tensor(out=actT[:, j, :], in0=t_g[:], in1=t_v[:], op=ALU.mult), [t_g, t_v], [actT])
        for t in range(2):
            for half in range(2):
                p_d = pd[(2 * t + half) % 4]
                for j in range(C2_NJ):
                    P.op("pe", lambda e: e.matmul(p_d[:], lhsT=actT[:, j, t * 128:(t + 1) * 128], rhs=dnb[:, j, half * 512:(half + 1) * 512],
                                                  start=(j == 0), stop=(j == C2_NJ - 1)), [actT, dnb], [p_d])
                P.op("dve", lambda e: e.tensor_tensor(out=hres[t][:, half * 512:(half + 1) * 512], in0=hres[t][:, half * 512:(half + 1) * 512],
                                                      in1=p_d[:], op=ALU.add), [hres[t], p_d], [hres[t]])
            P.dma("act", o_d, o_d[r0 + t * 128:r0 + (t + 1) * 128, :], hres[t], hres[t][:], owner=hres[t], disjoint=True)
    return [o_d]


def c2_inputs(layer, inp, core, hmid_full):
    l = layer
    fmj = lambda v: np.ascontiguousarray(v.reshape(-1, 128).T)
    pp = np.zeros((128, 8 + 44 * 4), np.float32)
    pp[:, 0:8] = fmj(inp['norm_ffn_g'][l])
    cwb = np.concatenate([inp['ffn_conv_w'][l], inp['ffn_conv_b'][l][None]], axis=0)
    pp[:, 8:] = cwb.reshape(4, 44, 128).transpose(2, 1, 0).reshape(128, 176)
    return {"c2_h": np.ascontiguousarray(hmid_full[core * C2_NT:(core + 1) * C2_NT]),
            "c2_hh": np.ascontiguousarray(hmid_full[core * C2_NT - 128:core * C2_NT]) if core > 0 else np.zeros((128, 1024), np.float32),
            "c2_up": np.ascontiguousarray(inp['ffn_up'][l]), "c2_dn": np.ascontiguousarray(inp['ffn_down'][l]),
            "c2_pp": pp, "c2_ident": np.eye(128).astype(ml_dtypes.bfloat16)}


C3_NT = 2048


def emit_c3(P, final):
    h_d = P.dram("c3_h", [C3_NT, 1024], F32, "ExternalInput")
    pT_d = P.dram("c3_pT", [256, C3_NT], F32, "ExternalInput")
    gt_d = P.dram("c3_gate", [1024, 1024], F32, "ExternalInput")
    pj_d = P.dram("c3_proj", [256, 1024], F32, "ExternalInput")
    pp_d = P.dram("c3_pp", [128, 8], F32, "ExternalInput")
    id_d = P.dram("c3_ident", [128, 128], BF16, "ExternalInput")
    if final:
        gf_d = P.dram("c3_gfin", [128, 1024], F32, "ExternalInput")
    o_d = P.dram("c3_out", [C3_NT, 1024], F32, "ExternalOutput")

    def sb(name, shape, dt=F32):
        return P.sbuf("c3s_" + name, shape, dt)
    gtb = sb("gtb", [128, 8, 1024], BF16)
    pjb = sb("pjb", [128, 2, 1024], BF16)
    wst = [sb("wst%d" % i, [128, 1024]) for i in range(2)]
    pp = sb("pp", [128, 8])
    ident = sb("ident", [128, 128], BF16)
    epsc = sb("epsc", [128, 1])
    p32 = sb("p32", [128, 2, C3_NT])
    pTb = sb("pTb", [128, 2, C3_NT], BF16)
    hres = [sb("hres%d" % i, [128, 1024]) for i in range(2)]
    hnb = [sb("hnb%d" % i, [128, 1024], BF16) for i in range(2)]
    st = [sb("st%d" % i, [128, 4]) for i in range(2)]
    st2 = [sb("st2%d" % i, [128, 4]) for i in range(2)]
    junk = sb("junk", [128, 1024])
    hnT = [sb("hnT%d" % i, [128, 8, 128], BF16) for i in range(2)]
    sig = [sb("sig%d" % i, [128, 1024]) for i in range(2)]
    gfin = sb("gfin", [128, 1024]) if final else None
    pst = P.psum("c3_pst", [128, 1024], BF16)
    pg = [P.psum("c3_pg%d" % i, [128, 512], F32) for i in range(4)]
    pq = [P.psum("c3_pq%d" % i, [128, 512], F32) for i in range(2)]

    P.dma("sp", pp, pp[:], pp_d, pp_d[:])
    P.dma("sp", ident, ident[:], id_d, id_d[:])
    if final:
        P.dma("sp", gfin, gfin[:], gf_d, gf_d[:])
    P.op("dve", lambda e: e.memset(epsc[:], 1e-6), [], [epsc])
    for k in range(8):
        s_ = wst[k % 2]
        P.dma("sp" if k % 2 == 0 else "act", s_, s_[:], gt_d, gt_d[k * 128:(k + 1) * 128, :])
        P.op("dve", lambda e: e.tensor_scalar(out=gtb[:, k, :], in0=s_[:], scalar1=pp[:, k:k + 1], scalar2=None, op0=ALU.mult),
             [s_, pp], [gtb])
    for c in range(2):
        s_ = wst[c % 2]
        P.dma("sp" if c % 2 == 0 else "act", s_, s_[:], pj_d, pj_d[c * 128:(c + 1) * 128, :])
        P.op("pool", lambda e: e.tensor_copy(out=pjb[:, c, :], in_=s_[:]), [s_], [pjb])
    for c in range(2):
        P.dma("pool", p32, p32[:, c, :], pT_d, pT_d[c * 128:(c + 1) * 128, :], disjoint=(c > 0))
    P.op("pool", lambda e: e.tensor_copy(out=pTb[:], in_=p32[:]), [p32], [pTb])

    for t in range(C3_NT // 128):
        i = t % 2
        r0 = t * 128
        hr = hres[i]; hT = hnT[i]; sg = sig[i]
        norm_transpose(P, h_d, h_d[r0:r0 + 128, :], hr, hnb[i], st[i], junk, epsc, ident, pst, hT, 0)
        for half in range(2):
            p_g = pg[(2 * t + half) % 4]; p_q = pq[half]
            for k in range(8):
                P.op("pe", lambda e: e.matmul(p_g[:], lhsT=hT[:, k, :], rhs=gtb[:, k, half * 512:(half + 1) * 512],
                                              start=(k == 0), stop=(k == 7)), [hT, gtb], [p_g])
            for c in range(2):
                P.op("pe", lambda e: e.matmul(p_q[:], lhsT=pTb[:, c, r0:r0 + 128], rhs=pjb[:, c, half * 512:(half + 1) * 512],
                                              start=(c == 0), stop=(c == 1)), [pTb, pjb], [p_q])
            hs = slice(half * 512, (half + 1) * 512)
            P.op("act", lambda e: e.activation(out=sg[:, hs], in_=p_g[:], func=AF.Sigmoid), [p_g], [sg])
            P.op("dve", lambda e: e.tensor_tensor(out=sg[:, hs], in0=sg[:, hs], in1=p_q[:], op=ALU.mult), [sg, p_q], [sg])
            P.op("pool", lambda e: e.tensor_tensor(out=hr[:, hs], in0=hr[:, hs], in1=sg[:, hs], op=ALU.add), [hr, sg], [hr])
        if final:
            s2 = st2[i]
            P.op("act", lambda e: e.activation(out=junk[:], in_=hr[:], func=AF.Square, accum_out=s2[:, 0:1]), [hr], [junk, s2])
            P.op("act", lambda e: e.activation(out=s2[:, 1:2], in_=s2[:, 0:1], func=AF.Sqrt, scale=1.0 / 1024, bias=epsc[:, 0:1]),
                 [s2, epsc], [s2])
            P.op("dve", lambda e: e.reciprocal(out=s2[:, 2:3], in_=s2[:, 1:2]), [s2], [s2])
            P.op("dve", lambda e: e.scalar_tensor_tensor(out=hr[:], in0=hr[:], scalar=s2[:, 2:3], in1=gfin[:], op0=ALU.mult, op1=ALU.mult),
                 [hr, s2, gfin], [hr])
        P.dma("act", o_d, o_d[r0:r0 + 128, :], hr, hr[:], owner=hr, disjoint=True)
    return [o_d]


def c3_inputs(layer, inp, core, hffn_full, final):
    l = layer
    fmj = lambda v: np.ascontiguousarray(v.reshape(-1, 128).T)
    m = {"c3_h": np.ascontiguousarray(hffn_full[core * C3_NT:(core + 1) * C3_NT]),
         "c3_pT": np.ascontiguousarray(inp['p'][l, 0, core * C3_NT:(core + 1) * C3_NT, :].T),
         "c3_gate": np.ascontiguousarray(inp['ple_gate'][l]), "c3_proj": np.ascontiguousarray(inp['ple_proj'][l]),
         "c3_pp": fmj(inp['norm_ple_g'][l]), "c3_ident": np.eye(128).astype(ml_dtypes.bfloat16)}
    if final:
        m["c3_gfin"] = np.ascontiguousarray(np.broadcast_to(inp['final_norm_g'][None, :], (128, 1024))).astype(np.float32)
    return m


def _launch(build, maps):
    nc = bass.Bass("TRN2", target_bir_lowering=False)
    P = Prog(nc)
    outs = build(P)
    P.final_wait("sp", outs)
    P.emit()
    res = run_bass_kernel_spmd(nc, maps, core_ids=list(range(8)))
    return res.results


def kernel(**inputs):
    inp = {k: np.asarray(v) for k, v in inputs.items()}
    S = 16384
    h = np.ascontiguousarray(inp['x'][0], dtype=np.float32)
    vfirst = None
    for l in range(2):
        nc, P = build_A(l)
        res = run_bass_kernel_spmd(nc, host_inputs_A(l, inp, h, vfirst), core_ids=list(range(8))).results
        fm = np.concatenate([r["fm"] for r in res], axis=1)
        tm = np.concatenate([r["tm"] for r in res], axis=0)
        del res
        if l == 0:
            vfirst = np.ascontiguousarray(fm[FM_V:FM_V + 256])
        res = _launch(emit_sw, [sw_inputs(c, fm[FM_BQ:FM_BQ + 256], fm[FM_BK:FM_BK + 256], tm[:, 512:768]) for c in range(8)])
        cf = np.empty((CF_ROWS, S), np.float32)
        for c in range(8):
            hd, s = c // 2, c % 2
            cf[CF_SWO + hd * 64:CF_SWO + (hd + 1) * 64, s * NOWN:(s + 1) * NOWN] = res[c]["sw_o"]
        res = _launch(emit_hgrn, [hg_inputs(c, fm[FM_QS:FM_QS + 512], fm[FM_LF:FM_LF + 512], tm[:, 0:512]) for c in range(8)])
        for c in range(8):
            hd, vh = c // 2, c % 2
            cf[CF_HGO + hd * 128 + vh * 64:CF_HGO + hd * 128 + (vh + 1) * 64] = res[c]["hg_o"]
        fmr = {"r": fm[FM_R:FM_R + 256], "kp": fm[FM_KP:FM_KP + 256], "kk": fm[FM_KK:FM_KK + 256],
               "a": fm[FM_A:FM_A + 256], "ld": fm[FM_LD:FM_LD + 256], "v": fm[FM_V:FM_V + 256]}
        res = _launch(emit_rwkv, [rw_inputs(c, fmr) for c in range(8)])
        for c in range(8):
            hd, vh = c // 2, c % 2
            cf[CF_RWY + hd * 64 + vh * 32:CF_RWY + hd * 64 + (vh + 1) * 32] = res[c]["rw_y"]
        cf[CF_GS:CF_GS + 512] = fm[FM_GS:FM_GS + 512]
        cf[CF_R:CF_R + 256] = fm[FM_R:FM_R + 256]
        cf[CF_KP:CF_KP + 256] = fm[FM_KP:FM_KP + 256]
        cf[CF_V:CF_V + 256] = fm[FM_V:FM_V + 256]
        cf[CF_G:CF_G + 256] = fm[FM_G:FM_G + 256]
        del fm, tm, fmr
        res = _launch(lambda P: emit_c1(P, l), [c1_inputs(l, inp, c, h, cf) for c in range(8)])
        hmid = np.concatenate([r["c1_hmid"] for r in res], axis=0)
        del cf
        res = _launch(emit_c2, [c2_inputs(l, inp, c, hmid) for c in range(8)])
        hffn = np.concatenate([r["c2_out"] for r in res], axis=0)
        final = (l == 1)
        res = _launch(lambda P: emit_c3(P, final), [c3_inputs(l, inp, c, hffn, final) for c in range(8)])
        h = np.concatenate([r["c3_out"] for r in res], axis=0)
    return h[None].astype(np.float32)
```

```python
import numpy as np
import ml_dtypes
from concourse.bass_utils import run_bass_kernel_spmd


import concourse.bass as bass
import concourse.mybir as mybir

F32 = mybir.dt.float32
BF16 = mybir.dt.bfloat16
AF = mybir.ActivationFunctionType
ALU = mybir.AluOpType
AX = mybir.AxisListType

ENGS = ("pe", "act", "dve", "pool", "sp")


class T:
    __slots__ = ("name", "h", "last_w", "readers", "sem", "cnt", "excl")

    def __init__(self, name, h):
        self.name = name
        self.h = h
        self.last_w = {}
        self.readers = []
        self.sem = None
        self.cnt = 0
        self.excl = False

    def __getitem__(self, idx):
        return self.h[idx]


class _Rec:
    def __getattr__(self, name):
        def f(*a, **k):
            self.call = (name, a, k)
        return f


def _eager(fn):
    r = _Rec()
    fn(r)
    name, a, k = r.call
    return lambda e: getattr(e, name)(*a, **k)


class Prog:
    def __init__(self, nc):
        self.nc = nc
        self.ops = {e: [] for e in ENGS}
        self.count = {e: 0 for e in ENGS}
        self.waited = {e: {} for e in ENGS}
        self.dma_sems = []
        self.ctx = []
        self.ntiles = 0

    def sbuf(self, name, shape, dt):
        g = self.nc.sbuf_tensor(name, list(shape), dt)
        h = g.__enter__()
        self.ctx.append(g)
        return T(name, h)

    def psum(self, name, shape, dt=F32):
        g = self.nc.psum_tensor(name, list(shape), dt)
        h = g.__enter__()
        self.ctx.append(g)
        t = T(name, h)
        t.excl = True
        return t

    def dram(self, name, shape, dt, kind="Internal"):
        h = self.nc.dram_tensor(name, list(shape), dt, kind=kind)
        return T(name, h.ap() if hasattr(h, "ap") else h)

    def view(self, name, h):
        return T(name, h)

    def _deps(self, eng, reads, writes):
        deps = []
        for t in reads:
            for ev in t.last_w.items():
                deps.append((ev, "raw"))
            if t.excl:
                for r in t.readers:
                    if r[0] != eng:
                        deps.append((r, "rar"))
        for t in writes:
            if not getattr(self, "_disjoint", False):
                for ev in t.last_w.items():
                    deps.append((ev, "waw"))
            for r in t.readers:
                deps.append((r, "war"))
        out = {}
        for (key, val), kind in deps:
            if key == eng:
                if eng == "pe":
                    continue
            if out.get(key, 0) < val:
                out[key] = val
        res = []
        w = self.waited[eng]
        for key, val in out.items():
            if w.get(key, 0) >= val:
                continue
            w[key] = val
            res.append((key, val))
        return res

    def op(self, eng, fn, reads=(), writes=()):
        waits = self._deps(eng, reads, writes)
        self.count[eng] += 1
        ev = (eng, self.count[eng])
        for t in reads:
            t.readers.append(ev)
        for t in writes:
            t.last_w = {ev[0]: ev[1]}
            t.readers = []
        self.ops[eng].append((waits, _eager(fn), None))

    def dma(self, eng, out_t, out_ap, in_t, in_ap, owner=None, disjoint=False, **kw):
        self._disjoint = disjoint
        waits = self._deps(eng, [in_t], [out_t])
        self._disjoint = False
        ow = owner if owner is not None else out_t
        if ow.sem is None:
            g = self.nc.semaphore("ds%d" % len(self.dma_sems))
            ow.sem = g.__enter__()
            self.ctx.append(g)
            self.dma_sems.append(ow.sem)
        ow.cnt += 16
        ev = (ow.sem, ow.cnt)
        in_t.readers.append(ev)
        if disjoint:
            out_t.last_w[ev[0]] = ev[1]
        else:
            out_t.last_w = {ev[0]: ev[1]}
            out_t.readers = []

        def fn(e, out_ap=out_ap, in_ap=in_ap, kw=kw):
            return e.dma_start(out=out_ap, in_=in_ap, **kw)
        self.ops[eng].append((waits, fn, ow.sem))

    def final_wait(self, eng, tiles):
        waits = self._deps(eng, tiles, [])
        self.ops[eng].append((waits, None, None))

    def emit(self):
        nc = self.nc
        esem = {}
        for e in ENGS:
            g = nc.semaphore("es_" + e)
            esem[e] = g.__enter__()
            self.ctx.append(g)
        engobj = {"pe": "tensor", "act": "scalar", "dve": "vector", "pool": "gpsimd", "sp": "sync"}

        def run(e, eng):
            for waits, fn, dsem in self.ops[e]:
                for key, val in waits:
                    s = esem[key] if isinstance(key, str) else key
                    eng.wait_ge(s, val)
                if fn is None:
                    continue
                ins = fn(eng)
                if dsem is not None:
                    ins.then_inc(dsem, 16)
                else:
                    ins.then_inc(esem[e], 1)

        with nc.Block() as block:
            for e in ENGS:
                if not self.ops[e]:
                    continue
                getattr(block, engobj[e])(lambda eng, e=e: run(e, eng))

    def close(self):
        for g in reversed(self.ctx):
            g.__exit__(None, None, None)
        self.ctx = []


A_NT = 2048
A_TG = 512
A_NG = A_NT // A_TG
FM_QS, FM_LF, FM_GS, FM_BQ, FM_BK = 0, 512, 1024, 1536, 1792
FM_R, FM_KP, FM_KK, FM_A, FM_LD, FM_V, FM_G = [2048 + 256 * i for i in range(7)]
FM_ROWS = 3840
PP_GMIX, PP_MU, PP_W0, PP_A0, PP_KK, PP_KA, PP_V0, PP_HB0, PP_HB1, PP_N = 0, 8, 16, 18, 20, 22, 24, 26, 30, 34
PM_W2, PM_A2, PM_G2, PM_V1, PM_V2, PM_N = 0, 256, 512, 768, 832, 1088


def build_A(layer):
    nc = bass.Bass("TRN2", target_bir_lowering=False)
    P = Prog(nc)
    h = P.dram("h", [A_NT, 1024], F32, "ExternalInput")
    hh = P.dram("hh", [128, 1024], F32, "ExternalInput")
    w_in = P.dram("w_in", [1024, 3840], F32, "ExternalInput")
    pp_d = P.dram("pp", [128, PP_N], F32, "ExternalInput")
    pm_d = P.dram("pm", [128, PM_N], F32, "ExternalInput")
    id_d = P.dram("ident", [128, 128], BF16, "ExternalInput")
    blk_d = P.dram("blk64", [128, 128], BF16, "ExternalInput")
    if layer == 1:
        vf_d = P.dram("vfirst", [256, A_NT], F32, "ExternalInput")
    fm = P.dram("fm", [FM_ROWS, A_NT], F32, "ExternalOutput")
    tm = P.dram("tm", [A_NT, 768], F32, "ExternalOutput")

    wbf = [P.sbuf("wbf%d" % k, [128, 3840], BF16) for k in range(8)]
    wst = [P.sbuf("wst%d" % i, [128, 1920], F32) for i in range(2)]
    pp = P.sbuf("pp_s", [128, PP_N], F32)
    pm32 = P.sbuf("pm32", [128, PM_N], F32)
    pm = P.sbuf("pm_s", [128, PM_N], BF16)
    ident = P.sbuf("ident_s", [128, 128], BF16)
    blk = P.sbuf("blk_s", [128, 128], BF16)
    hin = [P.sbuf("hin%d" % i, [128, 1024], F32) for i in range(2)]
    hsq = P.sbuf("hsq", [128, 1024], F32)
    hnb = [P.sbuf("hnb%d" % i, [128, 1024], BF16) for i in range(2)]
    st = [P.sbuf("st%d" % i, [128, 4], F32) for i in range(2)]
    hnT = [P.sbuf("hnT%d" % i, [128, 8, A_TG], BF16) for i in range(2)]
    gb = P.sbuf("gb", [128, 8, 128], F32)
    CB = [P.sbuf("CB%d" % i, [128, 8, A_TG + 1], F32) for i in range(2)]
    stg = [P.sbuf("stg%d" % i, [128, A_TG], F32) for i in range(6)]
    stt = [P.sbuf("stt%d" % i, [128, 768], F32) for i in range(2)]
    cm = P.sbuf("cm", [128, 8, A_TG], F32)
    tmpA = [P.sbuf("tmpA%d" % i, [128, A_TG], F32) for i in range(4)]
    tb = [P.sbuf("tb%d" % i, [128, A_TG], BF16) for i in range(4)]
    lbc = P.sbuf("lbc", [128, 8], F32)
    kac = P.sbuf("kac", [128, 2], F32)
    epsc = P.sbuf("epsc", [128, 1], F32)
    vfs = P.sbuf("vfs", [128, 2, A_TG], F32) if layer == 1 else None
    ps = [P.psum("ps%d" % i, [128, 512], F32) for i in range(6)]
    pst = P.psum("pst", [128, 1024], BF16)
    psm = P.psum("psm", [128, 512], F32)

    P.dma("sp", pp, pp[:], pp_d, pp_d[:])
    P.dma("sp", pm32, pm32[:], pm_d, pm_d[:])
    P.dma("sp", ident, ident[:], id_d, id_d[:])
    P.dma("sp", blk, blk[:], blk_d, blk_d[:])
    P.op("dve", lambda e: e.tensor_copy(out=pm[:], in_=pm32[:]), [pm32], [pm])
    P.op("dve", lambda e: e.memset(epsc[:], 1e-6), [], [epsc])
    for c in range(8):
        P.op("dve", lambda e, c=c: e.memset(gb[:, c, :], 1.0), [], [gb])
    for c in range(8):
        P.op("dve", lambda e, c=c: e.tensor_scalar(out=gb[:, c, :], in0=gb[:, c, :], scalar1=pp[:, PP_GMIX + c:PP_GMIX + c + 1],
                                                    scalar2=None, op0=ALU.mult), [gb, pp], [gb])
    P.op("dve", lambda e: e.tensor_scalar(out=kac[:], in0=pp[:, PP_KA:PP_KA + 2], scalar1=-1.0, scalar2=1.0,
                                          op0=ALU.mult, op1=ALU.add), [pp], [kac])
    if layer == 1:
        P.op("dve", lambda e: e.tensor_tensor(out=lbc[:, 0:4], in0=pp[:, PP_HB1:PP_HB1 + 4], in1=pp[:, PP_HB0:PP_HB0 + 4],
                                              op=ALU.subtract), [pp], [lbc])
        P.op("act", lambda e: e.activation(out=lbc[:, 0:4], in_=lbc[:, 0:4], func=AF.Sigmoid), [lbc], [lbc])
        P.op("dve", lambda e: e.tensor_scalar(out=lbc[:, 4:8], in0=lbc[:, 0:4], scalar1=-1.0, scalar2=1.0,
                                              op0=ALU.mult, op1=ALU.add), [lbc], [lbc])
    for k in range(8):
        for hf in range(2):
            s_ = wst[hf]
            P.dma("sp" if hf == 0 else "act", s_, s_[:], w_in, w_in[k * 128:(k + 1) * 128, hf * 1920:(hf + 1) * 1920])
            P.op("pool", lambda e, k=k, s_=s_, hf=hf: e.tensor_copy(out=wbf[k][:, hf * 1920:(hf + 1) * 1920], in_=s_[:]), [s_], [wbf[k]])

    outq = ["sp", "act", "pool"]
    oq = [0]

    def out_dma(dst_t, dst_ap, src_t, src_ap):
        q = outq[oq[0] % 3]
        oq[0] += 1
        P.dma(q, dst_t, dst_ap, src_t, src_ap, owner=src_t, disjoint=True)

    tcount = [0]

    def norm_tile(src_ap, dstT, col0):
        i = tcount[0] % 2
        tcount[0] += 1
        hi, hb, s_ = hin[i], hnb[i], st[i]
        P.dma("sp", hi, hi[:], h, src_ap)
        P.op("act", lambda e: e.activation(out=hsq[:], in_=hi[:], func=AF.Square, accum_out=s_[:, 0:1]), [hi], [hsq, s_])
        P.op("act", lambda e: e.activation(out=s_[:, 1:2], in_=s_[:, 0:1], func=AF.Sqrt, scale=1.0 / 1024, bias=epsc[:, 0:1]),
             [s_, epsc], [s_])
        P.op("dve", lambda e: e.reciprocal(out=s_[:, 2:3], in_=s_[:, 1:2]), [s_], [s_])
        P.op("dve", lambda e: e.tensor_scalar(out=hb[:], in0=hi[:], scalar1=s_[:, 2:3], scalar2=None, op0=ALU.mult),
             [hi, s_], [hb])
        for c in range(8):
            P.op("pe", lambda e, c=c: e.transpose(out=pst[:, c * 128:(c + 1) * 128], in_=hb[:, c * 128:(c + 1) * 128],
                                                   identity=ident[:]), [hb, ident], [pst])
        P.op("dve", lambda e: e.tensor_tensor(out=dstT[:, :, col0:col0 + 128],
                                              in0=pst[:].rearrange("p (c t) -> p c t", c=8), in1=gb[:], op=ALU.mult),
             [pst, gb], [dstT])

    def mm_fm(dst_ps, cc, src):
        for k in range(8):
            P.op("pe", lambda e, k=k: e.matmul(dst_ps[:], lhsT=wbf[k][:, cc * 128:(cc + 1) * 128], rhs=src[:, k, :],
                                                start=(k == 0), stop=(k == 7)), [wbf[k], src], [dst_ps])

    hT_h = hnT[1]
    P.hsrc = hh
    i0 = tcount[0]
    hi, hb, s_ = hin[0], hnb[0], st[0]
    tcount[0] += 1
    P.dma("sp", hi, hi[:], hh, hh[:])
    P.op("act", lambda e: e.activation(out=hsq[:], in_=hi[:], func=AF.Square, accum_out=s_[:, 0:1]), [hi], [hsq, s_])
    P.op("act", lambda e: e.activation(out=s_[:, 1:2], in_=s_[:, 0:1], func=AF.Sqrt, scale=1.0 / 1024, bias=epsc[:, 0:1]),
         [s_, epsc], [s_])
    P.op("dve", lambda e: e.reciprocal(out=s_[:, 2:3], in_=s_[:, 1:2]), [s_], [s_])
    P.op("dve", lambda e: e.tensor_scalar(out=hb[:], in0=hi[:], scalar1=s_[:, 2:3], scalar2=None, op0=ALU.mult), [hi, s_], [hb])
    for c in range(8):
        P.op("pe", lambda e, c=c: e.transpose(out=pst[:, c * 128:(c + 1) * 128], in_=hb[:, c * 128:(c + 1) * 128],
                                               identity=ident[:]), [hb, ident], [pst])
    P.op("dve", lambda e: e.tensor_tensor(out=hT_h[:, :, 0:128], in0=pst[:].rearrange("p (c t) -> p c t", c=8),
                                          in1=gb[:], op=ALU.mult), [pst, gb], [hT_h])
    for c8 in range(8):
        cc = 22 + c8
        pz = ps[c8 % 6]
        for k in range(8):
            P.op("pe", lambda e, k=k, cc=cc, pz=pz: e.matmul(pz[:, 0:128], lhsT=wbf[k][:, cc * 128:(cc + 1) * 128],
                                                              rhs=hT_h[:, k, 0:128], start=(k == 0), stop=(k == 7)),
                 [wbf[k], hT_h], [pz])
        P.op("act", lambda e, c8=c8, pz=pz: e.activation(out=CB[0][:, c8, 0:1], in_=pz[:, 127:128], func=AF.Copy), [pz], [CB[0]])

    sti = [0]

    def stage():
        s_ = stg[sti[0] % 6]
        sti[0] += 1
        return s_

    psi = [0]

    def nps():
        p_ = ps[psi[0] % 6]
        psi[0] += 1
        return p_

    for g in range(A_NG):
        hT = hnT[g % 2]
        cb = CB[g % 2]
        cbn = CB[(g + 1) % 2]
        t0 = g * A_TG
        for t in range(4):
            norm_tile(h[t0 + t * 128:t0 + (t + 1) * 128, :], hT, t * 128)
        for c in range(4):
            pz = nps(); mm_fm(pz, c, hT); s_ = stage()
            P.op("act", lambda e, pz=pz, s_=s_: e.activation(out=s_[:], in_=pz[:], func=AF.Silu), [pz], [s_])
            out_dma(fm, fm[FM_QS + c * 128:FM_QS + (c + 1) * 128, t0:t0 + A_TG], s_, s_[:])
        for c in range(4):
            pz = nps(); mm_fm(pz, 4 + c, hT); s_ = stage()
            P.op("act", lambda e, pz=pz, s_=s_: e.activation(out=s_[:], in_=pz[:], func=AF.Sigmoid), [pz], [s_])
            if layer == 1:
                P.op("dve", lambda e, s_=s_, c=c: e.tensor_scalar(out=s_[:], in0=s_[:], scalar1=lbc[:, 4 + c:5 + c],
                                                                    scalar2=lbc[:, c:c + 1], op0=ALU.mult, op1=ALU.add),
                     [s_, lbc], [s_])
            P.op("act", lambda e, s_=s_: e.activation(out=s_[:], in_=s_[:], func=AF.Ln), [s_], [s_])
            out_dma(fm, fm[FM_LF + c * 128:FM_LF + (c + 1) * 128, t0:t0 + A_TG], s_, s_[:])
        for c in range(4):
            pz = nps(); mm_fm(pz, 12 + c, hT); s_ = stage()
            P.op("act", lambda e, pz=pz, s_=s_: e.activation(out=s_[:], in_=pz[:], func=AF.Silu), [pz], [s_])
            out_dma(fm, fm[FM_GS + c * 128:FM_GS + (c + 1) * 128, t0:t0 + A_TG], s_, s_[:])
        for c in range(4):
            pz = nps(); mm_fm(pz, 16 + c, hT); s_ = stage()
            P.op("dve", lambda e, pz=pz, s_=s_: e.tensor_copy(out=s_[:], in_=pz[:]), [pz], [s_])
            out_dma(fm, fm[FM_BQ + c * 128:FM_BQ + (c + 1) * 128, t0:t0 + A_TG], s_, s_[:])
        for t in range(4):
            pz = nps(); pz2 = nps(); s_ = stt[t % 2]
            for k in range(8):
                P.op("pe", lambda e, k=k, t=t, pz=pz: e.matmul(pz[:], lhsT=hT[:, k, t * 128:(t + 1) * 128],
                                                                rhs=wbf[k][:, 1024:1536], start=(k == 0), stop=(k == 7)),
                     [wbf[k], hT], [pz])
            for k in range(8):
                P.op("pe", lambda e, k=k, t=t, pz2=pz2: e.matmul(pz2[:, 0:256], lhsT=hT[:, k, t * 128:(t + 1) * 128],
                                                                  rhs=wbf[k][:, 2560:2816], start=(k == 0), stop=(k == 7)),
                     [wbf[k], hT], [pz2])
            P.op("dve", lambda e, pz=pz, s_=s_: e.tensor_copy(out=s_[:, 0:512], in_=pz[:]), [pz], [s_])
            P.op("act", lambda e, pz2=pz2, s_=s_: e.activation(out=s_[:, 512:768], in_=pz2[:, 0:256], func=AF.Copy), [pz2], [s_])
            out_dma(tm, tm[t0 + t * 128:t0 + (t + 1) * 128, :], s_, s_[:])
        for c8 in range(8):
            pz = nps(); mm_fm(pz, 22 + c8, hT)
            if c8 % 2 == 0:
                P.op("dve", lambda e, pz=pz, c8=c8: e.tensor_copy(out=cb[:, c8, 1:A_TG + 1], in_=pz[:]), [pz], [cb])
            else:
                P.op("act", lambda e, pz=pz, c8=c8: e.activation(out=cb[:, c8, 1:A_TG + 1], in_=pz[:], func=AF.Copy), [pz], [cb])
        P.op("pool", lambda e: e.tensor_copy(out=cbn[:, :, 0:1], in_=cb[:, :, A_TG:A_TG + 1]), [cb], [cbn])
        for c8 in range(8):
            ta = tmpA[c8 % 2]
            eng = "dve"
            P.op(eng, lambda e, c8=c8, ta=ta: e.tensor_tensor(out=ta[:], in0=cb[:, c8, 0:A_TG], in1=cb[:, c8, 1:A_TG + 1],
                                                              op=ALU.subtract), [cb], [ta])
            P.op(eng, lambda e, c8=c8, ta=ta: e.scalar_tensor_tensor(out=cm[:, c8, :], in0=ta[:], scalar=pp[:, PP_MU + c8:PP_MU + c8 + 1],
                                                                     in1=cb[:, c8, 1:A_TG + 1], op0=ALU.mult, op1=ALU.add),
                 [ta, pp, cb], [cm])
        for c in range(2):
            out_dma(fm, fm[FM_R + c * 128:FM_R + (c + 1) * 128, t0:t0 + A_TG], cm, cm[:, c, :])
        P.op("act", lambda e: e.activation(out=tb[0][0:64, :], in_=cm[0:64, 6, :], func=AF.Tanh), [cm], [tb[0]])
        P.op("dve", lambda e: e.tensor_copy(out=tb[0][64:128, :], in_=cm[64:128, 6, :]), [cm], [tb[0]])
        P.op("act", lambda e: e.activation(out=tb[1][:], in_=cm[:, 7, :], func=AF.Sigmoid), [cm], [tb[1]])
        E05 = float(np.exp(-0.5))
        for c in range(2):
            pz = nps(); s_ = stage()
            P.op("pe", lambda e, pz=pz, c=c: e.matmul(pz[:], lhsT=pm[0:64, PM_W2 + c * 128:PM_W2 + (c + 1) * 128],
                                                       rhs=tb[0][0:64, :], start=True, stop=True), [pm, tb[0]], [pz])
            P.op("act", lambda e, pz=pz, s_=s_, c=c: e.activation(out=s_[:], in_=pz[:], func=AF.Sigmoid,
                                                                   bias=pp[:, PP_W0 + c:PP_W0 + c + 1]), [pz, pp], [s_])
            P.op("dve", lambda e, s_=s_: e.tensor_scalar(out=s_[:], in0=s_[:], scalar1=-E05, scalar2=None, op0=ALU.mult), [s_], [s_])
            out_dma(fm, fm[FM_LD + c * 128:FM_LD + (c + 1) * 128, t0:t0 + A_TG], s_, s_[:])
        a_t = [tmpA[2], tmpA[3]]
        for c in range(2):
            pz = nps()
            P.op("pe", lambda e, pz=pz, c=c: e.matmul(pz[:], lhsT=pm[64:128, PM_A2 + c * 128:PM_A2 + (c + 1) * 128],
                                                       rhs=tb[0][64:128, :], start=True, stop=True), [pm, tb[0]], [pz])
            P.op("act", lambda e, pz=pz, c=c: e.activation(out=a_t[c][:], in_=pz[:], func=AF.Sigmoid,
                                                           bias=pp[:, PP_A0 + c:PP_A0 + c + 1]), [pz, pp], [a_t[c]])
            out_dma(fm, fm[FM_A + c * 128:FM_A + (c + 1) * 128, t0:t0 + A_TG], a_t[c], a_t[c][:])
        for c in range(2):
            pz = nps(); s_ = stage()
            P.op("pe", lambda e, pz=pz, c=c: e.matmul(pz[:], lhsT=pm[:, PM_G2 + c * 128:PM_G2 + (c + 1) * 128],
                                                       rhs=tb[1][:], start=True, stop=True), [pm, tb[1]], [pz])
            P.op("dve", lambda e, pz=pz, s_=s_: e.tensor_copy(out=s_[:], in_=pz[:]), [pz], [s_])
            out_dma(fm, fm[FM_G + c * 128:FM_G + (c + 1) * 128, t0:t0 + A_TG], s_, s_[:])
        if layer == 1:
            P.dma("sp", vfs, vfs[:], vf_d, vf_d[:, t0:t0 + A_TG].rearrange("(c p) t -> p c t", p=128))
            for c in range(2):
                P.op("dve", lambda e, c=c: e.tensor_copy(out=tb[2 + c][:], in_=cm[:, 4 + c, :]), [cm], [tb[2 + c]])
            for c in range(2):
                P.op("pe", lambda e, c=c: e.matmul(psm[0:32, :], lhsT=pm[:, PM_V1 + c * 32:PM_V1 + (c + 1) * 32],
                                                   rhs=tb[2 + c][:], start=(c == 0), stop=(c == 1)), [pm, tb[2 + c]], [psm])
            P.op("dve", lambda e: e.tensor_copy(out=tb[1][0:32, :], in_=psm[0:32, :]), [psm], [tb[1]])
            for c in range(2):
                pz = nps(); ta = tmpA[c]
                P.op("pe", lambda e, pz=pz, c=c: e.matmul(pz[:], lhsT=pm[0:32, PM_V2 + c * 128:PM_V2 + (c + 1) * 128],
                                                           rhs=tb[1][0:32, :], start=True, stop=True), [pm, tb[1]], [pz])
                P.op("act", lambda e, pz=pz, c=c, ta=ta: e.activation(out=ta[:], in_=pz[:], func=AF.Sigmoid,
                                                                        bias=pp[:, PP_V0 + c:PP_V0 + c + 1]), [pz, pp], [ta])
                s_ = stage()
                P.op("dve", lambda e, c=c, s_=s_: e.tensor_tensor(out=s_[:], in0=vfs[:, c, :], in1=cm[:, 4 + c, :],
                                                                   op=ALU.subtract), [vfs, cm], [s_])
                P.op("dve", lambda e, s_=s_, ta=ta: e.tensor_tensor(out=s_[:], in0=s_[:], in1=ta[:], op=ALU.mult), [s_, ta], [s_])
                P.op("dve", lambda e, s_=s_, c=c: e.tensor_tensor(out=s_[:], in0=s_[:], in1=cm[:, 4 + c, :], op=ALU.add),
                     [s_, cm], [s_])
                out_dma(fm, fm[FM_V + c * 128:FM_V + (c + 1) * 128, t0:t0 + A_TG], s_, s_[:])
        else:
            for c in range(2):
                out_dma(fm, fm[FM_V + c * 128:FM_V + (c + 1) * 128, t0:t0 + A_TG], cm, cm[:, 4 + c, :])
        for c in range(2):
            kx = tmpA[c]; s_ = stage(); s2 = stage(); pz = nps()
            P.op("dve", lambda e, c=c, kx=kx: e.tensor_scalar(out=kx[:], in0=cm[:, 2 + c, :], scalar1=pp[:, PP_KK + c:PP_KK + c + 1],
                                                               scalar2=None, op0=ALU.mult), [cm, pp], [kx])
            P.op("pool", lambda e, c=c, kx=kx: e.tensor_tensor(out=tb[2 + c][:], in0=kx[:], in1=kx[:], op=ALU.mult), [kx], [tb[2 + c]])
            P.op("pe", lambda e, pz=pz, c=c: e.matmul(pz[:], lhsT=blk[:], rhs=tb[2 + c][:], start=True, stop=True),
                 [blk, tb[2 + c]], [pz])
            P.op("act", lambda e, pz=pz, s_=s_: e.activation(out=s_[:], in_=pz[:], func=AF.Sqrt), [pz], [s_])
            P.op("dve", lambda e, s_=s_: e.tensor_scalar(out=s_[:], in0=s_[:], scalar1=1e-12, scalar2=None, op0=ALU.max), [s_], [s_])
            P.op("dve", lambda e, s_=s_: e.reciprocal(out=s_[:], in_=s_[:]), [s_], [s_])
            P.op("dve", lambda e, s_=s_, kx=kx: e.tensor_tensor(out=s_[:], in0=s_[:], in1=kx[:], op=ALU.mult), [s_, kx], [s_])
            out_dma(fm, fm[FM_KK + c * 128:FM_KK + (c + 1) * 128, t0:t0 + A_TG], s_, s_[:])
            P.op("dve", lambda e, s2=s2, c=c: e.tensor_scalar(out=s2[:], in0=a_t[c][:], scalar1=pp[:, PP_KA + c:PP_KA + c + 1],
                                                               scalar2=kac[:, c:c + 1], op0=ALU.mult, op1=ALU.add),
                 [a_t[c], pp, kac], [s2])
            P.op("dve", lambda e, s2=s2, c=c: e.tensor_tensor(out=s2[:], in0=s2[:], in1=cm[:, 2 + c, :], op=ALU.mult), [s2, cm], [s2])
            out_dma(fm, fm[FM_KP + c * 128:FM_KP + (c + 1) * 128, t0:t0 + A_TG], s2, s2[:])

    P.final_wait("sp", [fm, tm])
    P.emit()
    return nc, P


def host_inputs_A(layer, inp, h_full, vfirst_full=None):
    l = layer
    pp = np.zeros((128, PP_N), np.float32)
    fmj = lambda v: np.ascontiguousarray(v.reshape(-1, 128).T)
    pp[:, PP_GMIX:PP_GMIX + 8] = fmj(inp['norm_mix_g'][l])
    pp[:, PP_MU:PP_MU + 8] = fmj(inp['rwkv_mu'][l])
    pp[:, PP_W0:PP_W0 + 2] = fmj(inp['rwkv_w0'][l])
    pp[:, PP_A0:PP_A0 + 2] = fmj(inp['rwkv_a0'][l])
    pp[:, PP_KK:PP_KK + 2] = fmj(inp['rwkv_k_k'][l])
    pp[:, PP_KA:PP_KA + 2] = fmj(inp['rwkv_k_a'][l])
    if l == 1:
        pp[:, PP_V0:PP_V0 + 2] = fmj(inp['rwkv_v0'][0])
    pp[:, PP_HB0:PP_HB0 + 4] = fmj(inp['hgrn_lower_bounds'][0])
    pp[:, PP_HB1:PP_HB1 + 4] = fmj(inp['hgrn_lower_bounds'][1])
    pm = np.zeros((128, PM_N), np.float32)
    pm[0:64, PM_W2:PM_W2 + 256] = inp['rwkv_w2'][l]
    pm[64:128, PM_A2:PM_A2 + 256] = inp['rwkv_a2'][l]
    pm[:, PM_G2:PM_G2 + 256] = inp['rwkv_g2'][l]
    if l == 1:
        pm[:, PM_V1:PM_V1 + 64] = inp['rwkv_v1'][0].reshape(2, 128, 32).transpose(1, 0, 2).reshape(128, 64)
        pm[0:32, PM_V2:PM_V2 + 256] = inp['rwkv_v2'][0]
    ident = np.eye(128).astype(ml_dtypes.bfloat16)
    blk = np.kron(np.eye(2), np.ones((64, 64))).astype(ml_dtypes.bfloat16)
    maps = []
    w = np.ascontiguousarray(inp['w_in'][l])
    for c in range(8):
        m = {"h": np.ascontiguousarray(h_full[c * A_NT:(c + 1) * A_NT]),
             "hh": np.ascontiguousarray(h_full[c * A_NT - 128:c * A_NT]) if c > 0 else np.zeros((128, 1024), np.float32),
             "w_in": w, "pp": pp, "pm": pm, "ident": ident, "blk64": blk}
        if l == 1:
            m["vfirst"] = np.ascontiguousarray(vfirst_full[:, c * A_NT:(c + 1) * A_NT])
        maps.append(m)
    return maps


NOWN = 8192
NHALO = 2048
NTOT = NOWN + NHALO
PATTERNS = (1, 4, 16)


def emit_sw(P):
    qT_d = P.dram("sw_qT", [64, NOWN], F32, "ExternalInput")
    kT_d = P.dram("sw_kT", [64, NTOT], F32, "ExternalInput")
    v_d = P.dram("sw_v", [NTOT, 64], F32, "ExternalInput")
    msk_d = P.dram("sw_mask", [128, 512], BF16, "ExternalInput")
    id_d = P.dram("sw_ident", [128, 128], BF16, "ExternalInput")
    flag_d = P.dram("sw_flag", [128, 1], F32, "ExternalInput")
    o_d = P.dram("sw_o", [64, NOWN], F32, "ExternalOutput")

    q32 = P.sbuf("q32", [64, NOWN], F32)
    k32 = P.sbuf("k32", [64, NTOT], F32)
    qd = P.sbuf("qd", [64, NOWN], BF16)
    kd = P.sbuf("kd", [64, NTOT], BF16)
    vst = P.sbuf("vst", [128, 85 * 64], F32)
    vaug = P.sbuf("vaug", [128, 85, 65], BF16)
    acc = P.sbuf("acc", [65, NOWN], F32)
    msk = P.sbuf("msk", [128, 512], BF16)
    ident = P.sbuf("identsw", [128, 128], BF16)
    flag = P.sbuf("flag", [128, 1], F32)
    ones = P.sbuf("ones", [65, 64], F32)
    PT = [P.sbuf("PT%d" % i, [128, 512], BF16) for i in range(3)]
    ost = [P.sbuf("ost%d" % i, [64, 512], F32) for i in range(2)]
    rz = [P.sbuf("rz%d" % i, [64, 512], F32) for i in range(2)]
    psS = [P.psum("psS%d" % i, [128, 512], F32) for i in range(3)]
    psN = [P.psum("psN%d" % i, [128, 512], F32) for i in range(3)]

    P.dma("sp", msk, msk[:], msk_d, msk_d[:])
    P.dma("sp", ident, ident[:], id_d, id_d[:])
    P.dma("sp", flag, flag[:], flag_d, flag_d[:])
    for i in range(4):
        P.dma("sp" if i % 2 == 0 else "act", q32, q32[:, i * 2048:(i + 1) * 2048], qT_d, qT_d[:, i * 2048:(i + 1) * 2048],
              disjoint=True)
    for i in range(5):
        P.dma("act" if i % 2 == 0 else "sp", k32, k32[:, i * 2048:(i + 1) * 2048], kT_d, kT_d[:, i * 2048:(i + 1) * 2048],
              disjoint=True)
    P.op("dve", lambda e: e.memset(ones[:], 1.0), [], [ones])

    si = [0]
    for pi, D in enumerate(PATTERNS):
        nb = NOWN // (128 * D)
        nbt = nb + 1
        LQ = NOWN // D
        LK = LQ + 128
        koff = NHALO - 128 * D
        if D == 1:
            P.op("dve", lambda e: e.tensor_copy(out=qd[:, 0:NOWN], in_=q32[:, :]), [q32], [qd])
            P.op("pool", lambda e: e.tensor_copy(out=kd[:, 0:LK], in_=k32[:, koff:koff + LK]), [k32], [kd])
        else:
            P.op("dve", lambda e: e.tensor_copy(out=qd[:, 0:NOWN].rearrange("p (r l) -> p r l", r=D),
                                                in_=q32[:, :].rearrange("p (l r) -> p r l", r=D)), [q32], [qd])
            P.op("pool", lambda e: e.tensor_copy(out=kd[:, 0:D * LK].rearrange("p (r l) -> p r l", r=D),
                                                 in_=k32[:, koff:koff + D * LK].rearrange("p (l r) -> p r l", r=D)), [k32], [kd])
        vsrc = v_d[koff:koff + nbt * 128 * D, :].rearrange("(bb i r) c -> i r bb c", i=128, r=D)
        vv = vst[:, 0:D * nbt * 64].rearrange("p (r bb c) -> p r bb c", r=D, bb=nbt)
        for r in range(D):
            P.dma("sp" if r % 2 == 0 else "act", vst, vv[:, r], v_d, vsrc[:, r], disjoint=(r > 0))
        va = vaug[:, 0:D * nbt, :]
        P.op("dve", lambda e: e.memset(va[:, :, 64:65], 1.0), [], [vaug])
        P.op("dve", lambda e: e.tensor_copy(out=va[:, :, 0:64], in_=vst[:, 0:D * nbt * 64].rearrange("p (n c) -> p n c", c=64)),
             [vst], [vaug])
        va4 = va.rearrange("p (r bb) c -> p r bb c", r=D)
        P.op("dve", lambda e: e.tensor_scalar(out=va4[:, :, 0, :], in0=va4[:, :, 0, :], scalar1=flag[:, 0:1], scalar2=None,
                                              op0=ALU.mult), [vaug, flag], [vaug])
        for r in range(D):
            for b0 in range(0, nb, 4):
                pn = psN[si[0] % 3]
                pts = []
                for pr in range(2):
                    p_s = psS[(2 * si[0] + pr) % 3]
                    pt = PT[(2 * si[0] + pr) % 3]
                    P.op("pe", lambda e: e.matmul(p_s[:], lhsT=ident[:], rhs=msk[:], start=True, stop=False), [ident, msk], [p_s])
                    for j in range(2):
                        b = b0 + 2 * pr + j
                        qb = qd[:, r * LQ + b * 128: r * LQ + (b + 1) * 128]
                        for kb in range(2):
                            kblk = kd[:, r * LK + (b + kb) * 128: r * LK + (b + kb + 1) * 128]
                            P.op("pe", lambda e: e.matmul(p_s[:, (2 * j + kb) * 128:(2 * j + kb + 1) * 128], lhsT=kblk, rhs=qb,
                                                          start=False, stop=True), [kd, qd], [p_s])
                    P.op("act", lambda e: e.activation(out=pt[:], in_=p_s[:], func=AF.Exp, scale=0.125), [p_s], [pt])
                    pts.append(pt)
                for pr in range(2):
                    for j in range(2):
                        b = b0 + 2 * pr + j
                        for kb in range(2):
                            P.op("pe", lambda e: e.matmul(pn[0:65, (2 * pr + j) * 128:(2 * pr + j + 1) * 128],
                                                          lhsT=vaug[:, r * nbt + b + kb, :],
                                                          rhs=pts[pr][:, (2 * j + kb) * 128:(2 * j + kb + 1) * 128],
                                                          start=(kb == 0), stop=(kb == 1)), [vaug, pts[pr]], [pn])
                tstart = r + D * 128 * b0
                av = acc[:, tstart: tstart + 512 * D] if D == 1 else \
                    acc[:, D * 128 * b0: D * 128 * b0 + 512 * D].rearrange("p (l r) -> p r l", r=D)[:, r, :]
                if pi == 0:
                    P.op("dve", lambda e: e.tensor_copy(out=av, in_=pn[0:65, :]), [pn], [acc])
                else:
                    P.op("dve", lambda e: e.tensor_tensor(out=av, in0=av, in1=pn[0:65, :], op=ALU.add), [pn, acc], [acc])
                si[0] += 1
    for i in range(NOWN // 512):
        pz = psS[i % 3]
        P.op("pe", lambda e: e.matmul(pz[0:64, :], lhsT=ones[64:65, :], rhs=acc[64:65, i * 512:(i + 1) * 512], start=True, stop=True),
             [ones, acc], [pz])
        rzi = rz[i % 2]; o_ = ost[i % 2]
        P.op("dve", lambda e: e.reciprocal(out=rzi[:], in_=pz[0:64, :]), [pz], [rzi])
        P.op("pool", lambda e: e.tensor_tensor(out=o_[:], in0=acc[0:64, i * 512:(i + 1) * 512], in1=rzi[:], op=ALU.mult), [acc, rzi], [o_])
        P.dma("sp" if i % 2 == 0 else "act", o_d, o_d[:, i * 512:(i + 1) * 512], o_, o_[:], owner=o_, disjoint=True)
    return [o_d]


def sw_consts():
    j = np.arange(128)[:, None]
    i = np.arange(128)[None, :]
    mp = np.where(j >= i, 0.0, -30000.0)
    mo = np.where(j <= i, 0.0, -30000.0)
    m = np.concatenate([mp, mo, mp, mo], axis=1).astype(ml_dtypes.bfloat16)
    return {"sw_mask": m, "sw_ident": np.eye(128).astype(ml_dtypes.bfloat16)}


def sw_inputs(core, bq_fm, bk_fm, bv_tm):
    hd, s = core // 2, core % 2
    t0 = s * NOWN
    qT = np.ascontiguousarray(bq_fm[hd * 64:(hd + 1) * 64, t0:t0 + NOWN])
    kT = np.zeros((64, NTOT), np.float32)
    v = np.zeros((NTOT, 64), np.float32)
    kT[:, NHALO:] = bk_fm[hd * 64:(hd + 1) * 64, t0:t0 + NOWN]
    v[NHALO:] = bv_tm[t0:t0 + NOWN, hd * 64:(hd + 1) * 64]
    if s > 0:
        kT[:, :NHALO] = bk_fm[hd * 64:(hd + 1) * 64, t0 - NHALO:t0]
        v[:NHALO] = bv_tm[t0 - NHALO:t0, hd * 64:(hd + 1) * 64]
    m = {"sw_qT": qT, "sw_kT": kT, "sw_v": v, "sw_flag": np.full((128, 1), 1.0 if s > 0 else 0.0, np.float32)}
    m.update(sw_consts())
    return m


HG_SEQ = 16384
HG_ST = 2048
HG_NST = HG_SEQ // HG_ST
HG_CL = 40.0


def emit_hgrn(P, NSETA=4):
    q_d = P.dram("hg_q", [128, HG_SEQ], F32, "ExternalInput")
    lf_d = P.dram("hg_lf", [128, HG_SEQ], F32, "ExternalInput")
    i_d = P.dram("hg_i", [HG_SEQ, 64], F32, "ExternalInput")
    rm_d = P.dram("hg_rmask", [128, HG_ST], F32, "ExternalInput")
    cm_d = P.dram("hg_cmask", [128, 128], F32, "ExternalInput")
    id_d = P.dram("hg_ident", [128, 128], BF16, "ExternalInput")
    o_d = P.dram("hg_o", [64, HG_SEQ], F32, "ExternalOutput")

    def sb(name, shape, dt=F32):
        return P.sbuf("hgs_" + name, shape, dt)
    qs = sb("qs", [128, HG_ST]); lf = sb("lf", [128, HG_ST]); bb = sb("b", [128, HG_ST]); kf = sb("kf", [128, HG_ST])
    t1 = sb("t1", [128, HG_ST]); t2 = sb("t2", [128, HG_ST])
    v32 = sb("v32", [128, 16, 64])
    dch = [sb("dch%d" % i, [128, 32]) for i in range(2)]
    Qt = [sb("Qt%d" % i, [128, HG_ST], BF16) for i in range(2)]
    Kt = [sb("Kt%d" % i, [128, HG_ST], BF16) for i in range(2)]
    Qh = [sb("Qh%d" % i, [128, HG_ST], BF16) for i in range(2)]
    Kh = [sb("Kh%d" % i, [128, HG_ST], BF16) for i in range(2)]
    vb = [sb("vb%d" % i, [128, 16, 64], BF16) for i in range(2)]
    ost = [sb("ost%d" % i, [64, HG_ST]) for i in range(2)]
    rmask = sb("rmask", [128, HG_ST]); cmask = sb("cmask", [128, 128]); ident = sb("ident", [128, 128], BF16)
    KhT = [sb("KhT%d" % i, [128, 128], BF16) for i in range(NSETA)]
    Am = [sb("Am%d" % i, [128, 128], BF16) for i in range(NSETA)]
    S32 = sb("S32", [128, 64])
    Sb = [sb("Sb%d" % i, [128, 64], BF16) for i in range(2)]
    psT = P.psum("hg_psT", [128, 128], BF16)
    psA = [P.psum("hg_psA%d" % i, [128, 128], F32) for i in range(2)]
    psO = [P.psum("hg_psO%d" % i, [128, 128], F32) for i in range(2)]
    psU = [P.psum("hg_psU%d" % i, [128, 128], F32) for i in range(3)]

    P.dma("sp", rmask, rmask[:], rm_d, rm_d[:])
    P.dma("sp", cmask, cmask[:], cm_d, cm_d[:])
    P.dma("sp", ident, ident[:], id_d, id_d[:])
    P.op("dve", lambda e: e.memset(S32[:], 0.0), [], [S32])
    P.op("dve", lambda e: e.memset(Sb[0][:], 0.0), [], [Sb[0]])
    b3 = bb[:, :].rearrange("p (n c) -> p n c", c=64)
    NP = HG_NST * 16
    state = {"prep_done": 0, "a_started": 0, "a_done": set(), "s_done": 0}

    def prep_thread():
        for st in range(HG_NST):
            while state["a_started"] < 16 * st - 8:
                yield
            t0 = st * HG_ST
            b = st % 2
            P.dma("sp", qs, qs[:], q_d, q_d[:, t0:t0 + HG_ST])
            P.dma("act", lf, lf[:], lf_d, lf_d[:, t0:t0 + HG_ST])
            P.dma("pool", v32, v32[:], i_d, i_d[t0:t0 + HG_ST, :].rearrange("(n i) c -> i n c", i=128))
            yield
            P.op("pool", lambda e: e.tensor_copy(out=vb[b][:], in_=v32[:]), [v32], [vb[b]])
            P.op("act", lambda e: e.activation(out=kf[:], in_=lf[:], func=AF.Exp), [lf], [kf])
            P.op("dve", lambda e: e.tensor_tensor_scan(out=bb[:], data0=rmask[:], data1=lf[:], initial=0.0, op0=ALU.mult, op1=ALU.add),
                 [rmask, lf], [bb])
            yield
            P.op("pool", lambda e: e.tensor_scalar(out=kf[:], in0=kf[:], scalar1=-1.0, scalar2=1.0, op0=ALU.mult, op1=ALU.add), [kf], [kf])
            bm = b3[:, :, 31:32].to_broadcast([128, 32, 64])
            bl = b3[:, :, 63:64].to_broadcast([128, 32, 64])
            t1v = t1[:, :].rearrange("p (n c) -> p n c", c=64)
            t2v = t2[:, :].rearrange("p (n c) -> p n c", c=64)
            P.op("dve", lambda e: e.tensor_tensor(out=t1v, in0=b3, in1=bm, op=ALU.subtract), [bb], [t1])
            yield
            P.op("pool", lambda e: e.tensor_scalar(out=t2[:], in0=t1[:], scalar1=-1.0, scalar2=HG_CL, op0=ALU.mult, op1=ALU.min), [t1], [t2])
            P.op("dve", lambda e: e.tensor_scalar(out=t1[:], in0=t1[:], scalar1=HG_CL, scalar2=None, op0=ALU.min), [t1], [t1])
            yield
            P.op("act", lambda e: e.activation(out=t1[:], in_=t1[:], func=AF.Exp), [t1], [t1])
            P.op("act", lambda e: e.activation(out=t2[:], in_=t2[:], func=AF.Exp), [t2], [t2])
            yield
            P.op("dve", lambda e: e.tensor_tensor(out=Qt[b][:], in0=qs[:], in1=t1[:], op=ALU.mult), [qs, t1], [Qt[b]])
            P.op("pool", lambda e: e.tensor_tensor(out=Kt[b][:], in0=kf[:], in1=t2[:], op=ALU.mult), [kf, t2], [Kt[b]])
            yield
            P.op("act", lambda e: e.activation(out=t1[:], in_=bb[:], func=AF.Exp), [bb], [t1])
            P.op("dve", lambda e: e.tensor_tensor(out=t2v, in0=bl, in1=b3, op=ALU.subtract), [bb], [t2])
            yield
            P.op("dve", lambda e: e.tensor_tensor(out=Qh[b][:], in0=qs[:], in1=t1[:], op=ALU.mult), [qs, t1], [Qh[b]])
            P.op("act", lambda e: e.activation(out=t2[:], in_=t2[:], func=AF.Exp), [t2], [t2])
            yield
            P.op("pool", lambda e: e.tensor_tensor(out=Kh[b][:], in0=kf[:], in1=t2[:], op=ALU.mult), [kf, t2], [Kh[b]])
            P.op("act", lambda e: e.activation(out=dch[b][:], in_=b3[:, :, 63], func=AF.Exp), [bb], [dch[b]])
            state["prep_done"] = st + 1
            yield

    def a_thread():
        for gp in range(NP):
            st, pr = gp // 16, gp % 16
            while state["prep_done"] <= st or state["s_done"] < gp - (NSETA - 2):
                yield
            state["a_started"] = gp
            b = st % 2
            c0 = pr * 128
            kh_t = KhT[gp % NSETA]; am = Am[gp % NSETA]; p_a = psA[gp % 2]; p_u = psU[gp % 3]
            P.op("pe", lambda e: e.transpose(out=psT[:], in_=Kh[b][:, c0:c0 + 128], identity=ident[:]), [Kh[b], ident], [psT])
            P.op("pe", lambda e: e.matmul(p_a[:], lhsT=Kt[b][:, c0:c0 + 128], rhs=Qt[b][:, c0:c0 + 128], start=True, stop=True),
                 [Kt[b], Qt[b]], [p_a])
            yield
            P.op("act", lambda e: e.activation(out=kh_t[:], in_=psT[:], func=AF.Copy), [psT], [kh_t])
            P.op("dve", lambda e: e.tensor_tensor(out=am[:], in0=p_a[:], in1=cmask[:], op=ALU.mult), [p_a, cmask], [am])
            yield
            state["a_done"].add(gp)
            yield

    def s_thread():
        sbi = 0
        for gp in range(NP):
            while gp not in state["a_done"]:
                yield
            st, pr = gp // 16, gp % 16
            b = st % 2
            c0 = pr * 128
            am = Am[gp % NSETA]; p_o = psO[gp % 2]; kh_t = KhT[gp % NSETA]
            for ch in range(2):
                r0 = ch * 64
                p_u = psU[(2 * gp + ch) % 3]
                P.op("pe", lambda e: e.matmul(p_u[:, 0:64], lhsT=kh_t[r0:r0 + 64, :], rhs=vb[b][r0:r0 + 64, pr, :], start=True, stop=True), [kh_t, vb[b]], [p_u])
                s_cur = Sb[sbi % 2]; s_nxt = Sb[(sbi + 1) % 2]
                sbi += 1
                cidx = pr * 2 + ch
                P.op("pe", lambda e: e.matmul(p_o[0:64, r0:r0 + 64], lhsT=vb[b][r0:r0 + 64, pr, :], rhs=am[r0:r0 + 64, r0:r0 + 64],
                                              start=True, stop=False), [vb[b], am], [p_o])
                P.op("pe", lambda e: e.matmul(p_o[0:64, r0:r0 + 64], lhsT=s_cur[:], rhs=Qh[b][:, c0 + r0:c0 + r0 + 64],
                                              start=False, stop=True), [s_cur, Qh[b]], [p_o])
                P.op("dve", lambda e: e.scalar_tensor_tensor(out=s_nxt[:], in0=S32[:], scalar=dch[b][:, cidx:cidx + 1], in1=p_u[:, 0:64],
                                                             op0=ALU.mult, op1=ALU.add), [S32, dch[b], p_u], [s_nxt])
                P.op("dve", lambda e: e.scalar_tensor_tensor(out=S32[:], in0=S32[:], scalar=dch[b][:, cidx:cidx + 1], in1=p_u[:, 0:64],
                                                             op0=ALU.mult, op1=ALU.add), [S32, dch[b], p_u], [S32])
                yield
            P.op("act", lambda e: e.activation(out=ost[b][:, c0:c0 + 128], in_=p_o[0:64, :], func=AF.Copy), [p_o], [ost[b]])
            state["s_done"] = gp + 1
            if pr == 15:
                P.dma("sp", o_d, o_d[:, st * HG_ST:(st + 1) * HG_ST], ost[b], ost[b][:], owner=ost[b], disjoint=True)
            yield

    threads = [prep_thread(), a_thread(), s_thread()]
    guard = 0
    while threads:
        for g in list(threads):
            try:
                next(g)
            except StopIteration:
                threads.remove(g)
        guard += 1
        assert guard < 200000, "scheduler stuck"
    return [o_d]


def hg_consts():
    t = np.arange(HG_ST)
    rm = np.tile(((t % 64) != 0).astype(np.float32)[None, :], (128, 1))
    j = np.arange(128)[:, None]; i = np.arange(128)[None, :]
    cmk = ((j // 64 == i // 64) & (j <= i)).astype(np.float32)
    return {"hg_rmask": rm, "hg_cmask": cmk, "hg_ident": np.eye(128).astype(ml_dtypes.bfloat16)}


def hg_inputs(core, qs_fm, lf_fm, i_tm):
    hd, vh = core // 2, core % 2
    m = {"hg_q": np.ascontiguousarray(qs_fm[hd * 128:(hd + 1) * 128]),
         "hg_lf": np.ascontiguousarray(lf_fm[hd * 128:(hd + 1) * 128]),
         "hg_i": np.ascontiguousarray(i_tm[:, hd * 128 + vh * 64: hd * 128 + (vh + 1) * 64])}
    m.update(hg_consts())
    return m


RW_SEQ = 16384
RW_ST = 2048
RW_NST = RW_SEQ // RW_ST
RW_C = 128
RW_NCH = RW_ST // RW_C


def emit_rwkv(P, KCH=2, NSET=4, CHDT=BF16):
    r_d = P.dram("rw_r", [64, RW_SEQ], F32, "ExternalInput")
    kp_d = P.dram("rw_kp", [64, RW_SEQ], F32, "ExternalInput")
    kk_d = P.dram("rw_kk", [64, RW_SEQ], F32, "ExternalInput")
    a_d = P.dram("rw_a", [64, RW_SEQ], F32, "ExternalInput")
    ld_d = P.dram("rw_ld", [64, RW_SEQ], F32, "ExternalInput")
    v_d = P.dram("rw_v", [32, RW_SEQ], F32, "ExternalInput")
    rm_d = P.dram("rw_rmask", [64, RW_ST], F32, "ExternalInput")
    mk_d = P.dram("rw_masks", [128, 4 * 128], F32, "ExternalInput")
    id_d = P.dram("rw_ident", [128, 128], BF16, "ExternalInput")
    y_d = P.dram("rw_y", [32, RW_SEQ], F32, "ExternalOutput")

    def sb(name, shape, dt=F32):
        return P.sbuf("rws_" + name, shape, dt)
    r_s, kp_s, kk_s, a_s, ld_s = [sb(n, [64, RW_ST]) for n in ("r", "kp", "kk", "a", "ld")]
    v_s = sb("v", [32, RW_ST])
    G = sb("G", [64, RW_ST]); x1 = sb("x1", [64, RW_ST]); x2 = sb("x2", [64, RW_ST]); x3 = sb("x3", [64, RW_ST])
    dC = [sb("dC%d" % i, [64, RW_NCH]) for i in range(2)]
    AR = [sb("AR%d" % i, [64, RW_NCH, 2, RW_C], BF16) for i in range(2)]
    Bt = [sb("Bt%d" % i, [64, RW_ST], BF16) for i in range(2)]
    Kt = [sb("Kt%d" % i, [64, RW_ST], BF16) for i in range(2)]
    Bh = [sb("Bh%d" % i, [64, RW_ST], BF16) for i in range(2)]
    Kh = [sb("Kh%d" % i, [64, RW_ST], BF16) for i in range(2)]
    vb = [sb("vb%d" % i, [32, RW_ST], BF16) for i in range(2)]
    yst = [sb("yst%d" % i, [32, RW_ST]) for i in range(2)]
    rmask = sb("rmask", [64, RW_ST])
    masks = sb("masks", [128, 4 * 128])
    ident = sb("ident", [128, 128], BF16)
    tok = [sb("tok%d" % i, [128, 160], BF16) for i in range(NSET)]
    XTb = [sb("XTb%d" % i, [128, 128], BF16) for i in range(NSET)]
    Arb = [sb("Arb%d" % i, [128, 128], BF16) for i in range(NSET)]
    Ak = [sb("Ak%d" % i, [128, 256], BF16) for i in range(NSET)]
    Mf = [[sb("Mf%d_%d" % (k, i), [128, 128], CHDT) for i in range(2)] for k in range(KCH)]
    Nf = [[sb("Nf%d_%d" % (k, i), [128, 128], CHDT) for i in range(2)] for k in range(KCH)]
    XT = [[sb("XT%d_%d" % (k, i), [128, 128], CHDT) for i in range(2)] for k in range(KCH)]
    H32 = sb("H32", [64, 32])
    Hb = [sb("Hb%d" % i, [64, 32], BF16) for i in range(2)]
    Wb = [sb("Wb%d" % i, [128, 32], BF16) for i in range(2)]
    Ub = [sb("Ub%d" % i, [128, 32], BF16) for i in range(2)]
    bT = P.psum("rw_bT", [128, 1024], BF16)
    bA = [P.psum("rw_bA%d" % k, [128, 512], F32) for k in range(KCH)]
    bC1 = [P.psum("rw_bC1_%d" % k, [128, 512], F32) for k in range(KCH)]
    bC2 = [P.psum("rw_bC2_%d" % k, [128, 512], F32) for k in range(KCH)]
    bS = P.psum("rw_bS", [128, 512], F32)

    mSU = masks[:, 0:128]; mIU = masks[:, 128:256]; mSL = masks[:, 256:384]; mI = masks[:, 384:512]
    P.dma("sp", rmask, rmask[:], rm_d, rm_d[:])
    P.dma("sp", masks, masks[:], mk_d, mk_d[:])
    P.dma("sp", ident, ident[:], id_d, id_d[:])
    P.op("dve", lambda e: e.memset(H32[:], 0.0), [], [H32])
    P.op("dve", lambda e: e.memset(Hb[0][:], 0.0), [], [Hb[0]])
    G3 = G[:, :].rearrange("p (n c) -> p n c", c=RW_C)
    x13 = x1[:, :].rearrange("p (n c) -> p n c", c=RW_C)
    x23 = x2[:, :].rearrange("p (n c) -> p n c", c=RW_C)
    NCHUNK = RW_NST * RW_NCH
    state = {"prep_done": 0, "inv_started": 0, "inv_done": set(), "state_done": 0}

    def prep_thread():
        for st in range(RW_NST):
            while state["inv_started"] < RW_NCH * st - 10:
                yield
            t0 = st * RW_ST
            b = st % 2
            for i, (s_, d_) in enumerate(((r_s, r_d), (kp_s, kp_d), (kk_s, kk_d), (a_s, a_d), (ld_s, ld_d))):
                P.dma(("sp", "act", "pool")[i % 3], s_, s_[:], d_, d_[:, t0:t0 + RW_ST])
            P.dma("sp", v_s, v_s[:], v_d, v_d[:, t0:t0 + RW_ST])
            yield
            P.op("pool", lambda e: e.tensor_copy(out=vb[b][:], in_=v_s[:]), [v_s], [vb[b]])
            P.op("dve", lambda e: e.tensor_tensor_scan(out=G[:], data0=rmask[:], data1=ld_s[:], initial=0.0, op0=ALU.mult, op1=ALU.add),
                 [rmask, ld_s], [G])
            yield
            Gl = G3[:, :, RW_C - 1:RW_C].to_broadcast([64, RW_NCH, RW_C])
            P.op("dve", lambda e: e.tensor_tensor(out=x1[:], in0=G[:], in1=ld_s[:], op=ALU.subtract), [G, ld_s], [x1])
            P.op("act", lambda e: e.activation(out=x1[:], in_=x1[:], func=AF.Exp), [x1], [x1])
            yield
            P.op("dve", lambda e: e.scalar_tensor_tensor(out=AR[b][:, :, 0, :], in0=x13, scalar=-1.0,
                                                         in1=kk_s[:, :].rearrange("p (n c) -> p n c", c=RW_C),
                                                         op0=ALU.mult, op1=ALU.mult), [x1, kk_s], [AR[b]])
            P.op("act", lambda e: e.activation(out=x2[:], in_=G[:], func=AF.Exp), [G], [x2])
            yield
            P.op("pool", lambda e: e.tensor_tensor(out=AR[b][:, :, 1, :], in0=x23, in1=r_s[:, :].rearrange("p (n c) -> p n c", c=RW_C),
                                                   op=ALU.mult), [x2, r_s], [AR[b]])
            P.op("pool", lambda e: e.tensor_tensor(out=x3[:], in0=kk_s[:], in1=a_s[:], op=ALU.mult), [kk_s, a_s], [x3])
            P.op("act", lambda e: e.activation(out=x1[:], in_=G[:], func=AF.Exp, scale=-1.0), [G], [x1])
            yield
            P.op("dve", lambda e: e.tensor_tensor(out=Bt[b][:], in0=x3[:], in1=x1[:], op=ALU.mult), [x3, x1], [Bt[b]])
            P.op("pool", lambda e: e.tensor_tensor(out=Kt[b][:], in0=kp_s[:], in1=x1[:], op=ALU.mult), [kp_s, x1], [Kt[b]])
            yield
            P.op("dve", lambda e: e.tensor_tensor(out=x23, in0=Gl, in1=G3, op=ALU.subtract), [G], [x2])
            P.op("act", lambda e: e.activation(out=x2[:], in_=x2[:], func=AF.Exp), [x2], [x2])
            yield
            P.op("dve", lambda e: e.tensor_tensor(out=Bh[b][:], in0=x3[:], in1=x2[:], op=ALU.mult), [x3, x2], [Bh[b]])
            P.op("pool", lambda e: e.tensor_tensor(out=Kh[b][:], in0=kp_s[:], in1=x2[:], op=ALU.mult), [kp_s, x2], [Kh[b]])
            P.op("act", lambda e: e.activation(out=dC[b][:], in_=G3[:, :, RW_C - 1], func=AF.Exp), [G], [dC[b]])
            state["prep_done"] = st + 1
            yield

    def inv_thread(k):
        for gn in range(k, NCHUNK, KCH):
            st, n = gn // RW_NCH, gn % RW_NCH
            while state["prep_done"] <= st or state["state_done"] < gn - (NSET - 1):
                yield
            state["inv_started"] = max(state["inv_started"], gn)
            b = st % 2
            c0 = n * RW_C
            s_ = gn % NSET
            tk, xtb, arb, ak = tok[s_], XTb[s_], Arb[s_], Ak[s_]
            pT = bT; BA = bA[k]; B1 = bC1[k]; B2 = bC2[k]
            P.op("pe", lambda e: e.transpose(out=pT[:, k * 160:k * 160 + 64], in_=Bh[b][:, c0:c0 + RW_C], identity=ident[0:64, 0:64]), [Bh[b], ident], [pT])
            P.op("pe", lambda e: e.transpose(out=pT[:, k * 160 + 64:k * 160 + 128], in_=Kh[b][:, c0:c0 + RW_C], identity=ident[0:64, 0:64]), [Kh[b], ident], [pT])
            P.op("pe", lambda e: e.transpose(out=pT[:, k * 160 + 128:k * 160 + 160], in_=vb[b][:, c0:c0 + RW_C], identity=ident[0:32, 0:32]), [vb[b], ident], [pT])
            arv = AR[b][:, n, :, :].rearrange("p a c -> p (a c)")
            P.op("pe", lambda e: e.matmul(BA[:, 0:256], lhsT=Bt[b][:, c0:c0 + RW_C], rhs=arv, start=True, stop=True), [Bt[b], AR[b]], [BA])
            P.op("pe", lambda e: e.matmul(BA[:, 256:512], lhsT=Kt[b][:, c0:c0 + RW_C], rhs=arv, start=True, stop=True), [Kt[b], AR[b]], [BA])
            P.op("pe", lambda e: e.matmul(B2[:, 128:256], lhsT=AR[b][:, n, 0, :], rhs=Bt[b][:, c0:c0 + RW_C], start=True, stop=True), [AR[b], Bt[b]], [B2])
            yield
            mf, nf, xt = Mf[k][0], Nf[k][0], XT[k][0]
            P.op("act", lambda e: e.activation(out=tk[:], in_=pT[:, k * 160:(k + 1) * 160], func=AF.Copy), [pT], [tk])
            P.op("dve", lambda e: e.tensor_tensor(out=mf[:], in0=BA[:, 0:128], in1=mSU, op=ALU.mult), [BA, masks], [mf])
            P.op("dve", lambda e: e.tensor_tensor(out=nf[:], in0=B2[:, 128:256], in1=mSL, op=ALU.mult), [B2, masks], [nf])
            P.op("pool", lambda e: e.tensor_tensor(out=xt[:], in0=mf[:], in1=mI, op=ALU.add), [mf, masks], [xt])
            yield
            P.op("dve", lambda e: e.tensor_tensor(out=arb[:], in0=BA[:, 128:256], in1=mIU, op=ALU.mult), [BA, masks], [arb])
            P.op("dve", lambda e: e.tensor_tensor(out=ak[:], in0=BA[:, 256:512], in1=masks[:, 0:256], op=ALU.mult), [BA, masks], [ak])
            cur = 0
            for it in range(6):
                mo, no = Mf[k][cur], Nf[k][cur]
                mn, nn = Mf[k][1 - cur], Nf[k][1 - cur]
                xo, xn = XT[k][cur], XT[k][1 - cur]
                P.op("pe", lambda e: e.matmul(B1[:, 0:128], lhsT=mo[:], rhs=no[:], start=True, stop=True), [mo, no], [B1])
                if it < 5:
                    P.op("pe", lambda e: e.matmul(B2[:, 128:256], lhsT=no[:], rhs=mo[:], start=True, stop=True), [mo, no], [B2])
                yield
                P.op("act", lambda e: e.activation(out=nn[:], in_=B1[:, 0:128], func=AF.Copy), [B1], [nn])
                if it < 5:
                    P.op("dve", lambda e: e.tensor_copy(out=mn[:], in_=B2[:, 128:256]), [B2], [mn])
                yield
                P.op("pe", lambda e: e.matmul(BA[:, 0:128], lhsT=nn[:], rhs=xo[:], start=True, stop=True), [nn, xo], [BA])
                P.op("dve", lambda e: e.tensor_tensor(out=xn[:], in0=BA[:, 0:128], in1=xo[:], op=ALU.add), [BA, xo], [xn])
                cur = 1 - cur
            P.op("pool", lambda e: e.tensor_copy(out=xtb[:], in_=XT[k][cur][:]), [XT[k][cur]], [xtb])
            state["inv_done"].add(gn)
            yield

    def state_thread():
        hbi = 0
        for gn in range(NCHUNK):
            while gn not in state["inv_done"]:
                yield
            st, n = gn // RW_NCH, gn % RW_NCH
            b = st % 2
            c0 = n * RW_C
            s_ = gn % NSET
            tk, xtb, arb, ak = tok[s_], XTb[s_], Arb[s_], Ak[s_]
            hb_cur = Hb[hbi % 2]; hb_nxt = Hb[(hbi + 1) % 2]; wb = Wb[hbi % 2]; ub = Ub[hbi % 2]
            hbi += 1
            P.op("pe", lambda e: e.matmul(bS[:, 0:32], lhsT=AR[b][:, n, 0, :], rhs=hb_cur[:], start=True, stop=False), [AR[b], hb_cur], [bS])
            P.op("pe", lambda e: e.matmul(bS[:, 0:32], lhsT=ak[:, 0:128], rhs=tk[:, 128:160], start=False, stop=True), [ak, tk], [bS])
            P.op("act", lambda e: e.activation(out=wb[:], in_=bS[:, 0:32], func=AF.Copy), [bS], [wb])
            yield
            P.op("pe", lambda e: e.matmul(bS[:, 32:64], lhsT=xtb[:], rhs=wb[:], start=True, stop=True), [xtb, wb], [bS])
            P.op("act", lambda e: e.activation(out=ub[:], in_=bS[:, 32:64], func=AF.Copy), [bS], [ub])
            yield
            P.op("pe", lambda e: e.matmul(bS[0:64, 64:96], lhsT=tk[:, 0:64], rhs=ub[:], start=True, stop=False), [tk, ub], [bS])
            P.op("pe", lambda e: e.matmul(bS[0:64, 64:96], lhsT=tk[:, 64:128], rhs=tk[:, 128:160], start=False, stop=True), [tk], [bS])
            P.op("pe", lambda e: e.matmul(bS[0:32, 128:256], lhsT=hb_cur[:], rhs=AR[b][:, n, 1, :], start=True, stop=False), [hb_cur, AR[b]], [bS])
            P.op("pe", lambda e: e.matmul(bS[0:32, 128:256], lhsT=ub[:], rhs=arb[:], start=False, stop=False), [ub, arb], [bS])
            P.op("pe", lambda e: e.matmul(bS[0:32, 128:256], lhsT=tk[:, 128:160], rhs=ak[:, 128:256], start=False, stop=True), [tk, ak], [bS])
            P.op("dve", lambda e: e.scalar_tensor_tensor(out=H32[:], in0=H32[:], scalar=dC[b][:, n:n + 1], in1=bS[0:64, 64:96],
                                                         op0=ALU.mult, op1=ALU.add), [H32, dC[b], bS], [H32])
            P.op("pool", lambda e: e.tensor_copy(out=hb_nxt[:], in_=H32[:]), [H32], [hb_nxt])
            P.op("act", lambda e: e.activation(out=yst[b][:, c0:c0 + RW_C], in_=bS[0:32, 128:256], func=AF.Copy), [bS], [yst[b]])
            state["state_done"] = gn + 1
            if n == RW_NCH - 1:
                P.dma("sp", y_d, y_d[:, st * RW_ST:(st + 1) * RW_ST], yst[b], yst[b][:], owner=yst[b], disjoint=True)
            yield

    threads = [prep_thread()] + [inv_thread(k) for k in range(KCH)] + [state_thread()]
    guard = 0
    while threads:
        for g in list(threads):
            try:
                next(g)
            except StopIteration:
                threads.remove(g)
        guard += 1
        assert guard < 200000, "scheduler stuck"
    return [y_d]


def rw_consts():
    t = np.arange(RW_ST)
    rm = np.tile(((t % RW_C) != 0).astype(np.float32)[None, :], (64, 1))
    j = np.arange(128)[:, None]; i = np.arange(128)[None, :]
    su = (i > j).astype(np.float32); iu = (i >= j).astype(np.float32); sl = (j > i).astype(np.float32)
    masks = np.concatenate([su, iu, sl, np.eye(128, dtype=np.float32)], axis=1)
    return {"rw_rmask": rm, "rw_masks": masks, "rw_ident": np.eye(128).astype(ml_dtypes.bfloat16)}


def rw_inputs(core, fmr):
    hd, vh = core // 2, core % 2
    m = {"rw_" + n: np.ascontiguousarray(fmr[n][hd * 64:(hd + 1) * 64]) for n in ("r", "kp", "kk", "a", "ld")}
    m["rw_v"] = np.ascontiguousarray(fmr["v"][hd * 64 + vh * 32: hd * 64 + (vh + 1) * 32])
    m.update(rw_consts())
    return m


C1_NT = 2048
C1_TG = 512
C1_NG = C1_NT // C1_TG
CF_HGO, CF_GS, CF_SWO, CF_RWY, CF_R, CF_KP, CF_V, CF_G = 0, 512, 1024, 1280, 1536, 1792, 2048, 2304
CF_ROWS = 2560
CP_GN, CP_LNW, CP_LNB, CP_RK, CP_N = 0, 4, 6, 8, 10


def emit_c1(P, layer):
    h_d = P.dram("c1_h", [C1_NT, 1024], F32, "ExternalInput")
    cf_d = P.dram("c1_cf", [CF_ROWS, C1_NT], F32, "ExternalInput")
    wo_d = P.dram("c1_wout", [1024, 1024], F32, "ExternalInput")
    cp_d = P.dram("c1_cp", [128, CP_N], F32, "ExternalInput")
    k_d = P.dram("c1_consts", [128, 256], F32, "ExternalInput")
    hm_d = P.dram("c1_hmid", [C1_NT, 1024], F32, "ExternalOutput")

    def sb(name, shape, dt=F32):
        return P.sbuf("c1s_" + name, shape, dt)
    wob = sb("wob", [128, 8, 1024], BF16)
    wst = [sb("wst%d" % i, [128, 1024]) for i in range(2)]
    cp = sb("cp", [128, CP_N])
    kc = sb("kc", [128, 256])
    eps1 = sb("eps1", [128, 1]); eps2 = sb("eps2", [128, 1])
    oT = [sb("oT%d" % i, [128, 8, C1_TG], BF16) for i in range(2)]
    fin = [sb("fin%d" % i, [128, C1_TG]) for i in range(6)]
    tmp = [sb("tmp%d" % i, [128, C1_TG]) for i in range(6)]
    hin = [sb("hin%d" % i, [128, 1024]) for i in range(2)]
    ps = [P.psum("c1_ps%d" % i, [128, 512], F32) for i in range(4)]
    pd = [P.psum("c1_pd%d" % i, [128, 1024], F32) for i in range(2)]

    ones_m = kc[:, 0:128]; blk_m = kc[:, 128:256]
    P.dma("sp", cp, cp[:], cp_d, cp_d[:])
    P.dma("sp", kc, kc[:], k_d, k_d[:])
    P.op("dve", lambda e: e.memset(eps1[:], 1e-6), [], [eps1])
    P.op("dve", lambda e: e.memset(eps2[:], 64e-5), [], [eps2])
    for k in range(8):
        s_ = wst[k % 2]
        P.dma("sp" if k % 2 == 0 else "act", s_, s_[:], wo_d, wo_d[k * 128:(k + 1) * 128, :])
        P.op("pool", lambda e: e.tensor_copy(out=wob[:, k, :], in_=s_[:]), [s_], [wob])

    fi = [0]; ti = [0]; pi = [0]; qi = [0]

    def load(row0, t0):
        f = fin[fi[0] % 6]; fi[0] += 1
        q = ("sp", "act", "pool")[qi[0] % 3]; qi[0] += 1
        P.dma(q, f, f[:], cf_d, cf_d[row0:row0 + 128, t0:t0 + C1_TG])
        return f

    def T_():
        t = tmp[ti[0] % 6]; ti[0] += 1
        return t

    def PS():
        p = ps[pi[0] % 4]; pi[0] += 1
        return p

    for g in range(C1_NG):
        t0 = g * C1_TG
        ot = oT[g % 2]
        for c in range(4):
            o = load(CF_HGO + c * 128, t0); gs = load(CF_GS + c * 128, t0)
            sq = T_(); p_ = PS(); rs = T_()
            P.op("pool", lambda e: e.tensor_tensor(out=sq[:], in0=o[:], in1=o[:], op=ALU.mult), [o], [sq])
            P.op("pe", lambda e: e.matmul(p_[:], lhsT=ones_m, rhs=sq[:], start=True, stop=True), [kc, sq], [p_])
            P.op("act", lambda e: e.activation(out=rs[:], in_=p_[:], func=AF.Sqrt, bias=eps1[:, 0:1]), [p_, eps1], [rs])
            P.op("dve", lambda e: e.reciprocal(out=rs[:], in_=rs[:]), [rs], [rs])
            P.op("dve", lambda e: e.scalar_tensor_tensor(out=rs[:], in0=rs[:], scalar=cp[:, CP_GN + c:CP_GN + c + 1], in1=o[:],
                                                         op0=ALU.mult, op1=ALU.mult), [rs, cp, o], [rs])
            P.op("pool", lambda e: e.tensor_tensor(out=ot[:, c, :], in0=rs[:], in1=gs[:], op=ALU.mult), [rs, gs], [ot])
        for c in range(2):
            o = load(CF_SWO + c * 128, t0)
            P.op("pool", lambda e: e.tensor_copy(out=ot[:, 4 + c, :], in_=o[:]), [o], [ot])
        for c in range(2):
            y = load(CF_RWY + c * 128, t0); r_ = load(CF_R + c * 128, t0); kp = load(CF_KP + c * 128, t0)
            v_ = load(CF_V + c * 128, t0); g_ = load(CF_G + c * 128, t0)
            pm = PS(); pq = PS(); pb = PS()
            ysq = T_(); mean = T_(); var = T_(); rk = T_()
            P.op("pe", lambda e: e.matmul(pm[:], lhsT=blk_m, rhs=y[:], start=True, stop=True), [kc, y], [pm])
            P.op("pool", lambda e: e.tensor_tensor(out=ysq[:], in0=y[:], in1=y[:], op=ALU.mult), [y], [ysq])
            P.op("pe", lambda e: e.matmul(pq[:], lhsT=blk_m, rhs=ysq[:], start=True, stop=True), [kc, ysq], [pq])
            P.op("act", lambda e: e.activation(out=mean[:], in_=pm[:], func=AF.Copy), [pm], [mean])
            P.op("pool", lambda e: e.tensor_tensor(out=var[:], in0=mean[:], in1=mean[:], op=ALU.mult), [mean], [var])
            P.op("dve", lambda e: e.tensor_tensor(out=var[:], in0=pq[:], in1=var[:], op=ALU.subtract), [pq, var], [var])
            P.op("act", lambda e: e.activation(out=var[:], in_=var[:], func=AF.Sqrt, bias=eps2[:, 0:1]), [var, eps2], [var])
            P.op("dve", lambda e: e.reciprocal(out=var[:], in_=var[:]), [var], [var])
            P.op("dve", lambda e: e.tensor_tensor(out=mean[:], in0=y[:], in1=mean[:], op=ALU.subtract), [y, mean], [mean])
            P.op("dve", lambda e: e.tensor_tensor(out=mean[:], in0=mean[:], in1=var[:], op=ALU.mult), [mean, var], [mean])
            P.op("dve", lambda e: e.tensor_scalar(out=mean[:], in0=mean[:], scalar1=cp[:, CP_LNW + c:CP_LNW + c + 1],
                                                   scalar2=cp[:, CP_LNB + c:CP_LNB + c + 1], op0=ALU.mult, op1=ALU.add), [mean, cp], [mean])
            P.op("dve", lambda e: e.scalar_tensor_tensor(out=rk[:], in0=r_[:], scalar=cp[:, CP_RK + c:CP_RK + c + 1], in1=kp[:],
                                                         op0=ALU.mult, op1=ALU.mult), [r_, cp, kp], [rk])
            P.op("pe", lambda e: e.matmul(pb[:], lhsT=blk_m, rhs=rk[:], start=True, stop=True), [kc, rk], [pb])
            P.op("dve", lambda e: e.scalar_tensor_tensor(out=rk[:], in0=pb[:], scalar=64.0, in1=v_[:], op0=ALU.mult, op1=ALU.mult),
                 [pb, v_], [rk])
            P.op("pool", lambda e: e.tensor_tensor(out=mean[:], in0=mean[:], in1=rk[:], op=ALU.add), [mean, rk], [mean])
            P.op("pool", lambda e: e.tensor_tensor(out=ot[:, 6 + c, :], in0=mean[:], in1=g_[:], op=ALU.mult), [mean, g_], [ot])
        for t in range(4):
            hi = hin[t % 2]; p_d = pd[t % 2]
            r0 = t0 + t * 128
            P.dma("sp", hi, hi[:], h_d, h_d[r0:r0 + 128, :])
            for half in range(2):
                for c in range(8):
                    P.op("pe", lambda e: e.matmul(p_d[:, half * 512:(half + 1) * 512], lhsT=ot[:, c, t * 128:(t + 1) * 128],
                                                  rhs=wob[:, c, half * 512:(half + 1) * 512], start=(c == 0), stop=(c == 7)),
                         [ot, wob], [p_d])
            for half in range(2):
                P.op("dve", lambda e: e.tensor_tensor(out=hi[:, half * 512:(half + 1) * 512], in0=hi[:, half * 512:(half + 1) * 512],
                                                      in1=p_d[:, half * 512:(half + 1) * 512], op=ALU.add), [hi, p_d], [hi])
            P.dma("act", hm_d, hm_d[r0:r0 + 128, :], hi, hi[:], owner=hi, disjoint=True)
    return [hm_d]


def c1_inputs(layer, inp, core, h_full, cf_full):
    l = layer
    fmj = lambda v: np.ascontiguousarray(v.reshape(-1, 128).T)
    cp = np.zeros((128, CP_N), np.float32)
    cp[:, CP_GN:CP_GN + 4] = fmj(inp['hgrn_gnorm_g'][l])
    cp[:, CP_LNW:CP_LNW + 2] = fmj(inp['rwkv_ln_w'][l])
    cp[:, CP_LNB:CP_LNB + 2] = fmj(inp['rwkv_ln_b'][l])
    cp[:, CP_RK:CP_RK + 2] = fmj(inp['rwkv_r_k'][l].reshape(-1))
    kc = np.concatenate([np.full((128, 128), 1.0 / 128), np.kron(np.eye(2), np.full((64, 64), 1.0 / 64))], axis=1).astype(np.float32)
    return {"c1_h": np.ascontiguousarray(h_full[core * C1_NT:(core + 1) * C1_NT]),
            "c1_cf": np.ascontiguousarray(cf_full[:, core * C1_NT:(core + 1) * C1_NT]),
            "c1_wout": np.ascontiguousarray(inp['w_out'][l]), "c1_cp": cp, "c1_consts": kc}


C2_NT = 2048
C2_GT = 256
C2_NGR = C2_NT // C2_GT
C2_DFF = 2816
C2_NJ = C2_DFF // 128


def norm_transpose(P, src_dram_t, src_ap, hres, hnb, st, junk, epsc, ident, pst, dstT, col0, dma_q="sp", gB=None):
    P.dma(dma_q, hres, hres[:], src_dram_t, src_ap)
    P.op("act", lambda e: e.activation(out=junk[:], in_=hres[:], func=AF.Square, accum_out=st[:, 0:1]), [hres], [junk, st])
    P.op("act", lambda e: e.activation(out=st[:, 1:2], in_=st[:, 0:1], func=AF.Sqrt, scale=1.0 / 1024, bias=epsc[:, 0:1]), [st, epsc], [st])
    P.op("dve", lambda e: e.reciprocal(out=st[:, 2:3], in_=st[:, 1:2]), [st], [st])
    if gB is None:
        P.op("dve", lambda e: e.tensor_scalar(out=hnb[:], in0=hres[:], scalar1=st[:, 2:3], scalar2=None, op0=ALU.mult), [hres, st], [hnb])
    else:
        P.op("dve", lambda e: e.scalar_tensor_tensor(out=hnb[:], in0=hres[:], scalar=st[:, 2:3], in1=gB[:], op0=ALU.mult, op1=ALU.mult),
             [hres, st, gB], [hnb])
    for c in range(8):
        P.op("pe", lambda e: e.transpose(out=pst[:, c * 128:(c + 1) * 128], in_=hnb[:, c * 128:(c + 1) * 128], identity=ident[:]),
             [hnb, ident], [pst])
    P.op("act", lambda e: e.activation(out=dstT[:, :, col0:col0 + 128], in_=pst[:].rearrange("p (c t) -> p c t", c=8), func=AF.Copy),
         [pst], [dstT])


def emit_c2(P):
    h_d = P.dram("c2_h", [C2_NT, 1024], F32, "ExternalInput")
    hh_d = P.dram("c2_hh", [128, 1024], F32, "ExternalInput")
    up_d = P.dram("c2_up", [1024, 2 * C2_DFF], F32, "ExternalInput")
    dn_d = P.dram("c2_dn", [C2_DFF, 1024], F32, "ExternalInput")
    pp_d = P.dram("c2_pp", [128, 8 + 44 * 4], F32, "ExternalInput")
    id_d = P.dram("c2_ident", [128, 128], BF16, "ExternalInput")
    gb_d = P.dram("c2_gB", [128, 1024], F32, "ExternalInput")
    o_d = P.dram("c2_out", [C2_NT, 1024], F32, "ExternalOutput")

    def sb(name, shape, dt=F32):
        return P.sbuf("c2s_" + name, shape, dt)
    upb = [sb("upb%d" % k, [128, 2 * C2_DFF], BF16) for k in range(8)]
    dnb = sb("dnb", [128, C2_NJ, 1024], BF16)
    pp = sb("pp", [128, 8 + 44 * 4])
    ident = sb("ident", [128, 128], BF16)
    epsc = sb("epsc", [128, 1])
    hres = [sb("hres%d" % i, [128, 1024]) for i in range(2)]
    hnb = [sb("hnb%d" % i, [128, 1024], BF16) for i in range(2)]
    st = [sb("st%d" % i, [128, 4]) for i in range(2)]
    junk = sb("junk", [128, 1024])
    gB = sb("gB", [128, 1024])
    wstg = [sb("wstg%d" % i, [128, 1024]) for i in range(4)]
    hnT = sb("hnT", [128, 8, C2_GT], BF16)
    ug = [sb("ug%d" % i, [128, C2_GT + 2]) for i in range(2)]
    uv = [sb("uv%d" % i, [128, C2_GT + 2]) for i in range(2)]
    tg = [sb("tg%d" % i, [128, C2_GT]) for i in range(2)]
    tv = [sb("tv%d" % i, [128, C2_GT]) for i in range(2)]
    actT = sb("actT", [128, C2_NJ, C2_GT], BF16)
    uprev = sb("uprev", [128, 44, 2])
    pst = P.psum("c2_pst", [128, 1024], BF16)
    pu = [P.psum("c2_pu%d" % i, [128, 512], F32) for i in range(3)]
    pd = [P.psum("c2_pd%d" % i, [128, 512], F32) for i in range(4)]

    P.dma("sp", pp, pp[:], pp_d, pp_d[:])
    P.dma("sp", ident, ident[:], id_d, id_d[:])
    P.op("dve", lambda e: e.memset(epsc[:], 1e-6), [], [epsc])
    P.dma("sp", gB, gB[:], gb_d, gb_d[:])
    wi = 0
    for k in range(8):
        for c0 in range(0, 2 * C2_DFF, 1024):
            w_ = min(1024, 2 * C2_DFF - c0)
            s_ = wstg[wi % 4]; wi += 1
            P.dma(("sp", "act")[wi % 2], s_, s_[:, 0:w_], up_d, up_d[k * 128:(k + 1) * 128, c0:c0 + w_])
            if wi % 2 == 0:
                P.op("dve", lambda e: e.tensor_copy(out=upb[k][:, c0:c0 + w_], in_=s_[:, 0:w_]), [s_], [upb[k]])
            else:
                P.op("pool", lambda e: e.tensor_copy(out=upb[k][:, c0:c0 + w_], in_=s_[:, 0:w_]), [s_], [upb[k]])
    for j in range(C2_NJ):
        s_ = wstg[wi % 4]; wi += 1
        P.dma(("sp", "act")[wi % 2], s_, s_[:], dn_d, dn_d[j * 128:(j + 1) * 128, :])
        if wi % 2 == 0:
            P.op("dve", lambda e: e.tensor_copy(out=dnb[:, j, :], in_=s_[:]), [s_], [dnb])
        else:
            P.op("pool", lambda e: e.tensor_copy(out=dnb[:, j, :], in_=s_[:]), [s_], [dnb])

    def cw(c, i):
        o = 8 + c * 4 + i
        return pp[:, o:o + 1]

    norm_transpose(P, hh_d, hh_d[:], hres[0], hnb[0], st[0], junk, epsc, ident, pst, hnT, 0, gB=gB)
    for c in range(44):
        p_ = pu[c % 3]
        for k in range(8):
            P.op("pe", lambda e: e.matmul(p_[:, 0:2], lhsT=upb[k][:, c * 128:(c + 1) * 128], rhs=hnT[:, k, 126:128],
                                          start=(k == 0), stop=(k == 7)), [upb[k], hnT], [p_])
        P.op("act", lambda e: e.activation(out=uprev[:, c, :], in_=p_[:, 0:2], func=AF.Copy), [p_], [uprev])

    ui = 0
    for g in range(C2_NGR):
        r0 = g * C2_GT
        for t in range(2):
            norm_transpose(P, h_d, h_d[r0 + t * 128:r0 + (t + 1) * 128, :], hres[t], hnb[t], st[t], junk, epsc, ident, pst, hnT, t * 128, gB=gB)
        for j in range(C2_NJ):
            cg, cv = j, C2_NJ + j
            p_ = pu[ui % 3]; u_g = ug[ui % 2]; u_v = uv[ui % 2]; t_g = tg[ui % 2]; t_v = tv[ui % 2]
            ui += 1
            for k in range(8):
                P.op("pe", lambda e: e.matmul(p_[:, 0:C2_GT], lhsT=upb[k][:, cg * 128:(cg + 1) * 128], rhs=hnT[:, k, :],
                                              start=(k == 0), stop=(k == 7)), [upb[k], hnT], [p_])
            for k in range(8):
                P.op("pe", lambda e: e.matmul(p_[:, C2_GT:2 * C2_GT], lhsT=upb[k][:, cv * 128:(cv + 1) * 128], rhs=hnT[:, k, :],
                                              start=(k == 0), stop=(k == 7)), [upb[k], hnT], [p_])
            P.op("pool", lambda e: e.tensor_copy(out=u_g[:, 0:2], in_=uprev[:, cg, :]), [uprev], [u_g])
            P.op("pool", lambda e: e.tensor_copy(out=u_v[:, 0:2], in_=uprev[:, cv, :]), [uprev], [u_v])
            P.op("act", lambda e: e.activation(out=u_g[:, 2:C2_GT + 2], in_=p_[:, 0:C2_GT], func=AF.Copy), [p_], [u_g])
            P.op("act", lambda e: e.activation(out=u_v[:, 2:C2_GT + 2], in_=p_[:, C2_GT:2 * C2_GT], func=AF.Copy), [p_], [u_v])
            P.op("pool", lambda e: e.tensor_copy(out=uprev[:, cg, :], in_=u_g[:, C2_GT:C2_GT + 2]), [u_g], [uprev])
            P.op("pool", lambda e: e.tensor_copy(out=uprev[:, cv, :], in_=u_v[:, C2_GT:C2_GT + 2]), [u_v], [uprev])
            for (u_, t_, c_) in ((u_g, t_g, cg), (u_v, t_v, cv)):
                P.op("dve", lambda e: e.tensor_scalar(out=t_[:], in0=u_[:, 2:C2_GT + 2], scalar1=cw(c_, 2), scalar2=cw(c_, 3),
                                                      op0=ALU.mult, op1=ALU.add), [u_, pp], [t_])
                P.op("dve", lambda e: e.scalar_tensor_tensor(out=t_[:], in0=u_[:, 1:C2_GT + 1], scalar=cw(c_, 1), in1=t_[:],
                                                             op0=ALU.mult, op1=ALU.add), [u_, pp, t_], [t_])
                P.op("dve", lambda e: e.scalar_tensor_tensor(out=t_[:], in0=u_[:, 0:C2_GT], scalar=cw(c_, 0), in1=t_[:],
                                                             op0=ALU.mult, op1=ALU.add), [u_, pp, t_], [t_])
            P.op("act", lambda e: e.activation(out=t_g[:], in_=t_g[:], func=AF.Silu), [t_g], [t_g])
            P.op("pool", lambda e: e.tensor_tensor(out=actT[:, j, :], in0=t_g[:], in1=t_v[:], op=ALU.mult), [t_g, t_v], [actT])
        for t in range(2):
            for half in range(2):
                p_d = pd[(2 * t + half) % 4]
                for j in range(C2_NJ):
                    P.op("pe", lambda e: e.matmul(p_d[:], lhsT=actT[:, j, t * 128:(t + 1) * 128], rhs=dnb[:, j, half * 512:(half + 1) * 512],
                                                  start=(j == 0), stop=(j == C2_NJ - 1)), [actT, dnb], [p_d])
                P.op("dve", lambda e: e.tensor_tensor(out=hres[t][:, half * 512:(half + 1) * 512], in0=hres[t][:, half * 512:(half + 1) * 512],
                                                      in1=p_d[:], op=ALU.add), [hres[t], p_d], [hres[t]])
            P.dma("act", o_d, o_d[r0 + t * 128:r0 + (t + 1) * 128, :], hres[t], hres[t][:], owner=hres[t], disjoint=True)
    return [o_d]


def c2_inputs(layer, inp, core, hmid_full):
    l = layer
    fmj = lambda v: np.ascontiguousarray(v.reshape(-1, 128).T)
    pp = np.zeros((128, 8 + 44 * 4), np.float32)
    pp[:, 0:8] = fmj(inp['norm_ffn_g'][l])
    cwb = np.concatenate([inp['ffn_conv_w'][l], inp['ffn_conv_b'][l][None]], axis=0)
    pp[:, 8:] = cwb.reshape(4, 44, 128).transpose(2, 1, 0).reshape(128, 176)
    return {"c2_h": np.ascontiguousarray(hmid_full[core * C2_NT:(core + 1) * C2_NT]),
            "c2_hh": np.ascontiguousarray(hmid_full[core * C2_NT - 128:core * C2_NT]) if core > 0 else np.zeros((128, 1024), np.float32),
            "c2_up": np.ascontiguousarray(inp['ffn_up'][l]), "c2_dn": np.ascontiguousarray(inp['ffn_down'][l]),
            "c2_pp": pp, "c2_ident": np.eye(128).astype(ml_dtypes.bfloat16),
            "c2_gB": np.ascontiguousarray(np.broadcast_to(inp['norm_ffn_g'][l][None, :], (128, 1024))).astype(np.float32)}


C3_NT = 2048


def emit_c3(P, final):
    h_d = P.dram("c3_h", [C3_NT, 1024], F32, "ExternalInput")
    pT_d = P.dram("c3_pT", [256, C3_NT], F32, "ExternalInput")
    gt_d = P.dram("c3_gate", [1024, 1024], F32, "ExternalInput")
    pj_d = P.dram("c3_proj", [256, 1024], F32, "ExternalInput")
    pp_d = P.dram("c3_pp", [128, 8], F32, "ExternalInput")
    id_d = P.dram("c3_ident", [128, 128], BF16, "ExternalInput")
    gb_d = P.dram("c3_gB", [128, 1024], F32, "ExternalInput")
    if final:
        gf_d = P.dram("c3_gfin", [128, 1024], F32, "ExternalInput")
    o_d = P.dram("c3_out", [C3_NT, 1024], F32, "ExternalOutput")

    def sb(name, shape, dt=F32):
        return P.sbuf("c3s_" + name, shape, dt)
    gtb = sb("gtb", [128, 8, 1024], BF16)
    pjb = sb("pjb", [128, 2, 1024], BF16)
    gB = sb("gB", [128, 1024])
    wst = [sb("wst%d" % i, [128, C3_NT]) for i in range(2)]
    pp = sb("pp", [128, 8])
    ident = sb("ident", [128, 128], BF16)
    epsc = sb("epsc", [128, 1])
    pTb = sb("pTb", [128, 2, C3_NT], BF16)
    hres = [sb("hres%d" % i, [128, 1024]) for i in range(2)]
    hnb = [sb("hnb%d" % i, [128, 1024], BF16) for i in range(2)]
    st = [sb("st%d" % i, [128, 4]) for i in range(2)]
    st2 = [sb("st2%d" % i, [128, 4]) for i in range(2)]
    junk = sb("junk", [128, 1024])
    hnT = [sb("hnT%d" % i, [128, 8, 128], BF16) for i in range(2)]
    sig = [sb("sig%d" % i, [128, 1024]) for i in range(2)]
    gfin = sb("gfin", [128, 1024]) if final else None
    pst = P.psum("c3_pst", [128, 1024], BF16)
    pg = [P.psum("c3_pg%d" % i, [128, 512], F32) for i in range(4)]
    pq = [P.psum("c3_pq%d" % i, [128, 512], F32) for i in range(2)]

    P.dma("sp", pp, pp[:], pp_d, pp_d[:])
    P.dma("sp", ident, ident[:], id_d, id_d[:])
    if final:
        P.dma("sp", gfin, gfin[:], gf_d, gf_d[:])
    P.op("dve", lambda e: e.memset(epsc[:], 1e-6), [], [epsc])
    P.dma("sp", gB, gB[:], gb_d, gb_d[:])
    wi = 0
    for k in range(8):
        s_ = wst[wi % 2]; wi += 1
        P.dma(("sp", "act")[wi % 2], s_, s_[:, 0:1024], gt_d, gt_d[k * 128:(k + 1) * 128, :])
        P.op("pool", lambda e: e.tensor_copy(out=gtb[:, k, :], in_=s_[:, 0:1024]), [s_], [gtb])
    for k in range(2):
        s_ = wst[wi % 2]; wi += 1
        P.dma(("sp", "act")[wi % 2], s_, s_[:, 0:1024], pj_d, pj_d[k * 128:(k + 1) * 128, :])
        P.op("pool", lambda e: e.tensor_copy(out=pjb[:, k, :], in_=s_[:, 0:1024]), [s_], [pjb])
    for c in range(2):
        s_ = wst[wi % 2]; wi += 1
        P.dma(("sp", "act")[wi % 2], s_, s_[:], pT_d, pT_d[c * 128:(c + 1) * 128, :])
        P.op("pool", lambda e: e.tensor_copy(out=pTb[:, c, :], in_=s_[:]), [s_], [pTb])

    for t in range(C3_NT // 128):
        i = t % 2
        r0 = t * 128
        hr = hres[i]; hT = hnT[i]; sg = sig[i]
        norm_transpose(P, h_d, h_d[r0:r0 + 128, :], hr, hnb[i], st[i], junk, epsc, ident, pst, hT, 0, gB=gB)
        for half in range(2):
            p_g = pg[(2 * t + half) % 4]; p_q = pq[half]
            for k in range(8):
                P.op("pe", lambda e: e.matmul(p_g[:], lhsT=hT[:, k, :], rhs=gtb[:, k, half * 512:(half + 1) * 512],
                                              start=(k == 0), stop=(k == 7)), [hT, gtb], [p_g])
            for c in range(2):
                P.op("pe", lambda e: e.matmul(p_q[:], lhsT=pTb[:, c, r0:r0 + 128], rhs=pjb[:, c, half * 512:(half + 1) * 512],
                                              start=(c == 0), stop=(c == 1)), [pTb, pjb], [p_q])
            hs = slice(half * 512, (half + 1) * 512)
            P.op("act", lambda e: e.activation(out=sg[:, hs], in_=p_g[:], func=AF.Sigmoid), [p_g], [sg])
            P.op("dve", lambda e: e.tensor_tensor(out=sg[:, hs], in0=sg[:, hs], in1=p_q[:], op=ALU.mult), [sg, p_q], [sg])
            P.op("pool", lambda e: e.tensor_tensor(out=hr[:, hs], in0=hr[:, hs], in1=sg[:, hs], op=ALU.add), [hr, sg], [hr])
        if final:
            s2 = st2[i]
            P.op("act", lambda e: e.activation(out=junk[:], in_=hr[:], func=AF.Square, accum_out=s2[:, 0:1]), [hr], [junk, s2])
            P.op("act", lambda e: e.activation(out=s2[:, 1:2], in_=s2[:, 0:1], func=AF.Sqrt, scale=1.0 / 1024, bias=epsc[:, 0:1]),
                 [s2, epsc], [s2])
            P.op("dve", lambda e: e.reciprocal(out=s2[:, 2:3], in_=s2[:, 1:2]), [s2], [s2])
            P.op("dve", lambda e: e.scalar_tensor_tensor(out=hr[:], in0=hr[:], scalar=s2[:, 2:3], in1=gfin[:], op0=ALU.mult, op1=ALU.mult),
                 [hr, s2, gfin], [hr])
        P.dma("act", o_d, o_d[r0:r0 + 128, :], hr, hr[:], owner=hr, disjoint=True)
    return [o_d]


def c3_inputs(layer, inp, core, hffn_full, final):
    l = layer
    fmj = lambda v: np.ascontiguousarray(v.reshape(-1, 128).T)
    m = {"c3_h": np.ascontiguousarray(hffn_full[core * C3_NT:(core + 1) * C3_NT]),
         "c3_pT": np.ascontiguousarray(inp['p'][l, 0, core * C3_NT:(core + 1) * C3_NT, :].T),
         "c3_gate": np.ascontiguousarray(inp['ple_gate'][l]), "c3_proj": np.ascontiguousarray(inp['ple_proj'][l]),
         "c3_pp": fmj(inp['norm_ple_g'][l]), "c3_ident": np.eye(128).astype(ml_dtypes.bfloat16),
         "c3_gB": np.ascontiguousarray(np.broadcast_to(inp['norm_ple_g'][l][None, :], (128, 1024))).astype(np.float32)}
    if final:
        m["c3_gfin"] = np.ascontiguousarray(np.broadcast_to(inp['final_norm_g'][None, :], (128, 1024))).astype(np.float32)
    return m


def _launch(build, maps):
    nc = bass.Bass("TRN2", target_bir_lowering=False)
    P = Prog(nc)
    outs = build(P)
    P.final_wait("sp", outs)
    P.emit()
    res = run_bass_kernel_spmd(nc, maps, core_ids=list(range(8)))
    return res.results


def kernel(**inputs):
    inp = {k: np.asarray(v) for k, v in inputs.items()}
    S = 16384
    h = np.ascontiguousarray(inp['x'][0], dtype=np.float32)
    vfirst = None
    for l in range(2):
        nc, P = build_A(l)
        res = run_bass_kernel_spmd(nc, host_inputs_A(l, inp, h, vfirst), core_ids=list(range(8))).results
        fm = np.concatenate([r["fm"] for r in res], axis=1)
        tm = np.concatenate([r["tm"] for r in res], axis=0)
        del res
        if l == 0:
            vfirst = np.ascontiguousarray(fm[FM_V:FM_V + 256])
        res = _launch(emit_sw, [sw_inputs(c, fm[FM_BQ:FM_BQ + 256], fm[FM_BK:FM_BK + 256], tm[:, 512:768]) for c in range(8)])
        cf = np.empty((CF_ROWS, S), np.float32)
        for c in range(8):
            hd, s = c // 2, c % 2
            cf[CF_SWO + hd * 64:CF_SWO + (hd + 1) * 64, s * NOWN:(s + 1) * NOWN] = res[c]["sw_o"]
        res = _launch(emit_hgrn, [hg_inputs(c, fm[FM_QS:FM_QS + 512], fm[FM_LF:FM_LF + 512], tm[:, 0:512]) for c in range(8)])
        for c in range(8):
            hd, vh = c // 2, c % 2
            cf[CF_HGO + hd * 128 + vh * 64:CF_HGO + hd * 128 + (vh + 1) * 64] = res[c]["hg_o"]
        fmr = {"r": fm[FM_R:FM_R + 256], "kp": fm[FM_KP:FM_KP + 256], "kk": fm[FM_KK:FM_KK + 256],
               "a": fm[FM_A:FM_A + 256], "ld": fm[FM_LD:FM_LD + 256], "v": fm[FM_V:FM_V + 256]}
        res = _launch(emit_rwkv, [rw_inputs(c, fmr) for c in range(8)])
        for c in range(8):
            hd, vh = c // 2, c % 2
            cf[CF_RWY + hd * 64 + vh * 32:CF_RWY + hd * 64 + (vh + 1) * 32] = res[c]["rw_y"]
        cf[CF_GS:CF_GS + 512] = fm[FM_GS:FM_GS + 512]
        cf[CF_R:CF_R + 256] = fm[FM_R:FM_R + 256]
        cf[CF_KP:CF_KP + 256] = fm[FM_KP:FM_KP + 256]
        cf[CF_V:CF_V + 256] = fm[FM_V:FM_V + 256]
        cf[CF_G:CF_G + 256] = fm[FM_G:FM_G + 256]
        del fm, tm, fmr
        res = _launch(lambda P: emit_c1(P, l), [c1_inputs(l, inp, c, h, cf) for c in range(8)])
        hmid = np.concatenate([r["c1_hmid"] for r in res], axis=0)
        del cf
        res = _launch(emit_c2, [c2_inputs(l, inp, c, hmid) for c in range(8)])
        hffn = np.concatenate([r["c2_out"] for r in res], axis=0)
        final = (l == 1)
        res = _launch(lambda P: emit_c3(P, final), [c3_inputs(l, inp, c, hffn, final) for c in range(8)])
        h = np.concatenate([r["c3_out"] for r in res], axis=0)
    return h[None].astype(np.float32)
```

```python
import numpy as np
import ml_dtypes
from concourse.bass_utils import run_bass_kernel_spmd


import concourse.bass as bass
import concourse.mybir as mybir

F32 = mybir.dt.float32
BF16 = mybir.dt.bfloat16
AF = mybir.ActivationFunctionType
ALU = mybir.AluOpType
AX = mybir.AxisListType

ENGS = ("pe", "act", "dve", "pool", "sp")


class T:
    __slots__ = ("name", "h", "last_w", "readers", "sem", "cnt", "excl")

    def __init__(self, name, h):
        self.name = name
        self.h = h
        self.last_w = {}
        self.readers = []
        self.sem = None
        self.cnt = 0
        self.excl = False

    def __getitem__(self, idx):
        return self.h[idx]


class _Rec:
    def __getattr__(self, name):
        def f(*a, **k):
            self.call = (name, a, k)
        return f


def _eager(fn):
    r = _Rec()
    fn(r)
    name, a, k = r.call
    return lambda e: getattr(e, name)(*a, **k)


class Prog:
    def __init__(self, nc):
        self.nc = nc
        self.ops = {e: [] for e in ENGS}
        self.count = {e: 0 for e in ENGS}
        self.waited = {e: {} for e in ENGS}
        self.dma_sems = []
        self.ctx = []
        self.ntiles = 0

    def sbuf(self, name, shape, dt):
        g = self.nc.sbuf_tensor(name, list(shape), dt)
        h = g.__enter__()
        self.ctx.append(g)
        return T(name, h)

    def psum(self, name, shape, dt=F32):
        g = self.nc.psum_tensor(name, list(shape), dt)
        h = g.__enter__()
        self.ctx.append(g)
        t = T(name, h)
        t.excl = True
        return t

    def dram(self, name, shape, dt, kind="Internal"):
        h = self.nc.dram_tensor(name, list(shape), dt, kind=kind)
        return T(name, h.ap() if hasattr(h, "ap") else h)

    def view(self, name, h):
        return T(name, h)

    def _deps(self, eng, reads, writes):
        deps = []
        for t in reads:
            for ev in t.last_w.items():
                deps.append((ev, "raw"))
            if t.excl:
                for r in t.readers:
                    if r[0] != eng:
                        deps.append((r, "rar"))
        for t in writes:
            if not getattr(self, "_disjoint", False):
                for ev in t.last_w.items():
                    deps.append((ev, "waw"))
            for r in t.readers:
                deps.append((r, "war"))
        out = {}
        for (key, val), kind in deps:
            if key == eng:
                if eng == "pe":
                    continue
            if out.get(key, 0) < val:
                out[key] = val
        res = []
        w = self.waited[eng]
        for key, val in out.items():
            if w.get(key, 0) >= val:
                continue
            w[key] = val
            res.append((key, val))
        return res

    def op(self, eng, fn, reads=(), writes=()):
        waits = self._deps(eng, reads, writes)
        self.count[eng] += 1
        ev = (eng, self.count[eng])
        for t in reads:
            t.readers.append(ev)
        for t in writes:
            t.last_w = {ev[0]: ev[1]}
            t.readers = []
        self.ops[eng].append((waits, _eager(fn), None))

    def dma(self, eng, out_t, out_ap, in_t, in_ap, owner=None, disjoint=False, **kw):
        self._disjoint = disjoint
        waits = self._deps(eng, [in_t], [out_t])
        self._disjoint = False
        ow = owner if owner is not None else out_t
        if ow.sem is None:
            g = self.nc.semaphore("ds%d" % len(self.dma_sems))
            ow.sem = g.__enter__()
            self.ctx.append(g)
            self.dma_sems.append(ow.sem)
        ow.cnt += 16
        ev = (ow.sem, ow.cnt)
        in_t.readers.append(ev)
        if disjoint:
            out_t.last_w[ev[0]] = ev[1]
        else:
            out_t.last_w = {ev[0]: ev[1]}
            out_t.readers = []

        def fn(e, out_ap=out_ap, in_ap=in_ap, kw=kw):
            return e.dma_start(out=out_ap, in_=in_ap, **kw)
        self.ops[eng].append((waits, fn, ow.sem))

    def final_wait(self, eng, tiles):
        waits = self._deps(eng, tiles, [])
        self.ops[eng].append((waits, None, None))

    def emit(self):
        nc = self.nc
        esem = {}
        for e in ENGS:
            g = nc.semaphore("es_" + e)
            esem[e] = g.__enter__()
            self.ctx.append(g)
        engobj = {"pe": "tensor", "act": "scalar", "dve": "vector", "pool": "gpsimd", "sp": "sync"}

        def run(e, eng):
            for waits, fn, dsem in self.ops[e]:
                for key, val in waits:
                    s = esem[key] if isinstance(key, str) else key
                    eng.wait_ge(s, val)
                if fn is None:
                    continue
                ins = fn(eng)
                if dsem is not None:
                    ins.then_inc(dsem, 16)
                else:
                    ins.then_inc(esem[e], 1)

        with nc.Block() as block:
            for e in ENGS:
                if not self.ops[e]:
                    continue
                getattr(block, engobj[e])(lambda eng, e=e: run(e, eng))

    def close(self):
        for g in reversed(self.ctx):
            g.__exit__(None, None, None)
        self.ctx = []


A_NT = 2048
A_TG = 512
A_NG = A_NT // A_TG
FM_QS, FM_LF, FM_GS, FM_BQ, FM_BK = 0, 512, 1024, 1536, 1792
FM_R, FM_KP, FM_KK, FM_A, FM_LD, FM_V, FM_G = [2048 + 256 * i for i in range(7)]
FM_ROWS = 3840
PP_GMIX, PP_MU, PP_W0, PP_A0, PP_KK, PP_KA, PP_V0, PP_HB0, PP_HB1, PP_N = 0, 8, 16, 18, 20, 22, 24, 26, 30, 34
PM_W2, PM_A2, PM_G2, PM_V1, PM_V2, PM_N = 0, 256, 512, 768, 832, 1088


def build_A(layer):
    nc = bass.Bass("TRN2", target_bir_lowering=False)
    P = Prog(nc)
    h = P.dram("h", [A_NT, 1024], F32, "ExternalInput")
    hh = P.dram("hh", [128, 1024], F32, "ExternalInput")
    w_in = P.dram("w_in", [1024, 3840], F32, "ExternalInput")
    pp_d = P.dram("pp", [128, PP_N], F32, "ExternalInput")
    pm_d = P.dram("pm", [128, PM_N], F32, "ExternalInput")
    id_d = P.dram("ident", [128, 128], BF16, "ExternalInput")
    blk_d = P.dram("blk64", [128, 128], BF16, "ExternalInput")
    if layer == 1:
        vf_d = P.dram("vfirst", [256, A_NT], F32, "ExternalInput")
    fm = P.dram("fm", [FM_ROWS, A_NT], F32, "ExternalOutput")
    tm = P.dram("tm", [A_NT, 768], F32, "ExternalOutput")

    wbf = [P.sbuf("wbf%d" % k, [128, 3840], BF16) for k in range(8)]
    wst = [P.sbuf("wst%d" % i, [128, 1920], F32) for i in range(2)]
    pp = P.sbuf("pp_s", [128, PP_N], F32)
    pm32 = P.sbuf("pm32", [128, PM_N], F32)
    pm = P.sbuf("pm_s", [128, PM_N], BF16)
    ident = P.sbuf("ident_s", [128, 128], BF16)
    blk = P.sbuf("blk_s", [128, 128], BF16)
    hin = [P.sbuf("hin%d" % i, [128, 1024], F32) for i in range(2)]
    hsq = P.sbuf("hsq", [128, 1024], F32)
    hnb = [P.sbuf("hnb%d" % i, [128, 1024], BF16) for i in range(2)]
    st = [P.sbuf("st%d" % i, [128, 4], F32) for i in range(2)]
    hnT = [P.sbuf("hnT%d" % i, [128, 8, A_TG], BF16) for i in range(2)]
    gb = P.sbuf("gb", [128, 8, 128], F32)
    CB = [P.sbuf("CB%d" % i, [128, 8, A_TG + 1], F32) for i in range(2)]
    stg = [P.sbuf("stg%d" % i, [128, A_TG], F32) for i in range(6)]
    stt = [P.sbuf("stt%d" % i, [128, 768], F32) for i in range(2)]
    cm = P.sbuf("cm", [128, 8, A_TG], F32)
    tmpA = [P.sbuf("tmpA%d" % i, [128, A_TG], F32) for i in range(4)]
    tb = [P.sbuf("tb%d" % i, [128, A_TG], BF16) for i in range(4)]
    lbc = P.sbuf("lbc", [128, 8], F32)
    kac = P.sbuf("kac", [128, 2], F32)
    epsc = P.sbuf("epsc", [128, 1], F32)
    vfs = P.sbuf("vfs", [128, 2, A_TG], F32) if layer == 1 else None
    ps = [P.psum("ps%d" % i, [128, 512], F32) for i in range(6)]
    pst = P.psum("pst", [128, 1024], BF16)
    psm = P.psum("psm", [128, 512], F32)

    P.dma("sp", pp, pp[:], pp_d, pp_d[:])
    P.dma("sp", pm32, pm32[:], pm_d, pm_d[:])
    P.dma("sp", ident, ident[:], id_d, id_d[:])
    P.dma("sp", blk, blk[:], blk_d, blk_d[:])
    P.op("dve", lambda e: e.tensor_copy(out=pm[:], in_=pm32[:]), [pm32], [pm])
    P.op("dve", lambda e: e.memset(epsc[:], 1e-6), [], [epsc])
    for c in range(8):
        P.op("dve", lambda e, c=c: e.memset(gb[:, c, :], 1.0), [], [gb])
    for c in range(8):
        P.op("dve", lambda e, c=c: e.tensor_scalar(out=gb[:, c, :], in0=gb[:, c, :], scalar1=pp[:, PP_GMIX + c:PP_GMIX + c + 1],
                                                    scalar2=None, op0=ALU.mult), [gb, pp], [gb])
    P.op("dve", lambda e: e.tensor_scalar(out=kac[:], in0=pp[:, PP_KA:PP_KA + 2], scalar1=-1.0, scalar2=1.0,
                                          op0=ALU.mult, op1=ALU.add), [pp], [kac])
    if layer == 1:
        P.op("dve", lambda e: e.tensor_tensor(out=lbc[:, 0:4], in0=pp[:, PP_HB1:PP_HB1 + 4], in1=pp[:, PP_HB0:PP_HB0 + 4],
                                              op=ALU.subtract), [pp], [lbc])
        P.op("act", lambda e: e.activation(out=lbc[:, 0:4], in_=lbc[:, 0:4], func=AF.Sigmoid), [lbc], [lbc])
        P.op("dve", lambda e: e.tensor_scalar(out=lbc[:, 4:8], in0=lbc[:, 0:4], scalar1=-1.0, scalar2=1.0,
                                              op0=ALU.mult, op1=ALU.add), [lbc], [lbc])
    for k in range(8):
        for hf in range(2):
            s_ = wst[hf]
            P.dma("sp" if hf == 0 else "act", s_, s_[:], w_in, w_in[k * 128:(k + 1) * 128, hf * 1920:(hf + 1) * 1920])
            P.op("pool", lambda e, k=k, s_=s_, hf=hf: e.tensor_copy(out=wbf[k][:, hf * 1920:(hf + 1) * 1920], in_=s_[:]), [s_], [wbf[k]])

    outq = ["sp", "act"]
    oq = [0]

    def out_dma(dst_t, dst_ap, src_t, src_ap):
        q = outq[oq[0] % 2]
        oq[0] += 1
        P.dma(q, dst_t, dst_ap, src_t, src_ap, owner=src_t, disjoint=True)

    tcount = [0]

    def norm_tile(src_ap, dstT, col0):
        i = tcount[0] % 2
        tcount[0] += 1
        hi, hb, s_ = hin[i], hnb[i], st[i]
        P.dma("sp", hi, hi[:], h, src_ap)
        P.op("act", lambda e: e.activation(out=hsq[:], in_=hi[:], func=AF.Square, accum_out=s_[:, 0:1]), [hi], [hsq, s_])
        P.op("act", lambda e: e.activation(out=s_[:, 1:2], in_=s_[:, 0:1], func=AF.Sqrt, scale=1.0 / 1024, bias=epsc[:, 0:1]),
             [s_, epsc], [s_])
        P.op("dve", lambda e: e.reciprocal(out=s_[:, 2:3], in_=s_[:, 1:2]), [s_], [s_])
        P.op("dve", lambda e: e.tensor_scalar(out=hb[:], in0=hi[:], scalar1=s_[:, 2:3], scalar2=None, op0=ALU.mult),
             [hi, s_], [hb])
        for c in range(8):
            P.op("pe", lambda e, c=c: e.transpose(out=pst[:, c * 128:(c + 1) * 128], in_=hb[:, c * 128:(c + 1) * 128],
                                                   identity=ident[:]), [hb, ident], [pst])
        P.op("dve", lambda e: e.tensor_tensor(out=dstT[:, :, col0:col0 + 128],
                                              in0=pst[:].rearrange("p (c t) -> p c t", c=8), in1=gb[:], op=ALU.mult),
             [pst, gb], [dstT])

    def mm_fm(dst_ps, cc, src):
        for k in range(8):
            P.op("pe", lambda e, k=k: e.matmul(dst_ps[:], lhsT=wbf[k][:, cc * 128:(cc + 1) * 128], rhs=src[:, k, :],
                                                start=(k == 0), stop=(k == 7)), [wbf[k], src], [dst_ps])

    hT_h = hnT[1]
    P.hsrc = hh
    i0 = tcount[0]
    hi, hb, s_ = hin[0], hnb[0], st[0]
    tcount[0] += 1
    P.dma("sp", hi, hi[:], hh, hh[:])
    P.op("act", lambda e: e.activation(out=hsq[:], in_=hi[:], func=AF.Square, accum_out=s_[:, 0:1]), [hi], [hsq, s_])
    P.op("act", lambda e: e.activation(out=s_[:, 1:2], in_=s_[:, 0:1], func=AF.Sqrt, scale=1.0 / 1024, bias=epsc[:, 0:1]),
         [s_, epsc], [s_])
    P.op("dve", lambda e: e.reciprocal(out=s_[:, 2:3], in_=s_[:, 1:2]), [s_], [s_])
    P.op("dve", lambda e: e.tensor_scalar(out=hb[:], in0=hi[:], scalar1=s_[:, 2:3], scalar2=None, op0=ALU.mult), [hi, s_], [hb])
    for c in range(8):
        P.op("pe", lambda e, c=c: e.transpose(out=pst[:, c * 128:(c + 1) * 128], in_=hb[:, c * 128:(c + 1) * 128],
                                               identity=ident[:]), [hb, ident], [pst])
    P.op("dve", lambda e: e.tensor_tensor(out=hT_h[:, :, 0:128], in0=pst[:].rearrange("p (c t) -> p c t", c=8),
                                          in1=gb[:], op=ALU.mult), [pst, gb], [hT_h])
    for c8 in range(8):
        cc = 22 + c8
        pz = ps[c8 % 6]
        for k in range(8):
            P.op("pe", lambda e, k=k, cc=cc, pz=pz: e.matmul(pz[:, 0:128], lhsT=wbf[k][:, cc * 128:(cc + 1) * 128],
                                                              rhs=hT_h[:, k, 0:128], start=(k == 0), stop=(k == 7)),
                 [wbf[k], hT_h], [pz])
        P.op("act", lambda e, c8=c8, pz=pz: e.activation(out=CB[0][:, c8, 0:1], in_=pz[:, 127:128], func=AF.Copy), [pz], [CB[0]])

    sti = [0]

    def stage():
        s_ = stg[sti[0] % 6]
        sti[0] += 1
        return s_

    psi = [0]

    def nps():
        p_ = ps[psi[0] % 6]
        psi[0] += 1
        return p_

    for g in range(A_NG):
        hT = hnT[g % 2]
        cb = CB[g % 2]
        cbn = CB[(g + 1) % 2]
        t0 = g * A_TG
        for t in range(4):
            norm_tile(h[t0 + t * 128:t0 + (t + 1) * 128, :], hT, t * 128)
        for c in range(4):
            pz = nps(); mm_fm(pz, c, hT); s_ = stage()
            P.op("act", lambda e, pz=pz, s_=s_: e.activation(out=s_[:], in_=pz[:], func=AF.Silu), [pz], [s_])
            out_dma(fm, fm[FM_QS + c * 128:FM_QS + (c + 1) * 128, t0:t0 + A_TG], s_, s_[:])
        for c in range(4):
            pz = nps(); mm_fm(pz, 4 + c, hT); s_ = stage()
            P.op("act", lambda e, pz=pz, s_=s_: e.activation(out=s_[:], in_=pz[:], func=AF.Sigmoid), [pz], [s_])
            if layer == 1:
                P.op("dve", lambda e, s_=s_, c=c: e.tensor_scalar(out=s_[:], in0=s_[:], scalar1=lbc[:, 4 + c:5 + c],
                                                                    scalar2=lbc[:, c:c + 1], op0=ALU.mult, op1=ALU.add),
                     [s_, lbc], [s_])
            P.op("act", lambda e, s_=s_: e.activation(out=s_[:], in_=s_[:], func=AF.Ln), [s_], [s_])
            out_dma(fm, fm[FM_LF + c * 128:FM_LF + (c + 1) * 128, t0:t0 + A_TG], s_, s_[:])
        for c in range(4):
            pz = nps(); mm_fm(pz, 12 + c, hT); s_ = stage()
            P.op("act", lambda e, pz=pz, s_=s_: e.activation(out=s_[:], in_=pz[:], func=AF.Silu), [pz], [s_])
            out_dma(fm, fm[FM_GS + c * 128:FM_GS + (c + 1) * 128, t0:t0 + A_TG], s_, s_[:])
        for c in range(4):
            pz = nps(); mm_fm(pz, 16 + c, hT); s_ = stage()
            P.op("dve", lambda e, pz=pz, s_=s_: e.tensor_copy(out=s_[:], in_=pz[:]), [pz], [s_])
            out_dma(fm, fm[FM_BQ + c * 128:FM_BQ + (c + 1) * 128, t0:t0 + A_TG], s_, s_[:])
        for t in range(4):
            pz = nps(); pz2 = nps(); s_ = stt[t % 2]
            for k in range(8):
                P.op("pe", lambda e, k=k, t=t, pz=pz: e.matmul(pz[:], lhsT=hT[:, k, t * 128:(t + 1) * 128],
                                                                rhs=wbf[k][:, 1024:1536], start=(k == 0), stop=(k == 7)),
                     [wbf[k], hT], [pz])
            for k in range(8):
                P.op("pe", lambda e, k=k, t=t, pz2=pz2: e.matmul(pz2[:, 0:256], lhsT=hT[:, k, t * 128:(t + 1) * 128],
                                                                  rhs=wbf[k][:, 2560:2816], start=(k == 0), stop=(k == 7)),
                     [wbf[k], hT], [pz2])
            P.op("dve", lambda e, pz=pz, s_=s_: e.tensor_copy(out=s_[:, 0:512], in_=pz[:]), [pz], [s_])
            P.op("act", lambda e, pz2=pz2, s_=s_: e.activation(out=s_[:, 512:768], in_=pz2[:, 0:256], func=AF.Copy), [pz2], [s_])
            out_dma(tm, tm[t0 + t * 128:t0 + (t + 1) * 128, :], s_, s_[:])
        for c8 in range(8):
            pz = nps(); mm_fm(pz, 22 + c8, hT)
            if c8 % 2 == 0:
                P.op("dve", lambda e, pz=pz, c8=c8: e.tensor_copy(out=cb[:, c8, 1:A_TG + 1], in_=pz[:]), [pz], [cb])
            else:
                P.op("act", lambda e, pz=pz, c8=c8: e.activation(out=cb[:, c8, 1:A_TG + 1], in_=pz[:], func=AF.Copy), [pz], [cb])
        P.op("pool", lambda e: e.tensor_copy(out=cbn[:, :, 0:1], in_=cb[:, :, A_TG:A_TG + 1]), [cb], [cbn])
        for c8 in range(8):
            ta = tmpA[c8 % 2]
            eng = "dve"
            P.op(eng, lambda e, c8=c8, ta=ta: e.tensor_tensor(out=ta[:], in0=cb[:, c8, 0:A_TG], in1=cb[:, c8, 1:A_TG + 1],
                                                              op=ALU.subtract), [cb], [ta])
            P.op(eng, lambda e, c8=c8, ta=ta: e.scalar_tensor_tensor(out=cm[:, c8, :], in0=ta[:], scalar=pp[:, PP_MU + c8:PP_MU + c8 + 1],
                                                                     in1=cb[:, c8, 1:A_TG + 1], op0=ALU.mult, op1=ALU.add),
                 [ta, pp, cb], [cm])
        for c in range(2):
            out_dma(fm, fm[FM_R + c * 128:FM_R + (c + 1) * 128, t0:t0 + A_TG], cm, cm[:, c, :])
        P.op("act", lambda e: e.activation(out=tb[0][0:64, :], in_=cm[0:64, 6, :], func=AF.Tanh), [cm], [tb[0]])
        P.op("dve", lambda e: e.tensor_copy(out=tb[0][64:128, :], in_=cm[64:128, 6, :]), [cm], [tb[0]])
        P.op("act", lambda e: e.activation(out=tb[1][:], in_=cm[:, 7, :], func=AF.Sigmoid), [cm], [tb[1]])
        E05 = float(np.exp(-0.5))
        for c in range(2):
            pz = nps(); s_ = stage()
            P.op("pe", lambda e, pz=pz, c=c: e.matmul(pz[:], lhsT=pm[0:64, PM_W2 + c * 128:PM_W2 + (c + 1) * 128],
                                                       rhs=tb[0][0:64, :], start=True, stop=True), [pm, tb[0]], [pz])
            P.op("act", lambda e, pz=pz, s_=s_, c=c: e.activation(out=s_[:], in_=pz[:], func=AF.Sigmoid,
                                                                   bias=pp[:, PP_W0 + c:PP_W0 + c + 1]), [pz, pp], [s_])
            P.op("dve", lambda e, s_=s_: e.tensor_scalar(out=s_[:], in0=s_[:], scalar1=-E05, scalar2=None, op0=ALU.mult), [s_], [s_])
            out_dma(fm, fm[FM_LD + c * 128:FM_LD + (c + 1) * 128, t0:t0 + A_TG], s_, s_[:])
        a_t = [tmpA[2], tmpA[3]]
        for c in range(2):
            pz = nps()
            P.op("pe", lambda e, pz=pz, c=c: e.matmul(pz[:], lhsT=pm[64:128, PM_A2 + c * 128:PM_A2 + (c + 1) * 128],
                                                       rhs=tb[0][64:128, :], start=True, stop=True), [pm, tb[0]], [pz])
            P.op("act", lambda e, pz=pz, c=c: e.activation(out=a_t[c][:], in_=pz[:], func=AF.Sigmoid,
                                                           bias=pp[:, PP_A0 + c:PP_A0 + c + 1]), [pz, pp], [a_t[c]])
            out_dma(fm, fm[FM_A + c * 128:FM_A + (c + 1) * 128, t0:t0 + A_TG], a_t[c], a_t[c][:])
        for c in range(2):
            pz = nps(); s_ = stage()
            P.op("pe", lambda e, pz=pz, c=c: e.matmul(pz[:], lhsT=pm[:, PM_G2 + c * 128:PM_G2 + (c + 1) * 128],
                                                       rhs=tb[1][:], start=True, stop=True), [pm, tb[1]], [pz])
            P.op("dve", lambda e, pz=pz, s_=s_: e.tensor_copy(out=s_[:], in_=pz[:]), [pz], [s_])
            out_dma(fm, fm[FM_G + c * 128:FM_G + (c + 1) * 128, t0:t0 + A_TG], s_, s_[:])
        if layer == 1:
            P.dma("sp", vfs, vfs[:], vf_d, vf_d[:, t0:t0 + A_TG].rearrange("(c p) t -> p c t", p=128))
            for c in range(2):
                P.op("dve", lambda e, c=c: e.tensor_copy(out=tb[2 + c][:], in_=cm[:, 4 + c, :]), [cm], [tb[2 + c]])
            for c in range(2):
                P.op("pe", lambda e, c=c: e.matmul(psm[0:32, :], lhsT=pm[:, PM_V1 + c * 32:PM_V1 + (c + 1) * 32],
                                                   rhs=tb[2 + c][:], start=(c == 0), stop=(c == 1)), [pm, tb[2 + c]], [psm])
            P.op("dve", lambda e: e.tensor_copy(out=tb[1][0:32, :], in_=psm[0:32, :]), [psm], [tb[1]])
            for c in range(2):
                pz = nps(); ta = tmpA[c]
                P.op("pe", lambda e, pz=pz, c=c: e.matmul(pz[:], lhsT=pm[0:32, PM_V2 + c * 128:PM_V2 + (c + 1) * 128],
                                                           rhs=tb[1][0:32, :], start=True, stop=True), [pm, tb[1]], [pz])
                P.op("act", lambda e, pz=pz, c=c, ta=ta: e.activation(out=ta[:], in_=pz[:], func=AF.Sigmoid,
                                                                        bias=pp[:, PP_V0 + c:PP_V0 + c + 1]), [pz, pp], [ta])
                s_ = stage()
                P.op("dve", lambda e, c=c, s_=s_: e.tensor_tensor(out=s_[:], in0=vfs[:, c, :], in1=cm[:, 4 + c, :],
                                                                   op=ALU.subtract), [vfs, cm], [s_])
                P.op("dve", lambda e, s_=s_, ta=ta: e.tensor_tensor(out=s_[:], in0=s_[:], in1=ta[:], op=ALU.mult), [s_, ta], [s_])
                P.op("dve", lambda e, s_=s_, c=c: e.tensor_tensor(out=s_[:], in0=s_[:], in1=cm[:, 4 + c, :], op=ALU.add),
                     [s_, cm], [s_])
                out_dma(fm, fm[FM_V + c * 128:FM_V + (c + 1) * 128, t0:t0 + A_TG], s_, s_[:])
        else:
            for c in range(2):
                out_dma(fm, fm[FM_V + c * 128:FM_V + (c + 1) * 128, t0:t0 + A_TG], cm, cm[:, 4 + c, :])
        for c in range(2):
            kx = tmpA[c]; s_ = stage(); s2 = stage(); pz = nps()
            P.op("dve", lambda e, c=c, kx=kx: e.tensor_scalar(out=kx[:], in0=cm[:, 2 + c, :], scalar1=pp[:, PP_KK + c:PP_KK + c + 1],
                                                               scalar2=None, op0=ALU.mult), [cm, pp], [kx])
            P.op("pool", lambda e, c=c, kx=kx: e.tensor_tensor(out=tb[2 + c][:], in0=kx[:], in1=kx[:], op=ALU.mult), [kx], [tb[2 + c]])
            P.op("pe", lambda e, pz=pz, c=c: e.matmul(pz[:], lhsT=blk[:], rhs=tb[2 + c][:], start=True, stop=True),
                 [blk, tb[2 + c]], [pz])
            P.op("act", lambda e, pz=pz, s_=s_: e.activation(out=s_[:], in_=pz[:], func=AF.Sqrt), [pz], [s_])
            P.op("dve", lambda e, s_=s_: e.tensor_scalar(out=s_[:], in0=s_[:], scalar1=1e-12, scalar2=None, op0=ALU.max), [s_], [s_])
            P.op("dve", lambda e, s_=s_: e.reciprocal(out=s_[:], in_=s_[:]), [s_], [s_])
            P.op("dve", lambda e, s_=s_, kx=kx: e.tensor_tensor(out=s_[:], in0=s_[:], in1=kx[:], op=ALU.mult), [s_, kx], [s_])
            out_dma(fm, fm[FM_KK + c * 128:FM_KK + (c + 1) * 128, t0:t0 + A_TG], s_, s_[:])
            P.op("dve", lambda e, s2=s2, c=c: e.tensor_scalar(out=s2[:], in0=a_t[c][:], scalar1=pp[:, PP_KA + c:PP_KA + c + 1],
                                                               scalar2=kac[:, c:c + 1], op0=ALU.mult, op1=ALU.add),
                 [a_t[c], pp, kac], [s2])
            P.op("dve", lambda e, s2=s2, c=c: e.tensor_tensor(out=s2[:], in0=s2[:], in1=cm[:, 2 + c, :], op=ALU.mult), [s2, cm], [s2])
            out_dma(fm, fm[FM_KP + c * 128:FM_KP + (c + 1) * 128, t0:t0 + A_TG], s2, s2[:])

    P.final_wait("sp", [fm, tm])
    P.emit()
    return nc, P


def host_inputs_A(layer, inp, h_full, vfirst_full=None):
    l = layer
    pp = np.zeros((128, PP_N), np.float32)
    fmj = lambda v: np.ascontiguousarray(v.reshape(-1, 128).T)
    pp[:, PP_GMIX:PP_GMIX + 8] = fmj(inp['norm_mix_g'][l])
    pp[:, PP_MU:PP_MU + 8] = fmj(inp['rwkv_mu'][l])
    pp[:, PP_W0:PP_W0 + 2] = fmj(inp['rwkv_w0'][l])
    pp[:, PP_A0:PP_A0 + 2] = fmj(inp['rwkv_a0'][l])
    pp[:, PP_KK:PP_KK + 2] = fmj(inp['rwkv_k_k'][l])
    pp[:, PP_KA:PP_KA + 2] = fmj(inp['rwkv_k_a'][l])
    if l == 1:
        pp[:, PP_V0:PP_V0 + 2] = fmj(inp['rwkv_v0'][0])
    pp[:, PP_HB0:PP_HB0 + 4] = fmj(inp['hgrn_lower_bounds'][0])
    pp[:, PP_HB1:PP_HB1 + 4] = fmj(inp['hgrn_lower_bounds'][1])
    pm = np.zeros((128, PM_N), np.float32)
    pm[0:64, PM_W2:PM_W2 + 256] = inp['rwkv_w2'][l]
    pm[64:128, PM_A2:PM_A2 + 256] = inp['rwkv_a2'][l]
    pm[:, PM_G2:PM_G2 + 256] = inp['rwkv_g2'][l]
    if l == 1:
        pm[:, PM_V1:PM_V1 + 64] = inp['rwkv_v1'][0].reshape(2, 128, 32).transpose(1, 0, 2).reshape(128, 64)
        pm[0:32, PM_V2:PM_V2 + 256] = inp['rwkv_v2'][0]
    ident = np.eye(128).astype(ml_dtypes.bfloat16)
    blk = np.kron(np.eye(2), np.ones((64, 64))).astype(ml_dtypes.bfloat16)
    maps = []
    w = np.ascontiguousarray(inp['w_in'][l])
    for c in range(8):
        m = {"h": np.ascontiguousarray(h_full[c * A_NT:(c + 1) * A_NT]),
             "hh": np.ascontiguousarray(h_full[c * A_NT - 128:c * A_NT]) if c > 0 else np.zeros((128, 1024), np.float32),
             "w_in": w, "pp": pp, "pm": pm, "ident": ident, "blk64": blk}
        if l == 1:
            m["vfirst"] = np.ascontiguousarray(vfirst_full[:, c * A_NT:(c + 1) * A_NT])
        maps.append(m)
    return maps


NOWN = 8192
NHALO = 2048
NTOT = NOWN + NHALO
PATTERNS = (1, 4, 16)


def emit_sw(P):
    qT_d = P.dram("sw_qT", [64, NOWN], F32, "ExternalInput")
    kT_d = P.dram("sw_kT", [64, NTOT], F32, "ExternalInput")
    v_d = P.dram("sw_v", [NTOT, 64], F32, "ExternalInput")
    msk_d = P.dram("sw_mask", [128, 512], BF16, "ExternalInput")
    id_d = P.dram("sw_ident", [128, 128], BF16, "ExternalInput")
    flag_d = P.dram("sw_flag", [128, 1], F32, "ExternalInput")
    o_d = P.dram("sw_o", [64, NOWN], F32, "ExternalOutput")

    q32 = P.sbuf("q32", [64, NOWN], F32)
    k32 = P.sbuf("k32", [64, NTOT], F32)
    qd = P.sbuf("qd", [64, NOWN], BF16)
    kd = P.sbuf("kd", [64, NTOT], BF16)
    vst = P.sbuf("vst", [128, 85 * 64], F32)
    vaug = P.sbuf("vaug", [128, 85, 65], BF16)
    acc = P.sbuf("acc", [65, NOWN], F32)
    msk = P.sbuf("msk", [128, 512], BF16)
    ident = P.sbuf("identsw", [128, 128], BF16)
    flag = P.sbuf("flag", [128, 1], F32)
    ones = P.sbuf("ones", [65, 64], F32)
    PT = [P.sbuf("PT%d" % i, [128, 512], BF16) for i in range(3)]
    ost = [P.sbuf("ost%d" % i, [64, 512], F32) for i in range(2)]
    rz = [P.sbuf("rz%d" % i, [64, 512], F32) for i in range(2)]
    psS = [P.psum("psS%d" % i, [128, 512], F32) for i in range(3)]
    psN = [P.psum("psN%d" % i, [128, 512], F32) for i in range(3)]

    P.dma("sp", msk, msk[:], msk_d, msk_d[:])
    P.dma("sp", ident, ident[:], id_d, id_d[:])
    P.dma("sp", flag, flag[:], flag_d, flag_d[:])
    for i in range(4):
        P.dma("sp" if i % 2 == 0 else "act", q32, q32[:, i * 2048:(i + 1) * 2048], qT_d, qT_d[:, i * 2048:(i + 1) * 2048],
              disjoint=True)
    for i in range(5):
        P.dma("act" if i % 2 == 0 else "sp", k32, k32[:, i * 2048:(i + 1) * 2048], kT_d, kT_d[:, i * 2048:(i + 1) * 2048],
              disjoint=True)
    P.op("dve", lambda e: e.memset(ones[:], 1.0), [], [ones])

    si = [0]
    for pi, D in enumerate(PATTERNS):
        nb = NOWN // (128 * D)
        nbt = nb + 1
        LQ = NOWN // D
        LK = LQ + 128
        koff = NHALO - 128 * D
        if D == 1:
            P.op("dve", lambda e: e.tensor_copy(out=qd[:, 0:NOWN], in_=q32[:, :]), [q32], [qd])
            P.op("pool", lambda e: e.tensor_copy(out=kd[:, 0:LK], in_=k32[:, koff:koff + LK]), [k32], [kd])
        else:
            P.op("dve", lambda e: e.tensor_copy(out=qd[:, 0:NOWN].rearrange("p (r l) -> p r l", r=D),
                                                in_=q32[:, :].rearrange("p (l r) -> p r l", r=D)), [q32], [qd])
            P.op("pool", lambda e: e.tensor_copy(out=kd[:, 0:D * LK].rearrange("p (r l) -> p r l", r=D),
                                                 in_=k32[:, koff:koff + D * LK].rearrange("p (l r) -> p r l", r=D)), [k32], [kd])
        vsrc = v_d[koff:koff + nbt * 128 * D, :].rearrange("(bb i r) c -> i r bb c", i=128, r=D)
        vv = vst[:, 0:D * nbt * 64].rearrange("p (r bb c) -> p r bb c", r=D, bb=nbt)
        for r in range(D):
            P.dma("sp" if r % 2 == 0 else "act", vst, vv[:, r], v_d, vsrc[:, r], disjoint=(r > 0))
        va = vaug[:, 0:D * nbt, :]
        P.op("dve", lambda e: e.memset(va[:, :, 64:65], 1.0), [], [vaug])
        P.op("dve", lambda e: e.tensor_copy(out=va[:, :, 0:64], in_=vst[:, 0:D * nbt * 64].rearrange("p (n c) -> p n c", c=64)),
             [vst], [vaug])
        va4 = va.rearrange("p (r bb) c -> p r bb c", r=D)
        P.op("dve", lambda e: e.tensor_scalar(out=va4[:, :, 0, :], in0=va4[:, :, 0, :], scalar1=flag[:, 0:1], scalar2=None,
                                              op0=ALU.mult), [vaug, flag], [vaug])
        for r in range(D):
            for b0 in range(0, nb, 4):
                pn = psN[si[0] % 3]
                pts = []
                for pr in range(2):
                    p_s = psS[(2 * si[0] + pr) % 3]
                    pt = PT[(2 * si[0] + pr) % 3]
                    P.op("pe", lambda e: e.matmul(p_s[:], lhsT=ident[:], rhs=msk[:], start=True, stop=False), [ident, msk], [p_s])
                    for j in range(2):
                        b = b0 + 2 * pr + j
                        qb = qd[:, r * LQ + b * 128: r * LQ + (b + 1) * 128]
                        for kb in range(2):
                            kblk = kd[:, r * LK + (b + kb) * 128: r * LK + (b + kb + 1) * 128]
                            P.op("pe", lambda e: e.matmul(p_s[:, (2 * j + kb) * 128:(2 * j + kb + 1) * 128], lhsT=kblk, rhs=qb,
                                                          start=False, stop=(j == 1 and kb == 1)), [kd, qd], [p_s])
                    P.op("act", lambda e: e.activation(out=pt[:], in_=p_s[:], func=AF.Exp, scale=0.125), [p_s], [pt])
                    pts.append(pt)
                for pr in range(2):
                    for j in range(2):
                        b = b0 + 2 * pr + j
                        for kb in range(2):
                            P.op("pe", lambda e: e.matmul(pn[0:65, (2 * pr + j) * 128:(2 * pr + j + 1) * 128],
                                                          lhsT=vaug[:, r * nbt + b + kb, :],
                                                          rhs=pts[pr][:, (2 * j + kb) * 128:(2 * j + kb + 1) * 128],
                                                          start=(kb == 0), stop=(kb == 1)), [vaug, pts[pr]], [pn])
                tstart = r + D * 128 * b0
                av = acc[:, tstart: tstart + 512 * D] if D == 1 else \
                    acc[:, D * 128 * b0: D * 128 * b0 + 512 * D].rearrange("p (l r) -> p r l", r=D)[:, r, :]
                if pi == 0:
                    P.op("dve", lambda e: e.tensor_copy(out=av, in_=pn[0:65, :]), [pn], [acc])
                else:
                    P.op("dve", lambda e: e.tensor_tensor(out=av, in0=av, in1=pn[0:65, :], op=ALU.add), [pn, acc], [acc])
                si[0] += 1
    for i in range(NOWN // 512):
        pz = psS[i % 3]
        P.op("pe", lambda e: e.matmul(pz[0:64, :], lhsT=ones[64:65, :], rhs=acc[64:65, i * 512:(i + 1) * 512], start=True, stop=True),
             [ones, acc], [pz])
        rzi = rz[i % 2]; o_ = ost[i % 2]
        P.op("dve", lambda e: e.reciprocal(out=rzi[:], in_=pz[0:64, :]), [pz], [rzi])
        P.op("pool", lambda e: e.tensor_tensor(out=o_[:], in0=acc[0:64, i * 512:(i + 1) * 512], in1=rzi[:], op=ALU.mult), [acc, rzi], [o_])
        P.dma("sp" if i % 2 == 0 else "act", o_d, o_d[:, i * 512:(i + 1) * 512], o_, o_[:], owner=o_, disjoint=True)
    return [o_d]


def sw_consts():
    j = np.arange(128)[:, None]
    i = np.arange(128)[None, :]
    mp = np.where(j >= i, 0.0, -30000.0)
    mo = np.where(j <= i, 0.0, -30000.0)
    m = np.concatenate([mp, mo, mp, mo], axis=1).astype(ml_dtypes.bfloat16)
    return {"sw_mask": m, "sw_ident": np.eye(128).astype(ml_dtypes.bfloat16)}


def sw_inputs(core, bq_fm, bk_fm, bv_tm):
    hd, s = core // 2, core % 2
    t0 = s * NOWN
    qT = np.ascontiguousarray(bq_fm[hd * 64:(hd + 1) * 64, t0:t0 + NOWN])
    kT = np.zeros((64, NTOT), np.float32)
    v = np.zeros((NTOT, 64), np.float32)
    kT[:, NHALO:] = bk_fm[hd * 64:(hd + 1) * 64, t0:t0 + NOWN]
    v[NHALO:] = bv_tm[t0:t0 + NOWN, hd * 64:(hd + 1) * 64]
    if s > 0:
        kT[:, :NHALO] = bk_fm[hd * 64:(hd + 1) * 64, t0 - NHALO:t0]
        v[:NHALO] = bv_tm[t0 - NHALO:t0, hd * 64:(hd + 1) * 64]
    m = {"sw_qT": qT, "sw_kT": kT, "sw_v": v, "sw_flag": np.full((128, 1), 1.0 if s > 0 else 0.0, np.float32)}
    m.update(sw_consts())
    return m


HG_SEQ = 16384
HG_ST = 2048
HG_NST = HG_SEQ // HG_ST
HG_CL = 40.0


def emit_hgrn(P, NSETA=4):
    q_d = P.dram("hg_q", [128, HG_SEQ], F32, "ExternalInput")
    lf_d = P.dram("hg_lf", [128, HG_SEQ], F32, "ExternalInput")
    i_d = P.dram("hg_i", [HG_SEQ, 64], F32, "ExternalInput")
    rm_d = P.dram("hg_rmask", [128, HG_ST], F32, "ExternalInput")
    cm_d = P.dram("hg_cmask", [128, 128], F32, "ExternalInput")
    id_d = P.dram("hg_ident", [128, 128], BF16, "ExternalInput")
    o_d = P.dram("hg_o", [64, HG_SEQ], F32, "ExternalOutput")

    def sb(name, shape, dt=F32):
        return P.sbuf("hgs_" + name, shape, dt)
    qs = sb("qs", [128, HG_ST]); lf = sb("lf", [128, HG_ST]); bb = sb("b", [128, HG_ST]); kf = sb("kf", [128, HG_ST])
    t1 = sb("t1", [128, HG_ST]); t2 = sb("t2", [128, HG_ST])
    v32 = sb("v32", [128, 16, 64])
    dch = [sb("dch%d" % i, [128, 32]) for i in range(2)]
    Qt = [sb("Qt%d" % i, [128, HG_ST], BF16) for i in range(2)]
    Kt = [sb("Kt%d" % i, [128, HG_ST], BF16) for i in range(2)]
    Qh = [sb("Qh%d" % i, [128, HG_ST], BF16) for i in range(2)]
    Kh = [sb("Kh%d" % i, [128, HG_ST], BF16) for i in range(2)]
    vb = [sb("vb%d" % i, [128, 16, 64], BF16) for i in range(2)]
    ost = [sb("ost%d" % i, [64, HG_ST]) for i in range(2)]
    rmask = sb("rmask", [128, HG_ST]); cmask = sb("cmask", [128, 128]); ident = sb("ident", [128, 128], BF16)
    KhT = [sb("KhT%d" % i, [128, 128], BF16) for i in range(NSETA)]
    Am = [sb("Am%d" % i, [128, 128], BF16) for i in range(NSETA)]
    S32 = sb("S32", [128, 64])
    Sb = [sb("Sb%d" % i, [128, 64], BF16) for i in range(2)]
    psT = P.psum("hg_psT", [128, 128], BF16)
    psA = [P.psum("hg_psA%d" % i, [128, 128], F32) for i in range(2)]
    psO = [P.psum("hg_psO%d" % i, [128, 128], F32) for i in range(2)]
    psU = [P.psum("hg_psU%d" % i, [128, 128], F32) for i in range(3)]

    P.dma("sp", rmask, rmask[:], rm_d, rm_d[:])
    P.dma("sp", cmask, cmask[:], cm_d, cm_d[:])
    P.dma("sp", ident, ident[:], id_d, id_d[:])
    P.op("dve", lambda e: e.memset(S32[:], 0.0), [], [S32])
    P.op("dve", lambda e: e.memset(Sb[0][:], 0.0), [], [Sb[0]])
    b3 = bb[:, :].rearrange("p (n c) -> p n c", c=64)
    NP = HG_NST * 16
    state = {"prep_done": 0, "a_started": 0, "a_done": set(), "s_done": 0}

    def prep_thread():
        for st in range(HG_NST):
            while state["a_started"] < 16 * st - 8:
                yield
            t0 = st * HG_ST
            b = st % 2
            P.dma("sp", qs, qs[:], q_d, q_d[:, t0:t0 + HG_ST])
            P.dma("act", lf, lf[:], lf_d, lf_d[:, t0:t0 + HG_ST])
            P.dma("pool", v32, v32[:], i_d, i_d[t0:t0 + HG_ST, :].rearrange("(n i) c -> i n c", i=128))
            yield
            P.op("pool", lambda e: e.tensor_copy(out=vb[b][:], in_=v32[:]), [v32], [vb[b]])
            P.op("act", lambda e: e.activation(out=kf[:], in_=lf[:], func=AF.Exp), [lf], [kf])
            P.op("dve", lambda e: e.tensor_tensor_scan(out=bb[:], data0=rmask[:], data1=lf[:], initial=0.0, op0=ALU.mult, op1=ALU.add),
                 [rmask, lf], [bb])
            yield
            P.op("pool", lambda e: e.tensor_scalar(out=kf[:], in0=kf[:], scalar1=-1.0, scalar2=1.0, op0=ALU.mult, op1=ALU.add), [kf], [kf])
            bm = b3[:, :, 31:32].to_broadcast([128, 32, 64])
            bl = b3[:, :, 63:64].to_broadcast([128, 32, 64])
            t1v = t1[:, :].rearrange("p (n c) -> p n c", c=64)
            t2v = t2[:, :].rearrange("p (n c) -> p n c", c=64)
            P.op("dve", lambda e: e.tensor_tensor(out=t1v, in0=b3, in1=bm, op=ALU.subtract), [bb], [t1])
            yield
            P.op("pool", lambda e: e.tensor_scalar(out=t2[:], in0=t1[:], scalar1=-1.0, scalar2=HG_CL, op0=ALU.mult, op1=ALU.min), [t1], [t2])
            P.op("dve", lambda e: e.tensor_scalar(out=t1[:], in0=t1[:], scalar1=HG_CL, scalar2=None, op0=ALU.min), [t1], [t1])
            yield
            P.op("act", lambda e: e.activation(out=t1[:], in_=t1[:], func=AF.Exp), [t1], [t1])
            P.op("act", lambda e: e.activation(out=t2[:], in_=t2[:], func=AF.Exp), [t2], [t2])
            yield
            P.op("dve", lambda e: e.tensor_tensor(out=Qt[b][:], in0=qs[:], in1=t1[:], op=ALU.mult), [qs, t1], [Qt[b]])
            P.op("pool", lambda e: e.tensor_tensor(out=Kt[b][:], in0=kf[:], in1=t2[:], op=ALU.mult), [kf, t2], [Kt[b]])
            yield
            P.op("act", lambda e: e.activation(out=t1[:], in_=bb[:], func=AF.Exp), [bb], [t1])
            P.op("dve", lambda e: e.tensor_tensor(out=t2v, in0=bl, in1=b3, op=ALU.subtract), [bb], [t2])
            yield
            P.op("dve", lambda e: e.tensor_tensor(out=Qh[b][:], in0=qs[:], in1=t1[:], op=ALU.mult), [qs, t1], [Qh[b]])
            P.op("act", lambda e: e.activation(out=t2[:], in_=t2[:], func=AF.Exp), [t2], [t2])
            yield
            P.op("pool", lambda e: e.tensor_tensor(out=Kh[b][:], in0=kf[:], in1=t2[:], op=ALU.mult), [kf, t2], [Kh[b]])
            P.op("act", lambda e: e.activation(out=dch[b][:], in_=b3[:, :, 63], func=AF.Exp), [bb], [dch[b]])
            state["prep_done"] = st + 1
            yield

    def a_thread():
        for gp in range(NP):
            st, pr = gp // 16, gp % 16
            while state["prep_done"] <= st or state["s_done"] < gp - (NSETA - 2):
                yield
            state["a_started"] = gp
            b = st % 2
            c0 = pr * 128
            kh_t = KhT[gp % NSETA]; am = Am[gp % NSETA]; p_a = psA[gp % 2]; p_u = psU[gp % 3]
            P.op("pe", lambda e: e.transpose(out=psT[:], in_=Kh[b][:, c0:c0 + 128], identity=ident[:]), [Kh[b], ident], [psT])
            P.op("pe", lambda e: e.matmul(p_a[:], lhsT=Kt[b][:, c0:c0 + 128], rhs=Qt[b][:, c0:c0 + 128], start=True, stop=True),
                 [Kt[b], Qt[b]], [p_a])
            yield
            P.op("act", lambda e: e.activation(out=kh_t[:], in_=psT[:], func=AF.Copy), [psT], [kh_t])
            P.op("dve", lambda e: e.tensor_tensor(out=am[:], in0=p_a[:], in1=cmask[:], op=ALU.mult), [p_a, cmask], [am])
            yield
            state["a_done"].add(gp)
            yield

    def s_thread():
        sbi = 0
        for gp in range(NP):
            while gp not in state["a_done"]:
                yield
            st, pr = gp // 16, gp % 16
            b = st % 2
            c0 = pr * 128
            am = Am[gp % NSETA]; p_o = psO[gp % 2]; kh_t = KhT[gp % NSETA]
            for ch in range(2):
                r0 = ch * 64
                p_u = psU[(2 * gp + ch) % 3]
                P.op("pe", lambda e: e.matmul(p_u[:, 0:64], lhsT=kh_t[r0:r0 + 64, :], rhs=vb[b][r0:r0 + 64, pr, :], start=True, stop=True), [kh_t, vb[b]], [p_u])
                s_cur = Sb[sbi % 2]; s_nxt = Sb[(sbi + 1) % 2]
                sbi += 1
                cidx = pr * 2 + ch
                P.op("pe", lambda e: e.matmul(p_o[0:64, r0:r0 + 64], lhsT=vb[b][r0:r0 + 64, pr, :], rhs=am[r0:r0 + 64, r0:r0 + 64],
                                              start=True, stop=False), [vb[b], am], [p_o])
                P.op("pe", lambda e: e.matmul(p_o[0:64, r0:r0 + 64], lhsT=s_cur[:], rhs=Qh[b][:, c0 + r0:c0 + r0 + 64],
                                              start=False, stop=True), [s_cur, Qh[b]], [p_o])
                P.op("dve", lambda e: e.scalar_tensor_tensor(out=s_nxt[:], in0=S32[:], scalar=dch[b][:, cidx:cidx + 1], in1=p_u[:, 0:64],
                                                             op0=ALU.mult, op1=ALU.add), [S32, dch[b], p_u], [s_nxt])
                P.op("dve", lambda e: e.scalar_tensor_tensor(out=S32[:], in0=S32[:], scalar=dch[b][:, cidx:cidx + 1], in1=p_u[:, 0:64],
                                                             op0=ALU.mult, op1=ALU.add), [S32, dch[b], p_u], [S32])
                yield
            P.op("act", lambda e: e.activation(out=ost[b][:, c0:c0 + 128], in_=p_o[0:64, :], func=AF.Copy), [p_o], [ost[b]])
            state["s_done"] = gp + 1
            if pr == 15:
                P.dma("sp", o_d, o_d[:, st * HG_ST:(st + 1) * HG_ST], ost[b], ost[b][:], owner=ost[b], disjoint=True)
            yield

    threads = [prep_thread(), a_thread(), s_thread()]
    guard = 0
    while threads:
        for g in list(threads):
            try:
                next(g)
            except StopIteration:
                threads.remove(g)
        guard += 1
        assert guard < 200000, "scheduler stuck"
    return [o_d]


def hg_consts():
    t = np.arange(HG_ST)
    rm = np.tile(((t % 64) != 0).astype(np.float32)[None, :], (128, 1))
    j = np.arange(128)[:, None]; i = np.arange(128)[None, :]
    cmk = ((j // 64 == i // 64) & (j <= i)).astype(np.float32)
    return {"hg_rmask": rm, "hg_cmask": cmk, "hg_ident": np.eye(128).astype(ml_dtypes.bfloat16)}


def hg_inputs(core, qs_fm, lf_fm, i_tm):
    hd, vh = core // 2, core % 2
    m = {"hg_q": np.ascontiguousarray(qs_fm[hd * 128:(hd + 1) * 128]),
         "hg_lf": np.ascontiguousarray(lf_fm[hd * 128:(hd + 1) * 128]),
         "hg_i": np.ascontiguousarray(i_tm[:, hd * 128 + vh * 64: hd * 128 + (vh + 1) * 64])}
    m.update(hg_consts())
    return m


RW_SEQ = 16384
RW_ST = 2048
RW_NST = RW_SEQ // RW_ST
RW_C = 128
RW_NCH = RW_ST // RW_C


def emit_rwkv(P, KCH=2, NSET=4, CHDT=BF16):
    r_d = P.dram("rw_r", [64, RW_SEQ], F32, "ExternalInput")
    kp_d = P.dram("rw_kp", [64, RW_SEQ], F32, "ExternalInput")
    kk_d = P.dram("rw_kk", [64, RW_SEQ], F32, "ExternalInput")
    a_d = P.dram("rw_a", [64, RW_SEQ], F32, "ExternalInput")
    ld_d = P.dram("rw_ld", [64, RW_SEQ], F32, "ExternalInput")
    v_d = P.dram("rw_v", [32, RW_SEQ], F32, "ExternalInput")
    rm_d = P.dram("rw_rmask", [64, RW_ST], F32, "ExternalInput")
    mk_d = P.dram("rw_masks", [128, 4 * 128], F32, "ExternalInput")
    id_d = P.dram("rw_ident", [128, 128], BF16, "ExternalInput")
    y_d = P.dram("rw_y", [32, RW_SEQ], F32, "ExternalOutput")

    def sb(name, shape, dt=F32):
        return P.sbuf("rws_" + name, shape, dt)
    r_s, kp_s, kk_s, a_s, ld_s = [sb(n, [64, RW_ST]) for n in ("r", "kp", "kk", "a", "ld")]
    v_s = sb("v", [32, RW_ST])
    G = sb("G", [64, RW_ST]); x1 = sb("x1", [64, RW_ST]); x2 = sb("x2", [64, RW_ST]); x3 = sb("x3", [64, RW_ST])
    dC = [sb("dC%d" % i, [64, RW_NCH]) for i in range(2)]
    AR = [sb("AR%d" % i, [64, RW_NCH, 2, RW_C], BF16) for i in range(2)]
    Bt = [sb("Bt%d" % i, [64, RW_ST], BF16) for i in range(2)]
    Kt = [sb("Kt%d" % i, [64, RW_ST], BF16) for i in range(2)]
    Bh = [sb("Bh%d" % i, [64, RW_ST], BF16) for i in range(2)]
    Kh = [sb("Kh%d" % i, [64, RW_ST], BF16) for i in range(2)]
    vb = [sb("vb%d" % i, [32, RW_ST], BF16) for i in range(2)]
    yst = [sb("yst%d" % i, [32, RW_ST]) for i in range(2)]
    rmask = sb("rmask", [64, RW_ST])
    masks = sb("masks", [128, 4 * 128])
    ident = sb("ident", [128, 128], BF16)
    tok = [sb("tok%d" % i, [128, 160], BF16) for i in range(NSET)]
    XTb = [sb("XTb%d" % i, [128, 128], BF16) for i in range(NSET)]
    Arb = [sb("Arb%d" % i, [128, 128], BF16) for i in range(NSET)]
    Ak = [sb("Ak%d" % i, [128, 256], BF16) for i in range(NSET)]
    Mf = [[sb("Mf%d_%d" % (k, i), [128, 128], CHDT) for i in range(2)] for k in range(KCH)]
    Nf = [[sb("Nf%d_%d" % (k, i), [128, 128], CHDT) for i in range(2)] for k in range(KCH)]
    XT = [[sb("XT%d_%d" % (k, i), [128, 128], CHDT) for i in range(2)] for k in range(KCH)]
    H32 = sb("H32", [64, 32])
    Hb = [sb("Hb%d" % i, [64, 32], BF16) for i in range(2)]
    Wb = [sb("Wb%d" % i, [128, 32], BF16) for i in range(2)]
    Ub = [sb("Ub%d" % i, [128, 32], BF16) for i in range(2)]
    bT = P.psum("rw_bT", [128, 1024], BF16)
    bA = [P.psum("rw_bA%d" % k, [128, 512], F32) for k in range(KCH)]
    bC1 = [P.psum("rw_bC1_%d" % k, [128, 512], F32) for k in range(KCH)]
    bC2 = [P.psum("rw_bC2_%d" % k, [128, 512], F32) for k in range(KCH)]
    bS = P.psum("rw_bS", [128, 512], F32)

    mSU = masks[:, 0:128]; mIU = masks[:, 128:256]; mSL = masks[:, 256:384]; mI = masks[:, 384:512]
    P.dma("sp", rmask, rmask[:], rm_d, rm_d[:])
    P.dma("sp", masks, masks[:], mk_d, mk_d[:])
    P.dma("sp", ident, ident[:], id_d, id_d[:])
    P.op("dve", lambda e: e.memset(H32[:], 0.0), [], [H32])
    P.op("dve", lambda e: e.memset(Hb[0][:], 0.0), [], [Hb[0]])
    G3 = G[:, :].rearrange("p (n c) -> p n c", c=RW_C)
    x13 = x1[:, :].rearrange("p (n c) -> p n c", c=RW_C)
    x23 = x2[:, :].rearrange("p (n c) -> p n c", c=RW_C)
    NCHUNK = RW_NST * RW_NCH
    state = {"prep_done": 0, "inv_started": 0, "inv_done": set(), "state_done": 0}

    def prep_thread():
        for st in range(RW_NST):
            while state["inv_started"] < RW_NCH * st - 10:
                yield
            t0 = st * RW_ST
            b = st % 2
            for i, (s_, d_) in enumerate(((r_s, r_d), (kp_s, kp_d), (kk_s, kk_d), (a_s, a_d), (ld_s, ld_d))):
                P.dma(("sp", "act", "pool")[i % 3], s_, s_[:], d_, d_[:, t0:t0 + RW_ST])
            P.dma("sp", v_s, v_s[:], v_d, v_d[:, t0:t0 + RW_ST])
            yield
            P.op("pool", lambda e: e.tensor_copy(out=vb[b][:], in_=v_s[:]), [v_s], [vb[b]])
            P.op("dve", lambda e: e.tensor_tensor_scan(out=G[:], data0=rmask[:], data1=ld_s[:], initial=0.0, op0=ALU.mult, op1=ALU.add),
                 [rmask, ld_s], [G])
            yield
            Gl = G3[:, :, RW_C - 1:RW_C].to_broadcast([64, RW_NCH, RW_C])
            P.op("dve", lambda e: e.tensor_tensor(out=x1[:], in0=G[:], in1=ld_s[:], op=ALU.subtract), [G, ld_s], [x1])
            P.op("act", lambda e: e.activation(out=x1[:], in_=x1[:], func=AF.Exp), [x1], [x1])
            yield
            P.op("dve", lambda e: e.scalar_tensor_tensor(out=AR[b][:, :, 0, :], in0=x13, scalar=-1.0,
                                                         in1=kk_s[:, :].rearrange("p (n c) -> p n c", c=RW_C),
                                                         op0=ALU.mult, op1=ALU.mult), [x1, kk_s], [AR[b]])
            P.op("act", lambda e: e.activation(out=x2[:], in_=G[:], func=AF.Exp), [G], [x2])
            yield
            P.op("pool", lambda e: e.tensor_tensor(out=AR[b][:, :, 1, :], in0=x23, in1=r_s[:, :].rearrange("p (n c) -> p n c", c=RW_C),
                                                   op=ALU.mult), [x2, r_s], [AR[b]])
            P.op("pool", lambda e: e.tensor_tensor(out=x3[:], in0=kk_s[:], in1=a_s[:], op=ALU.mult), [kk_s, a_s], [x3])
            P.op("act", lambda e: e.activation(out=x1[:], in_=G[:], func=AF.Exp, scale=-1.0), [G], [x1])
            yield
            P.op("dve", lambda e: e.tensor_tensor(out=Bt[b][:], in0=x3[:], in1=x1[:], op=ALU.mult), [x3, x1], [Bt[b]])
            P.op("pool", lambda e: e.tensor_tensor(out=Kt[b][:], in0=kp_s[:], in1=x1[:], op=ALU.mult), [kp_s, x1], [Kt[b]])
            yield
            P.op("dve", lambda e: e.tensor_tensor(out=x23, in0=Gl, in1=G3, op=ALU.subtract), [G], [x2])
            P.op("act", lambda e: e.activation(out=x2[:], in_=x2[:], func=AF.Exp), [x2], [x2])
            yield
            P.op("dve", lambda e: e.tensor_tensor(out=Bh[b][:], in0=x3[:], in1=x2[:], op=ALU.mult), [x3, x2], [Bh[b]])
            P.op("pool", lambda e: e.tensor_tensor(out=Kh[b][:], in0=kp_s[:], in1=x2[:], op=ALU.mult), [kp_s, x2], [Kh[b]])
            P.op("act", lambda e: e.activation(out=dC[b][:], in_=G3[:, :, RW_C - 1], func=AF.Exp), [G], [dC[b]])
            state["prep_done"] = st + 1
            yield

    def inv_thread(k):
        for gn in range(k, NCHUNK, KCH):
            st, n = gn // RW_NCH, gn % RW_NCH
            while state["prep_done"] <= st or state["state_done"] < gn - (NSET - 1):
                yield
            state["inv_started"] = max(state["inv_started"], gn)
            b = st % 2
            c0 = n * RW_C
            s_ = gn % NSET
            tk, xtb, arb, ak = tok[s_], XTb[s_], Arb[s_], Ak[s_]
            pT = bT; BA = bA[k]; B1 = bC1[k]; B2 = bC2[k]
            P.op("pe", lambda e: e.transpose(out=pT[:, k * 160:k * 160 + 64], in_=Bh[b][:, c0:c0 + RW_C], identity=ident[0:64, 0:64]), [Bh[b], ident], [pT])
            P.op("pe", lambda e: e.transpose(out=pT[:, k * 160 + 64:k * 160 + 128], in_=Kh[b][:, c0:c0 + RW_C], identity=ident[0:64, 0:64]), [Kh[b], ident], [pT])
            P.op("pe", lambda e: e.transpose(out=pT[:, k * 160 + 128:k * 160 + 160], in_=vb[b][:, c0:c0 + RW_C], identity=ident[0:32, 0:32]), [vb[b], ident], [pT])
            arv = AR[b][:, n, :, :].rearrange("p a c -> p (a c)")
            P.op("pe", lambda e: e.matmul(BA[:, 0:256], lhsT=Bt[b][:, c0:c0 + RW_C], rhs=arv, start=True, stop=True), [Bt[b], AR[b]], [BA])
            P.op("pe", lambda e: e.matmul(BA[:, 256:512], lhsT=Kt[b][:, c0:c0 + RW_C], rhs=arv, start=True, stop=True), [Kt[b], AR[b]], [BA])
            P.op("pe", lambda e: e.matmul(B2[:, 128:256], lhsT=AR[b][:, n, 0, :], rhs=Bt[b][:, c0:c0 + RW_C], start=True, stop=True), [AR[b], Bt[b]], [B2])
            yield
            mf, nf, xt = Mf[k][0], Nf[k][0], XT[k][0]
            P.op("act", lambda e: e.activation(out=tk[:], in_=pT[:, k * 160:(k + 1) * 160], func=AF.Copy), [pT], [tk])
            P.op("dve", lambda e: e.tensor_tensor(out=mf[:], in0=BA[:, 0:128], in1=mSU, op=ALU.mult), [BA, masks], [mf])
            P.op("dve", lambda e: e.tensor_tensor(out=nf[:], in0=B2[:, 128:256], in1=mSL, op=ALU.mult), [B2, masks], [nf])
            P.op("pool", lambda e: e.tensor_tensor(out=xt[:], in0=mf[:], in1=mI, op=ALU.add), [mf, masks], [xt])
            yield
            P.op("dve", lambda e: e.tensor_tensor(out=arb[:], in0=BA[:, 128:256], in1=mIU, op=ALU.mult), [BA, masks], [arb])
            P.op("dve", lambda e: e.tensor_tensor(out=ak[:], in0=BA[:, 256:512], in1=masks[:, 0:256], op=ALU.mult), [BA, masks], [ak])
            cur = 0
            for it in range(6):
                mo, no = Mf[k][cur], Nf[k][cur]
                mn, nn = Mf[k][1 - cur], Nf[k][1 - cur]
                xo, xn = XT[k][cur], XT[k][1 - cur]
                P.op("pe", lambda e: e.matmul(B1[:, 0:128], lhsT=mo[:], rhs=no[:], start=True, stop=True), [mo, no], [B1])
                if it < 5:
                    P.op("pe", lambda e: e.matmul(B2[:, 128:256], lhsT=no[:], rhs=mo[:], start=True, stop=True), [mo, no], [B2])
                yield
                P.op("act", lambda e: e.activation(out=nn[:], in_=B1[:, 0:128], func=AF.Copy), [B1], [nn])
                if it < 5:
                    P.op("dve", lambda e: e.tensor_copy(out=mn[:], in_=B2[:, 128:256]), [B2], [mn])
                yield
                P.op("pe", lambda e: e.matmul(BA[:, 0:128], lhsT=nn[:], rhs=xo[:], start=True, stop=True), [nn, xo], [BA])
                P.op("dve", lambda e: e.tensor_tensor(out=xn[:], in0=BA[:, 0:128], in1=xo[:], op=ALU.add), [BA, xo], [xn])
                cur = 1 - cur
            P.op("pool", lambda e: e.tensor_copy(out=xtb[:], in_=XT[k][cur][:]), [XT[k][cur]], [xtb])
            state["inv_done"].add(gn)
            yield

    def state_thread():
        hbi = 0
        for gn in range(NCHUNK):
            while gn not in state["inv_done"]:
                yield
            st, n = gn // RW_NCH, gn % RW_NCH
            b = st % 2
            c0 = n * RW_C
            s_ = gn % NSET
            tk, xtb, arb, ak = tok[s_], XTb[s_], Arb[s_], Ak[s_]
            hb_cur = Hb[hbi % 2]; hb_nxt = Hb[(hbi + 1) % 2]; wb = Wb[hbi % 2]; ub = Ub[hbi % 2]
            hbi += 1
            P.op("pe", lambda e: e.matmul(bS[:, 0:32], lhsT=AR[b][:, n, 0, :], rhs=hb_cur[:], start=True, stop=False), [AR[b], hb_cur], [bS])
            P.op("pe", lambda e: e.matmul(bS[:, 0:32], lhsT=ak[:, 0:128], rhs=tk[:, 128:160], start=False, stop=True), [ak, tk], [bS])
            P.op("act", lambda e: e.activation(out=wb[:], in_=bS[:, 0:32], func=AF.Copy), [bS], [wb])
            yield
            P.op("pe", lambda e: e.matmul(bS[:, 32:64], lhsT=xtb[:], rhs=wb[:], start=True, stop=True), [xtb, wb], [bS])
            P.op("act", lambda e: e.activation(out=ub[:], in_=bS[:, 32:64], func=AF.Copy), [bS], [ub])
            yield
            P.op("pe", lambda e: e.matmul(bS[0:64, 64:96], lhsT=tk[:, 0:64], rhs=ub[:], start=True, stop=False), [tk, ub], [bS])
            P.op("pe", lambda e: e.matmul(bS[0:64, 64:96], lhsT=tk[:, 64:128], rhs=tk[:, 128:160], start=False, stop=True), [tk], [bS])
            P.op("pe", lambda e: e.matmul(bS[0:32, 128:256], lhsT=hb_cur[:], rhs=AR[b][:, n, 1, :], start=True, stop=False), [hb_cur, AR[b]], [bS])
            P.op("pe", lambda e: e.matmul(bS[0:32, 128:256], lhsT=ub[:], rhs=arb[:], start=False, stop=False), [ub, arb], [bS])
            P.op("pe", lambda e: e.matmul(bS[0:32, 128:256], lhsT=tk[:, 128:160], rhs=ak[:, 128:256], start=False, stop=True), [tk, ak], [bS])
            P.op("dve", lambda e: e.scalar_tensor_tensor(out=H32[:], in0=H32[:], scalar=dC[b][:, n:n + 1], in1=bS[0:64, 64:96],
                                                         op0=ALU.mult, op1=ALU.add), [H32, dC[b], bS], [H32])
            P.op("pool", lambda e: e.tensor_copy(out=hb_nxt[:], in_=H32[:]), [H32], [hb_nxt])
            P.op("act", lambda e: e.activation(out=yst[b][:, c0:c0 + RW_C], in_=bS[0:32, 128:256], func=AF.Copy), [bS], [yst[b]])
            state["state_done"] = gn + 1
            if n == RW_NCH - 1:
                P.dma("sp", y_d, y_d[:, st * RW_ST:(st + 1) * RW_ST], yst[b], yst[b][:], owner=yst[b], disjoint=True)
            yield

    threads = [prep_thread()] + [inv_thread(k) for k in range(KCH)] + [state_thread()]
    guard = 0
    while threads:
        for g in list(threads):
            try:
                next(g)
            except StopIteration:
                threads.remove(g)
        guard += 1
        assert guard < 200000, "scheduler stuck"
    return [y_d]


def rw_consts():
    t = np.arange(RW_ST)
    rm = np.tile(((t % RW_C) != 0).astype(np.float32)[None, :], (64, 1))
    j = np.arange(128)[:, None]; i = np.arange(128)[None, :]
    su = (i > j).astype(np.float32); iu = (i >= j).astype(np.float32); sl = (j > i).astype(np.float32)
    masks = np.concatenate([su, iu, sl, np.eye(128, dtype=np.float32)], axis=1)
    return {"rw_rmask": rm, "rw_masks": masks, "rw_ident": np.eye(128).astype(ml_dtypes.bfloat16)}


def rw_inputs(core, fmr):
    hd, vh = core // 2, core % 2
    m = {"rw_" + n: np.ascontiguousarray(fmr[n][hd * 64:(hd + 1) * 64]) for n in ("r", "kp", "kk", "a", "ld")}
    m["rw_v"] = np.ascontiguousarray(fmr["v"][hd * 64 + vh * 32: hd * 64 + (vh + 1) * 32])
    m.update(rw_consts())
    return m


C1_NT = 2048
C1_TG = 512
C1_NG = C1_NT // C1_TG
CF_HGO, CF_GS, CF_SWO, CF_RWY, CF_R, CF_KP, CF_V, CF_G = 0, 512, 1024, 1280, 1536, 1792, 2048, 2304
CF_ROWS = 2560
CP_GN, CP_LNW, CP_LNB, CP_RK, CP_N = 0, 4, 6, 8, 10


def emit_c1(P, layer):
    h_d = P.dram("c1_h", [C1_NT, 1024], F32, "ExternalInput")
    cf_d = P.dram("c1_cf", [CF_ROWS, C1_NT], F32, "ExternalInput")
    wo_d = P.dram("c1_wout", [1024, 1024], F32, "ExternalInput")
    cp_d = P.dram("c1_cp", [128, CP_N], F32, "ExternalInput")
    k_d = P.dram("c1_consts", [128, 256], F32, "ExternalInput")
    hm_d = P.dram("c1_hmid", [C1_NT, 1024], F32, "ExternalOutput")

    def sb(name, shape, dt=F32):
        return P.sbuf("c1s_" + name, shape, dt)
    wob = sb("wob", [128, 8, 1024], BF16)
    wst = [sb("wst%d" % i, [128, 1024]) for i in range(2)]
    cp = sb("cp", [128, CP_N])
    kc = sb("kc", [128, 256])
    eps1 = sb("eps1", [128, 1]); eps2 = sb("eps2", [128, 1])
    oT = [sb("oT%d" % i, [128, 8, C1_TG], BF16) for i in range(2)]
    fin = [sb("fin%d" % i, [128, C1_TG]) for i in range(6)]
    tmp = [sb("tmp%d" % i, [128, C1_TG]) for i in range(6)]
    hin = [sb("hin%d" % i, [128, 1024]) for i in range(2)]
    ps = [P.psum("c1_ps%d" % i, [128, 512], F32) for i in range(4)]
    pd = [P.psum("c1_pd%d" % i, [128, 1024], F32) for i in range(2)]

    ones_m = kc[:, 0:128]; blk_m = kc[:, 128:256]
    P.dma("sp", cp, cp[:], cp_d, cp_d[:])
    P.dma("sp", kc, kc[:], k_d, k_d[:])
    P.op("dve", lambda e: e.memset(eps1[:], 1e-6), [], [eps1])
    P.op("dve", lambda e: e.memset(eps2[:], 64e-5), [], [eps2])
    for k in range(8):
        s_ = wst[k % 2]
        P.dma("sp" if k % 2 == 0 else "act", s_, s_[:], wo_d, wo_d[k * 128:(k + 1) * 128, :])
        P.op("pool", lambda e: e.tensor_copy(out=wob[:, k, :], in_=s_[:]), [s_], [wob])

    fi = [0]; ti = [0]; pi = [0]; qi = [0]

    def load(row0, t0):
        f = fin[fi[0] % 6]; fi[0] += 1
        q = ("sp", "act")[qi[0] % 2]; qi[0] += 1
        P.dma(q, f, f[:], cf_d, cf_d[row0:row0 + 128, t0:t0 + C1_TG])
        return f

    def T_():
        t = tmp[ti[0] % 6]; ti[0] += 1
        return t

    def PS():
        p = ps[pi[0] % 4]; pi[0] += 1
        return p

    for g in range(C1_NG):
        t0 = g * C1_TG
        ot = oT[g % 2]
        for c in range(4):
            o = load(CF_HGO + c * 128, t0); gs = load(CF_GS + c * 128, t0)
            sq = T_(); p_ = PS(); rs = T_()
            P.op("pool", lambda e: e.tensor_tensor(out=sq[:], in0=o[:], in1=o[:], op=ALU.mult), [o], [sq])
            P.op("pe", lambda e: e.matmul(p_[:], lhsT=ones_m, rhs=sq[:], start=True, stop=True), [kc, sq], [p_])
            P.op("act", lambda e: e.activation(out=rs[:], in_=p_[:], func=AF.Sqrt, bias=eps1[:, 0:1]), [p_, eps1], [rs])
            P.op("dve", lambda e: e.reciprocal(out=rs[:], in_=rs[:]), [rs], [rs])
            P.op("dve", lambda e: e.scalar_tensor_tensor(out=rs[:], in0=rs[:], scalar=cp[:, CP_GN + c:CP_GN + c + 1], in1=o[:],
                                                         op0=ALU.mult, op1=ALU.mult), [rs, cp, o], [rs])
            P.op("pool", lambda e: e.tensor_tensor(out=ot[:, c, :], in0=rs[:], in1=gs[:], op=ALU.mult), [rs, gs], [ot])
        for c in range(2):
            o = load(CF_SWO + c * 128, t0)
            P.op("pool", lambda e: e.tensor_copy(out=ot[:, 4 + c, :], in_=o[:]), [o], [ot])
        for c in range(2):
            y = load(CF_RWY + c * 128, t0); r_ = load(CF_R + c * 128, t0); kp = load(CF_KP + c * 128, t0)
            v_ = load(CF_V + c * 128, t0); g_ = load(CF_G + c * 128, t0)
            pm = PS(); pq = PS(); pb = PS()
            ysq = T_(); mean = T_(); var = T_(); rk = T_()
            P.op("pe", lambda e: e.matmul(pm[:], lhsT=blk_m, rhs=y[:], start=True, stop=True), [kc, y], [pm])
            P.op("pool", lambda e: e.tensor_tensor(out=ysq[:], in0=y[:], in1=y[:], op=ALU.mult), [y], [ysq])
            P.op("pe", lambda e: e.matmul(pq[:], lhsT=blk_m, rhs=ysq[:], start=True, stop=True), [kc, ysq], [pq])
            P.op("act", lambda e: e.activation(out=mean[:], in_=pm[:], func=AF.Copy), [pm], [mean])
            P.op("pool", lambda e: e.tensor_tensor(out=var[:], in0=mean[:], in1=mean[:], op=ALU.mult), [mean], [var])
            P.op("dve", lambda e: e.tensor_tensor(out=var[:], in0=pq[:], in1=var[:], op=ALU.subtract), [pq, var], [var])
            P.op("act", lambda e: e.activation(out=var[:], in_=var[:], func=AF.Sqrt, bias=eps2[:, 0:1]), [var, eps2], [var])
            P.op("dve", lambda e: e.reciprocal(out=var[:], in_=var[:]), [var], [var])
            P.op("dve", lambda e: e.tensor_tensor(out=mean[:], in0=y[:], in1=mean[:], op=ALU.subtract), [y, mean], [mean])
            P.op("dve", lambda e: e.tensor_tensor(out=mean[:], in0=mean[:], in1=var[:], op=ALU.mult), [mean, var], [mean])
            P.op("dve", lambda e: e.tensor_scalar(out=mean[:], in0=mean[:], scalar1=cp[:, CP_LNW + c:CP_LNW + c + 1],
                                                   scalar2=cp[:, CP_LNB + c:CP_LNB + c + 1], op0=ALU.mult, op1=ALU.add), [mean, cp], [mean])
            P.op("dve", lambda e: e.scalar_tensor_tensor(out=rk[:], in0=r_[:], scalar=cp[:, CP_RK + c:CP_RK + c + 1], in1=kp[:],
                                                         op0=ALU.mult, op1=ALU.mult), [r_, cp, kp], [rk])
            P.op("pe", lambda e: e.matmul(pb[:], lhsT=blk_m, rhs=rk[:], start=True, stop=True), [kc, rk], [pb])
            P.op("dve", lambda e: e.scalar_tensor_tensor(out=rk[:], in0=pb[:], scalar=64.0, in1=v_[:], op0=ALU.mult, op1=ALU.mult),
                 [pb, v_], [rk])
            P.op("pool", lambda e: e.tensor_tensor(out=mean[:], in0=mean[:], in1=rk[:], op=ALU.add), [mean, rk], [mean])
            P.op("pool", lambda e: e.tensor_tensor(out=ot[:, 6 + c, :], in0=mean[:], in1=g_[:], op=ALU.mult), [mean, g_], [ot])
        for t in range(4):
            hi = hin[t % 2]; p_d = pd[t % 2]
            r0 = t0 + t * 128
            P.dma("sp", hi, hi[:], h_d, h_d[r0:r0 + 128, :])
            for half in range(2):
                for c in range(8):
                    P.op("pe", lambda e: e.matmul(p_d[:, half * 512:(half + 1) * 512], lhsT=ot[:, c, t * 128:(t + 1) * 128],
                                                  rhs=wob[:, c, half * 512:(half + 1) * 512], start=(c == 0), stop=(c == 7)),
                         [ot, wob], [p_d])
            for half in range(2):
                P.op("dve", lambda e: e.tensor_tensor(out=hi[:, half * 512:(half + 1) * 512], in0=hi[:, half * 512:(half + 1) * 512],
                                                      in1=p_d[:, half * 512:(half + 1) * 512], op=ALU.add), [hi, p_d], [hi])
            P.dma("act", hm_d, hm_d[r0:r0 + 128, :], hi, hi[:], owner=hi, disjoint=True)
    return [hm_d]


def c1_inputs(layer, inp, core, h_full, cf_full):
    l = layer
    fmj = lambda v: np.ascontiguousarray(v.reshape(-1, 128).T)
    cp = np.zeros((128, CP_N), np.float32)
    cp[:, CP_GN:CP_GN + 4] = fmj(inp['hgrn_gnorm_g'][l])
    cp[:, CP_LNW:CP_LNW + 2] = fmj(inp['rwkv_ln_w'][l])
    cp[:, CP_LNB:CP_LNB + 2] = fmj(inp['rwkv_ln_b'][l])
    cp[:, CP_RK:CP_RK + 2] = fmj(inp['rwkv_r_k'][l].reshape(-1))
    kc = np.concatenate([np.full((128, 128), 1.0 / 128), np.kron(np.eye(2), np.full((64, 64), 1.0 / 64))], axis=1).astype(np.float32)
    return {"c1_h": np.ascontiguousarray(h_full[core * C1_NT:(core + 1) * C1_NT]),
            "c1_cf": np.ascontiguousarray(cf_full[:, core * C1_NT:(core + 1) * C1_NT]),
            "c1_wout": np.ascontiguousarray(inp['w_out'][l]), "c1_cp": cp, "c1_consts": kc}


C2_NT = 2048
C2_GT = 256
C2_NGR = C2_NT // C2_GT
C2_DFF = 2816
C2_NJ = C2_DFF // 128


def norm_transpose(P, src_dram_t, src_ap, hres, hnb, st, junk, epsc, ident, pst, dstT, col0, dma_q="sp", gB=None):
    P.dma(dma_q, hres, hres[:], src_dram_t, src_ap)
    P.op("act", lambda e: e.activation(out=junk[:], in_=hres[:], func=AF.Square, accum_out=st[:, 0:1]), [hres], [junk, st])
    P.op("act", lambda e: e.activation(out=st[:, 1:2], in_=st[:, 0:1], func=AF.Sqrt, scale=1.0 / 1024, bias=epsc[:, 0:1]), [st, epsc], [st])
    P.op("dve", lambda e: e.reciprocal(out=st[:, 2:3], in_=st[:, 1:2]), [st], [st])
    if gB is None:
        P.op("dve", lambda e: e.tensor_scalar(out=hnb[:], in0=hres[:], scalar1=st[:, 2:3], scalar2=None, op0=ALU.mult), [hres, st], [hnb])
    else:
        P.op("dve", lambda e: e.scalar_tensor_tensor(out=hnb[:], in0=hres[:], scalar=st[:, 2:3], in1=gB[:], op0=ALU.mult, op1=ALU.mult),
             [hres, st, gB], [hnb])
    for c in range(8):
        P.op("pe", lambda e: e.transpose(out=pst[:, c * 128:(c + 1) * 128], in_=hnb[:, c * 128:(c + 1) * 128], identity=ident[:]),
             [hnb, ident], [pst])
    P.op("act", lambda e: e.activation(out=dstT[:, :, col0:col0 + 128], in_=pst[:].rearrange("p (c t) -> p c t", c=8), func=AF.Copy),
         [pst], [dstT])


def emit_c2(P):
    h_d = P.dram("c2_h", [C2_NT, 1024], F32, "ExternalInput")
    hh_d = P.dram("c2_hh", [128, 1024], F32, "ExternalInput")
    up_d = P.dram("c2_up", [1024, 2 * C2_DFF], F32, "ExternalInput")
    dn_d = P.dram("c2_dn", [C2_DFF, 1024], F32, "ExternalInput")
    pp_d = P.dram("c2_pp", [128, 8 + 44 * 4], F32, "ExternalInput")
    id_d = P.dram("c2_ident", [128, 128], BF16, "ExternalInput")
    gb_d = P.dram("c2_gB", [128, 1024], F32, "ExternalInput")
    o_d = P.dram("c2_out", [C2_NT, 1024], F32, "ExternalOutput")

    def sb(name, shape, dt=F32):
        return P.sbuf("c2s_" + name, shape, dt)
    upb = [sb("upb%d" % k, [128, 2 * C2_DFF], BF16) for k in range(8)]
    dnb = sb("dnb", [128, C2_NJ, 1024], BF16)
    pp = sb("pp", [128, 8 + 44 * 4])
    ident = sb("ident", [128, 128], BF16)
    epsc = sb("epsc", [128, 1])
    hres = [sb("hres%d" % i, [128, 1024]) for i in range(2)]
    hnb = [sb("hnb%d" % i, [128, 1024], BF16) for i in range(2)]
    st = [sb("st%d" % i, [128, 4]) for i in range(2)]
    junk = sb("junk", [128, 1024])
    gB = sb("gB", [128, 1024])
    wstg = [sb("wstg%d" % i, [128, 1024]) for i in range(4)]
    hnT = sb("hnT", [128, 8, C2_GT], BF16)
    ug = [sb("ug%d" % i, [128, C2_GT + 2]) for i in range(2)]
    uv = [sb("uv%d" % i, [128, C2_GT + 2]) for i in range(2)]
    tg = [sb("tg%d" % i, [128, C2_GT]) for i in range(2)]
    tv = [sb("tv%d" % i, [128, C2_GT]) for i in range(2)]
    actT = sb("actT", [128, C2_NJ, C2_GT], BF16)
    uprev = sb("uprev", [128, 44, 2])
    pst = P.psum("c2_pst", [128, 1024], BF16)
    pu = [P.psum("c2_pu%d" % i, [128, 512], F32) for i in range(3)]
    pd = [P.psum("c2_pd%d" % i, [128, 512], F32) for i in range(4)]

    P.dma("sp", pp, pp[:], pp_d, pp_d[:])
    P.dma("sp", ident, ident[:], id_d, id_d[:])
    P.op("dve", lambda e: e.memset(epsc[:], 1e-6), [], [epsc])
    P.dma("sp", gB, gB[:], gb_d, gb_d[:])
    wi = 0
    for k in range(8):
        for c0 in range(0, 2 * C2_DFF, 1024):
            w_ = min(1024, 2 * C2_DFF - c0)
            s_ = wstg[wi % 4]; wi += 1
            P.dma(("sp", "act")[wi % 2], s_, s_[:, 0:w_], up_d, up_d[k * 128:(k + 1) * 128, c0:c0 + w_])
            if wi % 2 == 0:
                P.op("dve", lambda e: e.tensor_copy(out=upb[k][:, c0:c0 + w_], in_=s_[:, 0:w_]), [s_], [upb[k]])
            else:
                P.op("pool", lambda e: e.tensor_copy(out=upb[k][:, c0:c0 + w_], in_=s_[:, 0:w_]), [s_], [upb[k]])
    for j in range(C2_NJ):
        s_ = wstg[wi % 4]; wi += 1
        P.dma(("sp", "act")[wi % 2], s_, s_[:], dn_d, dn_d[j * 128:(j + 1) * 128, :])
        if wi % 2 == 0:
            P.op("dve", lambda e: e.tensor_copy(out=dnb[:, j, :], in_=s_[:]), [s_], [dnb])
        else:
            P.op("pool", lambda e: e.tensor_copy(out=dnb[:, j, :], in_=s_[:]), [s_], [dnb])

    def cw(c, i):
        o = 8 + c * 4 + i
        return pp[:, o:o + 1]

    norm_transpose(P, hh_d, hh_d[:], hres[0], hnb[0], st[0], junk, epsc, ident, pst, hnT, 0, gB=gB)
    for c in range(44):
        p_ = pu[c % 3]
        for k in range(8):
            P.op("pe", lambda e: e.matmul(p_[:, 0:2], lhsT=upb[k][:, c * 128:(c + 1) * 128], rhs=hnT[:, k, 126:128],
                                          start=(k == 0), stop=(k == 7)), [upb[k], hnT], [p_])
        P.op("act", lambda e: e.activation(out=uprev[:, c, :], in_=p_[:, 0:2], func=AF.Copy), [p_], [uprev])

    ui = 0
    for g in range(C2_NGR):
        r0 = g * C2_GT
        for t in range(2):
            norm_transpose(P, h_d, h_d[r0 + t * 128:r0 + (t + 1) * 128, :], hres[t], hnb[t], st[t], junk, epsc, ident, pst, hnT, t * 128, gB=gB)
        for j in range(C2_NJ):
            cg, cv = j, C2_NJ + j
            p_ = pu[ui % 3]; u_g = ug[ui % 2]; u_v = uv[ui % 2]; t_g = tg[ui % 2]; t_v = tv[ui % 2]
            ui += 1
            for k in range(8):
                P.op("pe", lambda e: e.matmul(p_[:, 0:C2_GT], lhsT=upb[k][:, cg * 128:(cg + 1) * 128], rhs=hnT[:, k, :],
                                              start=(k == 0), stop=(k == 7)), [upb[k], hnT], [p_])
            for k in range(8):
                P.op("pe", lambda e: e.matmul(p_[:, C2_GT:2 * C2_GT], lhsT=upb[k][:, cv * 128:(cv + 1) * 128], rhs=hnT[:, k, :],
                                              start=(k == 0), stop=(k == 7)), [upb[k], hnT], [p_])
            P.op("pool", lambda e: e.tensor_copy(out=u_g[:, 0:2], in_=uprev[:, cg, :]), [uprev], [u_g])
            P.op("pool", lambda e: e.tensor_copy(out=u_v[:, 0:2], in_=uprev[:, cv, :]), [uprev], [u_v])
            P.op("act", lambda e: e.activation(out=u_g[:, 2:C2_GT + 2], in_=p_[:, 0:C2_GT], func=AF.Copy), [p_], [u_g])
            P.op("act", lambda e: e.activation(out=u_v[:, 2:C2_GT + 2], in_=p_[:, C2_GT:2 * C2_GT], func=AF.Copy), [p_], [u_v])
            P.op("pool", lambda e: e.tensor_copy(out=uprev[:, cg, :], in_=u_g[:, C2_GT:C2_GT + 2]), [u_g], [uprev])
            P.op("pool", lambda e: e.tensor_copy(out=uprev[:, cv, :], in_=u_v[:, C2_GT:C2_GT + 2]), [u_v], [uprev])
            for (u_, t_, c_) in ((u_g, t_g, cg), (u_v, t_v, cv)):
                P.op("dve", lambda e: e.tensor_scalar(out=t_[:], in0=u_[:, 2:C2_GT + 2], scalar1=cw(c_, 2), scalar2=cw(c_, 3),
                                                      op0=ALU.mult, op1=ALU.add), [u_, pp], [t_])
                P.op("dve", lambda e: e.scalar_tensor_tensor(out=t_[:], in0=u_[:, 1:C2_GT + 1], scalar=cw(c_, 1), in1=t_[:],
                                                             op0=ALU.mult, op1=ALU.add), [u_, pp, t_], [t_])
                P.op("dve", lambda e: e.scalar_tensor_tensor(out=t_[:], in0=u_[:, 0:C2_GT], scalar=cw(c_, 0), in1=t_[:],
                                                             op0=ALU.mult, op1=ALU.add), [u_, pp, t_], [t_])
            P.op("act", lambda e: e.activation(out=t_g[:], in_=t_g[:], func=AF.Silu), [t_g], [t_g])
            P.op("pool", lambda e: e.tensor_tensor(out=actT[:, j, :], in0=t_g[:], in1=t_v[:], op=ALU.mult), [t_g, t_v], [actT])
        for t in range(2):
            for half in range(2):
                p_d = pd[(2 * t + half) % 4]
                for j in range(C2_NJ):
                    P.op("pe", lambda e: e.matmul(p_d[:], lhsT=actT[:, j, t * 128:(t + 1) * 128], rhs=dnb[:, j, half * 512:(half + 1) * 512],
                                                  start=(j == 0), stop=(j == C2_NJ - 1)), [actT, dnb], [p_d])
                P.op("dve", lambda e: e.tensor_tensor(out=hres[t][:, half * 512:(half + 1) * 512], in0=hres[t][:, half * 512:(half + 1) * 512],
                                                      in1=p_d[:], op=ALU.add), [hres[t], p_d], [hres[t]])
            P.dma("act", o_d, o_d[r0 + t * 128:r0 + (t + 1) * 128, :], hres[t], hres[t][:], owner=hres[t], disjoint=True)
    return [o_d]


def c2_inputs(layer, inp, core, hmid_full):
    l = layer
    fmj = lambda v: np.ascontiguousarray(v.reshape(-1, 128).T)
    pp = np.zeros((128, 8 + 44 * 4), np.float32)
    pp[:, 0:8] = fmj(inp['norm_ffn_g'][l])
    cwb = np.concatenate([inp['ffn_conv_w'][l], inp['ffn_conv_b'][l][None]], axis=0)
    pp[:, 8:] = cwb.reshape(4, 44, 128).transpose(2, 1, 0).reshape(128, 176)
    return {"c2_h": np.ascontiguousarray(hmid_full[core * C2_NT:(core + 1) * C2_NT]),
            "c2_hh": np.ascontiguousarray(hmid_full[core * C2_NT - 128:core * C2_NT]) if core > 0 else np.zeros((128, 1024), np.float32),
            "c2_up": np.ascontiguousarray(inp['ffn_up'][l]), "c2_dn": np.ascontiguousarray(inp['ffn_down'][l]),
            "c2_pp": pp, "c2_ident": np.eye(128).astype(ml_dtypes.bfloat16),
            "c2_gB": np.ascontiguousarray(np.broadcast_to(inp['norm_ffn_g'][l][None, :], (128, 1024))).astype(np.float32)}


C3_NT = 2048


def emit_c3(P, final):
    h_d = P.dram("c3_h", [C3_NT, 1024], F32, "ExternalInput")
    pT_d = P.dram("c3_pT", [256, C3_NT], F32, "ExternalInput")
    gt_d = P.dram("c3_gate", [1024, 1024], F32, "ExternalInput")
    pj_d = P.dram("c3_proj", [256, 1024], F32, "ExternalInput")
    pp_d = P.dram("c3_pp", [128, 8], F32, "ExternalInput")
    id_d = P.dram("c3_ident", [128, 128], BF16, "ExternalInput")
    gb_d = P.dram("c3_gB", [128, 1024], F32, "ExternalInput")
    if final:
        gf_d = P.dram("c3_gfin", [128, 1024], F32, "ExternalInput")
    o_d = P.dram("c3_out", [C3_NT, 1024], F32, "ExternalOutput")

    def sb(name, shape, dt=F32):
        return P.sbuf("c3s_" + name, shape, dt)
    gtb = sb("gtb", [128, 8, 1024], BF16)
    pjb = sb("pjb", [128, 2, 1024], BF16)
    gB = sb("gB", [128, 1024])
    wst = [sb("wst%d" % i, [128, C3_NT]) for i in range(2)]
    pp = sb("pp", [128, 8])
    ident = sb("ident", [128, 128], BF16)
    epsc = sb("epsc", [128, 1])
    pTb = sb("pTb", [128, 2, C3_NT], BF16)
    hres = [sb("hres%d" % i, [128, 1024]) for i in range(2)]
    hnb = [sb("hnb%d" % i, [128, 1024], BF16) for i in range(2)]
    st = [sb("st%d" % i, [128, 4]) for i in range(2)]
    st2 = [sb("st2%d" % i, [128, 4]) for i in range(2)]
    junk = sb("junk", [128, 1024])
    hnT = [sb("hnT%d" % i, [128, 8, 128], BF16) for i in range(2)]
    sig = [sb("sig%d" % i, [128, 1024]) for i in range(2)]
    gfin = sb("gfin", [128, 1024]) if final else None
    pst = P.psum("c3_pst", [128, 1024], BF16)
    pg = [P.psum("c3_pg%d" % i, [128, 512], F32) for i in range(4)]
    pq = [P.psum("c3_pq%d" % i, [128, 512], F32) for i in range(2)]

    P.dma("sp", pp, pp[:], pp_d, pp_d[:])
    P.dma("sp", ident, ident[:], id_d, id_d[:])
    if final:
        P.dma("sp", gfin, gfin[:], gf_d, gf_d[:])
    P.op("dve", lambda e: e.memset(epsc[:], 1e-6), [], [epsc])
    P.dma("sp", gB, gB[:], gb_d, gb_d[:])
    wi = 0
    for k in range(8):
        s_ = wst[wi % 2]; wi += 1
        P.dma(("sp", "act")[wi % 2], s_, s_[:, 0:1024], gt_d, gt_d[k * 128:(k + 1) * 128, :])
        P.op("pool", lambda e: e.tensor_copy(out=gtb[:, k, :], in_=s_[:, 0:1024]), [s_], [gtb])
    for k in range(2):
        s_ = wst[wi % 2]; wi += 1
        P.dma(("sp", "act")[wi % 2], s_, s_[:, 0:1024], pj_d, pj_d[k * 128:(k + 1) * 128, :])
        P.op("pool", lambda e: e.tensor_copy(out=pjb[:, k, :], in_=s_[:, 0:1024]), [s_], [pjb])
    for c in range(2):
        s_ = wst[wi % 2]; wi += 1
        P.dma(("sp", "act")[wi % 2], s_, s_[:], pT_d, pT_d[c * 128:(c + 1) * 128, :])
        P.op("pool", lambda e: e.tensor_copy(out=pTb[:, c, :], in_=s_[:]), [s_], [pTb])

    for t in range(C3_NT // 128):
        i = t % 2
        r0 = t * 128
        hr = hres[i]; hT = hnT[i]; sg = sig[i]
        norm_transpose(P, h_d, h_d[r0:r0 + 128, :], hr, hnb[i], st[i], junk, epsc, ident, pst, hT, 0, gB=gB)
        for half in range(2):
            p_g = pg[(2 * t + half) % 4]; p_q = pq[half]
            for k in range(8):
                P.op("pe", lambda e: e.matmul(p_g[:], lhsT=hT[:, k, :], rhs=gtb[:, k, half * 512:(half + 1) * 512],
                                              start=(k == 0), stop=(k == 7)), [hT, gtb], [p_g])
            for c in range(2):
                P.op("pe", lambda e: e.matmul(p_q[:], lhsT=pTb[:, c, r0:r0 + 128], rhs=pjb[:, c, half * 512:(half + 1) * 512],
                                              start=(c == 0), stop=(c == 1)), [pTb, pjb], [p_q])
            hs = slice(half * 512, (half + 1) * 512)
            P.op("act", lambda e: e.activation(out=sg[:, hs], in_=p_g[:], func=AF.Sigmoid), [p_g], [sg])
            P.op("dve", lambda e: e.tensor_tensor(out=sg[:, hs], in0=sg[:, hs], in1=p_q[:], op=ALU.mult), [sg, p_q], [sg])
            P.op("pool", lambda e: e.tensor_tensor(out=hr[:, hs], in0=hr[:, hs], in1=sg[:, hs], op=ALU.add), [hr, sg], [hr])
        if final:
            s2 = st2[i]
            P.op("act", lambda e: e.activation(out=junk[:], in_=hr[:], func=AF.Square, accum_out=s2[:, 0:1]), [hr], [junk, s2])
            P.op("act", lambda e: e.activation(out=s2[:, 1:2], in_=s2[:, 0:1], func=AF.Sqrt, scale=1.0 / 1024, bias=epsc[:, 0:1]),
                 [s2, epsc], [s2])
            P.op("dve", lambda e: e.reciprocal(out=s2[:, 2:3], in_=s2[:, 1:2]), [s2], [s2])
            P.op("dve", lambda e: e.scalar_tensor_tensor(out=hr[:], in0=hr[:], scalar=s2[:, 2:3], in1=gfin[:], op0=ALU.mult, op1=ALU.mult),
                 [hr, s2, gfin], [hr])
        P.dma("act", o_d, o_d[r0:r0 + 128, :], hr, hr[:], owner=hr, disjoint=True)
    return [o_d]


def c3_inputs(layer, inp, core, hffn_full, final):
    l = layer
    fmj = lambda v: np.ascontiguousarray(v.reshape(-1, 128).T)
    m = {"c3_h": np.ascontiguousarray(hffn_full[core * C3_NT:(core + 1) * C3_NT]),
         "c3_pT": np.ascontiguousarray(inp['p'][l, 0, core * C3_NT:(core + 1) * C3_NT, :].T),
         "c3_gate": np.ascontiguousarray(inp['ple_gate'][l]), "c3_proj": np.ascontiguousarray(inp['ple_proj'][l]),
         "c3_pp": fmj(inp['norm_ple_g'][l]), "c3_ident": np.eye(128).astype(ml_dtypes.bfloat16),
         "c3_gB": np.ascontiguousarray(np.broadcast_to(inp['norm_ple_g'][l][None, :], (128, 1024))).astype(np.float32)}
    if final:
        m["c3_gfin"] = np.ascontiguousarray(np.broadcast_to(inp['final_norm_g'][None, :], (128, 1024))).astype(np.float32)
    return m


def _launch(build, maps):
    nc = bass.Bass("TRN2", target_bir_lowering=False)
    P = Prog(nc)
    outs = build(P)
    P.final_wait("sp", outs)
    P.emit()
    res = run_bass_kernel_spmd(nc, maps, core_ids=list(range(8)))
    return res.results


def kernel(**inputs):
    inp = {k: np.asarray(v) for k, v in inputs.items()}
    S = 16384
    h = np.ascontiguousarray(inp['x'][0], dtype=np.float32)
    vfirst = None
    for l in range(2):
        nc, P = build_A(l)
        res = run_bass_kernel_spmd(nc, host_inputs_A(l, inp, h, vfirst), core_ids=list(range(8))).results
        fm = np.concatenate([r["fm"] for r in res], axis=1)
        tm = np.concatenate([r["tm"] for r in res], axis=0)
        del res
        if l == 0:
            vfirst = np.ascontiguousarray(fm[FM_V:FM_V + 256])
        res = _launch(emit_sw, [sw_inputs(c, fm[FM_BQ:FM_BQ + 256], fm[FM_BK:FM_BK + 256], tm[:, 512:768]) for c in range(8)])
        cf = np.empty((CF_ROWS, S), np.float32)
        for c in range(8):
            hd, s = c // 2, c % 2
            cf[CF_SWO + hd * 64:CF_SWO + (hd + 1) * 64, s * NOWN:(s + 1) * NOWN] = res[c]["sw_o"]
        res = _launch(emit_hgrn, [hg_inputs(c, fm[FM_QS:FM_QS + 512], fm[FM_LF:FM_LF + 512], tm[:, 0:512]) for c in range(8)])
        for c in range(8):
            hd, vh = c // 2, c % 2
            cf[CF_HGO + hd * 128 + vh * 64:CF_HGO + hd * 128 + (vh + 1) * 64] = res[c]["hg_o"]
        fmr = {"r": fm[FM_R:FM_R + 256], "kp": fm[FM_KP:FM_KP + 256], "kk": fm[FM_KK:FM_KK + 256],
               "a": fm[FM_A:FM_A + 256], "ld": fm[FM_LD:FM_LD + 256], "v": fm[FM_V:FM_V + 256]}
        res = _launch(emit_rwkv, [rw_inputs(c, fmr) for c in range(8)])
        for c in range(8):
            hd, vh = c // 2, c % 2
            cf[CF_RWY + hd * 64 + vh * 32:CF_RWY + hd * 64 + (vh + 1) * 32] = res[c]["rw_y"]
        cf[CF_GS:CF_GS + 512] = fm[FM_GS:FM_GS + 512]
        cf[CF_R:CF_R + 256] = fm[FM_R:FM_R + 256]
        cf[CF_KP:CF_KP + 256] = fm[FM_KP:FM_KP + 256]
        cf[CF_V:CF_V + 256] = fm[FM_V:FM_V + 256]
        cf[CF_G:CF_G + 256] = fm[FM_G:FM_G + 256]
        del fm, tm, fmr
        res = _launch(lambda P: emit_c1(P, l), [c1_inputs(l, inp, c, h, cf) for c in range(8)])
        hmid = np.concatenate([r["c1_hmid"] for r in res], axis=0)
        del cf
        res = _launch(emit_c2, [c2_inputs(l, inp, c, hmid) for c in range(8)])
        hffn = np.concatenate([r["c2_out"] for r in res], axis=0)
        final = (l == 1)
        res = _launch(lambda P: emit_c3(P, final), [c3_inputs(l, inp, c, hffn, final) for c in range(8)])
        h = np.concatenate([r["c3_out"] for r in res], axis=0)
    return h[None].astype(np.float32)
```
